# Optimizing a Trainium2 kernel written in Bass

```python
import jax, jax.numpy as jnp
from jax import lax
import numpy as np

D_MODEL = 1024
BATCH = 8
SEQ = 2048
DEPTH = 2

MEM_LEN = 256
EPS = 1e-6
MLA_HEADS = 8
QK_NOPE = 64
QK_ROPE = 32
QK_HEAD = QK_NOPE + QK_ROPE
V_HEAD = 64
Q_LORA = 256
KV_LORA = 128
ROPE_THETA = 10000.0
Q_BLOCK = 128
CONV_WIDTH = 512
CONV_K = 3
SG_WIDTH = 512
SG_GROUPS = 4
SG_CHUNK = 128
MEM_HEADS = 4
MEM_HEAD_DIM = 128
N_BRANCH = 4
BRANCH_WIDTH = 512

IN_SIZES = [Q_LORA, KV_LORA, QK_ROPE, 3 * CONV_WIDTH, 2 * SG_WIDTH,
            MEM_HEADS * MEM_HEAD_DIM, N_BRANCH * BRANCH_WIDTH, N_BRANCH * D_MODEL]
IN_WIDTH = sum(IN_SIZES)
IN_OFFSETS = [int(o) for o in np.cumsum(IN_SIZES)[:-1]]
NEG_INF = -1e30

kernel_name = "hybrid_gated_mla_conv_sgmlp_memory"


def rms_norm(x, g):
    x32 = x.astype(jnp.float32)
    y = x32 * lax.rsqrt(jnp.mean(x32 * x32, axis=-1, keepdims=True) + EPS)
    return (y * g.astype(jnp.float32)).astype(x.dtype)


def layer_norm(x, g, b):
    x32 = x.astype(jnp.float32)
    mu = jnp.mean(x32, axis=-1, keepdims=True)
    xc = x32 - mu
    y = xc * lax.rsqrt(jnp.mean(xc * xc, axis=-1, keepdims=True) + EPS)
    return (y * g.astype(jnp.float32) + b.astype(jnp.float32)).astype(x.dtype)


def apply_rope(x, pos):
    half = x.shape[-1] // 2
    inv_freq = ROPE_THETA ** (-jnp.arange(half, dtype=jnp.float32) / half)
    ang = pos.astype(jnp.float32)[..., None] * inv_freq
    cos = jnp.cos(ang)[:, :, None, :].astype(x.dtype)
    sin = jnp.sin(ang)[:, :, None, :].astype(x.dtype)
    x1, x2 = x[..., :half], x[..., half:]
    return jnp.concatenate([x1 * cos - x2 * sin, x1 * sin + x2 * cos], axis=-1)


def causal_block_attention(q, k, v, scale):
    S = q.shape[1]
    outs = []
    for i in range(S // Q_BLOCK):
        lo, hi = i * Q_BLOCK, (i + 1) * Q_BLOCK
        qb, kb, vb = q[:, lo:hi], k[:, :hi], v[:, :hi]
        s = jnp.einsum('bqhd,bkhd->bhqk', qb, kb).astype(jnp.float32) * scale
        mask = jnp.arange(hi)[None, :] <= (lo + jnp.arange(Q_BLOCK))[:, None]
        s = jnp.where(mask, s, NEG_INF)
        p = jax.nn.softmax(s, axis=-1).astype(vb.dtype)
        outs.append(jnp.einsum('bhqk,bkhd->bqhd', p, vb))
    return jnp.concatenate(outs, axis=1)


def mla_branch(c_q, c_kv, k_rope, pos, cq_g, ckv_g, w_uq, w_ukv, q_g, k_g):
    B, S, _ = c_q.shape
    q = (rms_norm(c_q, cq_g) @ w_uq).reshape(B, S, MLA_HEADS, QK_HEAD)
    kv = (rms_norm(c_kv, ckv_g) @ w_ukv).reshape(B, S, MLA_HEADS, QK_NOPE + V_HEAD)
    k_nope, v = kv[..., :QK_NOPE], kv[..., QK_NOPE:]
    k_r = jnp.broadcast_to(k_rope[:, :, None, :], (B, S, MLA_HEADS, QK_ROPE))
    k = jnp.concatenate([k_nope, k_r], axis=-1)
    q = rms_norm(q, q_g)
    k = rms_norm(k, k_g)
    q = jnp.concatenate([q[..., :QK_NOPE], apply_rope(q[..., QK_NOPE:], pos)], axis=-1)
    k = jnp.concatenate([k[..., :QK_NOPE], apply_rope(k[..., QK_NOPE:], pos)], axis=-1)
    o = causal_block_attention(q, k, v, QK_HEAD ** -0.5)
    return o.reshape(B, S, MLA_HEADS * V_HEAD)


def shortconv_branch(conv_in, conv_w, conv_b):
    b_gate, c_gate, xin = jnp.split(conv_in, 3, axis=-1)
    S = xin.shape[1]
    z = c_gate * xin
    zp = jnp.pad(z, ((0, 0), (CONV_K - 1, 0), (0, 0)))
    y = conv_b + conv_w[0] * zp[:, 0:S]
    for j in range(1, CONV_K):
        y = y + conv_w[j] * zp[:, j:j + S]
    return b_gate * y


def spatial_gating_branch(sg_in, ln_g, ln_b, w_s, b_s):
    u, v = jnp.split(sg_in, 2, axis=-1)
    B, S, _ = v.shape
    v = layer_norm(v, ln_g, ln_b)
    v = v.reshape(B, S // SG_CHUNK, SG_CHUNK, SG_GROUPS, SG_WIDTH // SG_GROUPS)
    tril = jnp.tril(jnp.ones((SG_CHUNK, SG_CHUNK), dtype=bool))
    w = jnp.where(tril[None], w_s, jnp.zeros((), w_s.dtype))
    mixed = jnp.einsum('gts,bcsgd->bctgd', w, v) + b_s.T[None, None, :, :, None]
    return u * mixed.reshape(B, S, SG_WIDTH)


def memory_branch(q_in, mem, mem_g, w_mem_kv, q_g, k_g):
    B, S, _ = q_in.shape
    M = mem.shape[1]
    q = rms_norm(q_in.reshape(B, S, MEM_HEADS, MEM_HEAD_DIM), q_g)
    kv = (rms_norm(mem, mem_g) @ w_mem_kv).reshape(B, M, 2, MEM_HEADS, MEM_HEAD_DIM)
    k = rms_norm(kv[:, :, 0], k_g)
    v = kv[:, :, 1]
    s = jnp.einsum('bqhd,bmhd->bhqm', q, k).astype(jnp.float32) * (MEM_HEAD_DIM ** -0.5)
    p = jax.nn.softmax(s, axis=-1).astype(v.dtype)
    o = jnp.einsum('bhqm,bmhd->bqhd', p, v)
    return o.reshape(B, S, MEM_HEADS * MEM_HEAD_DIM)


def hybrid_layer(x, mem, pos, norm_g, w_in, cq_g, ckv_g, w_uq, w_ukv, mla_qg, mla_kg,
                 conv_w, conv_b, sg_ln_g, sg_ln_b, w_s, b_s, mem_g, w_mem_kv, mem_qg, mem_kg,
                 b_merge, w_branch, w_out):
    B, S, D = x.shape
    h = rms_norm(x, norm_g)
    proj = h @ w_in
    c_q, c_kv, k_rope, conv_in, sg_in, mem_q, silu_gates, merge_logits = jnp.split(proj, IN_OFFSETS, axis=-1)
    y_a = mla_branch(c_q, c_kv, k_rope, pos, cq_g, ckv_g, w_uq, w_ukv, mla_qg, mla_kg)
    y_b = shortconv_branch(conv_in, conv_w, conv_b)
    y_c = spatial_gating_branch(sg_in, sg_ln_g, sg_ln_b, w_s, b_s)
    y_d = memory_branch(mem_q, mem, mem_g, w_mem_kv, mem_qg, mem_kg)
    ys = jnp.stack([y_a, y_b, y_c, y_d], axis=2)
    ys = ys * jax.nn.silu(silu_gates.reshape(B, S, N_BRANCH, BRANCH_WIDTH))
    z = jnp.einsum('bsnw,nwd->bsnd', ys, w_branch)
    gate = jax.nn.sigmoid(merge_logits.reshape(B, S, N_BRANCH, D) + b_merge)
    merged = jnp.sum(gate * z, axis=2)
    return x + merged @ w_out


def setup_inputs(seed: int = 0) -> dict:
    key = jax.random.key(seed)
    ks = jax.random.split(key, 32)
    f32 = jnp.float32
    L = DEPTH

    def nrm(k, shape, scale):
        return jax.random.normal(k, shape, f32) * scale

    def gain(k, shape):
        return 1.0 + 0.05 * jax.random.normal(k, shape, f32)

    x = jax.random.normal(ks[0], (BATCH, SEQ, D_MODEL), f32)
    mem = jax.random.normal(ks[1], (BATCH, MEM_LEN, D_MODEL), f32)
    offset = jax.random.randint(ks[2], (BATCH, 1), 0, 1024, dtype=jnp.int32)
    positions = offset + jnp.arange(SEQ, dtype=jnp.int32)[None, :]
    return {
        "x": x,
        "mem": mem,
        "positions": positions,
        "norm_g": gain(ks[3], (L, D_MODEL)),
        "w_in": nrm(ks[4], (L, D_MODEL, IN_WIDTH), D_MODEL ** -0.5),
        "cq_norm_g": gain(ks[5], (L, Q_LORA)),
        "ckv_norm_g": gain(ks[6], (L, KV_LORA)),
        "w_uq": nrm(ks[7], (L, Q_LORA, MLA_HEADS * QK_HEAD), Q_LORA ** -0.5),
        "w_ukv": nrm(ks[8], (L, KV_LORA, MLA_HEADS * (QK_NOPE + V_HEAD)), KV_LORA ** -0.5),
        "mla_q_norm_g": gain(ks[9], (L, QK_HEAD)),
        "mla_k_norm_g": gain(ks[10], (L, QK_HEAD)),
        "conv_w": nrm(ks[11], (L, CONV_K, CONV_WIDTH), CONV_K ** -0.5),
        "conv_b": nrm(ks[12], (L, CONV_WIDTH), 0.02),
        "sg_ln_g": gain(ks[13], (L, SG_WIDTH)),
        "sg_ln_b": nrm(ks[14], (L, SG_WIDTH), 0.02),
        "w_spatial": nrm(ks[15], (L, SG_GROUPS, SG_CHUNK, SG_CHUNK), 0.5 * SG_CHUNK ** -0.5),
        "b_spatial": 1.0 + 0.1 * jax.random.normal(ks[16], (L, SG_GROUPS, SG_CHUNK), f32),
        "mem_norm_g": gain(ks[17], (L, D_MODEL)),
        "w_mem_kv": nrm(ks[18], (L, D_MODEL, 2 * MEM_HEADS * MEM_HEAD_DIM), D_MODEL ** -0.5),
        "mem_q_norm_g": gain(ks[19], (L, MEM_HEAD_DIM)),
        "mem_k_norm_g": gain(ks[20], (L, MEM_HEAD_DIM)),
        "b_merge": nrm(ks[21], (L, N_BRANCH, D_MODEL), 0.1),
        "w_branch": nrm(ks[22], (L, N_BRANCH, BRANCH_WIDTH, D_MODEL), BRANCH_WIDTH ** -0.5),
        "w_out": nrm(ks[23], (L, D_MODEL, D_MODEL), D_MODEL ** -0.5),
    }


def reference(x, mem, positions, norm_g, w_in, cq_norm_g, ckv_norm_g, w_uq, w_ukv,
              mla_q_norm_g, mla_k_norm_g, conv_w, conv_b, sg_ln_g, sg_ln_b, w_spatial,
              b_spatial, mem_norm_g, w_mem_kv, mem_q_norm_g, mem_k_norm_g, b_merge,
              w_branch, w_out):
    for l in range(DEPTH):
        x = hybrid_layer(x, mem, positions, norm_g[l], w_in[l], cq_norm_g[l], ckv_norm_g[l],
                         w_uq[l], w_ukv[l], mla_q_norm_g[l], mla_k_norm_g[l], conv_w[l], conv_b[l],
                         sg_ln_g[l], sg_ln_b[l], w_spatial[l], b_spatial[l], mem_norm_g[l],
                         w_mem_kv[l], mem_q_norm_g[l], mem_k_norm_g[l], b_merge[l],
                         w_branch[l], w_out[l])
    return x
```

```python
import contextlib
import math
import numpy as np
import concourse.bass as bass
import concourse.mybir as mybir
from concourse.bass_utils import run_bass_kernel_spmd

F32 = mybir.dt.float32
BF16 = mybir.dt.bfloat16
I32 = mybir.dt.int32
AF = mybir.ActivationFunctionType
ALU = mybir.AluOpType
AX = mybir.AxisListType

S = 2048
D = 1024
T = 512
NT = S // T
EPS = 1e-6
IN_W = 9632
NVF = 79
NVR = 576
NSLOT = 4


class Prog:
    CH = 8000

    def __init__(self, nc):
        self.nc = nc
        self.ops = []
        self.last_w = {}
        self.readers = {}
        self.dma_counts = {}

    def op(self, eng, fn, reads=(), writes=(), dsem=None):
        idx = len(self.ops)
        deps = set()
        for k in reads:
            if k in self.last_w:
                deps.add((self.last_w[k], "raw"))
            if isinstance(k, tuple) and k[0] == "ps":
                for r in self.readers.get(k, ()):
                    deps.add((r, "war"))
        for k in writes:
            if k in self.last_w:
                deps.add((self.last_w[k], "waw"))
            for r in self.readers.get(k, ()):
                deps.add((r, "war"))
        o = dict(idx=idx, eng=eng, fn=fn, deps=deps, dsem=dsem)
        if dsem is not None:
            self.dma_counts[dsem] = self.dma_counts.get(dsem, 0) + 1
            o["dcount"] = self.dma_counts[dsem]
        self.ops.append(o)
        for k in reads:
            self.readers.setdefault(k, []).append(idx)
        for k in writes:
            self.last_w[k] = idx
            self.readers[k] = []
        return idx

    def emit(self):
        nc = self.nc
        ops = self.ops
        need = set()
        for o in ops:
            w = []
            for (d, typ) in o["deps"]:
                p = ops[d]
                if p["dsem"] is not None:
                    w.append(("dma", p["dsem"], p["dcount"]))
                    continue
                if p["eng"] == o["eng"] and o["dsem"] is None:
                    if o["eng"] == "pe" or typ == "war":
                        continue
                need.add(d)
                w.append(("eng", p["eng"], d))
            o["waits"] = w
        ordc = {}
        for o in ops:
            if o["idx"] in need:
                e = o["eng"]
                ordc[e] = ordc.get(e, 0) + 1
                o["ord"] = ordc[e]
        with contextlib.ExitStack() as es:
            esem = {}
            for e, n in ordc.items():
                esem[e] = [es.enter_context(nc.semaphore("s_%s_%d" % (e, c)))
                           for c in range((n + self.CH - 1) // self.CH + 1)]
            dsem = {}
            for j, k in enumerate(self.dma_counts):
                dsem[k] = es.enter_context(nc.semaphore("d_%d" % j))
            block = es.enter_context(nc.Block())
            per = {}
            for o in ops:
                per.setdefault(o["eng"], []).append(o)

            def run(ename, e):
                waited_e = {}
                waited_d = {}
                for o in per.get(ename, []):
                    we = {}
                    wd = {}
                    for w in o["waits"]:
                        if w[0] == "eng":
                            oo = ops[w[2]]["ord"]
                            we[w[1]] = max(we.get(w[1], 0), oo)
                        else:
                            wd[w[1]] = max(wd.get(w[1], 0), w[2])
                    for pe_, oo in we.items():
                        if waited_e.get(pe_, 0) >= oo:
                            continue
                        waited_e[pe_] = oo
                        e.wait_ge(esem[pe_][(oo - 1) // self.CH], (oo - 1) % self.CH + 1)
                    for k, cnt in wd.items():
                        if waited_d.get(k, 0) >= cnt:
                            continue
                        waited_d[k] = cnt
                        e.wait_ge(dsem[k], 16 * cnt)
                    if o["fn"] is None:
                        continue
                    ins = o["fn"](e)
                    if o["dsem"] is not None:
                        ins.then_inc(dsem[o["dsem"]], 16)
                    elif "ord" in o:
                        oo = o["ord"]
                        ins.then_inc(esem[ename][(oo - 1) // self.CH], 1)

            @block.tensor
            def _(e):
                run("pe", e)

            @block.scalar
            def _(e):
                run("act", e)

            @block.vector
            def _(e):
                run("dve", e)

            @block.gpsimd
            def _(e):
                run("pool", e)

            @block.sync
            def _(e):
                run("sp", e)


class _Stop(Exception):
    pass


def build(n_layers=2, first_layer=0, stop=None):
    nc = bass.Bass("TRN2", target_bir_lowering=False)
    L = n_layers

    def din(name, shape, dt=F32):
        return nc.dram_tensor(name, shape, dt, kind="ExternalInput").ap()

    x_d = din("x", [S, D])
    mem_d = din("mem", [256, D])
    pos_d = din("pos", [16, 128], I32)
    vfm_d = din("vfm", [2, 128, NVF])
    vrow_d = din("vrow", [2, NVR])
    w_in_d = din("w_in", [2, D, IN_W])
    w_uq_d = din("w_uq", [2, 256, 768])
    w_ukv_d = din("w_ukv", [2, 128, 1024])
    w_sp_d = din("w_spatial", [2, 4, 128, 128])
    w_mkv_d = din("w_mem_kv", [2, D, 1024])
    w_br_d = din("w_branch", [2, 4, 512, D])
    w_out_d = din("w_out", [2, D, D])
    y_d = nc.dram_tensor("y", [S, D], F32, kind="ExternalOutput").ap()
    x1_d = nc.dram_tensor("x1s", [S, D], F32, kind="Internal").ap()

    P = Prog(nc)
    es = contextlib.ExitStack()

    def sb(name, shape, dt):
        return es.enter_context(nc.sbuf_tensor(name, shape, dt))

    xwb = [sb("xw%d" % j, [128, D], F32) for j in range(2)]
    KT = sb("KT", [128, 8, S], BF16)
    Vaug = sb("Vaug", [128, 16, 8, 65], BF16)
    hTb = [sb("hT%d" % j, [128, 8, T], BF16) for j in range(2)]
    QT = sb("QT", [128, 8, T], BF16)
    ys = sb("ys", [128, 4, T], BF16)
    acc = sb("acc", [128, 8, T], BF16)
    wslot = [sb("wslot%d" % j, [128, 4096], BF16) for j in range(NSLOT)]
    vfm = sb("vfm_sb", [128, NVF], F32)
    vrow = sb("vrow_sb", [128, NVR], F32)
    hbm = sb("hbm", [128, 32], F32)
    wuq = sb("wuq", [128, 2, 768], BF16)
    wukv = sb("wukv", [128, 1024], BF16)
    wTs = sb("wTs", [128, 4, 128], BF16)
    Cg = sb("Cg", [128, 4, 128], F32)
    nb4 = sb("nb4", [128, 4], F32)
    ones_bf = sb("ones_bf", [128, 128], BF16)
    ident = sb("ident", [128, 128], BF16)
    identf = sb("identf", [128, 128], F32)
    tri = sb("tri", [128, 128], BF16)
    mhalf = sb("mhalf", [128, 32], F32)
    invf = sb("invf", [128, 16], F32)
    post = sb("post", [128, 16], F32)
    cosT = sb("cosT", [128, 16, 16], F32)
    sinT = sb("sinT", [128, 16, 16], F32)
    KmT = sb("KmT", [128, 4, 256], BF16)
    Vm = sb("Vm", [128, 2, 4, 128], BF16)
    QmT = sb("QmT", [128, 4, T], BF16)
    cqT = sb("cqT", [128, 3, T], BF16)
    ya = sb("ya", [128, 4, 512], BF16)
    yall = sb("yall", [128, 4, 512], BF16)
    zc = sb("zc", [128, 4, 516], BF16)
    junk = sb("junk", [128, 1024], BF16)
    kr = sb("kr", [128, 4, 32], F32)
    ssx = sb("ssx", [128, 4], F32)
    rx = sb("rx", [128, 4], F32)
    ss3 = sb("ss3", [128, 12], F32)
    r2 = sb("r2", [128, 2], F32)
    r2w = sb("r2w", [128, 8], F32)
    ssq = sb("ssq", [128, 8], F32)
    rq = sb("rq", [128, 8], F32)
    ssk = sb("ssk", [128, 8], F32)
    rk = sb("rk", [128, 8], F32)
    rtmp = sb("rtmp", [128, 8], F32)
    rtmpA = sb("rtmpA", [128, 16], F32)
    ssqk = sb("ssqk", [128, 16], F32)
    rqk = sb("rqk", [128, 16], F32)
    rtmpB = sb("rtmpB", [128, 8], F32)
    wAs = sb("wAs", [128, 8 * 416], BF16)
    bA = [sb("bA_%d" % j, [128, 512], BF16) for j in range(2)]
    st6 = sb("st6", [128, 6], F32)
    st6s = sb("st6s", [128, 4, 6], F32)
    mvs = sb("mvs", [128, 4, 2], F32)
    rs4 = sb("rs4", [128, 4], F32)
    ss16 = sb("ss16", [128, 16], F32)
    r16 = sb("r16", [128, 16], F32)
    rtmp16 = sb("rtmp16", [128, 16], F32)
    mv = sb("mv", [128, 2], F32)
    rs1 = sb("rs1", [128, 1], F32)
    ss4 = sb("ss4", [128, 4], F32)
    r4 = sb("r4", [128, 4], F32)
    rec = sb("rec", [128, 4], F32)
    ssm = sb("ssm", [128, 2], F32)
    rm = sb("rm", [128, 2], F32)
    NF1 = 2
    f1024 = [sb("f1024_%d" % j, [128, 1024], F32) for j in range(NF1)]
    NF5 = 4
    f512 = [sb("f512_%d" % j, [128, 512], F32) for j in range(NF5)]
    NB5 = 6
    b512 = [sb("b512_%d" % j, [128, 512], BF16) for j in range(NB5)]
    NB1 = 3
    b1024 = [sb("b1024_%d" % j, [128, 1024], BF16) for j in range(NB1)]
    NR = 0
    f256 = [sb("f256_%d" % j, [128, 256], F32) for j in range(NR)]
    f128 = []
    fqk = [sb("fqk_%d" % j, [128, 512], F32) for j in range(2)]
    ps = [es.enter_context(nc.psum_tensor("ps%d" % j, [128, 512], F32)) for j in range(8)]

    class Rot:
        def __init__(self, items, key):
            self.items = items
            self.key = key
            self.i = 0

        def next(self):
            j = self.i % len(self.items)
            self.i += 1
            return self.items[j], (self.key, j)

    wspf = f512[0][:].rearrange("p (g s) -> p g s", g=4)
    posi = f512[1][0:16, 0:128].bitcast(I32)
    posf = f512[2][0:16, 0:128]
    ones_f = f512[3][:, 0:128]
    angt = f512[0][:, 0:256]
    kkt = f512[1][:, 0:256]
    kit = f512[2][:, 0:256].bitcast(I32)
    redt = f512[3][:, 0:256]
    memnT = yall[:].rearrange("p c t -> p (c t)").rearrange("p (k m) -> p k m", k=8)
    vln = ya
    RbA = Rot(bA, "bA")
    wAv = wAs[:].rearrange("p (k n) -> p k n", k=8)
    Rxw = Rot(xwb, "xw")
    Rqk = Rot(fqk, "fqk")
    Rf1 = Rot(f1024, "f1024")
    ssqk2 = sb("ssqk2", [128, 16], F32)
    rqk2 = sb("rqk2", [128, 16], F32)
    rtmpA2 = sb("rtmpA2", [128, 16], F32)

    class RotK:
        def __init__(self, items):
            self.items = items
            self.i = 0

        def next(self):
            j = self.i % len(self.items)
            self.i += 1
            return self.items[j]

    Rf5 = Rot(f512, "f512")
    Rb5 = Rot(b512, "b512")
    Rb1 = Rot(b1024, "b1024")
    Rf2 = Rot(f256, "f256")
    Rf128 = Rot(f128, "f128")

    class PsRot:
        def __init__(self, banks):
            self.banks = banks
            self.i = 0

        def next(self):
            b = self.banks[self.i % len(self.banks)]
            self.i += 1
            return ps[b], ("ps", b)

    PG = PsRot([0, 1, 2, 3])
    PGA = PsRot([0, 1])
    PW = PsRot([4, 5, 6, 7])
    PGM = PsRot([2, 3, 4])
    PGB = PsRot([2, 3, 4, 5])
    cur = {"PG": PG}
    A_RATIO = 1.0
    PS_ = PsRot([2, 3, 4, 5])
    ATT_SKEW = 3
    PO = PsRot([6, 7])

    def bfv(p):
        return p[:].bitcast(BF16)

    RES_A = dict(pg=PGA, f1=Rf1, b1=Rb1, qk=Rqk, ssqk=ssqk, rqk=rqk, rtmp=rtmpA, n="")
    RES_B = dict(pg=PsRot([2, 3]),
                 f1=RotK([(xwb[0], ("xw", 0)), (xwb[1], ("xw", 1))]),
                 b1=RotK([(b1024[2], ("b1024", 2)), (junk, "junk")]),
                 qk=RotK([(f512[0], ("f512", 0)), (f512[1], ("f512", 1))]),
                 ssqk=ssqk2, rqk=rqk2, rtmp=rtmpA2, n="B")
    RES_A0 = dict(RES_A, b1=RotK([(b1024[0], ("b1024", 0)), (b1024[1], ("b1024", 1))]))

    sched = []
    for l in range(L):
        ll = l + first_layer
        sched.append(("memK", w_mkv_d[ll][:, 0:512], 8, 512))
        sched.append(("memV", w_mkv_d[ll][:, 512:1024], 8, 512))
        for i in range(NT):
            def wi(a, b_):
                return w_in_d[ll][:, a:b_]
            sched.append(("gate0", wi(3488, 4000), 8, 512))
            for n in range(4):
                if n == 1:
                    sched.append(("conv_c", wi(928, 1440), 8, 512))
                    sched.append(("conv_x", wi(1440, 1952), 8, 512))
                    sched.append(("conv_b", wi(416, 928), 8, 512))
                    sched.append(("gate1", wi(4000, 4512), 8, 512))
                if n == 2:
                    sched.append(("sg_v", wi(2464, 2976), 8, 512))
                    sched.append(("gate2", wi(4512, 5024), 8, 512))
                    sched.append(("sg_u", wi(1952, 2464), 8, 512))
                if n == 3:
                    sched.append(("memq", wi(2976, 3488), 8, 512))
                    sched.append(("gate3", wi(5024, 5536), 8, 512))
                for hf in range(2):
                    c0_ = 5536 + n * 1024 + hf * 512
                    sched.append(("m%d_%d" % (n, hf), wi(c0_, c0_ + 512), 8, 512))
                    sched.append(("wb%d_%d" % (n, hf), w_br_d[ll, n][:, hf * 512:(hf + 1) * 512], 4, 512))
            sched.append(("wo0", w_out_d[ll][:, 0:512], 8, 512))
            sched.append(("wo1", w_out_d[ll][:, 512:1024], 8, 512))

    class WStream:
        def __init__(self):
            self.issued = 0
            self.cur = 0

        def _issue(self):
            c = self.issued
            if c >= len(sched):
                return
            name, src, nk, ncol = sched[c]
            s = c % NSLOT
            view = wslot[s][:, 0:nk * ncol].rearrange("p (k n) -> p k n", k=nk)
            srcv = src.rearrange("(k p) n -> p k n", p=128)
            P.op("pool", lambda e, view=view, srcv=srcv: e.dma_start(out=view, in_=srcv),
                 writes=[("w", s)], dsem=("w", s))
            self.issued += 1

        def next(self, name):
            while self.issued < min(len(sched), self.cur + NSLOT - 1):
                self._issue()
            n2, src, nk, ncol = sched[self.cur]
            assert n2 == name, (n2, name)
            s = self.cur % NSLOT
            self.cur += 1
            view = wslot[s][:, 0:nk * ncol].rearrange("p (k n) -> p k n", k=nk)
            return view, ("w", s)

    W = WStream()

    def rsqrt(out_ap, in_ap, scale, n, rkeys, wkeys, tmp=None, tmpk="rtmp"):
        tmp = rtmp if tmp is None else tmp
        P.op("pool", lambda e: e.tensor_scalar(out=tmp[:, 0:n], in0=in_ap, scalar1=scale, scalar2=EPS,
                                               op0=ALU.mult, op1=ALU.add), reads=rkeys, writes=[tmpk])
        P.op("pool", lambda e: e.tensor_tensor(out=out_ap, in0=tmp[:, 0:n], in1=mhalf[:, 0:n], op=ALU.pow),
             reads=[tmpk, "mhalf"], writes=wkeys)

    def rsqrtA(out_ap, in_ap, scale, n, rkeys, wkeys):
        rsqrt(out_ap, in_ap, scale, n, rkeys, wkeys, tmp=rtmpA, tmpk="rtmpA")

    def mm_group(out_ap, pairs, reads, pskey):
        def fn(e):
            n = len(pairs)
            for j, (a, b_) in enumerate(pairs):
                ins = e.matmul(out_ap, a, b_, start=(j == 0), stop=(j == n - 1))
            return ins
        P.op("pe", fn, reads=reads, writes=[pskey])

    def transposes(pairs, reads, pskey, idt):
        def fn(e):
            for (o_, i_) in pairs:
                ins = e.transpose(out=o_, in_=i_, identity=idt)
            return ins
        P.op("pe", fn, reads=reads + ["ident"], writes=[pskey])

    final_keys = []
    dbg_n = [0]
    dbg_off = [0]

    def checkpoint(name, dumps):
        if stop != name:
            return
        yv = y_d.rearrange("(p a) d -> p (a d)", p=128)
        for (ap, n, keys) in dumps:
            for c0 in range(0, n, 1024):
                w = min(1024, n - c0)
                stg, stgk = Rf1.next()
                off = dbg_off[0]
                idx = dbg_n[0]
                P.op("dve", lambda e, stg=stg, ap=ap, c0=c0, w=w: e.tensor_copy(out=stg[:, 0:w], in_=ap[:, c0:c0 + w]),
                     reads=keys, writes=[stgk])
                P.op("sp", lambda e, stg=stg, off=off, w=w: e.dma_start(out=yv[:, off:off + w], in_=stg[:, 0:w]),
                     reads=[stgk], writes=[("ydbg", idx)], dsem=("dbg", idx))
                final_keys.append(("ydbg", idx))
                dbg_off[0] += w
                dbg_n[0] += 1
        raise _Stop()

    P.op("pool", lambda e: e.memset(ones_bf[:], 1.0), writes=["ones_bf"])
    P.op("pool", lambda e: e.memset(ones_f[:], 1.0), writes=[("f512", 3)])
    P.op("pool", lambda e: e.memset(mhalf[:], -0.5), writes=["mhalf"])
    P.op("pool", lambda e: e.affine_select(out=ident[:], in_=ones_bf[:], pattern=[[1, 128]],
                                           compare_op=ALU.is_equal, fill=0.0, base=0, channel_multiplier=-1),
         reads=["ones_bf"], writes=["ident"])
    P.op("pool", lambda e: e.affine_select(out=identf[:], in_=ones_f[:], pattern=[[1, 128]],
                                           compare_op=ALU.is_equal, fill=0.0, base=0, channel_multiplier=-1),
         reads=[("f512", 3)], writes=["identf"])
    P.op("pool", lambda e: e.affine_select(out=tri[:], in_=ones_bf[:], pattern=[[1, 128]],
                                           compare_op=ALU.is_ge, fill=0.0, base=0, channel_multiplier=-1),
         reads=["ones_bf"], writes=["tri"])
    P.op("pool", lambda e: e.memset(Vaug[:].rearrange("p a b c -> p (a b c)"), 1.0), writes=[("V", b_) for b_ in range(16)])
    inv = np.power(np.float32(10000.0), -np.arange(16, dtype=np.float32) / np.float32(16)).astype(np.float32)
    for j in range(16):
        P.op("pool", lambda e, j=j: e.memset(invf[:, j:j + 1], float(inv[j])), writes=["invf"])
    P.op("sp", lambda e: e.dma_start(out=posi[:], in_=pos_d), writes=[("f512", 1)], dsem=("f512", 1))
    P.op("dve", lambda e: e.tensor_copy(out=posf[:], in_=posi[:]), reads=[("f512", 1)], writes=[("f512", 2)])
    P.op("pe", lambda e: e.transpose(out=ps[0][:, 0:16], in_=posf[:], identity=identf[0:16, 0:16]),
         reads=[("f512", 2), "identf"], writes=[("ps", 0)])
    P.op("dve", lambda e: e.tensor_copy(out=post[:], in_=ps[0][:, 0:16]), reads=[("ps", 0)], writes=["post"])
    for s_ in range(16):
        P.op("dve", lambda e, s_=s_: e.tensor_scalar(out=angt[:, s_ * 16:(s_ + 1) * 16], in0=invf[:],
                                                      scalar1=post[:, s_:s_ + 1], scalar2=None, op0=ALU.mult),
             reads=["post", "invf"], writes=[("f512", 0)])
    C1 = 6.28125
    C2 = 2 * math.pi - 6.28125
    P.op("dve", lambda e: e.tensor_scalar(out=kkt[:], in0=angt[:], scalar1=1.0 / (2 * math.pi), scalar2=None,
                                          op0=ALU.mult), reads=[("f512", 0)], writes=[("f512", 1)])
    P.op("dve", lambda e: e.tensor_copy(out=kit[:], in_=kkt[:]), reads=[("f512", 1)], writes=[("f512", 2)])
    P.op("dve", lambda e: e.tensor_copy(out=kkt[:], in_=kit[:]), reads=[("f512", 2)], writes=[("f512", 1)])
    P.op("dve", lambda e: e.scalar_tensor_tensor(out=redt[:], in0=kkt[:], scalar=-C1, in1=angt[:],
                                                 op0=ALU.mult, op1=ALU.add), reads=[("f512", 1), ("f512", 0)], writes=[("f512", 3)])
    P.op("dve", lambda e: e.scalar_tensor_tensor(out=redt[:], in0=kkt[:], scalar=-C2, in1=redt[:],
                                                 op0=ALU.mult, op1=ALU.add), reads=[("f512", 1), ("f512", 3)], writes=[("f512", 3)])

    def wrap():
        P.op("dve", lambda e: e.tensor_scalar(out=kkt[:], in0=redt[:], scalar1=math.pi, scalar2=-2 * math.pi,
                                              op0=ALU.is_gt, op1=ALU.mult), reads=[("f512", 3)], writes=[("f512", 1)])
        P.op("dve", lambda e: e.tensor_tensor(out=redt[:], in0=redt[:], in1=kkt[:], op=ALU.add),
             reads=[("f512", 3), ("f512", 1)], writes=[("f512", 3)])
        P.op("dve", lambda e: e.tensor_scalar(out=kkt[:], in0=redt[:], scalar1=-math.pi, scalar2=2 * math.pi,
                                              op0=ALU.is_lt, op1=ALU.mult), reads=[("f512", 3)], writes=[("f512", 1)])
        P.op("dve", lambda e: e.tensor_tensor(out=redt[:], in0=redt[:], in1=kkt[:], op=ALU.add),
             reads=[("f512", 3), ("f512", 1)], writes=[("f512", 3)])
        P.op("dve", lambda e: e.tensor_scalar(out=redt[:], in0=redt[:], scalar1=math.pi, scalar2=-math.pi,
                                              op0=ALU.min, op1=ALU.max), reads=[("f512", 3)], writes=[("f512", 3)])

    wrap()
    P.op("act", lambda e: e.activation(out=sinT[:].rearrange("p a b -> p (a b)"), in_=redt[:], func=AF.Sin),
         reads=[("f512", 3)], writes=["sinT"])
    P.op("dve", lambda e: e.tensor_scalar(out=redt[:], in0=redt[:], scalar1=math.pi / 2, scalar2=None,
                                          op0=ALU.add), reads=[("f512", 3), "sinT"], writes=[("f512", 3)])
    wrap()
    P.op("act", lambda e: e.activation(out=cosT[:].rearrange("p a b -> p (a b)"), in_=redt[:], func=AF.Sin),
         reads=[("f512", 3)], writes=["cosT"])

    done_ = False
    try:
        checkpoint("setup", [(cosT[:].rearrange("p a b -> p (a b)"), 256, ["cosT"]),
                             (sinT[:].rearrange("p a b -> p (a b)"), 256, ["sinT"]),
                             (post[:], 16, ["post"]), (tri[:], 128, ["tri"]), (ident[:], 128, ["ident"])])
    except _Stop:
        done_ = True
    def rope(src, dst, sg, skey, dkey):
        cosb = cosT[:, sg, :].unsqueeze(1).broadcast_to([128, 8, 16])
        sinb = sinT[:, sg, :].unsqueeze(1).broadcast_to([128, 8, 16])
        t1, k1 = Rf128.next()
        t2, k2 = Rf128.next()
        t1v = t1[:].rearrange("p (h d) -> p h d", h=8)
        t2v = t2[:].rearrange("p (h d) -> p h d", h=8)
        P.op("dve", lambda e: e.tensor_tensor(out=t1v, in0=src[:, :, 0:16], in1=cosb, op=ALU.mult),
             reads=[skey, "cosT"], writes=[k1])
        P.op("dve", lambda e: e.tensor_tensor(out=t2v, in0=src[:, :, 16:32], in1=sinb, op=ALU.mult),
             reads=[skey, "sinT"], writes=[k2])
        P.op("dve", lambda e: e.tensor_tensor(out=dst[:, :, 0:16], in0=t1v, in1=t2v, op=ALU.subtract),
             reads=[k1, k2], writes=[dkey])
        t3, k3 = Rf128.next()
        t4, k4 = Rf128.next()
        t3v = t3[:].rearrange("p (h d) -> p h d", h=8)
        t4v = t4[:].rearrange("p (h d) -> p h d", h=8)
        P.op("dve", lambda e: e.tensor_tensor(out=t3v, in0=src[:, :, 0:16], in1=sinb, op=ALU.mult),
             reads=[skey, "sinT"], writes=[k3])
        P.op("dve", lambda e: e.tensor_tensor(out=t4v, in0=src[:, :, 16:32], in1=cosb, op=ALU.mult),
             reads=[skey, "cosT"], writes=[k4])
        P.op("dve", lambda e: e.tensor_tensor(out=dst[:, :, 16:32], in0=t3v, in1=t4v, op=ALU.add),
             reads=[k3, k4], writes=[dkey])

    YALLK = [("yall", c) for c in range(4)]

    def gate_chunk(wg, wgk, c, ysrc_ap, ysrc_keys):
        pg, pgk = cur["PG"].next()
        mm_group(pg[:], [(wg[:, k, c * 128:(c + 1) * 128], cur["hT"][:, k, :]) for k in range(8)],
                 [wgk] + cur["hTk"], pgk)
        tg, tgk = Rb5.next()
        P.op("act", lambda e: e.activation(out=tg[:], in_=pg[:], func=AF.Tanh, scale=0.5),
             reads=[pgk], writes=[tgk])
        u, uk = Rb5.next()
        P.op("dve", lambda e: e.scalar_tensor_tensor(out=u[:], in0=tg[:], scalar=1.0, in1=pg[:],
                                                     op0=ALU.add, op1=ALU.mult), reads=[tgk, pgk], writes=[uk])
        P.op("dve", lambda e: e.tensor_tensor(out=ys[:, c, :], in0=u[:], in1=ysrc_ap, op=ALU.mult),
             reads=[uk] + ysrc_keys, writes=[("ys", c)])

    def gate_pre(wg, wgk, c):
        pg, pgk = cur["PG"].next()
        mm_group(pg[:], [(wg[:, k, c * 128:(c + 1) * 128], cur["hT"][:, k, :]) for k in range(8)],
                 [wgk] + cur["hTk"], pgk)
        tg, tgk = Rb5.next()
        P.op("act", lambda e: e.activation(out=tg[:], in_=pg[:], func=AF.Tanh, scale=0.5),
             reads=[pgk], writes=[tgk])
        P.op("dve", lambda e: e.scalar_tensor_tensor(out=ys[:, c, :], in0=tg[:], scalar=1.0, in1=pg[:],
                                                     op0=ALU.add, op1=ALU.mult), reads=[tgk, pgk],
             writes=[("ys", c)])

    def gate_post(c, ysrc_ap, ysrc_keys):
        P.op("dve", lambda e: e.tensor_tensor(out=ys[:, c, :], in0=ys[:, c, :], in1=ysrc_ap, op=ALU.mult),
             reads=[("ys", c)] + ysrc_keys, writes=[("ys", c)])

    def tm_post(ytm, ykey):
        for c in range(4):
            pt, ptk = cur["PG"].next()
            ptb = bfv(pt)
            transposes([(ptb[:, qs * 128:(qs + 1) * 128], ytm[:, qs, c * 128:(c + 1) * 128]) for qs in range(4)],
                       [ykey], ptk, ident[:])
            gate_post(c, ptb[:, 0:512], [ptk])
            yield

    def merge_branch(n, first):
        for hf in range(2):
            wm, wmk = W.next("m%d_%d" % (n, hf))
            wb, wbk = W.next("wb%d_%d" % (n, hf))
            for c4 in range(4):
                dc = hf * 4 + c4
                pl, plk = cur["PG"].next()
                mm_group(pl[:], [(wm[:, k, c4 * 128:(c4 + 1) * 128], cur["hT"][:, k, :]) for k in range(8)],
                         [wmk] + cur["hTk"], plk)
                tm, tmk = Rb5.next()
                P.op("act", lambda e, pl=pl, tm=tm, dc=dc: e.activation(out=tm[:], in_=pl[:], func=AF.Tanh,
                                                                       bias=hbm[:, n * 8 + dc:n * 8 + dc + 1],
                                                                       scale=0.5),
                     reads=[plk, "hbm"], writes=[tmk])
                yield
                pz, pzk = cur["PG"].next()
                mm_group(pz[:], [(wb[:, kc, c4 * 128:(c4 + 1) * 128], ys[:, kc, :]) for kc in range(4)],
                         [wbk] + [("ys", c) for c in range(4)], pzk)
                if first:
                    P.op("dve", lambda e, tm=tm, pz=pz, dc=dc: e.scalar_tensor_tensor(
                        out=acc[:, dc, :], in0=tm[:], scalar=1.0, in1=pz[:], op0=ALU.add, op1=ALU.mult),
                        reads=[tmk, pzk], writes=[("acc", dc)])
                else:
                    tp, tpk = Rf5.next()
                    P.op("dve", lambda e, tm=tm, pz=pz, tp=tp: e.scalar_tensor_tensor(
                        out=tp[:], in0=tm[:], scalar=1.0, in1=pz[:], op0=ALU.add, op1=ALU.mult),
                        reads=[tmk, pzk], writes=[tpk])
                    P.op("dve", lambda e, tp=tp, dc=dc: e.tensor_tensor(out=acc[:, dc, :], in0=acc[:, dc, :],
                                                                       in1=tp[:], op=ALU.add),
                         reads=[tpk, ("acc", dc)], writes=[("acc", dc)])
                yield

    def tm_to_ys(ytm, ykey, wg, wgk):
        for c in range(4):
            pt, ptk = cur["PG"].next()
            ptb = bfv(pt)
            transposes([(ptb[:, qs * 128:(qs + 1) * 128], ytm[:, qs, c * 128:(c + 1) * 128]) for qs in range(4)],
                       [ykey], ptk, ident[:])
            gate_chunk(wg, wgk, c, ptb[:, 0:512], [ptk])
            yield

    def _layers():
        for l in range(L):
            ll = l + first_layer
            last = (l == L - 1)
            P.op("sp", lambda e, ll=ll: e.dma_start(out=vfm[:], in_=vfm_d[ll]), writes=["vfm"], dsem="vfm")
            P.op("sp", lambda e, ll=ll: e.dma_start(out=vrow[:], in_=vrow_d[ll, :].partition_broadcast(128)),
                 writes=["vrow"], dsem="vrow")
            P.op("dve", lambda e: e.tensor_scalar(out=hbm[:], in0=vfm[:, 39:71], scalar1=0.5, scalar2=None,
                                                  op0=ALU.mult), reads=["vfm"], writes=["hbm"])
            P.op("pool", lambda e, ll=ll: e.dma_start(out=wuq[:], in_=w_uq_d[ll].rearrange("(k p) n -> p k n", p=128)),
                 writes=["wuq"], dsem="wuq")
            P.op("pool", lambda e, ll=ll: e.dma_start(out=wukv[:], in_=w_ukv_d[ll]), writes=["wukv"], dsem="wukv")
            P.op("pool", lambda e, ll=ll: e.dma_start(
                out=wAv, in_=w_in_d[ll][:, 0:416].rearrange("(k p) n -> p k n", p=128)), writes=["wA"], dsem="wA")
            P.op("sp", lambda e, ll=ll: e.dma_start(out=wspf[:], in_=w_sp_d[ll].rearrange("g t s -> t g s")),
                 writes=[("f512", 0)], dsem=("f512", 0))
            for g in range(4):
                pt, ptk = PG.next()
                P.op("pe", lambda e, pt=pt, g=g: e.transpose(out=pt[:, 0:128], in_=wspf[:, g, :], identity=identf[:]),
                     reads=[("f512", 0), "identf"], writes=[ptk])
                tf, tfk = f512[1 + g % 3][:, 0:128], ("f512", 1 + g % 3)
                P.op("act", lambda e, pt=pt, tf=tf: e.activation(out=tf[:], in_=pt[:, 0:128], func=AF.Copy),
                     reads=[ptk], writes=[tfk])
                P.op("pool", lambda e, tf=tf, g=g: e.affine_select(out=wTs[:, g, :], in_=tf[:], pattern=[[1, 128]],
                                                                  compare_op=ALU.is_ge, fill=0.0, base=0,
                                                                  channel_multiplier=-1),
                     reads=[tfk], writes=["wTs"])
                pr, prk = PG.next()
                P.op("pe", lambda e, pr=pr, g=g: e.matmul(pr[:, 0:128], ones_bf[:, 0:128], wTs[:, g, :], start=True, stop=True),
                     reads=["ones_bf", "wTs"], writes=[prk])
                P.op("dve", lambda e, pr=pr, g=g: e.scalar_tensor_tensor(
                    out=Cg[:, g, :], in0=pr[:, 0:128], scalar=vfm[:, 75 + g:76 + g], in1=vrow[:, 64 + g * 128:64 + (g + 1) * 128],
                    op0=ALU.mult, op1=ALU.add), reads=[prk, "vfm", "vrow"], writes=["Cg"])
            P.op("dve", lambda e: e.memset(zc[:, :, 0:2], 0.0), writes=[("z", c) for c in range(4)])
            memt = [f1024[0], f1024[1]]
            mkeys = [("f1024", 0), ("f1024", 1)]
            Rf1.i = 2
            for mb in range(2):
                P.op("sp", lambda e, mb=mb: e.dma_start(out=memt[mb][:], in_=mem_d[mb * 128:(mb + 1) * 128, :]),
                     writes=[mkeys[mb]], dsem=("memt", mb))
            for mb in range(2):
                P.op("act", lambda e, mb=mb: e.activation(out=junk[:], in_=memt[mb][:], func=AF.Square,
                                                          accum_out=ssm[:, mb:mb + 1]),
                     reads=[mkeys[mb]], writes=[("ssm", mb), "junk"])
            rsqrt(rm[:], ssm[:], 1.0 / D, 2, [("ssm", 0), ("ssm", 1)], ["rm"])
            for mb in range(2):
                mnb, mnbk = Rb1.next()
                P.op("dve", lambda e, mb=mb, mnb=mnb: e.tensor_scalar(out=mnb[:], in0=memt[mb][:], scalar1=rm[:, mb:mb + 1],
                                                                      scalar2=None, op0=ALU.mult),
                     reads=[mkeys[mb], "rm"], writes=[mnbk])
                pt, ptk = PG.next()
                ptb = bfv(pt)
                transposes([(ptb[:, k * 128:(k + 1) * 128], mnb[:, k * 128:(k + 1) * 128]) for k in range(8)],
                           [mnbk], ptk, ident[:])
                P.op("dve", lambda e, ptb=ptb, mb=mb: e.tensor_tensor(
                    out=memnT[:, :, mb * 128:(mb + 1) * 128], in0=ptb.rearrange("p (k t) -> p k t", k=8),
                    in1=vfm[:, 8:16].unsqueeze(2).broadcast_to([128, 8, 128]), op=ALU.mult),
                    reads=[ptk, "vfm"], writes=[*YALLK])
            wk_, wkk = W.next("memK")
            for mb in range(2):
                pk, pkk = PG.next()
                mm_group(pk[:], [(memnT[:, k, mb * 128:(mb + 1) * 128], wk_[:, k, :]) for k in range(8)],
                         [wkk, *YALLK], pkk)
                kf, kfk = Rf5.next()
                P.op("act", lambda e, pk=pk, kf=kf: e.activation(out=kf[:], in_=pk[:], func=AF.Copy),
                     reads=[pkk], writes=[kfk])
                sq, sqk = Rf5.next()
                P.op("act", lambda e, kf=kf, sq=sq: e.activation(out=sq[:], in_=kf[:], func=AF.Square),
                     reads=[kfk], writes=[sqk])
                P.op("dve", lambda e, sq=sq: e.tensor_reduce(out=ss4[:], in_=sq[:].rearrange("p (h d) -> p h d", h=4),
                                                            axis=AX.X, op=ALU.add), reads=[sqk], writes=["ss4"])
                rsqrt(r4[:], ss4[:], 1.0 / 128, 4, ["ss4"], ["r4"])
                knb, knbk = Rb5.next()
                P.op("dve", lambda e, kf=kf, knb=knb: e.tensor_tensor(
                    out=knb[:].rearrange("p (h d) -> p h d", h=4), in0=kf[:].rearrange("p (h d) -> p h d", h=4),
                    in1=r4[:].unsqueeze(2).broadcast_to([128, 4, 128]), op=ALU.mult),
                    reads=[kfk, "r4"], writes=[knbk])
                pt, ptk = PG.next()
                ptb = bfv(pt)
                transposes([(ptb[:, h * 128:(h + 1) * 128], knb[:, h * 128:(h + 1) * 128]) for h in range(4)],
                           [knbk], ptk, ident[:])
                P.op("dve", lambda e, ptb=ptb, mb=mb: e.tensor_scalar(
                    out=KmT[:, :, mb * 128:(mb + 1) * 128], in0=ptb[:, 0:512].rearrange("p (h t) -> p h t", h=4),
                    scalar1=vfm[:, 38:39], scalar2=None, op0=ALU.mult),
                    reads=[ptk, "vfm"], writes=[("KmT", mb)])
            wv_, wvk = W.next("memV")
            for mb in range(2):
                pv, pvk = PG.next()
                mm_group(pv[:], [(memnT[:, k, mb * 128:(mb + 1) * 128], wv_[:, k, :]) for k in range(8)],
                         [wvk, *YALLK], pvk)
                P.op("act", lambda e, pv=pv, mb=mb: e.activation(out=Vm[:, mb, :, :].rearrange("p h d -> p (h d)"),
                                                                in_=pv[:], func=AF.Copy),
                     reads=[pvk], writes=[("Vm", mb)])

            if l == 0:
                checkpoint("lsetup", [(wTs[:].rearrange("p a b -> p (a b)"), 512, ["wTs"]),
                                      (KmT[:].rearrange("p a b -> p (a b)"), 1024, [("KmT", 0), ("KmT", 1)]),
                                      (Vm[:].rearrange("p a b c -> p (a b c)"), 1024, [("Vm", 0), ("Vm", 1)]),
                                      (hbm[:], 32, ["hbm"]), (vrow[:, 0:64], 64, ["vrow"])])
            def thA_head(i):
                hT_ = hTb[i % 2]
                t0 = i * T
                for st in range(4):
                    xs, xsk = Rf1.next()
                    P.op('sp', lambda e, xs=xs, st=st, xsrc=xsrc, t0=t0: e.dma_start(out=xs[:], in_=xsrc[t0 + st * 128:t0 + (st + 1) * 128, :]), reads=[(skey, i, st)], writes=[xsk], dsem=xsk)
                    yield
                    P.op('act', lambda e, xs=xs, st=st: e.activation(out=junk[:], in_=xs[:], func=AF.Square, accum_out=ssx[:, st:st + 1]), reads=[xsk], writes=[('ssx', st), 'junk'])
                    yield
                    rsqrtA(rx[:, st:st + 1], ssx[:, st:st + 1], 1.0 / D, 1, [('ssx', st)], [('rx', st)])
                    yield
                    hb, hbk = Rb1.next()
                    P.op('dve', lambda e, xs=xs, st=st, hb=hb: e.tensor_scalar(out=hb[:], in0=xs[:], scalar1=rx[:, st:st + 1], scalar2=None, op0=ALU.mult), reads=[xsk, ('rx', st)], writes=[hbk])
                    yield
                    pt, ptk = PGA.next()
                    ptb = bfv(pt)
                    transposes([(ptb[:, k * 128:(k + 1) * 128], hb[:, k * 128:(k + 1) * 128]) for k in range(8)], [hbk], ptk, ident[:])
                    yield
                    P.op('dve', lambda e, ptb=ptb, st=st: e.tensor_tensor(out=hT_[:, :, st * 128:(st + 1) * 128], in0=ptb.rearrange('p (k t) -> p k t', k=8), in1=vfm[:, 0:8].unsqueeze(2).broadcast_to([128, 8, 128]), op=ALU.mult), reads=[ptk, 'vfm'], writes=[('hT', i % 2, st)])
                    yield
                wA, wAk = (wAv, 'wA')
                for st in range(4):
                    pg, pgk = PGA.next()
                    mm_group(pg[:, 0:416], [(hT_[:, k, st * 128:(st + 1) * 128], wA[:, k, 0:416]) for k in range(8)], [wAk, ('hT', i % 2, st)], pgk)
                    yield
                    for j, (a, b_) in enumerate([(0, 256), (256, 384), (384, 416)]):
                        P.op('act', lambda e, pg=pg, a=a, b_=b_, st=st, j=j: e.activation(out=junk[:, a:b_], in_=pg[:, a:b_], func=AF.Square, accum_out=ss3[:, st * 3 + j:st * 3 + j + 1]), reads=[pgk], writes=[('ss3', st, j)])
                        yield
                    P.op('dve', lambda e, pg=pg, st=st: e.tensor_copy(out=kr[:, st, :], in_=pg[:, 384:416]), reads=[pgk], writes=[('kr', st)])
                    yield
                    P.op('pool', lambda e, st=st: e.tensor_scalar(out=rtmpA[:, 0:1], in0=ss3[:, st * 3:st * 3 + 1], scalar1=1.0 / 256, scalar2=EPS, op0=ALU.mult, op1=ALU.add), reads=[('ss3', st, 0)], writes=['rtmpA'])
                    yield
                    P.op('pool', lambda e, st=st: e.tensor_scalar(out=rtmpA[:, 1:2], in0=ss3[:, st * 3 + 1:st * 3 + 2], scalar1=1.0 / 128, scalar2=EPS, op0=ALU.mult, op1=ALU.add), reads=[('ss3', st, 1)], writes=['rtmpA'])
                    yield
                    P.op('pool', lambda e: e.tensor_tensor(out=r2[:], in0=rtmpA[:, 0:2], in1=mhalf[:, 0:2], op=ALU.pow), reads=['rtmpA', 'mhalf'], writes=['r2'])
                    yield
                    cqn, cqnk = RbA.next()
                    P.op('dve', lambda e, pg=pg, cqn=cqn: e.tensor_scalar(out=cqn[:, 0:256], in0=pg[:, 0:256], scalar1=r2[:, 0:1], scalar2=None, op0=ALU.mult), reads=[pgk, 'r2'], writes=[cqnk])
                    yield
                    P.op('dve', lambda e, pg=pg, cqn=cqn: e.tensor_scalar(out=cqn[:, 256:384], in0=pg[:, 256:384], scalar1=r2[:, 1:2], scalar2=None, op0=ALU.mult), reads=[pgk, 'r2'], writes=[cqnk])
                    yield
                    pt, ptk = PGA.next()
                    ptb = bfv(pt)
                    transposes([(ptb[:, k * 128:(k + 1) * 128], cqn[:, k * 128:(k + 1) * 128]) for k in range(3)], [cqnk], ptk, ident[:])
                    yield
                    P.op('dve', lambda e, ptb=ptb, st=st: e.tensor_tensor(out=cqT[:, :, st * 128:(st + 1) * 128], in0=ptb[:, 0:384].rearrange('p (k t) -> p k t', k=3), in1=vfm[:, 16:19].unsqueeze(2).broadcast_to([128, 3, 128]), op=ALU.mult), reads=[ptk, 'vfm'], writes=[('cqT', st)])
                    yield
            def thA_head_wide(i):
                hT_ = hTb[i % 2]
                t0 = i * T
                xbufs = [(f1024[0], ('f1024', 0)), (f1024[1], ('f1024', 1)), (xwb[0], ('xw', 0)), (xwb[1], ('xw', 1))]
                for st in range(4):
                    xs, xsk = xbufs[st]
                    P.op('sp', lambda e, xs=xs, st=st, xsrc=xsrc, t0=t0: e.dma_start(out=xs[:], in_=xsrc[t0 + st * 128:t0 + (st + 1) * 128, :]), reads=[(skey, i, st)], writes=[xsk], dsem=xsk)
                for st in range(4):
                    xs, xsk = xbufs[st]
                    P.op('act', lambda e, xs=xs, st=st: e.activation(out=junk[:], in_=xs[:], func=AF.Square, accum_out=ssx[:, st:st + 1]), reads=[xsk], writes=[('ssx', st), 'junk'])
                rsqrtA(rx[:, 0:4], ssx[:, 0:4], 1.0 / D, 4, [('ssx', st) for st in range(4)], [('rx', st) for st in range(4)])
                yield
                for st in range(4):
                    xs, xsk = xbufs[st]
                    hb, hbk = Rb1.next()
                    P.op('dve', lambda e, xs=xs, st=st, hb=hb: e.tensor_scalar(out=hb[:], in0=xs[:], scalar1=rx[:, st:st + 1], scalar2=None, op0=ALU.mult), reads=[xsk, ('rx', st)], writes=[hbk])
                    pt, ptk = PG.next()
                    ptb = bfv(pt)
                    transposes([(ptb[:, k * 128:(k + 1) * 128], hb[:, k * 128:(k + 1) * 128]) for k in range(8)], [hbk], ptk, ident[:])
                    P.op('dve', lambda e, ptb=ptb, st=st: e.tensor_tensor(out=hT_[:, :, st * 128:(st + 1) * 128], in0=ptb.rearrange('p (k t) -> p k t', k=8), in1=vfm[:, 0:8].unsqueeze(2).broadcast_to([128, 8, 128]), op=ALU.mult), reads=[ptk, 'vfm'], writes=[('hT', i % 2, st)])
                    yield
                wA, wAk = (wAv, 'wA')
                pgs = []
                for st in range(4):
                    pg, pgk = PW.next()
                    mm_group(pg[:, 0:416], [(hT_[:, k, st * 128:(st + 1) * 128], wA[:, k, 0:416]) for k in range(8)], [wAk, ('hT', i % 2, st)], pgk)
                    pgs.append((pg, pgk))
                for st in range(4):
                    pg, pgk = pgs[st]
                    for j, (a, b_) in enumerate([(0, 256), (256, 384), (384, 416)]):
                        P.op('act', lambda e, pg=pg, a=a, b_=b_, st=st, j=j: e.activation(out=junk[:, a:b_], in_=pg[:, a:b_], func=AF.Square, accum_out=ss3[:, st * 3 + j:st * 3 + j + 1]), reads=[pgk], writes=[('ss3', st, j)])
                    P.op('dve', lambda e, pg=pg, st=st: e.tensor_copy(out=kr[:, st, :], in_=pg[:, 384:416]), reads=[pgk], writes=[('kr', st)])
                ss3v = ss3[:].rearrange('p (s j) -> p s j', j=3)
                rtv = rtmpA[:, 0:8].rearrange('p (s j) -> p s j', j=2)
                P.op('pool', lambda e: e.tensor_scalar(out=rtv[:, :, 0], in0=ss3v[:, :, 0], scalar1=1.0 / 256, scalar2=EPS, op0=ALU.mult, op1=ALU.add), reads=[('ss3', st, 0) for st in range(4)], writes=['rtmpA'])
                P.op('pool', lambda e: e.tensor_scalar(out=rtv[:, :, 1], in0=ss3v[:, :, 1], scalar1=1.0 / 128, scalar2=EPS, op0=ALU.mult, op1=ALU.add), reads=[('ss3', st, 1) for st in range(4)], writes=['rtmpA'])
                P.op('pool', lambda e: e.tensor_tensor(out=r2w[:], in0=rtmpA[:, 0:8], in1=mhalf[:, 0:8], op=ALU.pow), reads=['rtmpA', 'mhalf'], writes=['r2w'])
                yield
                for st in range(4):
                    pg, pgk = pgs[st]
                    cqn, cqnk = Rb5.next()
                    P.op('dve', lambda e, pg=pg, cqn=cqn, st=st: e.tensor_scalar(out=cqn[:, 0:256], in0=pg[:, 0:256], scalar1=r2w[:, 2 * st:2 * st + 1], scalar2=None, op0=ALU.mult), reads=[pgk, 'r2w'], writes=[cqnk])
                    P.op('dve', lambda e, pg=pg, cqn=cqn, st=st: e.tensor_scalar(out=cqn[:, 256:384], in0=pg[:, 256:384], scalar1=r2w[:, 2 * st + 1:2 * st + 2], scalar2=None, op0=ALU.mult), reads=[pgk, 'r2w'], writes=[cqnk])
                    pt, ptk = PG.next()
                    ptb = bfv(pt)
                    transposes([(ptb[:, k * 128:(k + 1) * 128], cqn[:, k * 128:(k + 1) * 128]) for k in range(3)], [cqnk], ptk, ident[:])
                    P.op('dve', lambda e, ptb=ptb, st=st: e.tensor_tensor(out=cqT[:, :, st * 128:(st + 1) * 128], in0=ptb[:, 0:384].rearrange('p (k t) -> p k t', k=3), in1=vfm[:, 16:19].unsqueeze(2).broadcast_to([128, 3, 128]), op=ALU.mult), reads=[ptk, 'vfm'], writes=[('cqT', st)])
                    yield
            def thA_m2(i, sts, R):
                for st in sts:
                    blk = 4 * i + st
                    pa, pak = R['pg'].next()
                    pb, pbk = R['pg'].next()
                    mm_group(pa[:, 0:384], [(cqT[:, kc, st * 128:(st + 1) * 128], wuq[:, kc, 0:384]) for kc in range(2)], ['wuq', ('cqT', st)], pak)
                    yield
                    mm_group(pb[:, 0:384], [(cqT[:, kc, st * 128:(st + 1) * 128], wuq[:, kc, 384:768]) for kc in range(2)], ['wuq', ('cqT', st)], pbk)
                    yield
                    qf, qfk = R['f1'].next()
                    P.op('act', lambda e, pa=pa, qf=qf: e.activation(out=qf[:, 0:384], in_=pa[:, 0:384], func=AF.Copy), reads=[pak], writes=[qfk])
                    P.op('act', lambda e, pb=pb, qf=qf: e.activation(out=qf[:, 384:768], in_=pb[:, 0:384], func=AF.Copy), reads=[pbk], writes=[qfk])
                    yield
                    pa2, pa2k = R['pg'].next()
                    pb2, pb2k = R['pg'].next()
                    mm_group(pa2[:], [(cqT[:, 2, st * 128:(st + 1) * 128], wukv[:, 0:512])], ['wukv', ('cqT', st)], pa2k)
                    yield
                    mm_group(pb2[:], [(cqT[:, 2, st * 128:(st + 1) * 128], wukv[:, 512:1024])], ['wukv', ('cqT', st)], pb2k)
                    yield
                    kvf, kvfk = R['f1'].next()
                    P.op('act', lambda e, pa2=pa2, kvf=kvf: e.activation(out=kvf[:, 0:512], in_=pa2[:], func=AF.Copy), reads=[pa2k], writes=[kvfk])
                    P.op('act', lambda e, pb2=pb2, kvf=kvf: e.activation(out=kvf[:, 512:1024], in_=pb2[:], func=AF.Copy), reads=[pb2k], writes=[kvfk])
                    yield
                    qf3 = qf[:, 0:768].rearrange('p (h d) -> p h d', h=8)
                    kvf3 = kvf[:].rearrange('p (h d) -> p h d', h=8)
                    sq, sqk = R['b1'].next()
                    P.op('act', lambda e, qf=qf, sq=sq: e.activation(out=sq[:, 0:768], in_=qf[:, 0:768], func=AF.Square), reads=[qfk], writes=[sqk])
                    yield
                    P.op('dve', lambda e, sq=sq: e.tensor_reduce(out=R['ssqk'][:, 0:8], in_=sq[:, 0:768].rearrange('p (h d) -> p h d', h=8), axis=AX.X, op=ALU.add), reads=[sqk], writes=[(R['n'] + 'ssqk', 0)])
                    yield
                    sq2, sq2k = R['b1'].next()
                    P.op('act', lambda e, kvf=kvf, sq2=sq2: e.activation(out=sq2[:], in_=kvf[:], func=AF.Square), reads=[kvfk], writes=[sq2k])
                    yield
                    P.op('dve', lambda e, sq2=sq2: e.tensor_reduce(out=R['ssqk'][:, 8:16], in_=sq2[:].rearrange('p (h d) -> p h d', h=8)[:, :, 0:64], axis=AX.X, op=ALU.add), reads=[sq2k], writes=[(R['n'] + 'ssqk', 1)])
                    yield
                    P.op('dve', lambda e, st=st: e.tensor_scalar(out=R['ssqk'][:, 8:16], in0=R['ssqk'][:, 8:16], scalar1=ss3[:, st * 3 + 2:st * 3 + 3], scalar2=None, op0=ALU.add), reads=[(R['n'] + 'ssqk', 1), ('ss3', st, 2)], writes=[(R['n'] + 'ssqk', 1)])
                    yield
                    rsqrt(R['rqk'][:], R['ssqk'][:], 1.0 / 96, 16, [(R['n'] + 'ssqk', 0), (R['n'] + 'ssqk', 1)], [(R['n'] + 'rqk')], tmp=R['rtmp'], tmpk=R['n'] + 'rtmpA')
                    yield
                    P.op('act', lambda e, kvf3=kvf3, blk=blk: e.activation(out=Vaug[:, blk, :, 0:64], in_=kvf3[:, :, 64:128], func=AF.Copy), reads=[kvfk], writes=[('V', blk)])
                    yield
                    qb, qbk = R['b1'].next()
                    qb3 = qb[:, 0:768].rearrange('p (h d) -> p h d', h=8)
                    P.op('dve', lambda e, qb3=qb3, qf3=qf3: e.tensor_tensor(out=qb3[:, :, 0:64], in0=qf3[:, :, 0:64], in1=R['rqk'][:, 0:8].unsqueeze(2).broadcast_to([128, 8, 64]), op=ALU.mult), reads=[qfk, (R['n'] + 'rqk')], writes=[qbk])
                    yield
                    kb, kbk = R['b1'].next()
                    kb3 = kb[:, 0:768].rearrange('p (h d) -> p h d', h=8)
                    P.op('dve', lambda e, kb3=kb3, kvf3=kvf3: e.tensor_tensor(out=kb3[:, :, 0:64], in0=kvf3[:, :, 0:64], in1=R['rqk'][:, 8:16].unsqueeze(2).broadcast_to([128, 8, 64]), op=ALU.mult), reads=[kvfk, (R['n'] + 'rqk')], writes=[kbk])
                    yield
                    xr, xrk = R['qk'].next()
                    tr_, trk = R['qk'].next()
                    xr3 = xr[:].rearrange('p (h d) -> p h d', h=16)
                    tr3 = tr_[:].rearrange('p (h d) -> p h d', h=16)
                    P.op('dve', lambda e, xr3=xr3, qf3=qf3: e.tensor_tensor(out=xr3[:, 0:8, :], in0=qf3[:, :, 64:96], in1=R['rqk'][:, 0:8].unsqueeze(2).broadcast_to([128, 8, 32]), op=ALU.mult), reads=[qfk, (R['n'] + 'rqk')], writes=[xrk])
                    yield
                    P.op('dve', lambda e, xr3=xr3, st=st: e.tensor_tensor(out=xr3[:, 8:16, :], in0=kr[:, st, :].unsqueeze(1).broadcast_to([128, 8, 32]), in1=R['rqk'][:, 8:16].unsqueeze(2).broadcast_to([128, 8, 32]), op=ALU.mult), reads=[('kr', st), (R['n'] + 'rqk')], writes=[xrk])
                    yield
                    P.op('dve', lambda e, xr=xr: e.tensor_tensor(out=xr[:].rearrange('p (a h d) -> p a h d', a=2, h=8), in0=xr[:].rearrange('p (a h d) -> p a h d', a=2, h=8), in1=vrow[:, 0:64].rearrange('p (a d) -> p a d', a=2).unsqueeze(2).broadcast_to([128, 2, 8, 32]), op=ALU.mult), reads=[xrk, 'vrow'], writes=[xrk])
                    yield
                    cosb = cosT[:, blk, :].unsqueeze(1).broadcast_to([128, 16, 16])
                    sinb = sinT[:, blk, :].unsqueeze(1).broadcast_to([128, 16, 16])
                    P.op('dve', lambda e, xr3=xr3, tr3=tr3, sinb=sinb: e.scalar_tensor_tensor(out=tr3[:, :, 0:16], in0=xr3[:, :, 16:32], scalar=-1.0, in1=sinb, op0=ALU.mult, op1=ALU.mult), reads=[xrk, 'sinT'], writes=[trk])
                    yield
                    P.op('dve', lambda e, xr3=xr3, tr3=tr3, sinb=sinb: e.tensor_tensor(out=tr3[:, :, 16:32], in0=xr3[:, :, 0:16], in1=sinb, op=ALU.mult), reads=[xrk, 'sinT'], writes=[trk])
                    yield
                    P.op('dve', lambda e, xr3=xr3, cosb=cosb: e.tensor_tensor(out=xr3[:, :, 0:16], in0=xr3[:, :, 0:16], in1=cosb, op=ALU.mult), reads=[xrk, trk, 'cosT'], writes=[xrk])
                    P.op('dve', lambda e, xr3=xr3, cosb=cosb: e.tensor_tensor(out=xr3[:, :, 16:32], in0=xr3[:, :, 16:32], in1=cosb, op=ALU.mult), reads=[xrk, trk, 'cosT'], writes=[xrk])
                    yield
                    P.op('dve', lambda e, xr3=xr3, tr3=tr3, qb3=qb3: e.tensor_tensor(out=qb3[:, :, 64:96], in0=xr3[:, 0:8, :], in1=tr3[:, 0:8, :], op=ALU.add), reads=[xrk, trk], writes=[qbk])
                    yield
                    P.op('dve', lambda e, xr3=xr3, tr3=tr3, kb3=kb3: e.tensor_tensor(out=kb3[:, :, 64:96], in0=xr3[:, 8:16, :], in1=tr3[:, 8:16, :], op=ALU.add), reads=[xrk, trk], writes=[kbk])
                    yield
                    pt, ptk = R['pg'].next()
                    ptb = bfv(pt)
                    transposes([(ptb[0:96, h * 128:(h + 1) * 128], qb3[:, h, :]) for h in range(8)], [qbk], ptk, ident[:])
                    yield
                    pt2, pt2k = R['pg'].next()
                    pt2b = bfv(pt2)
                    transposes([(pt2b[0:96, h * 128:(h + 1) * 128], kb3[:, h, :]) for h in range(8)], [kbk], pt2k, ident[:])
                    yield
                    P.op('dve', lambda e, ptb=ptb, st=st: e.tensor_scalar(out=QT[0:96, :, st * 128:(st + 1) * 128], in0=ptb[0:96, :].rearrange('p (h t) -> p h t', h=8), scalar1=vfm[0:96, 19:20], scalar2=None, op0=ALU.mult), reads=[ptk, 'vfm'], writes=[('QT', st)])
                    yield
                    P.op('dve', lambda e, pt2b=pt2b, blk=blk: e.tensor_scalar(out=KT[0:96, :, blk * 128:(blk + 1) * 128], in0=pt2b[0:96, :].rearrange('p (h t) -> p h t', h=8), scalar1=vfm[0:96, 20:21], scalar2=None, op0=ALU.mult), reads=[pt2k, 'vfm'], writes=[('KT', blk)])
                    yield
                return
                yield
            def thA(i, wide=False):
                if wide:
                    yield from thA_head_wide(i)
                    g0 = thA_m2(i, [0, 2], RES_A0)
                    g1 = thA_m2(i, [1, 3], RES_B)
                    al = [True, True]
                    for _ in range(14):
                        next(g0)
                        yield
                    while al[0] or al[1]:
                        for j, g in enumerate((g0, g1)):
                            if al[j]:
                                try:
                                    next(g)
                                except StopIteration:
                                    al[j] = False
                        yield
                else:
                    yield from thA_head(i)
                    yield from thA_m2(i, range(4), RES_A)
            def thM(i):
                t0 = i * T
                wg, wgk = W.next('gate0')
                yield from tm_to_ys(ya, 'ya', wg, wgk)
                yield from merge_branch(0, True)
                wcg, wcgk = W.next('conv_c')
                wxi, wxik = W.next('conv_x')
                for c in range(4):
                    pc, pck = cur['PG'].next()
                    mm_group(pc[:], [(wcg[:, k, c * 128:(c + 1) * 128], cur['hT'][:, k, :]) for k in range(8)], [wcgk] + cur['hTk'], pck)
                    yield
                    px, pxk = cur['PG'].next()
                    mm_group(px[:], [(wxi[:, k, c * 128:(c + 1) * 128], cur['hT'][:, k, :]) for k in range(8)], [wxik] + cur['hTk'], pxk)
                    yield
                    xs, xsk = Rf5.next()
                    P.op('act', lambda e, px=px, xs=xs: e.activation(out=xs[:], in_=px[:], func=AF.Copy), reads=[pxk], writes=[xsk])
                    P.op('dve', lambda e, pc=pc, xs=xs, c=c: e.tensor_tensor(out=zc[:, c, 2:514], in0=pc[:], in1=xs[:], op=ALU.mult), reads=[pck, xsk], writes=[('z', c)])
                    y0, y0k = Rf5.next()
                    P.op('dve', lambda e, y0=y0, c=c: e.tensor_scalar(out=y0[:], in0=zc[:, c, 2:514], scalar1=vfm[:, 21 + 8 + c:21 + 8 + c + 1], scalar2=vfm[:, 33 + c:34 + c], op0=ALU.mult, op1=ALU.add), reads=[('z', c), 'vfm'], writes=[y0k])
                    y1, y1k = Rf5.next()
                    P.op('dve', lambda e, y0=y0, y1=y1, c=c: e.scalar_tensor_tensor(out=y1[:], in0=zc[:, c, 1:513], scalar=vfm[:, 21 + 4 + c:21 + 4 + c + 1], in1=y0[:], op0=ALU.mult, op1=ALU.add), reads=[('z', c), 'vfm', y0k], writes=[y1k])
                    P.op('dve', lambda e, y1=y1, c=c: e.scalar_tensor_tensor(out=yall[:, c, :], in0=zc[:, c, 0:512], scalar=vfm[:, 21 + c:21 + c + 1], in1=y1[:], op0=ALU.mult, op1=ALU.add), reads=[('z', c), 'vfm', y1k], writes=[('yall', c)])
                    P.op('dve', lambda e, c=c: e.tensor_copy(out=zc[:, c, 0:2], in_=zc[:, c, 512:514]), reads=[('z', c)], writes=[('z', c)])
                    yield
                wbg, wbgk = W.next('conv_b')
                for c in range(4):
                    pbg, pbgk = cur['PG'].next()
                    mm_group(pbg[:], [(wbg[:, k, c * 128:(c + 1) * 128], cur['hT'][:, k, :]) for k in range(8)], [wbgk] + cur['hTk'], pbgk)
                    yield
                    P.op('dve', lambda e, pbg=pbg, c=c: e.tensor_tensor(out=yall[:, c, :], in0=yall[:, c, :], in1=pbg[:], op=ALU.mult), reads=[('yall', c), pbgk], writes=[('yall', c)])
                    yield
                wg, wgk = W.next('gate1')
                for c in range(4):
                    gate_chunk(wg, wgk, c, yall[:, c, :], [('yall', c)])
                    yield
                yield from merge_branch(1, False)
                wv2, wv2k = W.next('sg_v')
                vcs = []
                for st in range(4):
                    pv_, pvk_ = cur['PG'].next()
                    mm_group(pv_[:], [(cur['hT'][:, k, st * 128:(st + 1) * 128], wv2[:, k, :]) for k in range(8)], [wv2k, ('hT', i % 2, st)], pvk_)
                    vn, vnk = Rf5.next()
                    P.op('act', lambda e, pv_=pv_, vn=vn: e.activation(out=vn[:], in_=pv_[:], func=AF.Copy), reads=[pvk_], writes=[vnk])
                    vcs.append((vn, vnk))
                    yield
                wg, wgk = W.next('gate2')
                for st in range(4):
                    vn, vnk = vcs[st]
                    P.op('dve', lambda e, vn=vn, st=st: e.bn_stats(out=st6s[:, st, :], in_=vn[:]), reads=[vnk], writes=[('st6', st)])
                    P.op('dve', lambda e, st=st: e.bn_aggr(out=mvs[:, st, :], in_=st6s[:, st, :]), reads=[('st6', st)], writes=[('mv', st)])
                P.op('pool', lambda e: e.tensor_scalar(out=rtmpB[:, 0:4], in0=mvs[:, :, 1], scalar1=EPS, scalar2=None, op0=ALU.add), reads=[('mv', st) for st in range(4)], writes=['rtmpB'])
                P.op('pool', lambda e: e.tensor_tensor(out=rs4[:], in0=rtmpB[:, 0:4], in1=mhalf[:, 0:4], op=ALU.pow), reads=['rtmpB', 'mhalf'], writes=['rs4'])
                yield
                for g in range(4):
                    gate_pre(wg, wgk, g)
                    yield
                P.op('dve', lambda e: e.scalar_tensor_tensor(out=nb4[:], in0=mvs[:, :, 0], scalar=-1.0, in1=rs4[:], op0=ALU.mult, op1=ALU.mult), reads=[('mv', st) for st in range(4)] + ['rs4'], writes=['nb4'])
                for st in range(4):
                    vn, vnk = vcs[st]
                    P.op('act', lambda e, vn=vn, st=st: e.activation(out=vln[:, st, :], in_=vn[:], func=AF.Identity, bias=nb4[:, st:st + 1], scale=rs4[:, st:st + 1]), reads=[vnk, 'nb4', 'rs4'], writes=['ya'])
                    yield
                wu, wuk = W.next('sg_u')
                for g in range(4):
                    pm, pmk = cur['PG'].next()

                    def mix(e, pm=pm, g=g):
                        for st in range(4):
                            ins = e.matmul(pm[:, st * 128:(st + 1) * 128], vln[:, st, g * 128:(g + 1) * 128], wTs[:, g, :], start=True, stop=True)
                        return ins
                    P.op('pe', mix, reads=['ya', 'wTs'], writes=[pmk])
                    yield
                    pu, puk = cur['PG'].next()
                    mm_group(pu[:], [(wu[:, k, g * 128:(g + 1) * 128], cur['hT'][:, k, :]) for k in range(8)], [wuk] + cur['hTk'], puk)
                    yield
                    mt, mtk = Rb5.next()
                    P.op('dve', lambda e, pm=pm, mt=mt, g=g: e.scalar_tensor_tensor(out=mt[:].rearrange('p (s t) -> p s t', s=4), in0=pm[:].rearrange('p (s t) -> p s t', s=4), scalar=vfm[:, 71 + g:72 + g], in1=Cg[:, g, :].unsqueeze(1).broadcast_to([128, 4, 128]), op0=ALU.mult, op1=ALU.add), reads=[pmk, 'vfm', 'Cg'], writes=[mtk])
                    P.op('dve', lambda e, pu=pu, mt=mt, g=g: e.tensor_tensor(out=mt[:], in0=pu[:], in1=mt[:], op=ALU.mult), reads=[puk, mtk], writes=[mtk])
                    gate_post(g, mt[:], [mtk])
                    yield
                yield from merge_branch(2, False)
                wq_, wqk = W.next('memq')
                wg, wgk = W.next('gate3')
                pqs = []
                for st in range(4):
                    pq, pqk = cur['PG'].next()
                    mm_group(pq[:], [(cur['hT'][:, k, st * 128:(st + 1) * 128], wq_[:, k, :]) for k in range(8)], [wqk, ('hT', i % 2, st)], pqk)
                    pqs.append((pq, pqk))
                    yield
                mqfs = []
                for st in range(4):
                    pq, pqk = pqs[st]
                    mqf, mqfk = Rf5.next()
                    P.op('act', lambda e, pq=pq, mqf=mqf: e.activation(out=mqf[:], in_=pq[:], func=AF.Copy), reads=[pqk], writes=[mqfk])
                    mqfs.append((mqf, mqfk))
                for st in range(4):
                    mqf, mqfk = mqfs[st]
                    sq, sqk = Rb5.next()
                    P.op('act', lambda e, mqf=mqf, sq=sq: e.activation(out=sq[:], in_=mqf[:], func=AF.Square), reads=[mqfk], writes=[sqk])
                    P.op('dve', lambda e, sq=sq, st=st: e.tensor_reduce(out=ss16[:, st * 4:(st + 1) * 4], in_=sq[:].rearrange('p (h d) -> p h d', h=4), axis=AX.X, op=ALU.add), reads=[sqk], writes=[('ss16', st)])
                rsqrt(r16[:], ss16[:], 1.0 / 128, 16, [('ss16', st) for st in range(4)], ['r16'], tmp=rtmp16, tmpk='rtmp16')
                yield
                for c in range(4):
                    gate_pre(wg, wgk, c)
                    yield
                for st in range(4):
                    mqf, mqfk = mqfs[st]
                    mqn, mqnk = Rb5.next()
                    P.op('dve', lambda e, mqf=mqf, mqn=mqn, st=st: e.tensor_tensor(out=mqn[:].rearrange('p (h d) -> p h d', h=4), in0=mqf[:].rearrange('p (h d) -> p h d', h=4), in1=r16[:, st * 4:(st + 1) * 4].unsqueeze(2).broadcast_to([128, 4, 128]), op=ALU.mult), reads=[mqfk, 'r16'], writes=[mqnk])
                    pt, ptk = cur['PG'].next()
                    ptb = bfv(pt)
                    transposes([(ptb[:, h * 128:(h + 1) * 128], mqn[:, h * 128:(h + 1) * 128]) for h in range(4)], [mqnk], ptk, ident[:])
                    yield
                    P.op('dve', lambda e, ptb=ptb, st=st: e.tensor_scalar(out=QmT[:, :, st * 128:(st + 1) * 128], in0=ptb[:, 0:512].rearrange('p (h t) -> p h t', h=4), scalar1=vfm[:, 37:38], scalar2=None, op0=ALU.mult), reads=[ptk, 'vfm'], writes=[('QmT', st)])
                QmT_keys = [('QmT', st) for st in range(4)]
                pR, pRk = (ps[7], ('ps', 7))
                mpairs = [(h, mb) for h in range(4) for mb in range(2)]
                mpend = []
                for step in range(len(mpairs) + 1):
                    if step < len(mpairs):
                        h, mb = mpairs[step]
                        pS, pSk = PGM.next()
                        P.op('pe', lambda e, pS=pS, h=h, mb=mb: e.matmul(pS[:], KmT[:, h, mb * 128:(mb + 1) * 128], QmT[:, h, :], start=True, stop=True), reads=[('KmT', mb)] + QmT_keys, writes=[pSk])
                        PT, PTk = Rb5.next()
                        P.op('act', lambda e, pS=pS, PT=PT: e.activation(out=PT[:], in_=pS[:], func=AF.Exp, scale=128 ** (-0.5)), reads=[pSk], writes=[PTk])
                        mpend.append((h, mb, PT, PTk))
                        yield
                    if step >= 1:
                        h, mb, PT, PTk = mpend.pop(0)
                        pO, pOk = (ps[5 + h % 2], ('ps', 5 + h % 2))

                        def pvm(e, PT=PT, h=h, mb=mb, pO=pO, pR=pR):
                            for qs in range(4):
                                e.matmul(pO[:, qs * 128:(qs + 1) * 128], PT[:, qs * 128:(qs + 1) * 128], Vm[:, mb, h, :], start=mb == 0 and qs == 0, stop=mb == 1, skip_group_check=True)
                            for qs in range(4):
                                ins = e.matmul(pR[:, h * 8 + qs * 2:h * 8 + qs * 2 + 2], PT[:, qs * 128:(qs + 1) * 128], ones_bf[:, 0:2], start=h == 0 and mb == 0 and qs == 0, stop=mb == 1, skip_group_check=True)
                            return ins
                        P.op('pe', pvm, reads=[PTk, ('Vm', mb), 'ones_bf'], writes=[pOk, pRk])
                        yield
                        if mb == 1:
                            P.op('dve', lambda e, pR=pR, h=h: e.reciprocal(out=rec[:], in_=pR[:, h * 8:h * 8 + 8].rearrange('p (q t) -> p q t', t=2)[:, :, 0]), reads=[pRk], writes=['rec'])
                            P.op('dve', lambda e, pO=pO, h=h: e.tensor_tensor(out=ya[:, :, h * 128:(h + 1) * 128], in0=pO[:].rearrange('p (q d) -> p q d', q=4), in1=rec[:].unsqueeze(2).broadcast_to([128, 4, 128]), op=ALU.mult), reads=[pOk, 'rec'], writes=['ya'])
                yield from tm_post(ya, 'ya')
                yield from merge_branch(3, False)
                acc_keys = [('acc', dc) for dc in range(8)]
                wo0, wo0k = W.next('wo0')
                wo1, wo1k = W.next('wo1')
                for st in range(4):
                    xw, xwk = Rxw.next()
                    P.op('sp', lambda e, xw=xw, st=st, xsrc=xsrc, t0=t0: e.dma_start(out=xw[:], in_=xsrc[t0 + st * 128:t0 + (st + 1) * 128, :]), reads=[(skey, i, st)], writes=[xwk], dsem=xwk)
                    for half, (wo, wok) in enumerate([(wo0, wo0k), (wo1, wo1k)]):
                        po, pok = cur['PG'].next()
                        mm_group(po[:], [(acc[:, kc, st * 128:(st + 1) * 128], wo[:, kc, :]) for kc in range(8)], [wok] + acc_keys, pok)
                        yield
                        P.op('dve', lambda e, po=po, xw=xw, half=half: e.scalar_tensor_tensor(out=xw[:, half * 512:(half + 1) * 512], in0=po[:], scalar=0.25, in1=xw[:, half * 512:(half + 1) * 512], op0=ALU.mult, op1=ALU.add), reads=[pok, xwk], writes=[xwk])
                    P.op('sp', lambda e, dst=dst, t0=t0, st=st, xw=xw: e.dma_start(out=dst[t0 + st * 128:t0 + (st + 1) * 128, :], in_=xw[:]), reads=[xwk], writes=[(dkey, i, st)], dsem=('xst', xwk[1]))
                    yield
                return
                yield
            def attention(i):
                nkb = 4 * (i + 1)
                pairs = [(h, kb_) for h in range(8) for kb_ in range(nkb)]
                pend = []
                psO_cur = {}
                QT_keys = [("QT", st) for st in range(4)]
                for step in range(len(pairs) + ATT_SKEW):
                    new = None
                    if step < len(pairs):
                        h, kb_ = pairs[step]
                        j = kb_ - 4 * i
                        c0 = 128 * j if j > 0 else 0
                        pS, pSk = PS_.next()
                        P.op("pe", lambda e, pS=pS, h=h, kb_=kb_, c0=c0: e.matmul(
                            pS[:, c0:512], KT[0:96, h, kb_ * 128:(kb_ + 1) * 128], QT[0:96, h, c0:512],
                            start=True, stop=True),
                            reads=[("KT", kb_)] + QT_keys, writes=[pSk])
                        PT, PTk = Rb5.next()
                        P.op("act", lambda e, pS=pS, PT=PT, c0=c0: e.activation(
                            out=PT[:, c0:512], in_=pS[:, c0:512], func=AF.Exp, scale=96 ** -0.5),
                            reads=[pSk], writes=[PTk])
                        if j >= 0:
                            P.op("dve", lambda e, PT=PT, c0=c0: e.tensor_tensor(
                                out=PT[:, c0:c0 + 128], in0=PT[:, c0:c0 + 128], in1=tri[:], op=ALU.mult),
                                reads=[PTk, "tri"], writes=[PTk])
                        new = (h, kb_, j, PT, PTk)
                    if new is not None:
                        pend.append(new)
                    if step >= ATT_SKEW:
                        h, kb_, j, PT, PTk = pend.pop(0)
                        if kb_ == 0:
                            psO_cur[h] = PO.next()
                        pO, pOk = psO_cur[h]

                        def pv(e, pO=pO, PT=PT, h=h, kb_=kb_, j=j):
                            for qs in range(max(j, 0), 4):
                                ins = e.matmul(pO[:, qs * 128:qs * 128 + 65], PT[:, qs * 128:(qs + 1) * 128],
                                               Vaug[:, kb_, h, :], start=(kb_ == 0 and qs == 0),
                                               stop=(kb_ == 4 * i + qs), skip_group_check=True)
                            return ins
                        P.op("pe", pv, reads=[PTk, ("V", kb_)], writes=[pOk])
                        if kb_ == nkb - 1:
                            pO3 = pO[:].rearrange("p (q d) -> p q d", q=4)
                            P.op("dve", lambda e, pO3=pO3: e.reciprocal(out=rec[:], in_=pO3[:, :, 64]),
                                 reads=[pOk], writes=["rec"])
                            P.op("dve", lambda e, pO3=pO3, h=h: e.tensor_tensor(
                                out=ya[:, :, h * 64:(h + 1) * 64], in0=pO3[:, :, 0:64],
                                in1=rec[:].unsqueeze(2).broadcast_to([128, 4, 64]), op=ALU.mult),
                                reads=[pOk, "rec"], writes=["ya"])
            xsrc = x_d if l == 0 else x1_d
            dst = y_d if last else x1_d
            dkey = "y" if last else "x1"
            skey = "x" if xsrc is x_d else "x1"
            cur["PG"] = PG
            for _ in thA(0, True):
                pass
            for i in range(NT):
                cur["hT"] = hTb[i % 2]
                cur["hTk"] = [("hT", i % 2, st) for st in range(4)]
                cur["PG"] = PG
                attention(i)
                cur["PG"] = PGB
                ga = thA(i + 1) if i + 1 < NT else iter(())
                gm = thM(i)
                a_alive = m_alive = True
                credit = 0.0
                while a_alive or m_alive:
                    credit += A_RATIO if m_alive else 1000.0
                    while credit >= 1.0 and a_alive:
                        credit -= 1.0
                        try:
                            next(ga)
                        except StopIteration:
                            a_alive = False
                    if not a_alive:
                        credit = 0.0
                    if m_alive:
                        try:
                            next(gm)
                        except StopIteration:
                            m_alive = False
                cur["PG"] = PG
    try:
        if not done_:
            _layers()
            assert W.cur == len(sched), (W.cur, len(sched))
    except _Stop:
        pass
    P.op("sp", None, reads=[("y", i, st) for i in range(NT) for st in range(4)] + final_keys)
    P.emit()
    es.close()
    return nc


def _pack(inputs):
    f = lambda k: np.asarray(inputs[k], dtype=np.float32)
    vfm = np.zeros((2, 128, NVF), np.float32)
    vrow = np.zeros((2, NVR), np.float32)
    for l in range(2):
        vfm[l, :, 0:8] = f("norm_g")[l].reshape(8, 128).T
        vfm[l, :, 8:16] = f("mem_norm_g")[l].reshape(8, 128).T
        vfm[l, :, 16:18] = f("cq_norm_g")[l].reshape(2, 128).T
        vfm[l, :, 18] = f("ckv_norm_g")[l]
        vfm[l, :, 19] = 1.0
        vfm[l, 0:64, 19] = f("mla_q_norm_g")[l][0:64]
        vfm[l, :, 20] = 1.0
        vfm[l, 0:64, 20] = f("mla_k_norm_g")[l][0:64]
        cw = f("conv_w")[l]
        for j in range(3):
            vfm[l, :, 21 + j * 4:21 + (j + 1) * 4] = cw[j].reshape(4, 128).T
        vfm[l, :, 33:37] = f("conv_b")[l].reshape(4, 128).T
        vfm[l, :, 37] = f("mem_q_norm_g")[l]
        vfm[l, :, 38] = f("mem_k_norm_g")[l]
        vfm[l, :, 39:71] = f("b_merge")[l].reshape(32, 128).T
        vrow[l, 0:32] = f("mla_q_norm_g")[l][64:96]
        vrow[l, 32:64] = f("mla_k_norm_g")[l][64:96]
        vfm[l, :, 71:75] = f("sg_ln_g")[l].reshape(4, 128).T
        vfm[l, :, 75:79] = f("sg_ln_b")[l].reshape(4, 128).T
        vrow[l, 64:576] = f("b_spatial")[l].reshape(512)
    return vfm, vrow


_NC_CACHE = {}


def _in_maps(inputs, cores, x_override=None):
    vfm, vrow = _pack(inputs)
    f = lambda k: np.ascontiguousarray(np.asarray(inputs[k], dtype=np.float32))
    shared = dict(vfm=vfm, vrow=vrow, w_in=f("w_in"), w_uq=f("w_uq"), w_ukv=f("w_ukv"),
                  w_spatial=f("w_spatial"), w_mem_kv=f("w_mem_kv"), w_branch=f("w_branch"), w_out=f("w_out"))
    x = f("x") if x_override is None else x_override
    mem = f("mem")
    pos = np.ascontiguousarray(np.asarray(inputs["positions"], dtype=np.int32))
    maps = []
    for b in cores:
        m = dict(shared)
        m["x"] = np.ascontiguousarray(x[b])
        m["mem"] = np.ascontiguousarray(mem[b])
        m["pos"] = np.ascontiguousarray(pos[b].reshape(16, 128))
        maps.append(m)
    return maps


def kernel(**inputs):
    if "nc" not in _NC_CACHE:
        _NC_CACHE["nc"] = build(2, 0)
    nc = _NC_CACHE["nc"]
    maps = _in_maps(inputs, list(range(8)))
    res = run_bass_kernel_spmd(nc, maps, core_ids=list(range(8)))
    out = np.stack([np.asarray(r["y"], dtype=np.float32) for r in res.results], axis=0)
    return out
```

```python
import contextlib
import math
import numpy as np
import concourse.bass as bass
import concourse.mybir as mybir
from concourse.bass_utils import run_bass_kernel_spmd

F32 = mybir.dt.float32
BF16 = mybir.dt.bfloat16
I32 = mybir.dt.int32
AF = mybir.ActivationFunctionType
ALU = mybir.AluOpType
AX = mybir.AxisListType

S = 2048
D = 1024
T = 512
NT = S // T
EPS = 1e-6
IN_W = 9632
NVF = 79
NVR = 576
NSLOT = 4


class Prog:
    CH = 8000

    def __init__(self, nc):
        self.nc = nc
        self.ops = []
        self.last_w = {}
        self.readers = {}
        self.dma_counts = {}

    def op(self, eng, fn, reads=(), writes=(), dsem=None):
        idx = len(self.ops)
        deps = set()
        for k in reads:
            if k in self.last_w:
                deps.add((self.last_w[k], "raw"))
            if isinstance(k, tuple) and k[0] == "ps":
                for r in self.readers.get(k, ()):
                    deps.add((r, "war"))
        for k in writes:
            if k in self.last_w:
                deps.add((self.last_w[k], "waw"))
            for r in self.readers.get(k, ()):
                deps.add((r, "war"))
        o = dict(idx=idx, eng=eng, fn=fn, deps=deps, dsem=dsem)
        if dsem is not None:
            self.dma_counts[dsem] = self.dma_counts.get(dsem, 0) + 1
            o["dcount"] = self.dma_counts[dsem]
        self.ops.append(o)
        for k in reads:
            self.readers.setdefault(k, []).append(idx)
        for k in writes:
            self.last_w[k] = idx
            self.readers[k] = []
        return idx

    def emit(self):
        nc = self.nc
        ops = self.ops
        need = set()
        for o in ops:
            w = []
            for (d, typ) in o["deps"]:
                p = ops[d]
                if p["dsem"] is not None:
                    w.append(("dma", p["dsem"], p["dcount"]))
                    continue
                if p["eng"] == o["eng"] and o["dsem"] is None:
                    if o["eng"] == "pe" or typ == "war":
                        continue
                need.add(d)
                w.append(("eng", p["eng"], d))
            o["waits"] = w
        ordc = {}
        for o in ops:
            if o["idx"] in need:
                e = o["eng"]
                ordc[e] = ordc.get(e, 0) + 1
                o["ord"] = ordc[e]
        with contextlib.ExitStack() as es:
            esem = {}
            for e, n in ordc.items():
                esem[e] = [es.enter_context(nc.semaphore("s_%s_%d" % (e, c)))
                           for c in range((n + self.CH - 1) // self.CH + 1)]
            dsem = {}
            for j, k in enumerate(self.dma_counts):
                dsem[k] = es.enter_context(nc.semaphore("d_%d" % j))
            block = es.enter_context(nc.Block())
            per = {}
            for o in ops:
                per.setdefault(o["eng"], []).append(o)

            def run(ename, e):
                waited_e = {}
                waited_d = {}
                for o in per.get(ename, []):
                    we = {}
                    wd = {}
                    for w in o["waits"]:
                        if w[0] == "eng":
                            oo = ops[w[2]]["ord"]
                            we[w[1]] = max(we.get(w[1], 0), oo)
                        else:
                            wd[w[1]] = max(wd.get(w[1], 0), w[2])
                    for pe_, oo in we.items():
                        if waited_e.get(pe_, 0) >= oo:
                            continue
                        waited_e[pe_] = oo
                        e.wait_ge(esem[pe_][(oo - 1) // self.CH], (oo - 1) % self.CH + 1)
                    for k, cnt in wd.items():
                        if waited_d.get(k, 0) >= cnt:
                            continue
                        waited_d[k] = cnt
                        e.wait_ge(dsem[k], 16 * cnt)
                    if o["fn"] is None:
                        continue
                    ins = o["fn"](e)
                    if o["dsem"] is not None:
                        ins.then_inc(dsem[o["dsem"]], 16)
                    elif "ord" in o:
                        oo = o["ord"]
                        ins.then_inc(esem[ename][(oo - 1) // self.CH], 1)

            @block.tensor
            def _(e):
                run("pe", e)

            @block.scalar
            def _(e):
                run("act", e)

            @block.vector
            def _(e):
                run("dve", e)

            @block.gpsimd
            def _(e):
                run("pool", e)

            @block.sync
            def _(e):
                run("sp", e)


class _Stop(Exception):
    pass


def build(n_layers=2, first_layer=0, stop=None):
    nc = bass.Bass("TRN2", target_bir_lowering=False)
    L = n_layers

    def din(name, shape, dt=F32):
        return nc.dram_tensor(name, shape, dt, kind="ExternalInput").ap()

    x_d = din("x", [S, D])
    mem_d = din("mem", [256, D])
    pos_d = din("pos", [16, 128], I32)
    vfm_d = din("vfm", [2, 128, NVF])
    vrow_d = din("vrow", [2, NVR])
    w_in_d = din("w_in", [2, D, IN_W])
    w_uq_d = din("w_uq", [2, 256, 768])
    w_ukv_d = din("w_ukv", [2, 128, 1024])
    w_sp_d = din("w_spatial", [2, 4, 128, 128])
    w_mkv_d = din("w_mem_kv", [2, D, 1024])
    w_br_d = din("w_branch", [2, 4, 512, D])
    w_out_d = din("w_out", [2, D, D])
    y_d = nc.dram_tensor("y", [S, D], F32, kind="ExternalOutput").ap()
    x1_d = nc.dram_tensor("x1s", [S, D], F32, kind="Internal").ap()

    P = Prog(nc)
    es = contextlib.ExitStack()

    def sb(name, shape, dt):
        return es.enter_context(nc.sbuf_tensor(name, shape, dt))

    xwb = [sb("xw%d" % j, [128, D], F32) for j in range(2)]
    KT = sb("KT", [128, 8, S], BF16)
    Vaug = sb("Vaug", [128, 16, 8, 65], BF16)
    hTb = [sb("hT%d" % j, [128, 8, T], BF16) for j in range(2)]
    QT = sb("QT", [128, 8, T], BF16)
    ys = sb("ys", [128, 4, T], BF16)
    acc = sb("acc", [128, 8, T], BF16)
    wslot = [sb("wslot%d" % j, [128, 4096], BF16) for j in range(NSLOT)]
    vfm = sb("vfm_sb", [128, NVF], F32)
    vrow = sb("vrow_sb", [128, NVR], F32)
    hbm = sb("hbm", [128, 32], F32)
    wuq = sb("wuq", [128, 2, 768], BF16)
    wukv = sb("wukv", [128, 1024], BF16)
    wTs = sb("wTs", [128, 4, 128], BF16)
    Cg = sb("Cg", [128, 4, 128], F32)
    nb4 = sb("nb4", [128, 4], F32)
    ones_bf = sb("ones_bf", [128, 128], BF16)
    ident = sb("ident", [128, 128], BF16)
    identf = sb("identf", [128, 128], F32)
    tri = sb("tri", [128, 128], BF16)
    mhalf = sb("mhalf", [128, 32], F32)
    invf = sb("invf", [128, 16], F32)
    post = sb("post", [128, 16], F32)
    cosT = sb("cosT", [128, 16, 16], F32)
    sinT = sb("sinT", [128, 16, 16], F32)
    KmT = sb("KmT", [128, 4, 256], BF16)
    Vm = sb("Vm", [128, 2, 4, 128], BF16)
    QmT = sb("QmT", [128, 4, T], BF16)
    cqT = sb("cqT", [128, 3, T], BF16)
    ya = sb("ya", [128, 4, 512], BF16)
    yall = sb("yall", [128, 4, 512], BF16)
    zc = sb("zc", [128, 4, 516], BF16)
    junk = sb("junk", [128, 1024], BF16)
    kr = sb("kr", [128, 4, 32], F32)
    ssx = sb("ssx", [128, 4], F32)
    rx = sb("rx", [128, 4], F32)
    ss3 = sb("ss3", [128, 12], F32)
    r2 = sb("r2", [128, 2], F32)
    r2w = sb("r2w", [128, 8], F32)
    ssq = sb("ssq", [128, 8], F32)
    rq = sb("rq", [128, 8], F32)
    ssk = sb("ssk", [128, 8], F32)
    rk = sb("rk", [128, 8], F32)
    rtmp = sb("rtmp", [128, 8], F32)
    rtmpA = sb("rtmpA", [128, 16], F32)
    ssqk = sb("ssqk", [128, 16], F32)
    rqk = sb("rqk", [128, 16], F32)
    rtmpB = sb("rtmpB", [128, 8], F32)
    wAs = sb("wAs", [128, 8 * 416], BF16)
    bA = [sb("bA_%d" % j, [128, 512], BF16) for j in range(2)]
    st6 = sb("st6", [128, 6], F32)
    st6s = sb("st6s", [128, 4, 6], F32)
    mvs = sb("mvs", [128, 4, 2], F32)
    rs4 = sb("rs4", [128, 4], F32)
    ss16 = sb("ss16", [128, 16], F32)
    r16 = sb("r16", [128, 16], F32)
    rtmp16 = sb("rtmp16", [128, 16], F32)
    mv = sb("mv", [128, 2], F32)
    rs1 = sb("rs1", [128, 1], F32)
    ss4 = sb("ss4", [128, 4], F32)
    r4 = sb("r4", [128, 4], F32)
    rec = sb("rec", [128, 4], F32)
    ssm = sb("ssm", [128, 2], F32)
    rm = sb("rm", [128, 2], F32)
    NF1 = 2
    f1024 = [sb("f1024_%d" % j, [128, 1024], F32) for j in range(NF1)]
    NF5 = 4
    f512 = [sb("f512_%d" % j, [128, 512], F32) for j in range(NF5)]
    NB5 = 6
    b512 = [sb("b512_%d" % j, [128, 512], BF16) for j in range(NB5)]
    NB1 = 3
    b1024 = [sb("b1024_%d" % j, [128, 1024], BF16) for j in range(NB1)]
    NR = 0
    f256 = [sb("f256_%d" % j, [128, 256], F32) for j in range(NR)]
    f128 = []
    fqk = [sb("fqk_%d" % j, [128, 512], F32) for j in range(2)]
    ps = [es.enter_context(nc.psum_tensor("ps%d" % j, [128, 512], F32)) for j in range(8)]

    class Rot:
        def __init__(self, items, key):
            self.items = items
            self.key = key
            self.i = 0

        def next(self):
            j = self.i % len(self.items)
            self.i += 1
            return self.items[j], (self.key, j)

    wspf = f512[0][:].rearrange("p (g s) -> p g s", g=4)
    posi = f512[1][0:16, 0:128].bitcast(I32)
    posf = f512[2][0:16, 0:128]
    ones_f = f512[3][:, 0:128]
    angt = f512[0][:, 0:256]
    kkt = f512[1][:, 0:256]
    kit = f512[2][:, 0:256].bitcast(I32)
    redt = f512[3][:, 0:256]
    memnT = yall[:].rearrange("p c t -> p (c t)").rearrange("p (k m) -> p k m", k=8)
    vln = ya
    RbA = Rot(bA, "bA")
    wAv = wAs[:].rearrange("p (k n) -> p k n", k=8)
    Rxw = Rot(xwb, "xw")
    Rqk = Rot(fqk, "fqk")
    Rf1 = Rot(f1024, "f1024")
    ssqk2 = sb("ssqk2", [128, 16], F32)
    rqk2 = sb("rqk2", [128, 16], F32)
    rtmpA2 = sb("rtmpA2", [128, 16], F32)

    class RotK:
        def __init__(self, items):
            self.items = items
            self.i = 0

        def next(self):
            j = self.i % len(self.items)
            self.i += 1
            return self.items[j]

    Rf5 = Rot(f512, "f512")
    Rb5 = Rot(b512, "b512")
    Rb1 = Rot(b1024, "b1024")
    Rf2 = Rot(f256, "f256")
    Rf128 = Rot(f128, "f128")

    class PsRot:
        def __init__(self, banks):
            self.banks = banks
            self.i = 0

        def next(self):
            b = self.banks[self.i % len(self.banks)]
            self.i += 1
            return ps[b], ("ps", b)

    PG = PsRot([0, 1, 2, 3])
    PGA = PsRot([0, 1])
    PW = PsRot([4, 5, 6, 7])
    PGM = PsRot([2, 3, 4])
    PGB = PsRot([2, 3, 4, 5])
    cur = {"PG": PG}
    A_RATIO = 1.0
    PS_ = PsRot([2, 3, 4, 5])
    ATT_SKEW = 3
    PO = PsRot([6, 7])

    def bfv(p):
        return p[:].bitcast(BF16)

    RES_A = dict(pg=PGA, f1=Rf1, b1=Rb1, qk=Rqk, ssqk=ssqk, rqk=rqk, rtmp=rtmpA, n="")
    RES_B = dict(pg=PsRot([2, 3]),
                 f1=RotK([(xwb[0], ("xw", 0)), (xwb[1], ("xw", 1))]),
                 b1=RotK([(b1024[2], ("b1024", 2)), (junk, "junk")]),
                 qk=RotK([(f512[0], ("f512", 0)), (f512[1], ("f512", 1))]),
                 ssqk=ssqk2, rqk=rqk2, rtmp=rtmpA2, n="B")
    RES_A0 = dict(RES_A, b1=RotK([(b1024[0], ("b1024", 0)), (b1024[1], ("b1024", 1))]))

    sched = []
    for l in range(L):
        ll = l + first_layer
        sched.append(("memK", w_mkv_d[ll][:, 0:512], 8, 512))
        sched.append(("memV", w_mkv_d[ll][:, 512:1024], 8, 512))
        for i in range(NT):
            def wi(a, b_):
                return w_in_d[ll][:, a:b_]
            sched.append(("gate0", wi(3488, 4000), 8, 512))
            for n in range(4):
                if n == 1:
                    sched.append(("conv_c", wi(928, 1440), 8, 512))
                    sched.append(("conv_x", wi(1440, 1952), 8, 512))
                    sched.append(("conv_b", wi(416, 928), 8, 512))
                    sched.append(("gate1", wi(4000, 4512), 8, 512))
                if n == 2:
                    sched.append(("sg_v", wi(2464, 2976), 8, 512))
                    sched.append(("gate2", wi(4512, 5024), 8, 512))
                    sched.append(("sg_u", wi(1952, 2464), 8, 512))
                if n == 3:
                    sched.append(("memq", wi(2976, 3488), 8, 512))
                    sched.append(("gate3", wi(5024, 5536), 8, 512))
                for hf in range(2):
                    c0_ = 5536 + n * 1024 + hf * 512
                    sched.append(("m%d_%d" % (n, hf), wi(c0_, c0_ + 512), 8, 512))
                    sched.append(("wb%d_%d" % (n, hf), w_br_d[ll, n][:, hf * 512:(hf + 1) * 512], 4, 512))
            sched.append(("wo0", w_out_d[ll][:, 0:512], 8, 512))
            sched.append(("wo1", w_out_d[ll][:, 512:1024], 8, 512))

    class WStream:
        def __init__(self):
            self.issued = 0
            self.cur = 0

        def _issue(self):
            c = self.issued
            if c >= len(sched):
                return
            name, src, nk, ncol = sched[c]
            s = c % NSLOT
            view = wslot[s][:, 0:nk * ncol].rearrange("p (k n) -> p k n", k=nk)
            srcv = src.rearrange("(k p) n -> p k n", p=128)
            P.op("pool", lambda e, view=view, srcv=srcv: e.dma_start(out=view, in_=srcv),
                 writes=[("w", s)], dsem=("w", s))
            self.issued += 1

        def next(self, name):
            while self.issued < min(len(sched), self.cur + NSLOT - 1):
                self._issue()
            n2, src, nk, ncol = sched[self.cur]
            assert n2 == name, (n2, name)
            s = self.cur % NSLOT
            self.cur += 1
            view = wslot[s][:, 0:nk * ncol].rearrange("p (k n) -> p k n", k=nk)
            return view, ("w", s)

    W = WStream()

    def rsqrt(out_ap, in_ap, scale, n, rkeys, wkeys, tmp=None, tmpk="rtmp"):
        tmp = rtmp if tmp is None else tmp
        P.op("pool", lambda e: e.tensor_scalar(out=tmp[:, 0:n], in0=in_ap, scalar1=scale, scalar2=EPS,
                                               op0=ALU.mult, op1=ALU.add), reads=rkeys, writes=[tmpk])
        P.op("pool", lambda e: e.tensor_tensor(out=out_ap, in0=tmp[:, 0:n], in1=mhalf[:, 0:n], op=ALU.pow),
             reads=[tmpk, "mhalf"], writes=wkeys)

    def rsqrtA(out_ap, in_ap, scale, n, rkeys, wkeys):
        rsqrt(out_ap, in_ap, scale, n, rkeys, wkeys, tmp=rtmpA, tmpk="rtmpA")

    def mm_group(out_ap, pairs, reads, pskey):
        def fn(e):
            n = len(pairs)
            for j, (a, b_) in enumerate(pairs):
                ins = e.matmul(out_ap, a, b_, start=(j == 0), stop=(j == n - 1))
            return ins
        P.op("pe", fn, reads=reads, writes=[pskey])

    def transposes(pairs, reads, pskey, idt):
        def fn(e):
            for (o_, i_) in pairs:
                ins = e.transpose(out=o_, in_=i_, identity=idt)
            return ins
        P.op("pe", fn, reads=reads + ["ident"], writes=[pskey])

    final_keys = []
    dbg_n = [0]
    dbg_off = [0]

    def checkpoint(name, dumps):
        if stop != name:
            return
        yv = y_d.rearrange("(p a) d -> p (a d)", p=128)
        for (ap, n, keys) in dumps:
            for c0 in range(0, n, 1024):
                w = min(1024, n - c0)
                stg, stgk = Rf1.next()
                off = dbg_off[0]
                idx = dbg_n[0]
                P.op("dve", lambda e, stg=stg, ap=ap, c0=c0, w=w: e.tensor_copy(out=stg[:, 0:w], in_=ap[:, c0:c0 + w]),
                     reads=keys, writes=[stgk])
                P.op("sp", lambda e, stg=stg, off=off, w=w: e.dma_start(out=yv[:, off:off + w], in_=stg[:, 0:w]),
                     reads=[stgk], writes=[("ydbg", idx)], dsem=("dbg", idx))
                final_keys.append(("ydbg", idx))
                dbg_off[0] += w
                dbg_n[0] += 1
        raise _Stop()

    P.op("pool", lambda e: e.memset(ones_bf[:], 1.0), writes=["ones_bf"])
    P.op("pool", lambda e: e.memset(ones_f[:], 1.0), writes=[("f512", 3)])
    P.op("pool", lambda e: e.memset(mhalf[:], -0.5), writes=["mhalf"])
    P.op("pool", lambda e: e.affine_select(out=ident[:], in_=ones_bf[:], pattern=[[1, 128]],
                                           compare_op=ALU.is_equal, fill=0.0, base=0, channel_multiplier=-1),
         reads=["ones_bf"], writes=["ident"])
    P.op("pool", lambda e: e.affine_select(out=identf[:], in_=ones_f[:], pattern=[[1, 128]],
                                           compare_op=ALU.is_equal, fill=0.0, base=0, channel_multiplier=-1),
         reads=[("f512", 3)], writes=["identf"])
    P.op("pool", lambda e: e.affine_select(out=tri[:], in_=ones_bf[:], pattern=[[1, 128]],
                                           compare_op=ALU.is_ge, fill=0.0, base=0, channel_multiplier=-1),
         reads=["ones_bf"], writes=["tri"])
    P.op("pool", lambda e: e.memset(Vaug[:].rearrange("p a b c -> p (a b c)"), 1.0), writes=[("V", b_) for b_ in range(16)])
    inv = np.power(np.float32(10000.0), -np.arange(16, dtype=np.float32) / np.float32(16)).astype(np.float32)
    for j in range(16):
        P.op("pool", lambda e, j=j: e.memset(invf[:, j:j + 1], float(inv[j])), writes=["invf"])
    P.op("sp", lambda e: e.dma_start(out=posi[:], in_=pos_d), writes=[("f512", 1)], dsem=("f512", 1))
    P.op("dve", lambda e: e.tensor_copy(out=posf[:], in_=posi[:]), reads=[("f512", 1)], writes=[("f512", 2)])
    P.op("pe", lambda e: e.transpose(out=ps[0][:, 0:16], in_=posf[:], identity=identf[0:16, 0:16]),
         reads=[("f512", 2), "identf"], writes=[("ps", 0)])
    P.op("dve", lambda e: e.tensor_copy(out=post[:], in_=ps[0][:, 0:16]), reads=[("ps", 0)], writes=["post"])
    for s_ in range(16):
        P.op("dve", lambda e, s_=s_: e.tensor_scalar(out=angt[:, s_ * 16:(s_ + 1) * 16], in0=invf[:],
                                                      scalar1=post[:, s_:s_ + 1], scalar2=None, op0=ALU.mult),
             reads=["post", "invf"], writes=[("f512", 0)])
    C1 = 6.28125
    C2 = 2 * math.pi - 6.28125
    P.op("dve", lambda e: e.tensor_scalar(out=kkt[:], in0=angt[:], scalar1=1.0 / (2 * math.pi), scalar2=None,
                                          op0=ALU.mult), reads=[("f512", 0)], writes=[("f512", 1)])
    P.op("dve", lambda e: e.tensor_copy(out=kit[:], in_=kkt[:]), reads=[("f512", 1)], writes=[("f512", 2)])
    P.op("dve", lambda e: e.tensor_copy(out=kkt[:], in_=kit[:]), reads=[("f512", 2)], writes=[("f512", 1)])
    P.op("dve", lambda e: e.scalar_tensor_tensor(out=redt[:], in0=kkt[:], scalar=-C1, in1=angt[:],
                                                 op0=ALU.mult, op1=ALU.add), reads=[("f512", 1), ("f512", 0)], writes=[("f512", 3)])
    P.op("dve", lambda e: e.scalar_tensor_tensor(out=redt[:], in0=kkt[:], scalar=-C2, in1=redt[:],
                                                 op0=ALU.mult, op1=ALU.add), reads=[("f512", 1), ("f512", 3)], writes=[("f512", 3)])

    def wrap():
        P.op("dve", lambda e: e.tensor_scalar(out=kkt[:], in0=redt[:], scalar1=math.pi, scalar2=-2 * math.pi,
                                              op0=ALU.is_gt, op1=ALU.mult), reads=[("f512", 3)], writes=[("f512", 1)])
        P.op("dve", lambda e: e.tensor_tensor(out=redt[:], in0=redt[:], in1=kkt[:], op=ALU.add),
             reads=[("f512", 3), ("f512", 1)], writes=[("f512", 3)])
        P.op("dve", lambda e: e.tensor_scalar(out=kkt[:], in0=redt[:], scalar1=-math.pi, scalar2=2 * math.pi,
                                              op0=ALU.is_lt, op1=ALU.mult), reads=[("f512", 3)], writes=[("f512", 1)])
        P.op("dve", lambda e: e.tensor_tensor(out=redt[:], in0=redt[:], in1=kkt[:], op=ALU.add),
             reads=[("f512", 3), ("f512", 1)], writes=[("f512", 3)])
        P.op("dve", lambda e: e.tensor_scalar(out=redt[:], in0=redt[:], scalar1=math.pi, scalar2=-math.pi,
                                              op0=ALU.min, op1=ALU.max), reads=[("f512", 3)], writes=[("f512", 3)])

    wrap()
    P.op("act", lambda e: e.activation(out=sinT[:].rearrange("p a b -> p (a b)"), in_=redt[:], func=AF.Sin),
         reads=[("f512", 3)], writes=["sinT"])
    P.op("dve", lambda e: e.tensor_scalar(out=redt[:], in0=redt[:], scalar1=math.pi / 2, scalar2=None,
                                          op0=ALU.add), reads=[("f512", 3), "sinT"], writes=[("f512", 3)])
    wrap()
    P.op("act", lambda e: e.activation(out=cosT[:].rearrange("p a b -> p (a b)"), in_=redt[:], func=AF.Sin),
         reads=[("f512", 3)], writes=["cosT"])

    done_ = False
    try:
        checkpoint("setup", [(cosT[:].rearrange("p a b -> p (a b)"), 256, ["cosT"]),
                             (sinT[:].rearrange("p a b -> p (a b)"), 256, ["sinT"]),
                             (post[:], 16, ["post"]), (tri[:], 128, ["tri"]), (ident[:], 128, ["ident"])])
    except _Stop:
        done_ = True
    def rope(src, dst, sg, skey, dkey):
        cosb = cosT[:, sg, :].unsqueeze(1).broadcast_to([128, 8, 16])
        sinb = sinT[:, sg, :].unsqueeze(1).broadcast_to([128, 8, 16])
        t1, k1 = Rf128.next()
        t2, k2 = Rf128.next()
        t1v = t1[:].rearrange("p (h d) -> p h d", h=8)
        t2v = t2[:].rearrange("p (h d) -> p h d", h=8)
        P.op("dve", lambda e: e.tensor_tensor(out=t1v, in0=src[:, :, 0:16], in1=cosb, op=ALU.mult),
             reads=[skey, "cosT"], writes=[k1])
        P.op("dve", lambda e: e.tensor_tensor(out=t2v, in0=src[:, :, 16:32], in1=sinb, op=ALU.mult),
             reads=[skey, "sinT"], writes=[k2])
        P.op("dve", lambda e: e.tensor_tensor(out=dst[:, :, 0:16], in0=t1v, in1=t2v, op=ALU.subtract),
             reads=[k1, k2], writes=[dkey])
        t3, k3 = Rf128.next()
        t4, k4 = Rf128.next()
        t3v = t3[:].rearrange("p (h d) -> p h d", h=8)
        t4v = t4[:].rearrange("p (h d) -> p h d", h=8)
        P.op("dve", lambda e: e.tensor_tensor(out=t3v, in0=src[:, :, 0:16], in1=sinb, op=ALU.mult),
             reads=[skey, "sinT"], writes=[k3])
        P.op("dve", lambda e: e.tensor_tensor(out=t4v, in0=src[:, :, 16:32], in1=cosb, op=ALU.mult),
             reads=[skey, "cosT"], writes=[k4])
        P.op("dve", lambda e: e.tensor_tensor(out=dst[:, :, 16:32], in0=t3v, in1=t4v, op=ALU.add),
             reads=[k3, k4], writes=[dkey])

    YALLK = [("yall", c) for c in range(4)]

    def gate_chunk(wg, wgk, c, ysrc_ap, ysrc_keys):
        pg, pgk = cur["PG"].next()
        mm_group(pg[:], [(wg[:, k, c * 128:(c + 1) * 128], cur["hT"][:, k, :]) for k in range(8)],
                 [wgk] + cur["hTk"], pgk)
        tg, tgk = Rb5.next()
        P.op("act", lambda e: e.activation(out=tg[:], in_=pg[:], func=AF.Tanh, scale=0.5),
             reads=[pgk], writes=[tgk])
        u, uk = Rb5.next()
        P.op("dve", lambda e: e.scalar_tensor_tensor(out=u[:], in0=tg[:], scalar=1.0, in1=pg[:],
                                                     op0=ALU.add, op1=ALU.mult), reads=[tgk, pgk], writes=[uk])
        P.op("dve", lambda e: e.tensor_tensor(out=ys[:, c, :], in0=u[:], in1=ysrc_ap, op=ALU.mult),
             reads=[uk] + ysrc_keys, writes=[("ys", c)])

    def gate_pre(wg, wgk, c):
        pg, pgk = cur["PG"].next()
        mm_group(pg[:], [(wg[:, k, c * 128:(c + 1) * 128], cur["hT"][:, k, :]) for k in range(8)],
                 [wgk] + cur["hTk"], pgk)
        tg, tgk = Rb5.next()
        P.op("act", lambda e: e.activation(out=tg[:], in_=pg[:], func=AF.Tanh, scale=0.5),
             reads=[pgk], writes=[tgk])
        P.op("dve", lambda e: e.scalar_tensor_tensor(out=ys[:, c, :], in0=tg[:], scalar=1.0, in1=pg[:],
                                                     op0=ALU.add, op1=ALU.mult), reads=[tgk, pgk],
             writes=[("ys", c)])

    def gate_post(c, ysrc_ap, ysrc_keys):
        P.op("dve", lambda e: e.tensor_tensor(out=ys[:, c, :], in0=ys[:, c, :], in1=ysrc_ap, op=ALU.mult),
             reads=[("ys", c)] + ysrc_keys, writes=[("ys", c)])

    def tm_post(ytm, ykey):
        for c in range(4):
            pt, ptk = cur["PG"].next()
            ptb = bfv(pt)
            transposes([(ptb[:, qs * 128:(qs + 1) * 128], ytm[:, qs, c * 128:(c + 1) * 128]) for qs in range(4)],
                       [ykey], ptk, ident[:])
            gate_post(c, ptb[:, 0:512], [ptk])
            yield

    def merge_branch(n, first):
        for hf in range(2):
            wm, wmk = W.next("m%d_%d" % (n, hf))
            wb, wbk = W.next("wb%d_%d" % (n, hf))

            def logits(c4):
                dc = hf * 4 + c4
                pl, plk = cur["PG"].next()
                mm_group(pl[:], [(wm[:, k, c4 * 128:(c4 + 1) * 128], cur["hT"][:, k, :]) for k in range(8)],
                         [wmk] + cur["hTk"], plk)
                tm, tmk = Rb5.next()
                P.op("act", lambda e, pl=pl, tm=tm, dc=dc: e.activation(out=tm[:], in_=pl[:], func=AF.Tanh,
                                                                       bias=hbm[:, n * 8 + dc:n * 8 + dc + 1],
                                                                       scale=0.5),
                     reads=[plk, "hbm"], writes=[tmk])
                return tm, tmk

            nxt = logits(0)
            yield
            for c4 in range(4):
                dc = hf * 4 + c4
                tm, tmk = nxt
                if c4 + 1 < 4:
                    nxt = logits(c4 + 1)
                    yield
                pz, pzk = cur["PG"].next()
                mm_group(pz[:], [(wb[:, kc, c4 * 128:(c4 + 1) * 128], ys[:, kc, :]) for kc in range(4)],
                         [wbk] + [("ys", c) for c in range(4)], pzk)
                if first:
                    P.op("dve", lambda e, tm=tm, pz=pz, dc=dc: e.scalar_tensor_tensor(
                        out=acc[:, dc, :], in0=tm[:], scalar=1.0, in1=pz[:], op0=ALU.add, op1=ALU.mult),
                        reads=[tmk, pzk], writes=[("acc", dc)])
                else:
                    tp, tpk = Rf5.next()
                    P.op("dve", lambda e, tm=tm, pz=pz, tp=tp: e.scalar_tensor_tensor(
                        out=tp[:], in0=tm[:], scalar=1.0, in1=pz[:], op0=ALU.add, op1=ALU.mult),
                        reads=[tmk, pzk], writes=[tpk])
                    P.op("dve", lambda e, tp=tp, dc=dc: e.tensor_tensor(out=acc[:, dc, :], in0=acc[:, dc, :],
                                                                       in1=tp[:], op=ALU.add),
                         reads=[tpk, ("acc", dc)], writes=[("acc", dc)])
                yield

    def tm_to_ys(ytm, ykey, wg, wgk):
        for c in range(4):
            pt, ptk = cur["PG"].next()
            ptb = bfv(pt)
            transposes([(ptb[:, qs * 128:(qs + 1) * 128], ytm[:, qs, c * 128:(c + 1) * 128]) for qs in range(4)],
                       [ykey], ptk, ident[:])
            gate_chunk(wg, wgk, c, ptb[:, 0:512], [ptk])
            yield

    def _layers():
        for l in range(L):
            ll = l + first_layer
            last = (l == L - 1)
            P.op("sp", lambda e, ll=ll: e.dma_start(out=vfm[:], in_=vfm_d[ll]), writes=["vfm"], dsem="vfm")
            P.op("sp", lambda e, ll=ll: e.dma_start(out=vrow[:], in_=vrow_d[ll, :].partition_broadcast(128)),
                 writes=["vrow"], dsem="vrow")
            P.op("dve", lambda e: e.tensor_scalar(out=hbm[:], in0=vfm[:, 39:71], scalar1=0.5, scalar2=None,
                                                  op0=ALU.mult), reads=["vfm"], writes=["hbm"])
            P.op("pool", lambda e, ll=ll: e.dma_start(out=wuq[:], in_=w_uq_d[ll].rearrange("(k p) n -> p k n", p=128)),
                 writes=["wuq"], dsem="wuq")
            P.op("pool", lambda e, ll=ll: e.dma_start(out=wukv[:], in_=w_ukv_d[ll]), writes=["wukv"], dsem="wukv")
            P.op("pool", lambda e, ll=ll: e.dma_start(
                out=wAv, in_=w_in_d[ll][:, 0:416].rearrange("(k p) n -> p k n", p=128)), writes=["wA"], dsem="wA")
            P.op("sp", lambda e, ll=ll: e.dma_start(out=wspf[:], in_=w_sp_d[ll].rearrange("g t s -> t g s")),
                 writes=[("f512", 0)], dsem=("f512", 0))
            for g in range(4):
                pt, ptk = PG.next()
                P.op("pe", lambda e, pt=pt, g=g: e.transpose(out=pt[:, 0:128], in_=wspf[:, g, :], identity=identf[:]),
                     reads=[("f512", 0), "identf"], writes=[ptk])
                tf, tfk = f512[1 + g % 3][:, 0:128], ("f512", 1 + g % 3)
                P.op("act", lambda e, pt=pt, tf=tf: e.activation(out=tf[:], in_=pt[:, 0:128], func=AF.Copy),
                     reads=[ptk], writes=[tfk])
                P.op("pool", lambda e, tf=tf, g=g: e.affine_select(out=wTs[:, g, :], in_=tf[:], pattern=[[1, 128]],
                                                                  compare_op=ALU.is_ge, fill=0.0, base=0,
                                                                  channel_multiplier=-1),
                     reads=[tfk], writes=["wTs"])
                pr, prk = PG.next()
                P.op("pe", lambda e, pr=pr, g=g: e.matmul(pr[:, 0:128], ones_bf[:, 0:128], wTs[:, g, :], start=True, stop=True),
                     reads=["ones_bf", "wTs"], writes=[prk])
                P.op("dve", lambda e, pr=pr, g=g: e.scalar_tensor_tensor(
                    out=Cg[:, g, :], in0=pr[:, 0:128], scalar=vfm[:, 75 + g:76 + g], in1=vrow[:, 64 + g * 128:64 + (g + 1) * 128],
                    op0=ALU.mult, op1=ALU.add), reads=[prk, "vfm", "vrow"], writes=["Cg"])
            P.op("dve", lambda e: e.memset(zc[:, :, 0:2], 0.0), writes=[("z", c) for c in range(4)])
            memt = [f1024[0], f1024[1]]
            mkeys = [("f1024", 0), ("f1024", 1)]
            Rf1.i = 2
            for mb in range(2):
                P.op("sp", lambda e, mb=mb: e.dma_start(out=memt[mb][:], in_=mem_d[mb * 128:(mb + 1) * 128, :]),
                     writes=[mkeys[mb]], dsem=("memt", mb))
            for mb in range(2):
                P.op("act", lambda e, mb=mb: e.activation(out=junk[:], in_=memt[mb][:], func=AF.Square,
                                                          accum_out=ssm[:, mb:mb + 1]),
                     reads=[mkeys[mb]], writes=[("ssm", mb), "junk"])
            rsqrt(rm[:], ssm[:], 1.0 / D, 2, [("ssm", 0), ("ssm", 1)], ["rm"])
            for mb in range(2):
                mnb, mnbk = Rb1.next()
                P.op("dve", lambda e, mb=mb, mnb=mnb: e.tensor_scalar(out=mnb[:], in0=memt[mb][:], scalar1=rm[:, mb:mb + 1],
                                                                      scalar2=None, op0=ALU.mult),
                     reads=[mkeys[mb], "rm"], writes=[mnbk])
                pt, ptk = PG.next()
                ptb = bfv(pt)
                transposes([(ptb[:, k * 128:(k + 1) * 128], mnb[:, k * 128:(k + 1) * 128]) for k in range(8)],
                           [mnbk], ptk, ident[:])
                P.op("dve", lambda e, ptb=ptb, mb=mb: e.tensor_tensor(
                    out=memnT[:, :, mb * 128:(mb + 1) * 128], in0=ptb.rearrange("p (k t) -> p k t", k=8),
                    in1=vfm[:, 8:16].unsqueeze(2).broadcast_to([128, 8, 128]), op=ALU.mult),
                    reads=[ptk, "vfm"], writes=[*YALLK])
            wk_, wkk = W.next("memK")
            for mb in range(2):
                pk, pkk = PG.next()
                mm_group(pk[:], [(memnT[:, k, mb * 128:(mb + 1) * 128], wk_[:, k, :]) for k in range(8)],
                         [wkk, *YALLK], pkk)
                kf, kfk = Rf5.next()
                P.op("act", lambda e, pk=pk, kf=kf: e.activation(out=kf[:], in_=pk[:], func=AF.Copy),
                     reads=[pkk], writes=[kfk])
                sq, sqk = Rf5.next()
                P.op("act", lambda e, kf=kf, sq=sq: e.activation(out=sq[:], in_=kf[:], func=AF.Square),
                     reads=[kfk], writes=[sqk])
                P.op("dve", lambda e, sq=sq: e.tensor_reduce(out=ss4[:], in_=sq[:].rearrange("p (h d) -> p h d", h=4),
                                                            axis=AX.X, op=ALU.add), reads=[sqk], writes=["ss4"])
                rsqrt(r4[:], ss4[:], 1.0 / 128, 4, ["ss4"], ["r4"])
                knb, knbk = Rb5.next()
                P.op("dve", lambda e, kf=kf, knb=knb: e.tensor_tensor(
                    out=knb[:].rearrange("p (h d) -> p h d", h=4), in0=kf[:].rearrange("p (h d) -> p h d", h=4),
                    in1=r4[:].unsqueeze(2).broadcast_to([128, 4, 128]), op=ALU.mult),
                    reads=[kfk, "r4"], writes=[knbk])
                pt, ptk = PG.next()
                ptb = bfv(pt)
                transposes([(ptb[:, h * 128:(h + 1) * 128], knb[:, h * 128:(h + 1) * 128]) for h in range(4)],
                           [knbk], ptk, ident[:])
                P.op("dve", lambda e, ptb=ptb, mb=mb: e.tensor_scalar(
                    out=KmT[:, :, mb * 128:(mb + 1) * 128], in0=ptb[:, 0:512].rearrange("p (h t) -> p h t", h=4),
                    scalar1=vfm[:, 38:39], scalar2=None, op0=ALU.mult),
                    reads=[ptk, "vfm"], writes=[("KmT", mb)])
            wv_, wvk = W.next("memV")
            for mb in range(2):
                pv, pvk = PG.next()
                mm_group(pv[:], [(memnT[:, k, mb * 128:(mb + 1) * 128], wv_[:, k, :]) for k in range(8)],
                         [wvk, *YALLK], pvk)
                P.op("act", lambda e, pv=pv, mb=mb: e.activation(out=Vm[:, mb, :, :].rearrange("p h d -> p (h d)"),
                                                                in_=pv[:], func=AF.Copy),
                     reads=[pvk], writes=[("Vm", mb)])

            if l == 0:
                checkpoint("lsetup", [(wTs[:].rearrange("p a b -> p (a b)"), 512, ["wTs"]),
                                      (KmT[:].rearrange("p a b -> p (a b)"), 1024, [("KmT", 0), ("KmT", 1)]),
                                      (Vm[:].rearrange("p a b c -> p (a b c)"), 1024, [("Vm", 0), ("Vm", 1)]),
                                      (hbm[:], 32, ["hbm"]), (vrow[:, 0:64], 64, ["vrow"])])
            def thA_head(i):
                hT_ = hTb[i % 2]
                t0 = i * T
                for st in range(4):
                    xs, xsk = Rf1.next()
                    P.op('sp', lambda e, xs=xs, st=st, xsrc=xsrc, t0=t0: e.dma_start(out=xs[:], in_=xsrc[t0 + st * 128:t0 + (st + 1) * 128, :]), reads=[(skey, i, st)], writes=[xsk], dsem=xsk)
                    yield
                    P.op('act', lambda e, xs=xs, st=st: e.activation(out=junk[:], in_=xs[:], func=AF.Square, accum_out=ssx[:, st:st + 1]), reads=[xsk], writes=[('ssx', st), 'junk'])
                    yield
                    rsqrtA(rx[:, st:st + 1], ssx[:, st:st + 1], 1.0 / D, 1, [('ssx', st)], [('rx', st)])
                    yield
                    hb, hbk = Rb1.next()
                    P.op('dve', lambda e, xs=xs, st=st, hb=hb: e.tensor_scalar(out=hb[:], in0=xs[:], scalar1=rx[:, st:st + 1], scalar2=None, op0=ALU.mult), reads=[xsk, ('rx', st)], writes=[hbk])
                    yield
                    pt, ptk = PGA.next()
                    ptb = bfv(pt)
                    transposes([(ptb[:, k * 128:(k + 1) * 128], hb[:, k * 128:(k + 1) * 128]) for k in range(8)], [hbk], ptk, ident[:])
                    yield
                    P.op('dve', lambda e, ptb=ptb, st=st: e.tensor_tensor(out=hT_[:, :, st * 128:(st + 1) * 128], in0=ptb.rearrange('p (k t) -> p k t', k=8), in1=vfm[:, 0:8].unsqueeze(2).broadcast_to([128, 8, 128]), op=ALU.mult), reads=[ptk, 'vfm'], writes=[('hT', i % 2, st)])
                    yield
                wA, wAk = (wAv, 'wA')
                for st in range(4):
                    pg, pgk = PGA.next()
                    mm_group(pg[:, 0:416], [(hT_[:, k, st * 128:(st + 1) * 128], wA[:, k, 0:416]) for k in range(8)], [wAk, ('hT', i % 2, st)], pgk)
                    yield
                    for j, (a, b_) in enumerate([(0, 256), (256, 384), (384, 416)]):
                        P.op('act', lambda e, pg=pg, a=a, b_=b_, st=st, j=j: e.activation(out=junk[:, a:b_], in_=pg[:, a:b_], func=AF.Square, accum_out=ss3[:, st * 3 + j:st * 3 + j + 1]), reads=[pgk], writes=[('ss3', st, j)])
                        yield
                    P.op('dve', lambda e, pg=pg, st=st: e.tensor_copy(out=kr[:, st, :], in_=pg[:, 384:416]), reads=[pgk], writes=[('kr', st)])
                    yield
                    P.op('pool', lambda e, st=st: e.tensor_scalar(out=rtmpA[:, 0:1], in0=ss3[:, st * 3:st * 3 + 1], scalar1=1.0 / 256, scalar2=EPS, op0=ALU.mult, op1=ALU.add), reads=[('ss3', st, 0)], writes=['rtmpA'])
                    yield
                    P.op('pool', lambda e, st=st: e.tensor_scalar(out=rtmpA[:, 1:2], in0=ss3[:, st * 3 + 1:st * 3 + 2], scalar1=1.0 / 128, scalar2=EPS, op0=ALU.mult, op1=ALU.add), reads=[('ss3', st, 1)], writes=['rtmpA'])
                    yield
                    P.op('pool', lambda e: e.tensor_tensor(out=r2[:], in0=rtmpA[:, 0:2], in1=mhalf[:, 0:2], op=ALU.pow), reads=['rtmpA', 'mhalf'], writes=['r2'])
                    yield
                    cqn, cqnk = RbA.next()
                    P.op('dve', lambda e, pg=pg, cqn=cqn: e.tensor_scalar(out=cqn[:, 0:256], in0=pg[:, 0:256], scalar1=r2[:, 0:1], scalar2=None, op0=ALU.mult), reads=[pgk, 'r2'], writes=[cqnk])
                    yield
                    P.op('dve', lambda e, pg=pg, cqn=cqn: e.tensor_scalar(out=cqn[:, 256:384], in0=pg[:, 256:384], scalar1=r2[:, 1:2], scalar2=None, op0=ALU.mult), reads=[pgk, 'r2'], writes=[cqnk])
                    yield
                    pt, ptk = PGA.next()
                    ptb = bfv(pt)
                    transposes([(ptb[:, k * 128:(k + 1) * 128], cqn[:, k * 128:(k + 1) * 128]) for k in range(3)], [cqnk], ptk, ident[:])
                    yield
                    P.op('dve', lambda e, ptb=ptb, st=st: e.tensor_tensor(out=cqT[:, :, st * 128:(st + 1) * 128], in0=ptb[:, 0:384].rearrange('p (k t) -> p k t', k=3), in1=vfm[:, 16:19].unsqueeze(2).broadcast_to([128, 3, 128]), op=ALU.mult), reads=[ptk, 'vfm'], writes=[('cqT', st)])
                    yield
            def thA_head_wide(i):
                hT_ = hTb[i % 2]
                t0 = i * T
                xbufs = [(f1024[0], ('f1024', 0)), (f1024[1], ('f1024', 1)), (xwb[0], ('xw', 0)), (xwb[1], ('xw', 1))]
                for st in range(4):
                    xs, xsk = xbufs[st]
                    P.op('sp', lambda e, xs=xs, st=st, xsrc=xsrc, t0=t0: e.dma_start(out=xs[:], in_=xsrc[t0 + st * 128:t0 + (st + 1) * 128, :]), reads=[(skey, i, st)], writes=[xsk], dsem=xsk)
                for st in range(4):
                    xs, xsk = xbufs[st]
                    P.op('act', lambda e, xs=xs, st=st: e.activation(out=junk[:], in_=xs[:], func=AF.Square, accum_out=ssx[:, st:st + 1]), reads=[xsk], writes=[('ssx', st), 'junk'])
                rsqrtA(rx[:, 0:4], ssx[:, 0:4], 1.0 / D, 4, [('ssx', st) for st in range(4)], [('rx', st) for st in range(4)])
                yield
                for st in range(4):
                    xs, xsk = xbufs[st]
                    hb, hbk = Rb1.next()
                    P.op('dve', lambda e, xs=xs, st=st, hb=hb: e.tensor_scalar(out=hb[:], in0=xs[:], scalar1=rx[:, st:st + 1], scalar2=None, op0=ALU.mult), reads=[xsk, ('rx', st)], writes=[hbk])
                    pt, ptk = PG.next()
                    ptb = bfv(pt)
                    transposes([(ptb[:, k * 128:(k + 1) * 128], hb[:, k * 128:(k + 1) * 128]) for k in range(8)], [hbk], ptk, ident[:])
                    P.op('dve', lambda e, ptb=ptb, st=st: e.tensor_tensor(out=hT_[:, :, st * 128:(st + 1) * 128], in0=ptb.rearrange('p (k t) -> p k t', k=8), in1=vfm[:, 0:8].unsqueeze(2).broadcast_to([128, 8, 128]), op=ALU.mult), reads=[ptk, 'vfm'], writes=[('hT', i % 2, st)])
                    yield
                wA, wAk = (wAv, 'wA')
                pgs = []
                for st in range(4):
                    pg, pgk = PW.next()
                    mm_group(pg[:, 0:416], [(hT_[:, k, st * 128:(st + 1) * 128], wA[:, k, 0:416]) for k in range(8)], [wAk, ('hT', i % 2, st)], pgk)
                    pgs.append((pg, pgk))
                for st in range(4):
                    pg, pgk = pgs[st]
                    for j, (a, b_) in enumerate([(0, 256), (256, 384), (384, 416)]):
                        P.op('act', lambda e, pg=pg, a=a, b_=b_, st=st, j=j: e.activation(out=junk[:, a:b_], in_=pg[:, a:b_], func=AF.Square, accum_out=ss3[:, st * 3 + j:st * 3 + j + 1]), reads=[pgk], writes=[('ss3', st, j)])
                    P.op('dve', lambda e, pg=pg, st=st: e.tensor_copy(out=kr[:, st, :], in_=pg[:, 384:416]), reads=[pgk], writes=[('kr', st)])
                ss3v = ss3[:].rearrange('p (s j) -> p s j', j=3)
                rtv = rtmpA[:, 0:8].rearrange('p (s j) -> p s j', j=2)
                P.op('pool', lambda e: e.tensor_scalar(out=rtv[:, :, 0], in0=ss3v[:, :, 0], scalar1=1.0 / 256, scalar2=EPS, op0=ALU.mult, op1=ALU.add), reads=[('ss3', st, 0) for st in range(4)], writes=['rtmpA'])
                P.op('pool', lambda e: e.tensor_scalar(out=rtv[:, :, 1], in0=ss3v[:, :, 1], scalar1=1.0 / 128, scalar2=EPS, op0=ALU.mult, op1=ALU.add), reads=[('ss3', st, 1) for st in range(4)], writes=['rtmpA'])
                P.op('pool', lambda e: e.tensor_tensor(out=r2w[:], in0=rtmpA[:, 0:8], in1=mhalf[:, 0:8], op=ALU.pow), reads=['rtmpA', 'mhalf'], writes=['r2w'])
                yield
                for st in range(4):
                    pg, pgk = pgs[st]
                    cqn, cqnk = Rb5.next()
                    P.op('dve', lambda e, pg=pg, cqn=cqn, st=st: e.tensor_scalar(out=cqn[:, 0:256], in0=pg[:, 0:256], scalar1=r2w[:, 2 * st:2 * st + 1], scalar2=None, op0=ALU.mult), reads=[pgk, 'r2w'], writes=[cqnk])
                    P.op('dve', lambda e, pg=pg, cqn=cqn, st=st: e.tensor_scalar(out=cqn[:, 256:384], in0=pg[:, 256:384], scalar1=r2w[:, 2 * st + 1:2 * st + 2], scalar2=None, op0=ALU.mult), reads=[pgk, 'r2w'], writes=[cqnk])
                    pt, ptk = PG.next()
                    ptb = bfv(pt)
                    transposes([(ptb[:, k * 128:(k + 1) * 128], cqn[:, k * 128:(k + 1) * 128]) for k in range(3)], [cqnk], ptk, ident[:])
                    P.op('dve', lambda e, ptb=ptb, st=st: e.tensor_tensor(out=cqT[:, :, st * 128:(st + 1) * 128], in0=ptb[:, 0:384].rearrange('p (k t) -> p k t', k=3), in1=vfm[:, 16:19].unsqueeze(2).broadcast_to([128, 3, 128]), op=ALU.mult), reads=[ptk, 'vfm'], writes=[('cqT', st)])
                    yield
            def thA_m2(i, sts, R):
                for st in sts:
                    blk = 4 * i + st
                    pa, pak = R['pg'].next()
                    pb, pbk = R['pg'].next()
                    mm_group(pa[:, 0:384], [(cqT[:, kc, st * 128:(st + 1) * 128], wuq[:, kc, 0:384]) for kc in range(2)], ['wuq', ('cqT', st)], pak)
                    yield
                    mm_group(pb[:, 0:384], [(cqT[:, kc, st * 128:(st + 1) * 128], wuq[:, kc, 384:768]) for kc in range(2)], ['wuq', ('cqT', st)], pbk)
                    yield
                    qf, qfk = R['f1'].next()
                    P.op('act', lambda e, pa=pa, qf=qf: e.activation(out=qf[:, 0:384], in_=pa[:, 0:384], func=AF.Copy), reads=[pak], writes=[qfk])
                    P.op('act', lambda e, pb=pb, qf=qf: e.activation(out=qf[:, 384:768], in_=pb[:, 0:384], func=AF.Copy), reads=[pbk], writes=[qfk])
                    yield
                    pa2, pa2k = R['pg'].next()
                    pb2, pb2k = R['pg'].next()
                    mm_group(pa2[:], [(cqT[:, 2, st * 128:(st + 1) * 128], wukv[:, 0:512])], ['wukv', ('cqT', st)], pa2k)
                    yield
                    mm_group(pb2[:], [(cqT[:, 2, st * 128:(st + 1) * 128], wukv[:, 512:1024])], ['wukv', ('cqT', st)], pb2k)
                    yield
                    kvf, kvfk = R['f1'].next()
                    P.op('act', lambda e, pa2=pa2, kvf=kvf: e.activation(out=kvf[:, 0:512], in_=pa2[:], func=AF.Copy), reads=[pa2k], writes=[kvfk])
                    P.op('act', lambda e, pb2=pb2, kvf=kvf: e.activation(out=kvf[:, 512:1024], in_=pb2[:], func=AF.Copy), reads=[pb2k], writes=[kvfk])
                    yield
                    qf3 = qf[:, 0:768].rearrange('p (h d) -> p h d', h=8)
                    kvf3 = kvf[:].rearrange('p (h d) -> p h d', h=8)
                    sq, sqk = R['b1'].next()
                    P.op('act', lambda e, qf=qf, sq=sq: e.activation(out=sq[:, 0:768], in_=qf[:, 0:768], func=AF.Square), reads=[qfk], writes=[sqk])
                    yield
                    P.op('dve', lambda e, sq=sq: e.tensor_reduce(out=R['ssqk'][:, 0:8], in_=sq[:, 0:768].rearrange('p (h d) -> p h d', h=8), axis=AX.X, op=ALU.add), reads=[sqk], writes=[(R['n'] + 'ssqk', 0)])
                    yield
                    sq2, sq2k = R['b1'].next()
                    P.op('act', lambda e, kvf=kvf, sq2=sq2: e.activation(out=sq2[:], in_=kvf[:], func=AF.Square), reads=[kvfk], writes=[sq2k])
                    yield
                    P.op('dve', lambda e, sq2=sq2: e.tensor_reduce(out=R['ssqk'][:, 8:16], in_=sq2[:].rearrange('p (h d) -> p h d', h=8)[:, :, 0:64], axis=AX.X, op=ALU.add), reads=[sq2k], writes=[(R['n'] + 'ssqk', 1)])
                    yield
                    P.op('dve', lambda e, st=st: e.tensor_scalar(out=R['ssqk'][:, 8:16], in0=R['ssqk'][:, 8:16], scalar1=ss3[:, st * 3 + 2:st * 3 + 3], scalar2=None, op0=ALU.add), reads=[(R['n'] + 'ssqk', 1), ('ss3', st, 2)], writes=[(R['n'] + 'ssqk', 1)])
                    yield
                    rsqrt(R['rqk'][:], R['ssqk'][:], 1.0 / 96, 16, [(R['n'] + 'ssqk', 0), (R['n'] + 'ssqk', 1)], [(R['n'] + 'rqk')], tmp=R['rtmp'], tmpk=R['n'] + 'rtmpA')
                    yield
                    P.op('act', lambda e, kvf3=kvf3, blk=blk: e.activation(out=Vaug[:, blk, :, 0:64], in_=kvf3[:, :, 64:128], func=AF.Copy), reads=[kvfk], writes=[('V', blk)])
                    yield
                    qb, qbk = R['b1'].next()
                    qb3 = qb[:, 0:768].rearrange('p (h d) -> p h d', h=8)
                    P.op('dve', lambda e, qb3=qb3, qf3=qf3: e.tensor_tensor(out=qb3[:, :, 0:64], in0=qf3[:, :, 0:64], in1=R['rqk'][:, 0:8].unsqueeze(2).broadcast_to([128, 8, 64]), op=ALU.mult), reads=[qfk, (R['n'] + 'rqk')], writes=[qbk])
                    yield
                    kb, kbk = R['b1'].next()
                    kb3 = kb[:, 0:768].rearrange('p (h d) -> p h d', h=8)
                    P.op('dve', lambda e, kb3=kb3, kvf3=kvf3: e.tensor_tensor(out=kb3[:, :, 0:64], in0=kvf3[:, :, 0:64], in1=R['rqk'][:, 8:16].unsqueeze(2).broadcast_to([128, 8, 64]), op=ALU.mult), reads=[kvfk, (R['n'] + 'rqk')], writes=[kbk])
                    yield
                    xr, xrk = R['qk'].next()
                    tr_, trk = R['qk'].next()
                    xr3 = xr[:].rearrange('p (h d) -> p h d', h=16)
                    tr3 = tr_[:].rearrange('p (h d) -> p h d', h=16)
                    P.op('dve', lambda e, xr3=xr3, qf3=qf3: e.tensor_tensor(out=xr3[:, 0:8, :], in0=qf3[:, :, 64:96], in1=R['rqk'][:, 0:8].unsqueeze(2).broadcast_to([128, 8, 32]), op=ALU.mult), reads=[qfk, (R['n'] + 'rqk')], writes=[xrk])
                    yield
                    P.op('dve', lambda e, xr3=xr3, st=st: e.tensor_tensor(out=xr3[:, 8:16, :], in0=kr[:, st, :].unsqueeze(1).broadcast_to([128, 8, 32]), in1=R['rqk'][:, 8:16].unsqueeze(2).broadcast_to([128, 8, 32]), op=ALU.mult), reads=[('kr', st), (R['n'] + 'rqk')], writes=[xrk])
                    yield
                    P.op('dve', lambda e, xr=xr: e.tensor_tensor(out=xr[:].rearrange('p (a h d) -> p a h d', a=2, h=8), in0=xr[:].rearrange('p (a h d) -> p a h d', a=2, h=8), in1=vrow[:, 0:64].rearrange('p (a d) -> p a d', a=2).unsqueeze(2).broadcast_to([128, 2, 8, 32]), op=ALU.mult), reads=[xrk, 'vrow'], writes=[xrk])
                    yield
                    cosb = cosT[:, blk, :].unsqueeze(1).broadcast_to([128, 16, 16])
                    sinb = sinT[:, blk, :].unsqueeze(1).broadcast_to([128, 16, 16])
                    P.op('dve', lambda e, xr3=xr3, tr3=tr3, sinb=sinb: e.scalar_tensor_tensor(out=tr3[:, :, 0:16], in0=xr3[:, :, 16:32], scalar=-1.0, in1=sinb, op0=ALU.mult, op1=ALU.mult), reads=[xrk, 'sinT'], writes=[trk])
                    yield
                    P.op('dve', lambda e, xr3=xr3, tr3=tr3, sinb=sinb: e.tensor_tensor(out=tr3[:, :, 16:32], in0=xr3[:, :, 0:16], in1=sinb, op=ALU.mult), reads=[xrk, 'sinT'], writes=[trk])
                    yield
                    P.op('dve', lambda e, xr3=xr3, cosb=cosb: e.tensor_tensor(out=xr3[:, :, 0:16], in0=xr3[:, :, 0:16], in1=cosb, op=ALU.mult), reads=[xrk, trk, 'cosT'], writes=[xrk])
                    P.op('dve', lambda e, xr3=xr3, cosb=cosb: e.tensor_tensor(out=xr3[:, :, 16:32], in0=xr3[:, :, 16:32], in1=cosb, op=ALU.mult), reads=[xrk, trk, 'cosT'], writes=[xrk])
                    yield
                    P.op('dve', lambda e, xr3=xr3, tr3=tr3, qb3=qb3: e.tensor_tensor(out=qb3[:, :, 64:96], in0=xr3[:, 0:8, :], in1=tr3[:, 0:8, :], op=ALU.add), reads=[xrk, trk], writes=[qbk])
                    yield
                    P.op('dve', lambda e, xr3=xr3, tr3=tr3, kb3=kb3: e.tensor_tensor(out=kb3[:, :, 64:96], in0=xr3[:, 8:16, :], in1=tr3[:, 8:16, :], op=ALU.add), reads=[xrk, trk], writes=[kbk])
                    yield
                    pt, ptk = R['pg'].next()
                    ptb = bfv(pt)
                    transposes([(ptb[0:96, h * 128:(h + 1) * 128], qb3[:, h, :]) for h in range(8)], [qbk], ptk, ident[:])
                    yield
                    pt2, pt2k = R['pg'].next()
                    pt2b = bfv(pt2)
                    transposes([(pt2b[0:96, h * 128:(h + 1) * 128], kb3[:, h, :]) for h in range(8)], [kbk], pt2k, ident[:])
                    yield
                    P.op('dve', lambda e, ptb=ptb, st=st: e.tensor_scalar(out=QT[0:96, :, st * 128:(st + 1) * 128], in0=ptb[0:96, :].rearrange('p (h t) -> p h t', h=8), scalar1=vfm[0:96, 19:20], scalar2=None, op0=ALU.mult), reads=[ptk, 'vfm'], writes=[('QT', st)])
                    yield
                    P.op('dve', lambda e, pt2b=pt2b, blk=blk: e.tensor_scalar(out=KT[0:96, :, blk * 128:(blk + 1) * 128], in0=pt2b[0:96, :].rearrange('p (h t) -> p h t', h=8), scalar1=vfm[0:96, 20:21], scalar2=None, op0=ALU.mult), reads=[pt2k, 'vfm'], writes=[('KT', blk)])
                    yield
                return
                yield
            def thA(i, wide=False):
                if wide:
                    yield from thA_head_wide(i)
                    g0 = thA_m2(i, [0, 2], RES_A0)
                    g1 = thA_m2(i, [1, 3], RES_B)
                    al = [True, True]
                    for _ in range(14):
                        next(g0)
                        yield
                    while al[0] or al[1]:
                        for j, g in enumerate((g0, g1)):
                            if al[j]:
                                try:
                                    next(g)
                                except StopIteration:
                                    al[j] = False
                        yield
                else:
                    yield from thA_head(i)
                    yield from thA_m2(i, range(4), RES_A)
            def thM(i):
                t0 = i * T
                wg, wgk = W.next('gate0')
                yield from tm_to_ys(ya, 'ya', wg, wgk)
                yield from merge_branch(0, True)
                wcg, wcgk = W.next('conv_c')
                wxi, wxik = W.next('conv_x')
                for c in range(4):
                    pc, pck = cur['PG'].next()
                    mm_group(pc[:], [(wcg[:, k, c * 128:(c + 1) * 128], cur['hT'][:, k, :]) for k in range(8)], [wcgk] + cur['hTk'], pck)
                    yield
                    px, pxk = cur['PG'].next()
                    mm_group(px[:], [(wxi[:, k, c * 128:(c + 1) * 128], cur['hT'][:, k, :]) for k in range(8)], [wxik] + cur['hTk'], pxk)
                    yield
                    xs, xsk = Rf5.next()
                    P.op('act', lambda e, px=px, xs=xs: e.activation(out=xs[:], in_=px[:], func=AF.Copy), reads=[pxk], writes=[xsk])
                    P.op('dve', lambda e, pc=pc, xs=xs, c=c: e.tensor_tensor(out=zc[:, c, 2:514], in0=pc[:], in1=xs[:], op=ALU.mult), reads=[pck, xsk], writes=[('z', c)])
                    y0, y0k = Rf5.next()
                    P.op('dve', lambda e, y0=y0, c=c: e.tensor_scalar(out=y0[:], in0=zc[:, c, 2:514], scalar1=vfm[:, 21 + 8 + c:21 + 8 + c + 1], scalar2=vfm[:, 33 + c:34 + c], op0=ALU.mult, op1=ALU.add), reads=[('z', c), 'vfm'], writes=[y0k])
                    y1, y1k = Rf5.next()
                    P.op('dve', lambda e, y0=y0, y1=y1, c=c: e.scalar_tensor_tensor(out=y1[:], in0=zc[:, c, 1:513], scalar=vfm[:, 21 + 4 + c:21 + 4 + c + 1], in1=y0[:], op0=ALU.mult, op1=ALU.add), reads=[('z', c), 'vfm', y0k], writes=[y1k])
                    P.op('dve', lambda e, y1=y1, c=c: e.scalar_tensor_tensor(out=yall[:, c, :], in0=zc[:, c, 0:512], scalar=vfm[:, 21 + c:21 + c + 1], in1=y1[:], op0=ALU.mult, op1=ALU.add), reads=[('z', c), 'vfm', y1k], writes=[('yall', c)])
                    P.op('dve', lambda e, c=c: e.tensor_copy(out=zc[:, c, 0:2], in_=zc[:, c, 512:514]), reads=[('z', c)], writes=[('z', c)])
                    yield
                wbg, wbgk = W.next('conv_b')
                for c in range(4):
                    pbg, pbgk = cur['PG'].next()
                    mm_group(pbg[:], [(wbg[:, k, c * 128:(c + 1) * 128], cur['hT'][:, k, :]) for k in range(8)], [wbgk] + cur['hTk'], pbgk)
                    yield
                    P.op('dve', lambda e, pbg=pbg, c=c: e.tensor_tensor(out=yall[:, c, :], in0=yall[:, c, :], in1=pbg[:], op=ALU.mult), reads=[('yall', c), pbgk], writes=[('yall', c)])
                    yield
                wg, wgk = W.next('gate1')
                for c in range(4):
                    gate_chunk(wg, wgk, c, yall[:, c, :], [('yall', c)])
                    yield
                yield from merge_branch(1, False)
                wv2, wv2k = W.next('sg_v')
                vcs = []
                for st in range(4):
                    pv_, pvk_ = cur['PG'].next()
                    mm_group(pv_[:], [(cur['hT'][:, k, st * 128:(st + 1) * 128], wv2[:, k, :]) for k in range(8)], [wv2k, ('hT', i % 2, st)], pvk_)
                    vn, vnk = Rf5.next()
                    P.op('act', lambda e, pv_=pv_, vn=vn: e.activation(out=vn[:], in_=pv_[:], func=AF.Copy), reads=[pvk_], writes=[vnk])
                    vcs.append((vn, vnk))
                    yield
                wg, wgk = W.next('gate2')
                for st in range(4):
                    vn, vnk = vcs[st]
                    P.op('dve', lambda e, vn=vn, st=st: e.bn_stats(out=st6s[:, st, :], in_=vn[:]), reads=[vnk], writes=[('st6', st)])
                    P.op('dve', lambda e, st=st: e.bn_aggr(out=mvs[:, st, :], in_=st6s[:, st, :]), reads=[('st6', st)], writes=[('mv', st)])
                P.op('pool', lambda e: e.tensor_scalar(out=rtmpB[:, 0:4], in0=mvs[:, :, 1], scalar1=EPS, scalar2=None, op0=ALU.add), reads=[('mv', st) for st in range(4)], writes=['rtmpB'])
                P.op('pool', lambda e: e.tensor_tensor(out=rs4[:], in0=rtmpB[:, 0:4], in1=mhalf[:, 0:4], op=ALU.pow), reads=['rtmpB', 'mhalf'], writes=['rs4'])
                yield
                for g in range(4):
                    gate_pre(wg, wgk, g)
                    yield
                P.op('dve', lambda e: e.scalar_tensor_tensor(out=nb4[:], in0=mvs[:, :, 0], scalar=-1.0, in1=rs4[:], op0=ALU.mult, op1=ALU.mult), reads=[('mv', st) for st in range(4)] + ['rs4'], writes=['nb4'])
                for st in range(4):
                    vn, vnk = vcs[st]
                    P.op('act', lambda e, vn=vn, st=st: e.activation(out=vln[:, st, :], in_=vn[:], func=AF.Identity, bias=nb4[:, st:st + 1], scale=rs4[:, st:st + 1]), reads=[vnk, 'nb4', 'rs4'], writes=['ya'])
                    yield
                wu, wuk = W.next('sg_u')
                for g in range(4):
                    pm, pmk = cur['PG'].next()

                    def mix(e, pm=pm, g=g):
                        for st in range(4):
                            ins = e.matmul(pm[:, st * 128:(st + 1) * 128], vln[:, st, g * 128:(g + 1) * 128], wTs[:, g, :], start=True, stop=True)
                        return ins
                    P.op('pe', mix, reads=['ya', 'wTs'], writes=[pmk])
                    yield
                    pu, puk = cur['PG'].next()
                    mm_group(pu[:], [(wu[:, k, g * 128:(g + 1) * 128], cur['hT'][:, k, :]) for k in range(8)], [wuk] + cur['hTk'], puk)
                    yield
                    mt, mtk = Rb5.next()
                    P.op('dve', lambda e, pm=pm, mt=mt, g=g: e.scalar_tensor_tensor(out=mt[:].rearrange('p (s t) -> p s t', s=4), in0=pm[:].rearrange('p (s t) -> p s t', s=4), scalar=vfm[:, 71 + g:72 + g], in1=Cg[:, g, :].unsqueeze(1).broadcast_to([128, 4, 128]), op0=ALU.mult, op1=ALU.add), reads=[pmk, 'vfm', 'Cg'], writes=[mtk])
                    P.op('dve', lambda e, pu=pu, mt=mt, g=g: e.tensor_tensor(out=mt[:], in0=pu[:], in1=mt[:], op=ALU.mult), reads=[puk, mtk], writes=[mtk])
                    gate_post(g, mt[:], [mtk])
                    yield
                yield from merge_branch(2, False)
                wq_, wqk = W.next('memq')
                wg, wgk = W.next('gate3')
                pqs = []
                for st in range(4):
                    pq, pqk = cur['PG'].next()
                    mm_group(pq[:], [(cur['hT'][:, k, st * 128:(st + 1) * 128], wq_[:, k, :]) for k in range(8)], [wqk, ('hT', i % 2, st)], pqk)
                    pqs.append((pq, pqk))
                    yield
                mqfs = []
                for st in range(4):
                    pq, pqk = pqs[st]
                    mqf, mqfk = Rf5.next()
                    P.op('act', lambda e, pq=pq, mqf=mqf: e.activation(out=mqf[:], in_=pq[:], func=AF.Copy), reads=[pqk], writes=[mqfk])
                    mqfs.append((mqf, mqfk))
                for st in range(4):
                    mqf, mqfk = mqfs[st]
                    sq, sqk = Rb5.next()
                    P.op('act', lambda e, mqf=mqf, sq=sq: e.activation(out=sq[:], in_=mqf[:], func=AF.Square), reads=[mqfk], writes=[sqk])
                    P.op('dve', lambda e, sq=sq, st=st: e.tensor_reduce(out=ss16[:, st * 4:(st + 1) * 4], in_=sq[:].rearrange('p (h d) -> p h d', h=4), axis=AX.X, op=ALU.add), reads=[sqk], writes=[('ss16', st)])
                rsqrt(r16[:], ss16[:], 1.0 / 128, 16, [('ss16', st) for st in range(4)], ['r16'], tmp=rtmp16, tmpk='rtmp16')
                yield
                for c in range(4):
                    gate_pre(wg, wgk, c)
                    yield
                for st in range(4):
                    mqf, mqfk = mqfs[st]
                    mqn, mqnk = Rb5.next()
                    P.op('dve', lambda e, mqf=mqf, mqn=mqn, st=st: e.tensor_tensor(out=mqn[:].rearrange('p (h d) -> p h d', h=4), in0=mqf[:].rearrange('p (h d) -> p h d', h=4), in1=r16[:, st * 4:(st + 1) * 4].unsqueeze(2).broadcast_to([128, 4, 128]), op=ALU.mult), reads=[mqfk, 'r16'], writes=[mqnk])
                    pt, ptk = cur['PG'].next()
                    ptb = bfv(pt)
                    transposes([(ptb[:, h * 128:(h + 1) * 128], mqn[:, h * 128:(h + 1) * 128]) for h in range(4)], [mqnk], ptk, ident[:])
                    yield
                    P.op('dve', lambda e, ptb=ptb, st=st: e.tensor_scalar(out=QmT[:, :, st * 128:(st + 1) * 128], in0=ptb[:, 0:512].rearrange('p (h t) -> p h t', h=4), scalar1=vfm[:, 37:38], scalar2=None, op0=ALU.mult), reads=[ptk, 'vfm'], writes=[('QmT', st)])
                QmT_keys = [('QmT', st) for st in range(4)]
                pR, pRk = (ps[7], ('ps', 7))
                mpairs = [(h, mb) for h in range(4) for mb in range(2)]
                mpend = []
                for step in range(len(mpairs) + 1):
                    if step < len(mpairs):
                        h, mb = mpairs[step]
                        pS, pSk = PGM.next()
                        P.op('pe', lambda e, pS=pS, h=h, mb=mb: e.matmul(pS[:], KmT[:, h, mb * 128:(mb + 1) * 128], QmT[:, h, :], start=True, stop=True), reads=[('KmT', mb)] + QmT_keys, writes=[pSk])
                        PT, PTk = Rb5.next()
                        P.op('act', lambda e, pS=pS, PT=PT: e.activation(out=PT[:], in_=pS[:], func=AF.Exp, scale=128 ** (-0.5)), reads=[pSk], writes=[PTk])
                        mpend.append((h, mb, PT, PTk))
                        yield
                    if step >= 1:
                        h, mb, PT, PTk = mpend.pop(0)
                        pO, pOk = (ps[5 + h % 2], ('ps', 5 + h % 2))

                        def pvm(e, PT=PT, h=h, mb=mb, pO=pO, pR=pR):
                            for qs in range(4):
                                e.matmul(pO[:, qs * 128:(qs + 1) * 128], PT[:, qs * 128:(qs + 1) * 128], Vm[:, mb, h, :], start=mb == 0 and qs == 0, stop=mb == 1, skip_group_check=True)
                            for qs in range(4):
                                ins = e.matmul(pR[:, h * 8 + qs * 2:h * 8 + qs * 2 + 2], PT[:, qs * 128:(qs + 1) * 128], ones_bf[:, 0:2], start=h == 0 and mb == 0 and qs == 0, stop=mb == 1, skip_group_check=True)
                            return ins
                        P.op('pe', pvm, reads=[PTk, ('Vm', mb), 'ones_bf'], writes=[pOk, pRk])
                        yield
                        if mb == 1:
                            P.op('dve', lambda e, pR=pR, h=h: e.reciprocal(out=rec[:], in_=pR[:, h * 8:h * 8 + 8].rearrange('p (q t) -> p q t', t=2)[:, :, 0]), reads=[pRk], writes=['rec'])
                            P.op('dve', lambda e, pO=pO, h=h: e.tensor_tensor(out=ya[:, :, h * 128:(h + 1) * 128], in0=pO[:].rearrange('p (q d) -> p q d', q=4), in1=rec[:].unsqueeze(2).broadcast_to([128, 4, 128]), op=ALU.mult), reads=[pOk, 'rec'], writes=['ya'])
                yield from tm_post(ya, 'ya')
                yield from merge_branch(3, False)
                acc_keys = [('acc', dc) for dc in range(8)]
                wo0, wo0k = W.next('wo0')
                wo1, wo1k = W.next('wo1')
                for st in range(4):
                    xw, xwk = Rxw.next()
                    P.op('sp', lambda e, xw=xw, st=st, xsrc=xsrc, t0=t0: e.dma_start(out=xw[:], in_=xsrc[t0 + st * 128:t0 + (st + 1) * 128, :]), reads=[(skey, i, st)], writes=[xwk], dsem=xwk)
                    for half, (wo, wok) in enumerate([(wo0, wo0k), (wo1, wo1k)]):
                        po, pok = cur['PG'].next()
                        mm_group(po[:], [(acc[:, kc, st * 128:(st + 1) * 128], wo[:, kc, :]) for kc in range(8)], [wok] + acc_keys, pok)
                        yield
                        P.op('dve', lambda e, po=po, xw=xw, half=half: e.scalar_tensor_tensor(out=xw[:, half * 512:(half + 1) * 512], in0=po[:], scalar=0.25, in1=xw[:, half * 512:(half + 1) * 512], op0=ALU.mult, op1=ALU.add), reads=[pok, xwk], writes=[xwk])
                    P.op('sp', lambda e, dst=dst, t0=t0, st=st, xw=xw: e.dma_start(out=dst[t0 + st * 128:t0 + (st + 1) * 128, :], in_=xw[:]), reads=[xwk], writes=[(dkey, i, st)], dsem=('xst', xwk[1]))
                    yield
                return
                yield
            def attention(i):
                nkb = 4 * (i + 1)
                pairs = [(h, kb_) for h in range(8) for kb_ in range(nkb)]
                pend = []
                psO_cur = {}
                QT_keys = [("QT", st) for st in range(4)]
                for step in range(len(pairs) + ATT_SKEW):
                    new = None
                    if step < len(pairs):
                        h, kb_ = pairs[step]
                        j = kb_ - 4 * i
                        c0 = 128 * j if j > 0 else 0
                        pS, pSk = PS_.next()
                        P.op("pe", lambda e, pS=pS, h=h, kb_=kb_, c0=c0: e.matmul(
                            pS[:, c0:512], KT[0:96, h, kb_ * 128:(kb_ + 1) * 128], QT[0:96, h, c0:512],
                            start=True, stop=True),
                            reads=[("KT", kb_)] + QT_keys, writes=[pSk])
                        PT, PTk = Rb5.next()
                        P.op("act", lambda e, pS=pS, PT=PT, c0=c0: e.activation(
                            out=PT[:, c0:512], in_=pS[:, c0:512], func=AF.Exp, scale=96 ** -0.5),
                            reads=[pSk], writes=[PTk])
                        if j >= 0:
                            P.op("dve", lambda e, PT=PT, c0=c0: e.tensor_tensor(
                                out=PT[:, c0:c0 + 128], in0=PT[:, c0:c0 + 128], in1=tri[:], op=ALU.mult),
                                reads=[PTk, "tri"], writes=[PTk])
                        new = (h, kb_, j, PT, PTk)
                    if new is not None:
                        pend.append(new)
                    if step >= ATT_SKEW:
                        h, kb_, j, PT, PTk = pend.pop(0)
                        if kb_ == 0:
                            psO_cur[h] = PO.next()
                        pO, pOk = psO_cur[h]

                        def pv(e, pO=pO, PT=PT, h=h, kb_=kb_, j=j):
                            for qs in range(max(j, 0), 4):
                                ins = e.matmul(pO[:, qs * 128:qs * 128 + 65], PT[:, qs * 128:(qs + 1) * 128],
                                               Vaug[:, kb_, h, :], start=(kb_ == 0 and qs == 0),
                                               stop=(kb_ == 4 * i + qs), skip_group_check=True)
                            return ins
                        P.op("pe", pv, reads=[PTk, ("V", kb_)], writes=[pOk])
                        if kb_ == nkb - 1:
                            pO3 = pO[:].rearrange("p (q d) -> p q d", q=4)
                            P.op("dve", lambda e, pO3=pO3: e.reciprocal(out=rec[:], in_=pO3[:, :, 64]),
                                 reads=[pOk], writes=["rec"])
                            P.op("dve", lambda e, pO3=pO3, h=h: e.tensor_tensor(
                                out=ya[:, :, h * 64:(h + 1) * 64], in0=pO3[:, :, 0:64],
                                in1=rec[:].unsqueeze(2).broadcast_to([128, 4, 64]), op=ALU.mult),
                                reads=[pOk, "rec"], writes=["ya"])
            xsrc = x_d if l == 0 else x1_d
            dst = y_d if last else x1_d
            dkey = "y" if last else "x1"
            skey = "x" if xsrc is x_d else "x1"
            cur["PG"] = PG
            for _ in thA(0, True):
                pass
            for i in range(NT):
                cur["hT"] = hTb[i % 2]
                cur["hTk"] = [("hT", i % 2, st) for st in range(4)]
                cur["PG"] = PG
                attention(i)
                cur["PG"] = PGB
                ga = thA(i + 1) if i + 1 < NT else iter(())
                gm = thM(i)
                a_alive = m_alive = True
                credit = 0.0
                while a_alive or m_alive:
                    credit += A_RATIO if m_alive else 1000.0
                    while credit >= 1.0 and a_alive:
                        credit -= 1.0
                        try:
                            next(ga)
                        except StopIteration:
                            a_alive = False
                    if not a_alive:
                        credit = 0.0
                    if m_alive:
                        try:
                            next(gm)
                        except StopIteration:
                            m_alive = False
                cur["PG"] = PG
    try:
        if not done_:
            _layers()
            assert W.cur == len(sched), (W.cur, len(sched))
    except _Stop:
        pass
    P.op("sp", None, reads=[("y", i, st) for i in range(NT) for st in range(4)] + final_keys)
    P.emit()
    es.close()
    return nc


def _pack(inputs):
    f = lambda k: np.asarray(inputs[k], dtype=np.float32)
    vfm = np.zeros((2, 128, NVF), np.float32)
    vrow = np.zeros((2, NVR), np.float32)
    for l in range(2):
        vfm[l, :, 0:8] = f("norm_g")[l].reshape(8, 128).T
        vfm[l, :, 8:16] = f("mem_norm_g")[l].reshape(8, 128).T
        vfm[l, :, 16:18] = f("cq_norm_g")[l].reshape(2, 128).T
        vfm[l, :, 18] = f("ckv_norm_g")[l]
        vfm[l, :, 19] = 1.0
        vfm[l, 0:64, 19] = f("mla_q_norm_g")[l][0:64]
        vfm[l, :, 20] = 1.0
        vfm[l, 0:64, 20] = f("mla_k_norm_g")[l][0:64]
        cw = f("conv_w")[l]
        for j in range(3):
            vfm[l, :, 21 + j * 4:21 + (j + 1) * 4] = cw[j].reshape(4, 128).T
        vfm[l, :, 33:37] = f("conv_b")[l].reshape(4, 128).T
        vfm[l, :, 37] = f("mem_q_norm_g")[l]
        vfm[l, :, 38] = f("mem_k_norm_g")[l]
        vfm[l, :, 39:71] = f("b_merge")[l].reshape(32, 128).T
        vrow[l, 0:32] = f("mla_q_norm_g")[l][64:96]
        vrow[l, 32:64] = f("mla_k_norm_g")[l][64:96]
        vfm[l, :, 71:75] = f("sg_ln_g")[l].reshape(4, 128).T
        vfm[l, :, 75:79] = f("sg_ln_b")[l].reshape(4, 128).T
        vrow[l, 64:576] = f("b_spatial")[l].reshape(512)
    return vfm, vrow


_NC_CACHE = {}


def _in_maps(inputs, cores, x_override=None):
    vfm, vrow = _pack(inputs)
    f = lambda k: np.ascontiguousarray(np.asarray(inputs[k], dtype=np.float32))
    shared = dict(vfm=vfm, vrow=vrow, w_in=f("w_in"), w_uq=f("w_uq"), w_ukv=f("w_ukv"),
                  w_spatial=f("w_spatial"), w_mem_kv=f("w_mem_kv"), w_branch=f("w_branch"), w_out=f("w_out"))
    x = f("x") if x_override is None else x_override
    mem = f("mem")
    pos = np.ascontiguousarray(np.asarray(inputs["positions"], dtype=np.int32))
    maps = []
    for b in cores:
        m = dict(shared)
        m["x"] = np.ascontiguousarray(x[b])
        m["mem"] = np.ascontiguousarray(mem[b])
        m["pos"] = np.ascontiguousarray(pos[b].reshape(16, 128))
        maps.append(m)
    return maps


def kernel(**inputs):
    if "nc" not in _NC_CACHE:
        _NC_CACHE["nc"] = build(2, 0)
    nc = _NC_CACHE["nc"]
    maps = _in_maps(inputs, list(range(8)))
    res = run_bass_kernel_spmd(nc, maps, core_ids=list(range(8)))
    out = np.stack([np.asarray(r["y"], dtype=np.float32) for r in res.results], axis=0)
    return out
```

```python
import contextlib
import math
import numpy as np
import concourse.bass as bass
import concourse.mybir as mybir
from concourse.bass_utils import run_bass_kernel_spmd

F32 = mybir.dt.float32
BF16 = mybir.dt.bfloat16
I32 = mybir.dt.int32
AF = mybir.ActivationFunctionType
ALU = mybir.AluOpType
AX = mybir.AxisListType

S = 2048
D = 1024
T = 512
NT = S // T
EPS = 1e-6
IN_W = 9632
NVF = 79
NVR = 576
NSLOT = 4


class Prog:
    CH = 8000

    def __init__(self, nc):
        self.nc = nc
        self.ops = []
        self.last_w = {}
        self.readers = {}
        self.dma_counts = {}

    def op(self, eng, fn, reads=(), writes=(), dsem=None):
        idx = len(self.ops)
        deps = set()
        for k in reads:
            if k in self.last_w:
                deps.add((self.last_w[k], "raw"))
            if isinstance(k, tuple) and k[0] == "ps":
                for r in self.readers.get(k, ()):
                    deps.add((r, "war"))
        for k in writes:
            if k in self.last_w:
                deps.add((self.last_w[k], "waw"))
            for r in self.readers.get(k, ()):
                deps.add((r, "war"))
        o = dict(idx=idx, eng=eng, fn=fn, deps=deps, dsem=dsem)
        if dsem is not None:
            self.dma_counts[dsem] = self.dma_counts.get(dsem, 0) + 1
            o["dcount"] = self.dma_counts[dsem]
        self.ops.append(o)
        for k in reads:
            self.readers.setdefault(k, []).append(idx)
        for k in writes:
            self.last_w[k] = idx
            self.readers[k] = []
        return idx

    def emit(self):
        nc = self.nc
        ops = self.ops
        need = set()
        for o in ops:
            w = []
            for (d, typ) in o["deps"]:
                p = ops[d]
                if p["dsem"] is not None:
                    w.append(("dma", p["dsem"], p["dcount"]))
                    continue
                if p["eng"] == o["eng"] and o["dsem"] is None:
                    if o["eng"] == "pe" or typ == "war":
                        continue
                need.add(d)
                w.append(("eng", p["eng"], d))
            o["waits"] = w
        ordc = {}
        for o in ops:
            if o["idx"] in need:
                e = o["eng"]
                ordc[e] = ordc.get(e, 0) + 1
                o["ord"] = ordc[e]
        with contextlib.ExitStack() as es:
            esem = {}
            for e, n in ordc.items():
                esem[e] = [es.enter_context(nc.semaphore("s_%s_%d" % (e, c)))
                           for c in range((n + self.CH - 1) // self.CH + 1)]
            dsem = {}
            for j, k in enumerate(self.dma_counts):
                dsem[k] = es.enter_context(nc.semaphore("d_%d" % j))
            block = es.enter_context(nc.Block())
            per = {}
            for o in ops:
                per.setdefault(o["eng"], []).append(o)

            def run(ename, e):
                waited_e = {}
                waited_d = {}
                for o in per.get(ename, []):
                    we = {}
                    wd = {}
                    for w in o["waits"]:
                        if w[0] == "eng":
                            oo = ops[w[2]]["ord"]
                            we[w[1]] = max(we.get(w[1], 0), oo)
                        else:
                            wd[w[1]] = max(wd.get(w[1], 0), w[2])
                    for pe_, oo in we.items():
                        if waited_e.get(pe_, 0) >= oo:
                            continue
                        waited_e[pe_] = oo
                        e.wait_ge(esem[pe_][(oo - 1) // self.CH], (oo - 1) % self.CH + 1)
                    for k, cnt in wd.items():
                        if waited_d.get(k, 0) >= cnt:
                            continue
                        waited_d[k] = cnt
                        e.wait_ge(dsem[k], 16 * cnt)
                    if o["fn"] is None:
                        continue
                    ins = o["fn"](e)
                    if o["dsem"] is not None:
                        ins.then_inc(dsem[o["dsem"]], 16)
                    elif "ord" in o:
                        oo = o["ord"]
                        ins.then_inc(esem[ename][(oo - 1) // self.CH], 1)

            @block.tensor
            def _(e):
                run("pe", e)

            @block.scalar
            def _(e):
                run("act", e)

            @block.vector
            def _(e):
                run("dve", e)

            @block.gpsimd
            def _(e):
                run("pool", e)

            @block.sync
            def _(e):
                run("sp", e)


class _Stop(Exception):
    pass


def build(n_layers=2, first_layer=0, stop=None):
    nc = bass.Bass("TRN2", target_bir_lowering=False)
    L = n_layers

    def din(name, shape, dt=F32):
        return nc.dram_tensor(name, shape, dt, kind="ExternalInput").ap()

    x_d = din("x", [S, D])
    mem_d = din("mem", [256, D])
    pos_d = din("pos", [16, 128], I32)
    vfm_d = din("vfm", [2, 128, NVF])
    vrow_d = din("vrow", [2, NVR])
    w_in_d = din("w_in", [2, D, IN_W])
    w_uq_d = din("w_uq", [2, 256, 768])
    w_ukv_d = din("w_ukv", [2, 128, 1024])
    w_sp_d = din("w_spatial", [2, 4, 128, 128])
    w_mkv_d = din("w_mem_kv", [2, D, 1024])
    w_br_d = din("w_branch", [2, 4, 512, D])
    w_out_d = din("w_out", [2, D, D])
    y_d = nc.dram_tensor("y", [S, D], F32, kind="ExternalOutput").ap()
    x1_d = nc.dram_tensor("x1s", [S, D], F32, kind="Internal").ap()

    P = Prog(nc)
    es = contextlib.ExitStack()

    def sb(name, shape, dt):
        return es.enter_context(nc.sbuf_tensor(name, shape, dt))

    xwb = [sb("xw%d" % j, [128, D], F32) for j in range(2)]
    KT = sb("KT", [128, 8, S], BF16)
    Vaug = sb("Vaug", [128, 16, 8, 65], BF16)
    hTb = [sb("hT%d" % j, [128, 8, T], BF16) for j in range(2)]
    QT = sb("QT", [128, 8, T], BF16)
    ys = sb("ys", [128, 4, T], BF16)
    acc = sb("acc", [128, 8, T], BF16)
    wslot = [sb("wslot%d" % j, [128, 4096], BF16) for j in range(NSLOT)]
    vfm = sb("vfm_sb", [128, NVF], F32)
    vrow = sb("vrow_sb", [128, NVR], F32)
    hbm = sb("hbm", [128, 32], F32)
    wuq = sb("wuq", [128, 2, 768], BF16)
    wukv = sb("wukv", [128, 1024], BF16)
    wTs = sb("wTs", [128, 4, 128], BF16)
    Cg = sb("Cg", [128, 4, 128], F32)
    nb4 = sb("nb4", [128, 4], F32)
    ones_bf = sb("ones_bf", [128, 128], BF16)
    ident = sb("ident", [128, 128], BF16)
    identf = sb("identf", [128, 128], F32)
    tri = sb("tri", [128, 128], BF16)
    mhalf = sb("mhalf", [128, 32], F32)
    invf = sb("invf", [128, 16], F32)
    post = sb("post", [128, 16], F32)
    cosT = sb("cosT", [128, 16, 16], F32)
    sinT = sb("sinT", [128, 16, 16], F32)
    KmT = sb("KmT", [128, 4, 256], BF16)
    Vm = sb("Vm", [128, 2, 4, 128], BF16)
    QmT = sb("QmT", [128, 4, T], BF16)
    cqT = sb("cqT", [128, 3, T], BF16)
    ya = sb("ya", [128, 4, 512], BF16)
    yall = sb("yall", [128, 4, 512], BF16)
    zc = sb("zc", [128, 4, 516], BF16)
    junk = sb("junk", [128, 1024], BF16)
    kr = sb("kr", [128, 4, 32], F32)
    ssx = sb("ssx", [128, 4], F32)
    rx = sb("rx", [128, 4], F32)
    ss3 = sb("ss3", [128, 12], F32)
    r2 = sb("r2", [128, 2], F32)
    r2w = sb("r2w", [128, 8], F32)
    ssq = sb("ssq", [128, 8], F32)
    rq = sb("rq", [128, 8], F32)
    ssk = sb("ssk", [128, 8], F32)
    rk = sb("rk", [128, 8], F32)
    rtmp = sb("rtmp", [128, 8], F32)
    rtmpA = sb("rtmpA", [128, 16], F32)
    ssqk = sb("ssqk", [128, 16], F32)
    rqk = sb("rqk", [128, 16], F32)
    rtmpB = sb("rtmpB", [128, 8], F32)
    wAs = sb("wAs", [128, 8 * 416], BF16)
    bA = [sb("bA_%d" % j, [128, 512], BF16) for j in range(2)]
    st6 = sb("st6", [128, 6], F32)
    st6s = sb("st6s", [128, 4, 6], F32)
    mvs = sb("mvs", [128, 4, 2], F32)
    rs4 = sb("rs4", [128, 4], F32)
    ss16 = sb("ss16", [128, 16], F32)
    r16 = sb("r16", [128, 16], F32)
    rtmp16 = sb("rtmp16", [128, 16], F32)
    mv = sb("mv", [128, 2], F32)
    rs1 = sb("rs1", [128, 1], F32)
    ss4 = sb("ss4", [128, 4], F32)
    r4 = sb("r4", [128, 4], F32)
    rec = sb("rec", [128, 4], F32)
    ssm = sb("ssm", [128, 2], F32)
    rm = sb("rm", [128, 2], F32)
    NF1 = 2
    f1024 = [sb("f1024_%d" % j, [128, 1024], F32) for j in range(NF1)]
    NF5 = 4
    f512 = [sb("f512_%d" % j, [128, 512], F32) for j in range(NF5)]
    NB5 = 6
    b512 = [sb("b512_%d" % j, [128, 512], BF16) for j in range(NB5)]
    NB1 = 3
    b1024 = [sb("b1024_%d" % j, [128, 1024], BF16) for j in range(NB1)]
    NR = 0
    f256 = [sb("f256_%d" % j, [128, 256], F32) for j in range(NR)]
    f128 = []
    fqk = [sb("fqk_%d" % j, [128, 512], F32) for j in range(2)]
    ps = [es.enter_context(nc.psum_tensor("ps%d" % j, [128, 512], F32)) for j in range(8)]

    class Rot:
        def __init__(self, items, key):
            self.items = items
            self.key = key
            self.i = 0

        def next(self):
            j = self.i % len(self.items)
            self.i += 1
            return self.items[j], (self.key, j)

    wspf = f512[0][:].rearrange("p (g s) -> p g s", g=4)
    posi = f512[1][0:16, 0:128].bitcast(I32)
    posf = f512[2][0:16, 0:128]
    ones_f = f512[3][:, 0:128]
    angt = f512[0][:, 0:256]
    kkt = f512[1][:, 0:256]
    kit = f512[2][:, 0:256].bitcast(I32)
    redt = f512[3][:, 0:256]
    memnT = yall[:].rearrange("p c t -> p (c t)").rearrange("p (k m) -> p k m", k=8)
    vln = ya
    RbA = Rot(bA, "bA")
    wAv = wAs[:].rearrange("p (k n) -> p k n", k=8)
    Rxw = Rot(xwb, "xw")
    Rqk = Rot(fqk, "fqk")
    Rf1 = Rot(f1024, "f1024")
    ssqk2 = sb("ssqk2", [128, 16], F32)
    rqk2 = sb("rqk2", [128, 16], F32)
    rtmpA2 = sb("rtmpA2", [128, 16], F32)

    class RotK:
        def __init__(self, items):
            self.items = items
            self.i = 0

        def next(self):
            j = self.i % len(self.items)
            self.i += 1
            return self.items[j]

    Rf5 = Rot(f512, "f512")
    Rb5 = Rot(b512, "b512")
    Rb1 = Rot(b1024, "b1024")
    Rf2 = Rot(f256, "f256")
    Rf128 = Rot(f128, "f128")

    class PsRot:
        def __init__(self, banks):
            self.banks = banks
            self.i = 0

        def next(self):
            b = self.banks[self.i % len(self.banks)]
            self.i += 1
            return ps[b], ("ps", b)

    PG = PsRot([0, 1, 2, 3])
    PGA = PsRot([0, 1])
    PW = PsRot([4, 5, 6, 7])
    PGM = PsRot([2, 3, 4])
    PGB = PsRot([2, 3, 4, 5])
    cur = {"PG": PG}
    A_RATIO = 1.15
    PRE_A_STEPS = 1
    PS_ = PsRot([2, 3, 4, 5])
    ATT_SKEW = 3
    PO = PsRot([6, 7])

    def bfv(p):
        return p[:].bitcast(BF16)

    RES_A = dict(pg=PGA, f1=Rf1, b1=Rb1, qk=Rqk, ssqk=ssqk, rqk=rqk, rtmp=rtmpA, n="")
    RES_B = dict(pg=PsRot([2, 3]),
                 f1=RotK([(xwb[0], ("xw", 0)), (xwb[1], ("xw", 1))]),
                 b1=RotK([(b1024[2], ("b1024", 2)), (junk, "junk")]),
                 qk=RotK([(f512[0], ("f512", 0)), (f512[1], ("f512", 1))]),
                 ssqk=ssqk2, rqk=rqk2, rtmp=rtmpA2, n="B")
    RES_A0 = dict(RES_A, b1=RotK([(b1024[0], ("b1024", 0)), (b1024[1], ("b1024", 1))]))

    sched = []
    for l in range(L):
        ll = l + first_layer
        sched.append(("memK", w_mkv_d[ll][:, 0:512], 8, 512))
        sched.append(("memV", w_mkv_d[ll][:, 512:1024], 8, 512))
        for i in range(NT):
            def wi(a, b_):
                return w_in_d[ll][:, a:b_]
            for n in (1, 0, 2, 3):
                if n == 0:
                    sched.append(("gate0", wi(3488, 4000), 8, 512))
                if n == 1:
                    sched.append(("conv_c", wi(928, 1440), 8, 512))
                    sched.append(("conv_x", wi(1440, 1952), 8, 512))
                    sched.append(("conv_b", wi(416, 928), 8, 512))
                    sched.append(("gate1", wi(4000, 4512), 8, 512))
                if n == 2:
                    sched.append(("sg_v", wi(2464, 2976), 8, 512))
                    sched.append(("gate2", wi(4512, 5024), 8, 512))
                    sched.append(("sg_u", wi(1952, 2464), 8, 512))
                if n == 3:
                    sched.append(("memq", wi(2976, 3488), 8, 512))
                    sched.append(("gate3", wi(5024, 5536), 8, 512))
                for hf in range(2):
                    c0_ = 5536 + n * 1024 + hf * 512
                    sched.append(("m%d_%d" % (n, hf), wi(c0_, c0_ + 512), 8, 512))
                    sched.append(("wb%d_%d" % (n, hf), w_br_d[ll, n][:, hf * 512:(hf + 1) * 512], 4, 512))
            sched.append(("wo0", w_out_d[ll][:, 0:512], 8, 512))
            sched.append(("wo1", w_out_d[ll][:, 512:1024], 8, 512))

    class WStream:
        def __init__(self):
            self.issued = 0
            self.cur = 0

        def _issue(self):
            c = self.issued
            if c >= len(sched):
                return
            name, src, nk, ncol = sched[c]
            s = c % NSLOT
            view = wslot[s][:, 0:nk * ncol].rearrange("p (k n) -> p k n", k=nk)
            srcv = src.rearrange("(k p) n -> p k n", p=128)
            P.op("pool", lambda e, view=view, srcv=srcv: e.dma_start(out=view, in_=srcv),
                 writes=[("w", s)], dsem=("w", s))
            self.issued += 1

        def next(self, name):
            while self.issued < min(len(sched), self.cur + NSLOT - 1):
                self._issue()
            n2, src, nk, ncol = sched[self.cur]
            assert n2 == name, (n2, name)
            s = self.cur % NSLOT
            self.cur += 1
            view = wslot[s][:, 0:nk * ncol].rearrange("p (k n) -> p k n", k=nk)
            return view, ("w", s)

    W = WStream()

    def rsqrt(out_ap, in_ap, scale, n, rkeys, wkeys, tmp=None, tmpk="rtmp"):
        tmp = rtmp if tmp is None else tmp
        P.op("pool", lambda e: e.tensor_scalar(out=tmp[:, 0:n], in0=in_ap, scalar1=scale, scalar2=EPS,
                                               op0=ALU.mult, op1=ALU.add), reads=rkeys, writes=[tmpk])
        P.op("pool", lambda e: e.tensor_tensor(out=out_ap, in0=tmp[:, 0:n], in1=mhalf[:, 0:n], op=ALU.pow),
             reads=[tmpk, "mhalf"], writes=wkeys)

    def rsqrtA(out_ap, in_ap, scale, n, rkeys, wkeys):
        rsqrt(out_ap, in_ap, scale, n, rkeys, wkeys, tmp=rtmpA, tmpk="rtmpA")

    def mm_group(out_ap, pairs, reads, pskey):
        def fn(e):
            n = len(pairs)
            for j, (a, b_) in enumerate(pairs):
                ins = e.matmul(out_ap, a, b_, start=(j == 0), stop=(j == n - 1))
            return ins
        P.op("pe", fn, reads=reads, writes=[pskey])

    def transposes(pairs, reads, pskey, idt):
        def fn(e):
            for (o_, i_) in pairs:
                ins = e.transpose(out=o_, in_=i_, identity=idt)
            return ins
        P.op("pe", fn, reads=reads + ["ident"], writes=[pskey])

    final_keys = []
    dbg_n = [0]
    dbg_off = [0]

    def checkpoint(name, dumps):
        if stop != name:
            return
        yv = y_d.rearrange("(p a) d -> p (a d)", p=128)
        for (ap, n, keys) in dumps:
            for c0 in range(0, n, 1024):
                w = min(1024, n - c0)
                stg, stgk = Rf1.next()
                off = dbg_off[0]
                idx = dbg_n[0]
                P.op("dve", lambda e, stg=stg, ap=ap, c0=c0, w=w: e.tensor_copy(out=stg[:, 0:w], in_=ap[:, c0:c0 + w]),
                     reads=keys, writes=[stgk])
                P.op("sp", lambda e, stg=stg, off=off, w=w: e.dma_start(out=yv[:, off:off + w], in_=stg[:, 0:w]),
                     reads=[stgk], writes=[("ydbg", idx)], dsem=("dbg", idx))
                final_keys.append(("ydbg", idx))
                dbg_off[0] += w
                dbg_n[0] += 1
        raise _Stop()

    P.op("pool", lambda e: e.memset(ones_bf[:], 1.0), writes=["ones_bf"])
    P.op("pool", lambda e: e.memset(ones_f[:], 1.0), writes=[("f512", 3)])
    P.op("pool", lambda e: e.memset(mhalf[:], -0.5), writes=["mhalf"])
    P.op("pool", lambda e: e.affine_select(out=ident[:], in_=ones_bf[:], pattern=[[1, 128]],
                                           compare_op=ALU.is_equal, fill=0.0, base=0, channel_multiplier=-1),
         reads=["ones_bf"], writes=["ident"])
    P.op("pool", lambda e: e.affine_select(out=identf[:], in_=ones_f[:], pattern=[[1, 128]],
                                           compare_op=ALU.is_equal, fill=0.0, base=0, channel_multiplier=-1),
         reads=[("f512", 3)], writes=["identf"])
    P.op("pool", lambda e: e.affine_select(out=tri[:], in_=ones_bf[:], pattern=[[1, 128]],
                                           compare_op=ALU.is_ge, fill=0.0, base=0, channel_multiplier=-1),
         reads=["ones_bf"], writes=["tri"])
    P.op("pool", lambda e: e.memset(Vaug[:].rearrange("p a b c -> p (a b c)"), 1.0), writes=[("V", b_) for b_ in range(16)])
    inv = np.power(np.float32(10000.0), -np.arange(16, dtype=np.float32) / np.float32(16)).astype(np.float32)
    for j in range(16):
        P.op("pool", lambda e, j=j: e.memset(invf[:, j:j + 1], float(inv[j])), writes=["invf"])
    P.op("sp", lambda e: e.dma_start(out=posi[:], in_=pos_d), writes=[("f512", 1)], dsem=("f512", 1))
    P.op("dve", lambda e: e.tensor_copy(out=posf[:], in_=posi[:]), reads=[("f512", 1)], writes=[("f512", 2)])
    P.op("pe", lambda e: e.transpose(out=ps[0][:, 0:16], in_=posf[:], identity=identf[0:16, 0:16]),
         reads=[("f512", 2), "identf"], writes=[("ps", 0)])
    P.op("dve", lambda e: e.tensor_copy(out=post[:], in_=ps[0][:, 0:16]), reads=[("ps", 0)], writes=["post"])
    for s_ in range(16):
        P.op("dve", lambda e, s_=s_: e.tensor_scalar(out=angt[:, s_ * 16:(s_ + 1) * 16], in0=invf[:],
                                                      scalar1=post[:, s_:s_ + 1], scalar2=None, op0=ALU.mult),
             reads=["post", "invf"], writes=[("f512", 0)])
    C1 = 6.28125
    C2 = 2 * math.pi - 6.28125
    P.op("dve", lambda e: e.tensor_scalar(out=kkt[:], in0=angt[:], scalar1=1.0 / (2 * math.pi), scalar2=None,
                                          op0=ALU.mult), reads=[("f512", 0)], writes=[("f512", 1)])
    P.op("dve", lambda e: e.tensor_copy(out=kit[:], in_=kkt[:]), reads=[("f512", 1)], writes=[("f512", 2)])
    P.op("dve", lambda e: e.tensor_copy(out=kkt[:], in_=kit[:]), reads=[("f512", 2)], writes=[("f512", 1)])
    P.op("dve", lambda e: e.scalar_tensor_tensor(out=redt[:], in0=kkt[:], scalar=-C1, in1=angt[:],
                                                 op0=ALU.mult, op1=ALU.add), reads=[("f512", 1), ("f512", 0)], writes=[("f512", 3)])
    P.op("dve", lambda e: e.scalar_tensor_tensor(out=redt[:], in0=kkt[:], scalar=-C2, in1=redt[:],
                                                 op0=ALU.mult, op1=ALU.add), reads=[("f512", 1), ("f512", 3)], writes=[("f512", 3)])

    def wrap():
        P.op("dve", lambda e: e.tensor_scalar(out=kkt[:], in0=redt[:], scalar1=math.pi, scalar2=-2 * math.pi,
                                              op0=ALU.is_gt, op1=ALU.mult), reads=[("f512", 3)], writes=[("f512", 1)])
        P.op("dve", lambda e: e.tensor_tensor(out=redt[:], in0=redt[:], in1=kkt[:], op=ALU.add),
             reads=[("f512", 3), ("f512", 1)], writes=[("f512", 3)])
        P.op("dve", lambda e: e.tensor_scalar(out=kkt[:], in0=redt[:], scalar1=-math.pi, scalar2=2 * math.pi,
                                              op0=ALU.is_lt, op1=ALU.mult), reads=[("f512", 3)], writes=[("f512", 1)])
        P.op("dve", lambda e: e.tensor_tensor(out=redt[:], in0=redt[:], in1=kkt[:], op=ALU.add),
             reads=[("f512", 3), ("f512", 1)], writes=[("f512", 3)])
        P.op("dve", lambda e: e.tensor_scalar(out=redt[:], in0=redt[:], scalar1=math.pi, scalar2=-math.pi,
                                              op0=ALU.min, op1=ALU.max), reads=[("f512", 3)], writes=[("f512", 3)])

    wrap()
    P.op("act", lambda e: e.activation(out=sinT[:].rearrange("p a b -> p (a b)"), in_=redt[:], func=AF.Sin),
         reads=[("f512", 3)], writes=["sinT"])
    P.op("dve", lambda e: e.tensor_scalar(out=redt[:], in0=redt[:], scalar1=math.pi / 2, scalar2=None,
                                          op0=ALU.add), reads=[("f512", 3), "sinT"], writes=[("f512", 3)])
    wrap()
    P.op("act", lambda e: e.activation(out=cosT[:].rearrange("p a b -> p (a b)"), in_=redt[:], func=AF.Sin),
         reads=[("f512", 3)], writes=["cosT"])

    done_ = False
    try:
        checkpoint("setup", [(cosT[:].rearrange("p a b -> p (a b)"), 256, ["cosT"]),
                             (sinT[:].rearrange("p a b -> p (a b)"), 256, ["sinT"]),
                             (post[:], 16, ["post"]), (tri[:], 128, ["tri"]), (ident[:], 128, ["ident"])])
    except _Stop:
        done_ = True
    def rope(src, dst, sg, skey, dkey):
        cosb = cosT[:, sg, :].unsqueeze(1).broadcast_to([128, 8, 16])
        sinb = sinT[:, sg, :].unsqueeze(1).broadcast_to([128, 8, 16])
        t1, k1 = Rf128.next()
        t2, k2 = Rf128.next()
        t1v = t1[:].rearrange("p (h d) -> p h d", h=8)
        t2v = t2[:].rearrange("p (h d) -> p h d", h=8)
        P.op("dve", lambda e: e.tensor_tensor(out=t1v, in0=src[:, :, 0:16], in1=cosb, op=ALU.mult),
             reads=[skey, "cosT"], writes=[k1])
        P.op("dve", lambda e: e.tensor_tensor(out=t2v, in0=src[:, :, 16:32], in1=sinb, op=ALU.mult),
             reads=[skey, "sinT"], writes=[k2])
        P.op("dve", lambda e: e.tensor_tensor(out=dst[:, :, 0:16], in0=t1v, in1=t2v, op=ALU.subtract),
             reads=[k1, k2], writes=[dkey])
        t3, k3 = Rf128.next()
        t4, k4 = Rf128.next()
        t3v = t3[:].rearrange("p (h d) -> p h d", h=8)
        t4v = t4[:].rearrange("p (h d) -> p h d", h=8)
        P.op("dve", lambda e: e.tensor_tensor(out=t3v, in0=src[:, :, 0:16], in1=sinb, op=ALU.mult),
             reads=[skey, "sinT"], writes=[k3])
        P.op("dve", lambda e: e.tensor_tensor(out=t4v, in0=src[:, :, 16:32], in1=cosb, op=ALU.mult),
             reads=[skey, "cosT"], writes=[k4])
        P.op("dve", lambda e: e.tensor_tensor(out=dst[:, :, 16:32], in0=t3v, in1=t4v, op=ALU.add),
             reads=[k3, k4], writes=[dkey])

    YALLK = [("yall", c) for c in range(4)]

    def gate_chunk(wg, wgk, c, ysrc_ap, ysrc_keys):
        pg, pgk = cur["PG"].next()
        mm_group(pg[:], [(wg[:, k, c * 128:(c + 1) * 128], cur["hT"][:, k, :]) for k in range(8)],
                 [wgk] + cur["hTk"], pgk)
        tg, tgk = Rb5.next()
        P.op("act", lambda e: e.activation(out=tg[:], in_=pg[:], func=AF.Tanh, scale=0.5),
             reads=[pgk], writes=[tgk])
        u, uk = Rb5.next()
        P.op("dve", lambda e: e.scalar_tensor_tensor(out=u[:], in0=tg[:], scalar=1.0, in1=pg[:],
                                                     op0=ALU.add, op1=ALU.mult), reads=[tgk, pgk], writes=[uk])
        P.op("dve", lambda e: e.tensor_tensor(out=ys[:, c, :], in0=u[:], in1=ysrc_ap, op=ALU.mult),
             reads=[uk] + ysrc_keys, writes=[("ys", c)])

    def gate_pre(wg, wgk, c):
        pg, pgk = cur["PG"].next()
        mm_group(pg[:], [(wg[:, k, c * 128:(c + 1) * 128], cur["hT"][:, k, :]) for k in range(8)],
                 [wgk] + cur["hTk"], pgk)
        tg, tgk = Rb5.next()
        P.op("act", lambda e: e.activation(out=tg[:], in_=pg[:], func=AF.Tanh, scale=0.5),
             reads=[pgk], writes=[tgk])
        P.op("dve", lambda e: e.scalar_tensor_tensor(out=ys[:, c, :], in0=tg[:], scalar=1.0, in1=pg[:],
                                                     op0=ALU.add, op1=ALU.mult), reads=[tgk, pgk],
             writes=[("ys", c)])

    def gate_post(c, ysrc_ap, ysrc_keys):
        P.op("dve", lambda e: e.tensor_tensor(out=ys[:, c, :], in0=ys[:, c, :], in1=ysrc_ap, op=ALU.mult),
             reads=[("ys", c)] + ysrc_keys, writes=[("ys", c)])

    def tm_post(ytm, ykey):
        for c in range(4):
            pt, ptk = cur["PG"].next()
            ptb = bfv(pt)
            transposes([(ptb[:, qs * 128:(qs + 1) * 128], ytm[:, qs, c * 128:(c + 1) * 128]) for qs in range(4)],
                       [ykey], ptk, ident[:])
            gate_post(c, ptb[:, 0:512], [ptk])
            yield

    def merge_branch(n, first):
        for hf in range(2):
            wm, wmk = W.next("m%d_%d" % (n, hf))
            wb, wbk = W.next("wb%d_%d" % (n, hf))

            def logits(c4):
                dc = hf * 4 + c4
                pl, plk = cur["PG"].next()
                mm_group(pl[:], [(wm[:, k, c4 * 128:(c4 + 1) * 128], cur["hT"][:, k, :]) for k in range(8)],
                         [wmk] + cur["hTk"], plk)
                tm, tmk = Rb5.next()
                P.op("act", lambda e, pl=pl, tm=tm, dc=dc: e.activation(out=tm[:], in_=pl[:], func=AF.Tanh,
                                                                       bias=hbm[:, n * 8 + dc:n * 8 + dc + 1],
                                                                       scale=0.5),
                     reads=[plk, "hbm"], writes=[tmk])
                return tm, tmk

            nxt = logits(0)
            yield
            for c4 in range(4):
                dc = hf * 4 + c4
                tm, tmk = nxt
                if c4 + 1 < 4:
                    nxt = logits(c4 + 1)
                    yield
                pz, pzk = cur["PG"].next()
                mm_group(pz[:], [(wb[:, kc, c4 * 128:(c4 + 1) * 128], ys[:, kc, :]) for kc in range(4)],
                         [wbk] + [("ys", c) for c in range(4)], pzk)
                if first:
                    P.op("dve", lambda e, tm=tm, pz=pz, dc=dc: e.scalar_tensor_tensor(
                        out=acc[:, dc, :], in0=tm[:], scalar=1.0, in1=pz[:], op0=ALU.add, op1=ALU.mult),
                        reads=[tmk, pzk], writes=[("acc", dc)])
                else:
                    tp, tpk = Rf5.next()
                    P.op("dve", lambda e, tm=tm, pz=pz, tp=tp: e.scalar_tensor_tensor(
                        out=tp[:], in0=tm[:], scalar=1.0, in1=pz[:], op0=ALU.add, op1=ALU.mult),
                        reads=[tmk, pzk], writes=[tpk])
                    P.op("dve", lambda e, tp=tp, dc=dc: e.tensor_tensor(out=acc[:, dc, :], in0=acc[:, dc, :],
                                                                       in1=tp[:], op=ALU.add),
                         reads=[tpk, ("acc", dc)], writes=[("acc", dc)])
                yield

    def tm_to_ys(ytm, ykey, wg, wgk):
        for c in range(4):
            pt, ptk = cur["PG"].next()
            ptb = bfv(pt)
            transposes([(ptb[:, qs * 128:(qs + 1) * 128], ytm[:, qs, c * 128:(c + 1) * 128]) for qs in range(4)],
                       [ykey], ptk, ident[:])
            gate_chunk(wg, wgk, c, ptb[:, 0:512], [ptk])
            yield

    def _layers():
        for l in range(L):
            ll = l + first_layer
            last = (l == L - 1)
            P.op("sp", lambda e, ll=ll: e.dma_start(out=vfm[:], in_=vfm_d[ll]), writes=["vfm"], dsem="vfm")
            P.op("sp", lambda e, ll=ll: e.dma_start(out=vrow[:], in_=vrow_d[ll, :].partition_broadcast(128)),
                 writes=["vrow"], dsem="vrow")
            P.op("dve", lambda e: e.tensor_scalar(out=hbm[:], in0=vfm[:, 39:71], scalar1=0.5, scalar2=None,
                                                  op0=ALU.mult), reads=["vfm"], writes=["hbm"])
            P.op("pool", lambda e, ll=ll: e.dma_start(out=wuq[:], in_=w_uq_d[ll].rearrange("(k p) n -> p k n", p=128)),
                 writes=["wuq"], dsem="wuq")
            P.op("pool", lambda e, ll=ll: e.dma_start(out=wukv[:], in_=w_ukv_d[ll]), writes=["wukv"], dsem="wukv")
            P.op("pool", lambda e, ll=ll: e.dma_start(
                out=wAv, in_=w_in_d[ll][:, 0:416].rearrange("(k p) n -> p k n", p=128)), writes=["wA"], dsem="wA")
            P.op("sp", lambda e, ll=ll: e.dma_start(out=wspf[:], in_=w_sp_d[ll].rearrange("g t s -> t g s")),
                 writes=[("f512", 0)], dsem=("f512", 0))
            for g in range(4):
                pt, ptk = PG.next()
                P.op("pe", lambda e, pt=pt, g=g: e.transpose(out=pt[:, 0:128], in_=wspf[:, g, :], identity=identf[:]),
                     reads=[("f512", 0), "identf"], writes=[ptk])
                tf, tfk = f512[1 + g % 3][:, 0:128], ("f512", 1 + g % 3)
                P.op("act", lambda e, pt=pt, tf=tf: e.activation(out=tf[:], in_=pt[:, 0:128], func=AF.Copy),
                     reads=[ptk], writes=[tfk])
                P.op("pool", lambda e, tf=tf, g=g: e.affine_select(out=wTs[:, g, :], in_=tf[:], pattern=[[1, 128]],
                                                                  compare_op=ALU.is_ge, fill=0.0, base=0,
                                                                  channel_multiplier=-1),
                     reads=[tfk], writes=["wTs"])
                pr, prk = PG.next()
                P.op("pe", lambda e, pr=pr, g=g: e.matmul(pr[:, 0:128], ones_bf[:, 0:128], wTs[:, g, :], start=True, stop=True),
                     reads=["ones_bf", "wTs"], writes=[prk])
                P.op("dve", lambda e, pr=pr, g=g: e.scalar_tensor_tensor(
                    out=Cg[:, g, :], in0=pr[:, 0:128], scalar=vfm[:, 75 + g:76 + g], in1=vrow[:, 64 + g * 128:64 + (g + 1) * 128],
                    op0=ALU.mult, op1=ALU.add), reads=[prk, "vfm", "vrow"], writes=["Cg"])
            P.op("dve", lambda e: e.memset(zc[:, :, 0:2], 0.0), writes=[("z", c) for c in range(4)])
            memt = [f1024[0], f1024[1]]
            mkeys = [("f1024", 0), ("f1024", 1)]
            Rf1.i = 2
            for mb in range(2):
                P.op("sp", lambda e, mb=mb: e.dma_start(out=memt[mb][:], in_=mem_d[mb * 128:(mb + 1) * 128, :]),
                     writes=[mkeys[mb]], dsem=("memt", mb))
            for mb in range(2):
                P.op("act", lambda e, mb=mb: e.activation(out=junk[:], in_=memt[mb][:], func=AF.Square,
                                                          accum_out=ssm[:, mb:mb + 1]),
                     reads=[mkeys[mb]], writes=[("ssm", mb), "junk"])
            rsqrt(rm[:], ssm[:], 1.0 / D, 2, [("ssm", 0), ("ssm", 1)], ["rm"])
            for mb in range(2):
                mnb, mnbk = Rb1.next()
                P.op("dve", lambda e, mb=mb, mnb=mnb: e.tensor_scalar(out=mnb[:], in0=memt[mb][:], scalar1=rm[:, mb:mb + 1],
                                                                      scalar2=None, op0=ALU.mult),
                     reads=[mkeys[mb], "rm"], writes=[mnbk])
                pt, ptk = PG.next()
                ptb = bfv(pt)
                transposes([(ptb[:, k * 128:(k + 1) * 128], mnb[:, k * 128:(k + 1) * 128]) for k in range(8)],
                           [mnbk], ptk, ident[:])
                P.op("dve", lambda e, ptb=ptb, mb=mb: e.tensor_tensor(
                    out=memnT[:, :, mb * 128:(mb + 1) * 128], in0=ptb.rearrange("p (k t) -> p k t", k=8),
                    in1=vfm[:, 8:16].unsqueeze(2).broadcast_to([128, 8, 128]), op=ALU.mult),
                    reads=[ptk, "vfm"], writes=[*YALLK])
            wk_, wkk = W.next("memK")
            for mb in range(2):
                pk, pkk = PG.next()
                mm_group(pk[:], [(memnT[:, k, mb * 128:(mb + 1) * 128], wk_[:, k, :]) for k in range(8)],
                         [wkk, *YALLK], pkk)
                kf, kfk = Rf5.next()
                P.op("act", lambda e, pk=pk, kf=kf: e.activation(out=kf[:], in_=pk[:], func=AF.Copy),
                     reads=[pkk], writes=[kfk])
                sq, sqk = Rf5.next()
                P.op("act", lambda e, kf=kf, sq=sq: e.activation(out=sq[:], in_=kf[:], func=AF.Square),
                     reads=[kfk], writes=[sqk])
                P.op("dve", lambda e, sq=sq: e.tensor_reduce(out=ss4[:], in_=sq[:].rearrange("p (h d) -> p h d", h=4),
                                                            axis=AX.X, op=ALU.add), reads=[sqk], writes=["ss4"])
                rsqrt(r4[:], ss4[:], 1.0 / 128, 4, ["ss4"], ["r4"])
                knb, knbk = Rb5.next()
                P.op("dve", lambda e, kf=kf, knb=knb: e.tensor_tensor(
                    out=knb[:].rearrange("p (h d) -> p h d", h=4), in0=kf[:].rearrange("p (h d) -> p h d", h=4),
                    in1=r4[:].unsqueeze(2).broadcast_to([128, 4, 128]), op=ALU.mult),
                    reads=[kfk, "r4"], writes=[knbk])
                pt, ptk = PG.next()
                ptb = bfv(pt)
                transposes([(ptb[:, h * 128:(h + 1) * 128], knb[:, h * 128:(h + 1) * 128]) for h in range(4)],
                           [knbk], ptk, ident[:])
                P.op("dve", lambda e, ptb=ptb, mb=mb: e.tensor_scalar(
                    out=KmT[:, :, mb * 128:(mb + 1) * 128], in0=ptb[:, 0:512].rearrange("p (h t) -> p h t", h=4),
                    scalar1=vfm[:, 38:39], scalar2=None, op0=ALU.mult),
                    reads=[ptk, "vfm"], writes=[("KmT", mb)])
            wv_, wvk = W.next("memV")
            for mb in range(2):
                pv, pvk = PG.next()
                mm_group(pv[:], [(memnT[:, k, mb * 128:(mb + 1) * 128], wv_[:, k, :]) for k in range(8)],
                         [wvk, *YALLK], pvk)
                P.op("act", lambda e, pv=pv, mb=mb: e.activation(out=Vm[:, mb, :, :].rearrange("p h d -> p (h d)"),
                                                                in_=pv[:], func=AF.Copy),
                     reads=[pvk], writes=[("Vm", mb)])

            if l == 0:
                checkpoint("lsetup", [(wTs[:].rearrange("p a b -> p (a b)"), 512, ["wTs"]),
                                      (KmT[:].rearrange("p a b -> p (a b)"), 1024, [("KmT", 0), ("KmT", 1)]),
                                      (Vm[:].rearrange("p a b c -> p (a b c)"), 1024, [("Vm", 0), ("Vm", 1)]),
                                      (hbm[:], 32, ["hbm"]), (vrow[:, 0:64], 64, ["vrow"])])
            def thA_head(i):
                hT_ = hTb[i % 2]
                t0 = i * T
                for st in range(4):
                    xs, xsk = Rf1.next()
                    P.op('sp', lambda e, xs=xs, st=st, xsrc=xsrc, t0=t0: e.dma_start(out=xs[:], in_=xsrc[t0 + st * 128:t0 + (st + 1) * 128, :]), reads=[(skey, i, st)], writes=[xsk], dsem=xsk)
                    yield
                    P.op('act', lambda e, xs=xs, st=st: e.activation(out=junk[:], in_=xs[:], func=AF.Square, accum_out=ssx[:, st:st + 1]), reads=[xsk], writes=[('ssx', st), 'junk'])
                    yield
                    rsqrtA(rx[:, st:st + 1], ssx[:, st:st + 1], 1.0 / D, 1, [('ssx', st)], [('rx', st)])
                    yield
                    hb, hbk = Rb1.next()
                    P.op('dve', lambda e, xs=xs, st=st, hb=hb: e.tensor_scalar(out=hb[:], in0=xs[:], scalar1=rx[:, st:st + 1], scalar2=None, op0=ALU.mult), reads=[xsk, ('rx', st)], writes=[hbk])
                    yield
                    pt, ptk = PGA.next()
                    ptb = bfv(pt)
                    transposes([(ptb[:, k * 128:(k + 1) * 128], hb[:, k * 128:(k + 1) * 128]) for k in range(8)], [hbk], ptk, ident[:])
                    yield
                    P.op('dve', lambda e, ptb=ptb, st=st: e.tensor_tensor(out=hT_[:, :, st * 128:(st + 1) * 128], in0=ptb.rearrange('p (k t) -> p k t', k=8), in1=vfm[:, 0:8].unsqueeze(2).broadcast_to([128, 8, 128]), op=ALU.mult), reads=[ptk, 'vfm'], writes=[('hT', i % 2, st)])
                    yield
                wA, wAk = (wAv, 'wA')
                for st in range(4):
                    pg, pgk = PGA.next()
                    mm_group(pg[:, 0:416], [(hT_[:, k, st * 128:(st + 1) * 128], wA[:, k, 0:416]) for k in range(8)], [wAk, ('hT', i % 2, st)], pgk)
                    yield
                    for j, (a, b_) in enumerate([(0, 256), (256, 384), (384, 416)]):
                        P.op('act', lambda e, pg=pg, a=a, b_=b_, st=st, j=j: e.activation(out=junk[:, a:b_], in_=pg[:, a:b_], func=AF.Square, accum_out=ss3[:, st * 3 + j:st * 3 + j + 1]), reads=[pgk], writes=[('ss3', st, j)])
                        yield
                    P.op('dve', lambda e, pg=pg, st=st: e.tensor_copy(out=kr[:, st, :], in_=pg[:, 384:416]), reads=[pgk], writes=[('kr', st)])
                    yield
                    P.op('pool', lambda e, st=st: e.tensor_scalar(out=rtmpA[:, 0:1], in0=ss3[:, st * 3:st * 3 + 1], scalar1=1.0 / 256, scalar2=EPS, op0=ALU.mult, op1=ALU.add), reads=[('ss3', st, 0)], writes=['rtmpA'])
                    yield
                    P.op('pool', lambda e, st=st: e.tensor_scalar(out=rtmpA[:, 1:2], in0=ss3[:, st * 3 + 1:st * 3 + 2], scalar1=1.0 / 128, scalar2=EPS, op0=ALU.mult, op1=ALU.add), reads=[('ss3', st, 1)], writes=['rtmpA'])
                    yield
                    P.op('pool', lambda e: e.tensor_tensor(out=r2[:], in0=rtmpA[:, 0:2], in1=mhalf[:, 0:2], op=ALU.pow), reads=['rtmpA', 'mhalf'], writes=['r2'])
                    yield
                    cqn, cqnk = RbA.next()
                    P.op('dve', lambda e, pg=pg, cqn=cqn: e.tensor_scalar(out=cqn[:, 0:256], in0=pg[:, 0:256], scalar1=r2[:, 0:1], scalar2=None, op0=ALU.mult), reads=[pgk, 'r2'], writes=[cqnk])
                    yield
                    P.op('dve', lambda e, pg=pg, cqn=cqn: e.tensor_scalar(out=cqn[:, 256:384], in0=pg[:, 256:384], scalar1=r2[:, 1:2], scalar2=None, op0=ALU.mult), reads=[pgk, 'r2'], writes=[cqnk])
                    yield
                    pt, ptk = PGA.next()
                    ptb = bfv(pt)
                    transposes([(ptb[:, k * 128:(k + 1) * 128], cqn[:, k * 128:(k + 1) * 128]) for k in range(3)], [cqnk], ptk, ident[:])
                    yield
                    P.op('dve', lambda e, ptb=ptb, st=st: e.tensor_tensor(out=cqT[:, :, st * 128:(st + 1) * 128], in0=ptb[:, 0:384].rearrange('p (k t) -> p k t', k=3), in1=vfm[:, 16:19].unsqueeze(2).broadcast_to([128, 3, 128]), op=ALU.mult), reads=[ptk, 'vfm'], writes=[('cqT', st)])
                    yield
            def thA_head_wide(i):
                hT_ = hTb[i % 2]
                t0 = i * T
                xbufs = [(f1024[0], ('f1024', 0)), (f1024[1], ('f1024', 1)), (xwb[0], ('xw', 0)), (xwb[1], ('xw', 1))]
                for st in range(4):
                    xs, xsk = xbufs[st]
                    P.op('sp', lambda e, xs=xs, st=st, xsrc=xsrc, t0=t0: e.dma_start(out=xs[:], in_=xsrc[t0 + st * 128:t0 + (st + 1) * 128, :]), reads=[(skey, i, st)], writes=[xsk], dsem=xsk)
                for st in range(4):
                    xs, xsk = xbufs[st]
                    P.op('act', lambda e, xs=xs, st=st: e.activation(out=junk[:], in_=xs[:], func=AF.Square, accum_out=ssx[:, st:st + 1]), reads=[xsk], writes=[('ssx', st), 'junk'])
                rsqrtA(rx[:, 0:4], ssx[:, 0:4], 1.0 / D, 4, [('ssx', st) for st in range(4)], [('rx', st) for st in range(4)])
                yield
                for st in range(4):
                    xs, xsk = xbufs[st]
                    hb, hbk = Rb1.next()
                    P.op('dve', lambda e, xs=xs, st=st, hb=hb: e.tensor_scalar(out=hb[:], in0=xs[:], scalar1=rx[:, st:st + 1], scalar2=None, op0=ALU.mult), reads=[xsk, ('rx', st)], writes=[hbk])
                    pt, ptk = PG.next()
                    ptb = bfv(pt)
                    transposes([(ptb[:, k * 128:(k + 1) * 128], hb[:, k * 128:(k + 1) * 128]) for k in range(8)], [hbk], ptk, ident[:])
                    P.op('dve', lambda e, ptb=ptb, st=st: e.tensor_tensor(out=hT_[:, :, st * 128:(st + 1) * 128], in0=ptb.rearrange('p (k t) -> p k t', k=8), in1=vfm[:, 0:8].unsqueeze(2).broadcast_to([128, 8, 128]), op=ALU.mult), reads=[ptk, 'vfm'], writes=[('hT', i % 2, st)])
                    yield
                wA, wAk = (wAv, 'wA')
                pgs = []
                for st in range(4):
                    pg, pgk = PW.next()
                    mm_group(pg[:, 0:416], [(hT_[:, k, st * 128:(st + 1) * 128], wA[:, k, 0:416]) for k in range(8)], [wAk, ('hT', i % 2, st)], pgk)
                    pgs.append((pg, pgk))
                for st in range(4):
                    pg, pgk = pgs[st]
                    for j, (a, b_) in enumerate([(0, 256), (256, 384), (384, 416)]):
                        P.op('act', lambda e, pg=pg, a=a, b_=b_, st=st, j=j: e.activation(out=junk[:, a:b_], in_=pg[:, a:b_], func=AF.Square, accum_out=ss3[:, st * 3 + j:st * 3 + j + 1]), reads=[pgk], writes=[('ss3', st, j)])
                    P.op('dve', lambda e, pg=pg, st=st: e.tensor_copy(out=kr[:, st, :], in_=pg[:, 384:416]), reads=[pgk], writes=[('kr', st)])
                ss3v = ss3[:].rearrange('p (s j) -> p s j', j=3)
                rtv = rtmpA[:, 0:8].rearrange('p (s j) -> p s j', j=2)
                P.op('pool', lambda e: e.tensor_scalar(out=rtv[:, :, 0], in0=ss3v[:, :, 0], scalar1=1.0 / 256, scalar2=EPS, op0=ALU.mult, op1=ALU.add), reads=[('ss3', st, 0) for st in range(4)], writes=['rtmpA'])
                P.op('pool', lambda e: e.tensor_scalar(out=rtv[:, :, 1], in0=ss3v[:, :, 1], scalar1=1.0 / 128, scalar2=EPS, op0=ALU.mult, op1=ALU.add), reads=[('ss3', st, 1) for st in range(4)], writes=['rtmpA'])
                P.op('pool', lambda e: e.tensor_tensor(out=r2w[:], in0=rtmpA[:, 0:8], in1=mhalf[:, 0:8], op=ALU.pow), reads=['rtmpA', 'mhalf'], writes=['r2w'])
                yield
                for st in range(4):
                    pg, pgk = pgs[st]
                    cqn, cqnk = Rb5.next()
                    P.op('dve', lambda e, pg=pg, cqn=cqn, st=st: e.tensor_scalar(out=cqn[:, 0:256], in0=pg[:, 0:256], scalar1=r2w[:, 2 * st:2 * st + 1], scalar2=None, op0=ALU.mult), reads=[pgk, 'r2w'], writes=[cqnk])
                    P.op('dve', lambda e, pg=pg, cqn=cqn, st=st: e.tensor_scalar(out=cqn[:, 256:384], in0=pg[:, 256:384], scalar1=r2w[:, 2 * st + 1:2 * st + 2], scalar2=None, op0=ALU.mult), reads=[pgk, 'r2w'], writes=[cqnk])
                    pt, ptk = PG.next()
                    ptb = bfv(pt)
                    transposes([(ptb[:, k * 128:(k + 1) * 128], cqn[:, k * 128:(k + 1) * 128]) for k in range(3)], [cqnk], ptk, ident[:])
                    P.op('dve', lambda e, ptb=ptb, st=st: e.tensor_tensor(out=cqT[:, :, st * 128:(st + 1) * 128], in0=ptb[:, 0:384].rearrange('p (k t) -> p k t', k=3), in1=vfm[:, 16:19].unsqueeze(2).broadcast_to([128, 3, 128]), op=ALU.mult), reads=[ptk, 'vfm'], writes=[('cqT', st)])
                    yield
            def thA_m2(i, sts, R):
                for st in sts:
                    blk = 4 * i + st
                    pa, pak = R['pg'].next()
                    pb, pbk = R['pg'].next()
                    mm_group(pa[:, 0:384], [(cqT[:, kc, st * 128:(st + 1) * 128], wuq[:, kc, 0:384]) for kc in range(2)], ['wuq', ('cqT', st)], pak)
                    yield
                    mm_group(pb[:, 0:384], [(cqT[:, kc, st * 128:(st + 1) * 128], wuq[:, kc, 384:768]) for kc in range(2)], ['wuq', ('cqT', st)], pbk)
                    yield
                    qf, qfk = R['f1'].next()
                    P.op('act', lambda e, pa=pa, qf=qf: e.activation(out=qf[:, 0:384], in_=pa[:, 0:384], func=AF.Copy), reads=[pak], writes=[qfk])
                    P.op('act', lambda e, pb=pb, qf=qf: e.activation(out=qf[:, 384:768], in_=pb[:, 0:384], func=AF.Copy), reads=[pbk], writes=[qfk])
                    yield
                    pa2, pa2k = R['pg'].next()
                    pb2, pb2k = R['pg'].next()
                    mm_group(pa2[:], [(cqT[:, 2, st * 128:(st + 1) * 128], wukv[:, 0:512])], ['wukv', ('cqT', st)], pa2k)
                    yield
                    mm_group(pb2[:], [(cqT[:, 2, st * 128:(st + 1) * 128], wukv[:, 512:1024])], ['wukv', ('cqT', st)], pb2k)
                    yield
                    kvf, kvfk = R['f1'].next()
                    P.op('act', lambda e, pa2=pa2, kvf=kvf: e.activation(out=kvf[:, 0:512], in_=pa2[:], func=AF.Copy), reads=[pa2k], writes=[kvfk])
                    P.op('act', lambda e, pb2=pb2, kvf=kvf: e.activation(out=kvf[:, 512:1024], in_=pb2[:], func=AF.Copy), reads=[pb2k], writes=[kvfk])
                    yield
                    qf3 = qf[:, 0:768].rearrange('p (h d) -> p h d', h=8)
                    kvf3 = kvf[:].rearrange('p (h d) -> p h d', h=8)
                    sq, sqk = R['b1'].next()
                    P.op('act', lambda e, qf=qf, sq=sq: e.activation(out=sq[:, 0:768], in_=qf[:, 0:768], func=AF.Square), reads=[qfk], writes=[sqk])
                    yield
                    P.op('dve', lambda e, sq=sq: e.tensor_reduce(out=R['ssqk'][:, 0:8], in_=sq[:, 0:768].rearrange('p (h d) -> p h d', h=8), axis=AX.X, op=ALU.add), reads=[sqk], writes=[(R['n'] + 'ssqk', 0)])
                    yield
                    sq2, sq2k = R['b1'].next()
                    P.op('act', lambda e, kvf=kvf, sq2=sq2: e.activation(out=sq2[:], in_=kvf[:], func=AF.Square), reads=[kvfk], writes=[sq2k])
                    yield
                    P.op('dve', lambda e, sq2=sq2: e.tensor_reduce(out=R['ssqk'][:, 8:16], in_=sq2[:].rearrange('p (h d) -> p h d', h=8)[:, :, 0:64], axis=AX.X, op=ALU.add), reads=[sq2k], writes=[(R['n'] + 'ssqk', 1)])
                    yield
                    P.op('dve', lambda e, st=st: e.tensor_scalar(out=R['ssqk'][:, 8:16], in0=R['ssqk'][:, 8:16], scalar1=ss3[:, st * 3 + 2:st * 3 + 3], scalar2=None, op0=ALU.add), reads=[(R['n'] + 'ssqk', 1), ('ss3', st, 2)], writes=[(R['n'] + 'ssqk', 1)])
                    yield
                    rsqrt(R['rqk'][:], R['ssqk'][:], 1.0 / 96, 16, [(R['n'] + 'ssqk', 0), (R['n'] + 'ssqk', 1)], [(R['n'] + 'rqk')], tmp=R['rtmp'], tmpk=R['n'] + 'rtmpA')
                    yield
                    P.op('act', lambda e, kvf3=kvf3, blk=blk: e.activation(out=Vaug[:, blk, :, 0:64], in_=kvf3[:, :, 64:128], func=AF.Copy), reads=[kvfk], writes=[('V', blk)])
                    yield
                    qb, qbk = R['b1'].next()
                    qb3 = qb[:, 0:768].rearrange('p (h d) -> p h d', h=8)
                    P.op('dve', lambda e, qb3=qb3, qf3=qf3: e.tensor_tensor(out=qb3[:, :, 0:64], in0=qf3[:, :, 0:64], in1=R['rqk'][:, 0:8].unsqueeze(2).broadcast_to([128, 8, 64]), op=ALU.mult), reads=[qfk, (R['n'] + 'rqk')], writes=[qbk])
                    yield
                    kb, kbk = R['b1'].next()
                    kb3 = kb[:, 0:768].rearrange('p (h d) -> p h d', h=8)
                    P.op('dve', lambda e, kb3=kb3, kvf3=kvf3: e.tensor_tensor(out=kb3[:, :, 0:64], in0=kvf3[:, :, 0:64], in1=R['rqk'][:, 8:16].unsqueeze(2).broadcast_to([128, 8, 64]), op=ALU.mult), reads=[kvfk, (R['n'] + 'rqk')], writes=[kbk])
                    yield
                    xr, xrk = R['qk'].next()
                    tr_, trk = R['qk'].next()
                    xr3 = xr[:].rearrange('p (h d) -> p h d', h=16)
                    tr3 = tr_[:].rearrange('p (h d) -> p h d', h=16)
                    P.op('dve', lambda e, xr3=xr3, qf3=qf3: e.tensor_tensor(out=xr3[:, 0:8, :], in0=qf3[:, :, 64:96], in1=R['rqk'][:, 0:8].unsqueeze(2).broadcast_to([128, 8, 32]), op=ALU.mult), reads=[qfk, (R['n'] + 'rqk')], writes=[xrk])
                    yield
                    P.op('dve', lambda e, xr3=xr3, st=st: e.tensor_tensor(out=xr3[:, 8:16, :], in0=kr[:, st, :].unsqueeze(1).broadcast_to([128, 8, 32]), in1=R['rqk'][:, 8:16].unsqueeze(2).broadcast_to([128, 8, 32]), op=ALU.mult), reads=[('kr', st), (R['n'] + 'rqk')], writes=[xrk])
                    yield
                    P.op('dve', lambda e, xr=xr: e.tensor_tensor(out=xr[:].rearrange('p (a h d) -> p a h d', a=2, h=8), in0=xr[:].rearrange('p (a h d) -> p a h d', a=2, h=8), in1=vrow[:, 0:64].rearrange('p (a d) -> p a d', a=2).unsqueeze(2).broadcast_to([128, 2, 8, 32]), op=ALU.mult), reads=[xrk, 'vrow'], writes=[xrk])
                    yield
                    cosb = cosT[:, blk, :].unsqueeze(1).broadcast_to([128, 16, 16])
                    sinb = sinT[:, blk, :].unsqueeze(1).broadcast_to([128, 16, 16])
                    P.op('dve', lambda e, xr3=xr3, tr3=tr3, sinb=sinb: e.scalar_tensor_tensor(out=tr3[:, :, 0:16], in0=xr3[:, :, 16:32], scalar=-1.0, in1=sinb, op0=ALU.mult, op1=ALU.mult), reads=[xrk, 'sinT'], writes=[trk])
                    yield
                    P.op('dve', lambda e, xr3=xr3, tr3=tr3, sinb=sinb: e.tensor_tensor(out=tr3[:, :, 16:32], in0=xr3[:, :, 0:16], in1=sinb, op=ALU.mult), reads=[xrk, 'sinT'], writes=[trk])
                    yield
                    P.op('dve', lambda e, xr3=xr3, cosb=cosb: e.tensor_tensor(out=xr3[:, :, 0:16], in0=xr3[:, :, 0:16], in1=cosb, op=ALU.mult), reads=[xrk, trk, 'cosT'], writes=[xrk])
                    P.op('dve', lambda e, xr3=xr3, cosb=cosb: e.tensor_tensor(out=xr3[:, :, 16:32], in0=xr3[:, :, 16:32], in1=cosb, op=ALU.mult), reads=[xrk, trk, 'cosT'], writes=[xrk])
                    yield
                    P.op('dve', lambda e, xr3=xr3, tr3=tr3, qb3=qb3: e.tensor_tensor(out=qb3[:, :, 64:96], in0=xr3[:, 0:8, :], in1=tr3[:, 0:8, :], op=ALU.add), reads=[xrk, trk], writes=[qbk])
                    yield
                    P.op('dve', lambda e, xr3=xr3, tr3=tr3, kb3=kb3: e.tensor_tensor(out=kb3[:, :, 64:96], in0=xr3[:, 8:16, :], in1=tr3[:, 8:16, :], op=ALU.add), reads=[xrk, trk], writes=[kbk])
                    yield
                    pt, ptk = R['pg'].next()
                    ptb = bfv(pt)
                    transposes([(ptb[0:96, h * 128:(h + 1) * 128], qb3[:, h, :]) for h in range(8)], [qbk], ptk, ident[:])
                    yield
                    pt2, pt2k = R['pg'].next()
                    pt2b = bfv(pt2)
                    transposes([(pt2b[0:96, h * 128:(h + 1) * 128], kb3[:, h, :]) for h in range(8)], [kbk], pt2k, ident[:])
                    yield
                    P.op('dve', lambda e, ptb=ptb, st=st: e.tensor_scalar(out=QT[0:96, :, st * 128:(st + 1) * 128], in0=ptb[0:96, :].rearrange('p (h t) -> p h t', h=8), scalar1=vfm[0:96, 19:20], scalar2=None, op0=ALU.mult), reads=[ptk, 'vfm'], writes=[('QT', st)])
                    yield
                    P.op('dve', lambda e, pt2b=pt2b, blk=blk: e.tensor_scalar(out=KT[0:96, :, blk * 128:(blk + 1) * 128], in0=pt2b[0:96, :].rearrange('p (h t) -> p h t', h=8), scalar1=vfm[0:96, 20:21], scalar2=None, op0=ALU.mult), reads=[pt2k, 'vfm'], writes=[('KT', blk)])
                    yield
                return
                yield
            def thA(i, wide=False):
                if wide:
                    yield from thA_head_wide(i)
                    g0 = thA_m2(i, [0, 2], RES_A0)
                    g1 = thA_m2(i, [1, 3], RES_B)
                    al = [True, True]
                    for _ in range(14):
                        next(g0)
                        yield
                    while al[0] or al[1]:
                        for j, g in enumerate((g0, g1)):
                            if al[j]:
                                try:
                                    next(g)
                                except StopIteration:
                                    al[j] = False
                        yield
                else:
                    yield from thA_head(i)
                    yield from thA_m2(i, range(4), RES_A)
            def thM_pre(i):
                cur['hT'] = hTb[i % 2]
                cur['hTk'] = [('hT', i % 2, st) for st in range(4)]
                wcg, wcgk = W.next('conv_c')
                wxi, wxik = W.next('conv_x')
                for c in range(4):
                    pc, pck = cur['PG'].next()
                    mm_group(pc[:], [(wcg[:, k, c * 128:(c + 1) * 128], cur['hT'][:, k, :]) for k in range(8)], [wcgk] + cur['hTk'], pck)
                    yield
                    px, pxk = cur['PG'].next()
                    mm_group(px[:], [(wxi[:, k, c * 128:(c + 1) * 128], cur['hT'][:, k, :]) for k in range(8)], [wxik] + cur['hTk'], pxk)
                    yield
                    xs, xsk = Rf5.next()
                    P.op('act', lambda e, px=px, xs=xs: e.activation(out=xs[:], in_=px[:], func=AF.Copy), reads=[pxk], writes=[xsk])
                    P.op('dve', lambda e, pc=pc, xs=xs, c=c: e.tensor_tensor(out=zc[:, c, 2:514], in0=pc[:], in1=xs[:], op=ALU.mult), reads=[pck, xsk], writes=[('z', c)])
                    y0, y0k = Rf5.next()
                    P.op('dve', lambda e, y0=y0, c=c: e.tensor_scalar(out=y0[:], in0=zc[:, c, 2:514], scalar1=vfm[:, 21 + 8 + c:21 + 8 + c + 1], scalar2=vfm[:, 33 + c:34 + c], op0=ALU.mult, op1=ALU.add), reads=[('z', c), 'vfm'], writes=[y0k])
                    y1, y1k = Rf5.next()
                    P.op('dve', lambda e, y0=y0, y1=y1, c=c: e.scalar_tensor_tensor(out=y1[:], in0=zc[:, c, 1:513], scalar=vfm[:, 21 + 4 + c:21 + 4 + c + 1], in1=y0[:], op0=ALU.mult, op1=ALU.add), reads=[('z', c), 'vfm', y0k], writes=[y1k])
                    P.op('dve', lambda e, y1=y1, c=c: e.scalar_tensor_tensor(out=yall[:, c, :], in0=zc[:, c, 0:512], scalar=vfm[:, 21 + c:21 + c + 1], in1=y1[:], op0=ALU.mult, op1=ALU.add), reads=[('z', c), 'vfm', y1k], writes=[('yall', c)])
                    P.op('dve', lambda e, c=c: e.tensor_copy(out=zc[:, c, 0:2], in_=zc[:, c, 512:514]), reads=[('z', c)], writes=[('z', c)])
                    yield
                wbg, wbgk = W.next('conv_b')
                for c in range(4):
                    pbg, pbgk = cur['PG'].next()
                    mm_group(pbg[:], [(wbg[:, k, c * 128:(c + 1) * 128], cur['hT'][:, k, :]) for k in range(8)], [wbgk] + cur['hTk'], pbgk)
                    yield
                    P.op('dve', lambda e, pbg=pbg, c=c: e.tensor_tensor(out=yall[:, c, :], in0=yall[:, c, :], in1=pbg[:], op=ALU.mult), reads=[('yall', c), pbgk], writes=[('yall', c)])
                    yield
                wg, wgk = W.next('gate1')
                for c in range(4):
                    gate_chunk(wg, wgk, c, yall[:, c, :], [('yall', c)])
                    yield
                yield from merge_branch(1, True)
                return
                yield
            def thM_post(i):
                t0 = i * T
                wg, wgk = W.next('gate0')
                yield from tm_to_ys(ya, 'ya', wg, wgk)
                yield from merge_branch(0, False)
                wv2, wv2k = W.next('sg_v')
                vcs = []
                for st in range(4):
                    pv_, pvk_ = cur['PG'].next()
                    mm_group(pv_[:], [(cur['hT'][:, k, st * 128:(st + 1) * 128], wv2[:, k, :]) for k in range(8)], [wv2k, ('hT', i % 2, st)], pvk_)
                    vn, vnk = Rf5.next()
                    P.op('act', lambda e, pv_=pv_, vn=vn: e.activation(out=vn[:], in_=pv_[:], func=AF.Copy), reads=[pvk_], writes=[vnk])
                    vcs.append((vn, vnk))
                    yield
                wg, wgk = W.next('gate2')
                for st in range(4):
                    vn, vnk = vcs[st]
                    P.op('dve', lambda e, vn=vn, st=st: e.bn_stats(out=st6s[:, st, :], in_=vn[:]), reads=[vnk], writes=[('st6', st)])
                    P.op('dve', lambda e, st=st: e.bn_aggr(out=mvs[:, st, :], in_=st6s[:, st, :]), reads=[('st6', st)], writes=[('mv', st)])
                P.op('pool', lambda e: e.tensor_scalar(out=rtmpB[:, 0:4], in0=mvs[:, :, 1], scalar1=EPS, scalar2=None, op0=ALU.add), reads=[('mv', st) for st in range(4)], writes=['rtmpB'])
                P.op('pool', lambda e: e.tensor_tensor(out=rs4[:], in0=rtmpB[:, 0:4], in1=mhalf[:, 0:4], op=ALU.pow), reads=['rtmpB', 'mhalf'], writes=['rs4'])
                yield
                for g in range(4):
                    gate_pre(wg, wgk, g)
                    yield
                P.op('dve', lambda e: e.scalar_tensor_tensor(out=nb4[:], in0=mvs[:, :, 0], scalar=-1.0, in1=rs4[:], op0=ALU.mult, op1=ALU.mult), reads=[('mv', st) for st in range(4)] + ['rs4'], writes=['nb4'])
                for st in range(4):
                    vn, vnk = vcs[st]
                    P.op('act', lambda e, vn=vn, st=st: e.activation(out=vln[:, st, :], in_=vn[:], func=AF.Identity, bias=nb4[:, st:st + 1], scale=rs4[:, st:st + 1]), reads=[vnk, 'nb4', 'rs4'], writes=['ya'])
                    yield
                wu, wuk = W.next('sg_u')
                for g in range(4):
                    pm, pmk = cur['PG'].next()

                    def mix(e, pm=pm, g=g):
                        for st in range(4):
                            ins = e.matmul(pm[:, st * 128:(st + 1) * 128], vln[:, st, g * 128:(g + 1) * 128], wTs[:, g, :], start=True, stop=True)
                        return ins
                    P.op('pe', mix, reads=['ya', 'wTs'], writes=[pmk])
                    yield
                    pu, puk = cur['PG'].next()
                    mm_group(pu[:], [(wu[:, k, g * 128:(g + 1) * 128], cur['hT'][:, k, :]) for k in range(8)], [wuk] + cur['hTk'], puk)
                    yield
                    mt, mtk = Rb5.next()
                    P.op('dve', lambda e, pm=pm, mt=mt, g=g: e.scalar_tensor_tensor(out=mt[:].rearrange('p (s t) -> p s t', s=4), in0=pm[:].rearrange('p (s t) -> p s t', s=4), scalar=vfm[:, 71 + g:72 + g], in1=Cg[:, g, :].unsqueeze(1).broadcast_to([128, 4, 128]), op0=ALU.mult, op1=ALU.add), reads=[pmk, 'vfm', 'Cg'], writes=[mtk])
                    P.op('dve', lambda e, pu=pu, mt=mt, g=g: e.tensor_tensor(out=mt[:], in0=pu[:], in1=mt[:], op=ALU.mult), reads=[puk, mtk], writes=[mtk])
                    gate_post(g, mt[:], [mtk])
                    yield
                yield from merge_branch(2, False)
                wq_, wqk = W.next('memq')
                wg, wgk = W.next('gate3')
                pqs = []
                for st in range(4):
                    pq, pqk = cur['PG'].next()
                    mm_group(pq[:], [(cur['hT'][:, k, st * 128:(st + 1) * 128], wq_[:, k, :]) for k in range(8)], [wqk, ('hT', i % 2, st)], pqk)
                    pqs.append((pq, pqk))
                    yield
                mqfs = []
                for st in range(4):
                    pq, pqk = pqs[st]
                    mqf, mqfk = Rf5.next()
                    P.op('act', lambda e, pq=pq, mqf=mqf: e.activation(out=mqf[:], in_=pq[:], func=AF.Copy), reads=[pqk], writes=[mqfk])
                    mqfs.append((mqf, mqfk))
                for st in range(4):
                    mqf, mqfk = mqfs[st]
                    sq, sqk = Rb5.next()
                    P.op('act', lambda e, mqf=mqf, sq=sq: e.activation(out=sq[:], in_=mqf[:], func=AF.Square), reads=[mqfk], writes=[sqk])
                    P.op('dve', lambda e, sq=sq, st=st: e.tensor_reduce(out=ss16[:, st * 4:(st + 1) * 4], in_=sq[:].rearrange('p (h d) -> p h d', h=4), axis=AX.X, op=ALU.add), reads=[sqk], writes=[('ss16', st)])
                rsqrt(r16[:], ss16[:], 1.0 / 128, 16, [('ss16', st) for st in range(4)], ['r16'], tmp=rtmp16, tmpk='rtmp16')
                yield
                for c in range(4):
                    gate_pre(wg, wgk, c)
                    yield
                for st in range(4):
                    mqf, mqfk = mqfs[st]
                    mqn, mqnk = Rb5.next()
                    P.op('dve', lambda e, mqf=mqf, mqn=mqn, st=st: e.tensor_tensor(out=mqn[:].rearrange('p (h d) -> p h d', h=4), in0=mqf[:].rearrange('p (h d) -> p h d', h=4), in1=r16[:, st * 4:(st + 1) * 4].unsqueeze(2).broadcast_to([128, 4, 128]), op=ALU.mult), reads=[mqfk, 'r16'], writes=[mqnk])
                    pt, ptk = cur['PG'].next()
                    ptb = bfv(pt)
                    transposes([(ptb[:, h * 128:(h + 1) * 128], mqn[:, h * 128:(h + 1) * 128]) for h in range(4)], [mqnk], ptk, ident[:])
                    yield
                    P.op('dve', lambda e, ptb=ptb, st=st: e.tensor_scalar(out=QmT[:, :, st * 128:(st + 1) * 128], in0=ptb[:, 0:512].rearrange('p (h t) -> p h t', h=4), scalar1=vfm[:, 37:38], scalar2=None, op0=ALU.mult), reads=[ptk, 'vfm'], writes=[('QmT', st)])
                QmT_keys = [('QmT', st) for st in range(4)]
                pR, pRk = (ps[7], ('ps', 7))
                mpairs = [(h, mb) for h in range(4) for mb in range(2)]
                mpend = []
                for step in range(len(mpairs) + 1):
                    if step < len(mpairs):
                        h, mb = mpairs[step]
                        pS, pSk = PGM.next()
                        P.op('pe', lambda e, pS=pS, h=h, mb=mb: e.matmul(pS[:], KmT[:, h, mb * 128:(mb + 1) * 128], QmT[:, h, :], start=True, stop=True), reads=[('KmT', mb)] + QmT_keys, writes=[pSk])
                        PT, PTk = Rb5.next()
                        P.op('act', lambda e, pS=pS, PT=PT: e.activation(out=PT[:], in_=pS[:], func=AF.Exp, scale=128 ** (-0.5)), reads=[pSk], writes=[PTk])
                        mpend.append((h, mb, PT, PTk))
                        yield
                    if step >= 1:
                        h, mb, PT, PTk = mpend.pop(0)
                        pO, pOk = (ps[5 + h % 2], ('ps', 5 + h % 2))

                        def pvm(e, PT=PT, h=h, mb=mb, pO=pO, pR=pR):
                            for qs in range(4):
                                e.matmul(pO[:, qs * 128:(qs + 1) * 128], PT[:, qs * 128:(qs + 1) * 128], Vm[:, mb, h, :], start=mb == 0 and qs == 0, stop=mb == 1, skip_group_check=True)
                            for qs in range(4):
                                ins = e.matmul(pR[:, h * 8 + qs * 2:h * 8 + qs * 2 + 2], PT[:, qs * 128:(qs + 1) * 128], ones_bf[:, 0:2], start=h == 0 and mb == 0 and qs == 0, stop=mb == 1, skip_group_check=True)
                            return ins
                        P.op('pe', pvm, reads=[PTk, ('Vm', mb), 'ones_bf'], writes=[pOk, pRk])
                        yield
                        if mb == 1:
                            P.op('dve', lambda e, pR=pR, h=h: e.reciprocal(out=rec[:], in_=pR[:, h * 8:h * 8 + 8].rearrange('p (q t) -> p q t', t=2)[:, :, 0]), reads=[pRk], writes=['rec'])
                            P.op('dve', lambda e, pO=pO, h=h: e.tensor_tensor(out=ya[:, :, h * 128:(h + 1) * 128], in0=pO[:].rearrange('p (q d) -> p q d', q=4), in1=rec[:].unsqueeze(2).broadcast_to([128, 4, 128]), op=ALU.mult), reads=[pOk, 'rec'], writes=['ya'])
                yield from tm_post(ya, 'ya')
                yield from merge_branch(3, False)
                acc_keys = [('acc', dc) for dc in range(8)]
                wo0, wo0k = W.next('wo0')
                wo1, wo1k = W.next('wo1')
                for st in range(4):
                    xw, xwk = Rxw.next()
                    P.op('sp', lambda e, xw=xw, st=st, xsrc=xsrc, t0=t0: e.dma_start(out=xw[:], in_=xsrc[t0 + st * 128:t0 + (st + 1) * 128, :]), reads=[(skey, i, st)], writes=[xwk], dsem=xwk)
                    for half, (wo, wok) in enumerate([(wo0, wo0k), (wo1, wo1k)]):
                        po, pok = cur['PG'].next()
                        mm_group(po[:], [(acc[:, kc, st * 128:(st + 1) * 128], wo[:, kc, :]) for kc in range(8)], [wok] + acc_keys, pok)
                        yield
                        P.op('dve', lambda e, po=po, xw=xw, half=half: e.scalar_tensor_tensor(out=xw[:, half * 512:(half + 1) * 512], in0=po[:], scalar=0.25, in1=xw[:, half * 512:(half + 1) * 512], op0=ALU.mult, op1=ALU.add), reads=[pok, xwk], writes=[xwk])
                    P.op('sp', lambda e, dst=dst, t0=t0, st=st, xw=xw: e.dma_start(out=dst[t0 + st * 128:t0 + (st + 1) * 128, :], in_=xw[:]), reads=[xwk], writes=[(dkey, i, st)], dsem=('xst', xwk[1]))
                    yield
                return
                yield
            def attention(i):
                nkb = 4 * (i + 1)
                pairs = [(h, kb_) for h in range(8) for kb_ in range(nkb)]
                pend = []
                psO_cur = {}
                QT_keys = [("QT", st) for st in range(4)]
                for step in range(len(pairs) + ATT_SKEW):
                    new = None
                    if step < len(pairs):
                        h, kb_ = pairs[step]
                        j = kb_ - 4 * i
                        c0 = 128 * j if j > 0 else 0
                        pS, pSk = PS_.next()
                        P.op("pe", lambda e, pS=pS, h=h, kb_=kb_, c0=c0: e.matmul(
                            pS[:, c0:512], KT[0:96, h, kb_ * 128:(kb_ + 1) * 128], QT[0:96, h, c0:512],
                            start=True, stop=True),
                            reads=[("KT", kb_)] + QT_keys, writes=[pSk])
                        PT, PTk = Rb5.next()
                        P.op("act", lambda e, pS=pS, PT=PT, c0=c0: e.activation(
                            out=PT[:, c0:512], in_=pS[:, c0:512], func=AF.Exp, scale=96 ** -0.5),
                            reads=[pSk], writes=[PTk])
                        if j >= 0:
                            P.op("dve", lambda e, PT=PT, c0=c0: e.tensor_tensor(
                                out=PT[:, c0:c0 + 128], in0=PT[:, c0:c0 + 128], in1=tri[:], op=ALU.mult),
                                reads=[PTk, "tri"], writes=[PTk])
                        new = (h, kb_, j, PT, PTk)
                    if new is not None:
                        pend.append(new)
                    if step >= ATT_SKEW:
                        h, kb_, j, PT, PTk = pend.pop(0)
                        if kb_ == 0:
                            psO_cur[h] = PO.next()
                        pO, pOk = psO_cur[h]

                        def pv(e, pO=pO, PT=PT, h=h, kb_=kb_, j=j):
                            for qs in range(max(j, 0), 4):
                                ins = e.matmul(pO[:, qs * 128:qs * 128 + 65], PT[:, qs * 128:(qs + 1) * 128],
                                               Vaug[:, kb_, h, :], start=(kb_ == 0 and qs == 0),
                                               stop=(kb_ == 4 * i + qs), skip_group_check=True)
                            return ins
                        P.op("pe", pv, reads=[PTk, ("V", kb_)], writes=[pOk])
                        if kb_ == nkb - 1:
                            pO3 = pO[:].rearrange("p (q d) -> p q d", q=4)
                            P.op("dve", lambda e, pO3=pO3: e.reciprocal(out=rec[:], in_=pO3[:, :, 64]),
                                 reads=[pOk], writes=["rec"])
                            P.op("dve", lambda e, pO3=pO3, h=h: e.tensor_tensor(
                                out=ya[:, :, h * 64:(h + 1) * 64], in0=pO3[:, :, 0:64],
                                in1=rec[:].unsqueeze(2).broadcast_to([128, 4, 64]), op=ALU.mult),
                                reads=[pOk, "rec"], writes=["ya"])
            xsrc = x_d if l == 0 else x1_d
            dst = y_d if last else x1_d
            dkey = "y" if last else "x1"
            skey = "x" if xsrc is x_d else "x1"
            cur["PG"] = PG
            ga = thA(0, True)
            for i in range(NT):
                cur["hT"] = hTb[i % 2]
                cur["hTk"] = [("hT", i % 2, st) for st in range(4)]
                cur["PG"] = PGB if i > 0 else PG
                gm = thM_pre(i)
                a_alive = m_alive = True
                if i == 0:
                    for _ in ga:
                        pass
                    a_alive = False
                    cur["PG"] = PGB
                while a_alive or m_alive:
                    if a_alive:
                        for _ in range(PRE_A_STEPS if m_alive else 1000):
                            try:
                                next(ga)
                            except StopIteration:
                                a_alive = False
                                break
                    if m_alive:
                        try:
                            next(gm)
                        except StopIteration:
                            m_alive = False
                cur["PG"] = PG
                attention(i)
                cur["PG"] = PGB
                ga = thA(i + 1) if i + 1 < NT else iter(())
                gm = thM_post(i)
                a_alive = m_alive = True
                credit = 0.0
                while m_alive:
                    credit += A_RATIO
                    while credit >= 1.0 and a_alive:
                        credit -= 1.0
                        try:
                            next(ga)
                        except StopIteration:
                            a_alive = False
                    try:
                        next(gm)
                    except StopIteration:
                        m_alive = False
            for _ in ga:
                pass
            cur["PG"] = PG
    try:
        if not done_:
            _layers()
            assert W.cur == len(sched), (W.cur, len(sched))
    except _Stop:
        pass
    P.op("sp", None, reads=[("y", i, st) for i in range(NT) for st in range(4)] + final_keys)
    P.emit()
    es.close()
    return nc


def _pack(inputs):
    f = lambda k: np.asarray(inputs[k], dtype=np.float32)
    vfm = np.zeros((2, 128, NVF), np.float32)
    vrow = np.zeros((2, NVR), np.float32)
    for l in range(2):
        vfm[l, :, 0:8] = f("norm_g")[l].reshape(8, 128).T
        vfm[l, :, 8:16] = f("mem_norm_g")[l].reshape(8, 128).T
        vfm[l, :, 16:18] = f("cq_norm_g")[l].reshape(2, 128).T
        vfm[l, :, 18] = f("ckv_norm_g")[l]
        vfm[l, :, 19] = 1.0
        vfm[l, 0:64, 19] = f("mla_q_norm_g")[l][0:64]
        vfm[l, :, 20] = 1.0
        vfm[l, 0:64, 20] = f("mla_k_norm_g")[l][0:64]
        cw = f("conv_w")[l]
        for j in range(3):
            vfm[l, :, 21 + j * 4:21 + (j + 1) * 4] = cw[j].reshape(4, 128).T
        vfm[l, :, 33:37] = f("conv_b")[l].reshape(4, 128).T
        vfm[l, :, 37] = f("mem_q_norm_g")[l]
        vfm[l, :, 38] = f("mem_k_norm_g")[l]
        vfm[l, :, 39:71] = f("b_merge")[l].reshape(32, 128).T
        vrow[l, 0:32] = f("mla_q_norm_g")[l][64:96]
        vrow[l, 32:64] = f("mla_k_norm_g")[l][64:96]
        vfm[l, :, 71:75] = f("sg_ln_g")[l].reshape(4, 128).T
        vfm[l, :, 75:79] = f("sg_ln_b")[l].reshape(4, 128).T
        vrow[l, 64:576] = f("b_spatial")[l].reshape(512)
    return vfm, vrow


_NC_CACHE = {}


def _in_maps(inputs, cores, x_override=None):
    vfm, vrow = _pack(inputs)
    f = lambda k: np.ascontiguousarray(np.asarray(inputs[k], dtype=np.float32))
    shared = dict(vfm=vfm, vrow=vrow, w_in=f("w_in"), w_uq=f("w_uq"), w_ukv=f("w_ukv"),
                  w_spatial=f("w_spatial"), w_mem_kv=f("w_mem_kv"), w_branch=f("w_branch"), w_out=f("w_out"))
    x = f("x") if x_override is None else x_override
    mem = f("mem")
    pos = np.ascontiguousarray(np.asarray(inputs["positions"], dtype=np.int32))
    maps = []
    for b in cores:
        m = dict(shared)
        m["x"] = np.ascontiguousarray(x[b])
        m["mem"] = np.ascontiguousarray(mem[b])
        m["pos"] = np.ascontiguousarray(pos[b].reshape(16, 128))
        maps.append(m)
    return maps


def kernel(**inputs):
    if "nc" not in _NC_CACHE:
        _NC_CACHE["nc"] = build(2, 0)
    nc = _NC_CACHE["nc"]
    maps = _in_maps(inputs, list(range(8)))
    res = run_bass_kernel_spmd(nc, maps, core_ids=list(range(8)))
    out = np.stack([np.asarray(r["y"], dtype=np.float32) for r in res.results], axis=0)
    return out
```

```python
import contextlib
import math
import numpy as np
import concourse.bass as bass
import concourse.mybir as mybir
from concourse.bass_utils import run_bass_kernel_spmd

F32 = mybir.dt.float32
BF16 = mybir.dt.bfloat16
I32 = mybir.dt.int32
AF = mybir.ActivationFunctionType
ALU = mybir.AluOpType
AX = mybir.AxisListType

S = 2048
D = 1024
T = 512
NT = S // T
EPS = 1e-6
IN_W = 9632
NVF = 79
NVR = 576
NSLOT = 4


class Prog:
    CH = 8000

    def __init__(self, nc):
        self.nc = nc
        self.ops = []
        self.last_w = {}
        self.readers = {}
        self.dma_counts = {}

    def op(self, eng, fn, reads=(), writes=(), dsem=None):
        idx = len(self.ops)
        deps = set()
        for k in reads:
            if k in self.last_w:
                deps.add((self.last_w[k], "raw"))
            if isinstance(k, tuple) and k[0] == "ps":
                for r in self.readers.get(k, ()):
                    deps.add((r, "war"))
        for k in writes:
            if k in self.last_w:
                deps.add((self.last_w[k], "waw"))
            for r in self.readers.get(k, ()):
                deps.add((r, "war"))
        o = dict(idx=idx, eng=eng, fn=fn, deps=deps, dsem=dsem)
        if dsem is not None:
            self.dma_counts[dsem] = self.dma_counts.get(dsem, 0) + 1
            o["dcount"] = self.dma_counts[dsem]
        self.ops.append(o)
        for k in reads:
            self.readers.setdefault(k, []).append(idx)
        for k in writes:
            self.last_w[k] = idx
            self.readers[k] = []
        return idx

    def emit(self):
        nc = self.nc
        ops = self.ops
        need = set()
        for o in ops:
            w = []
            for (d, typ) in o["deps"]:
                p = ops[d]
                if p["dsem"] is not None:
                    w.append(("dma", p["dsem"], p["dcount"]))
                    continue
                if p["eng"] == o["eng"] and o["dsem"] is None:
                    if o["eng"] == "pe" or typ == "war":
                        continue
                need.add(d)
                w.append(("eng", p["eng"], d))
            o["waits"] = w
        ordc = {}
        for o in ops:
            if o["idx"] in need:
                e = o["eng"]
                ordc[e] = ordc.get(e, 0) + 1
                o["ord"] = ordc[e]
        with contextlib.ExitStack() as es:
            esem = {}
            for e, n in ordc.items():
                esem[e] = [es.enter_context(nc.semaphore("s_%s_%d" % (e, c)))
                           for c in range((n + self.CH - 1) // self.CH + 1)]
            dsem = {}
            for j, k in enumerate(self.dma_counts):
                dsem[k] = es.enter_context(nc.semaphore("d_%d" % j))
            block = es.enter_context(nc.Block())
            per = {}
            for o in ops:
                per.setdefault(o["eng"], []).append(o)

            def run(ename, e):
                waited_e = {}
                waited_d = {}
                for o in per.get(ename, []):
                    we = {}
                    wd = {}
                    for w in o["waits"]:
                        if w[0] == "eng":
                            oo = ops[w[2]]["ord"]
                            we[w[1]] = max(we.get(w[1], 0), oo)
                        else:
                            wd[w[1]] = max(wd.get(w[1], 0), w[2])
                    for pe_, oo in we.items():
                        if waited_e.get(pe_, 0) >= oo:
                            continue
                        waited_e[pe_] = oo
                        e.wait_ge(esem[pe_][(oo - 1) // self.CH], (oo - 1) % self.CH + 1)
                    for k, cnt in wd.items():
                        if waited_d.get(k, 0) >= cnt:
                            continue
                        waited_d[k] = cnt
                        e.wait_ge(dsem[k], 16 * cnt)
                    if o["fn"] is None:
                        continue
                    ins = o["fn"](e)
                    if o["dsem"] is not None:
                        ins.then_inc(dsem[o["dsem"]], 16)
                    elif "ord" in o:
                        oo = o["ord"]
                        ins.then_inc(esem[ename][(oo - 1) // self.CH], 1)

            @block.tensor
            def _(e):
                run("pe", e)

            @block.scalar
            def _(e):
                run("act", e)

            @block.vector
            def _(e):
                run("dve", e)

            @block.gpsimd
            def _(e):
                run("pool", e)

            @block.sync
            def _(e):
                run("sp", e)


class _Stop(Exception):
    pass


def build(n_layers=2, first_layer=0, stop=None):
    nc = bass.Bass("TRN2", target_bir_lowering=False)
    L = n_layers

    def din(name, shape, dt=F32):
        return nc.dram_tensor(name, shape, dt, kind="ExternalInput").ap()

    x_d = din("x", [S, D])
    mem_d = din("mem", [256, D])
    pos_d = din("pos", [16, 128], I32)
    vfm_d = din("vfm", [2, 128, NVF])
    vrow_d = din("vrow", [2, NVR])
    w_in_d = din("w_in", [2, D, IN_W])
    w_uq_d = din("w_uq", [2, 256, 768])
    w_ukv_d = din("w_ukv", [2, 128, 1024])
    w_sp_d = din("w_spatial", [2, 4, 128, 128])
    w_mkv_d = din("w_mem_kv", [2, D, 1024])
    w_br_d = din("w_branch", [2, 4, 512, D])
    w_out_d = din("w_out", [2, D, D])
    y_d = nc.dram_tensor("y", [S, D], F32, kind="ExternalOutput").ap()
    x1_d = nc.dram_tensor("x1s", [S, D], F32, kind="Internal").ap()

    P = Prog(nc)
    es = contextlib.ExitStack()

    def sb(name, shape, dt):
        return es.enter_context(nc.sbuf_tensor(name, shape, dt))

    xwb = [sb("xw%d" % j, [128, D], F32) for j in range(2)]
    KT = sb("KT", [128, 8, S], BF16)
    Vaug = sb("Vaug", [128, 16, 8, 65], BF16)
    hTb = [sb("hT%d" % j, [128, 8, T], BF16) for j in range(2)]
    QT = sb("QT", [128, 8, T], BF16)
    ys = sb("ys", [128, 4, T], BF16)
    acc = sb("acc", [128, 8, T], BF16)
    wslot = [sb("wslot%d" % j, [128, 4096], BF16) for j in range(NSLOT)]
    vfm = sb("vfm_sb", [128, NVF], F32)
    vrow = sb("vrow_sb", [128, NVR], F32)
    hbm = sb("hbm", [128, 32], F32)
    wuq = sb("wuq", [128, 2, 768], BF16)
    wukv = sb("wukv", [128, 1024], BF16)
    wTs = sb("wTs", [128, 4, 128], BF16)
    Cg = sb("Cg", [128, 4, 128], F32)
    nb4 = sb("nb4", [128, 4], F32)
    ones_bf = sb("ones_bf", [128, 128], BF16)
    ident = sb("ident", [128, 128], BF16)
    identf = sb("identf", [128, 128], F32)
    tri = sb("tri", [128, 128], BF16)
    mhalf = sb("mhalf", [128, 32], F32)
    invf = sb("invf", [128, 16], F32)
    post = sb("post", [128, 16], F32)
    cosT = sb("cosT", [128, 16, 16], F32)
    sinT = sb("sinT", [128, 16, 16], F32)
    KmT = sb("KmT", [128, 4, 256], BF16)
    Vm = sb("Vm", [128, 2, 4, 128], BF16)
    QmT = sb("QmT", [128, 4, T], BF16)
    cqT = sb("cqT", [128, 3, T], BF16)
    ya = sb("ya", [128, 4, 512], BF16)
    yall = sb("yall", [128, 4, 512], BF16)
    zc = sb("zc", [128, 4, 516], BF16)
    junk = sb("junk", [128, 1024], BF16)
    kr = sb("kr", [128, 4, 32], F32)
    ssx = sb("ssx", [128, 4], F32)
    rx = sb("rx", [128, 4], F32)
    ss3 = sb("ss3", [128, 12], F32)
    r2 = sb("r2", [128, 2], F32)
    r2w = sb("r2w", [128, 8], F32)
    ssq = sb("ssq", [128, 8], F32)
    rq = sb("rq", [128, 8], F32)
    ssk = sb("ssk", [128, 8], F32)
    rk = sb("rk", [128, 8], F32)
    rtmp = sb("rtmp", [128, 8], F32)
    rtmpA = sb("rtmpA", [128, 16], F32)
    ssqk = sb("ssqk", [128, 16], F32)
    rqk = sb("rqk", [128, 16], F32)
    rtmpB = sb("rtmpB", [128, 8], F32)
    wAs = sb("wAs", [128, 8 * 416], BF16)
    bA = [sb("bA_%d" % j, [128, 512], BF16) for j in range(2)]
    st6 = sb("st6", [128, 6], F32)
    st6s = sb("st6s", [128, 4, 6], F32)
    mvs = sb("mvs", [128, 4, 2], F32)
    rs4 = sb("rs4", [128, 4], F32)
    ss16 = sb("ss16", [128, 16], F32)
    r16 = sb("r16", [128, 16], F32)
    rtmp16 = sb("rtmp16", [128, 16], F32)
    mv = sb("mv", [128, 2], F32)
    rs1 = sb("rs1", [128, 1], F32)
    ss4 = sb("ss4", [128, 4], F32)
    r4 = sb("r4", [128, 4], F32)
    rec = sb("rec", [128, 4], F32)
    ssm = sb("ssm", [128, 2], F32)
    rm = sb("rm", [128, 2], F32)
    NF1 = 2
    f1024 = [sb("f1024_%d" % j, [128, 1024], F32) for j in range(NF1)]
    NF5 = 4
    f512 = [sb("f512_%d" % j, [128, 512], F32) for j in range(NF5)]
    NB5 = 6
    b512 = [sb("b512_%d" % j, [128, 512], BF16) for j in range(NB5)]
    NB1 = 3
    b1024 = [sb("b1024_%d" % j, [128, 1024], BF16) for j in range(NB1)]
    NR = 0
    f256 = [sb("f256_%d" % j, [128, 256], F32) for j in range(NR)]
    f128 = []
    fqk = [sb("fqk_%d" % j, [128, 512], F32) for j in range(2)]
    ps = [es.enter_context(nc.psum_tensor("ps%d" % j, [128, 512], F32)) for j in range(8)]

    class Rot:
        def __init__(self, items, key):
            self.items = items
            self.key = key
            self.i = 0

        def next(self):
            j = self.i % len(self.items)
            self.i += 1
            return self.items[j], (self.key, j)

    wspf = f512[0][:].rearrange("p (g s) -> p g s", g=4)
    posi = f512[1][0:16, 0:128].bitcast(I32)
    posf = f512[2][0:16, 0:128]
    ones_f = f512[3][:, 0:128]
    angt = f512[0][:, 0:256]
    kkt = f512[1][:, 0:256]
    kit = f512[2][:, 0:256].bitcast(I32)
    redt = f512[3][:, 0:256]
    memnT = yall[:].rearrange("p c t -> p (c t)").rearrange("p (k m) -> p k m", k=8)
    vln = ya
    RbA = Rot(bA, "bA")
    wAv = wAs[:].rearrange("p (k n) -> p k n", k=8)
    Rxw = Rot(xwb, "xw")
    Rqk = Rot(fqk, "fqk")
    Rf1 = Rot(f1024, "f1024")
    ssqk2 = sb("ssqk2", [128, 16], F32)
    rqk2 = sb("rqk2", [128, 16], F32)
    rtmpA2 = sb("rtmpA2", [128, 16], F32)

    class RotK:
        def __init__(self, items):
            self.items = items
            self.i = 0

        def next(self):
            j = self.i % len(self.items)
            self.i += 1
            return self.items[j]

    Rf5 = Rot(f512, "f512")
    Rb5 = Rot(b512, "b512")
    Rb1 = Rot(b1024, "b1024")
    Rf2 = Rot(f256, "f256")
    Rf128 = Rot(f128, "f128")

    class PsRot:
        def __init__(self, banks):
            self.banks = banks
            self.i = 0

        def next(self):
            b = self.banks[self.i % len(self.banks)]
            self.i += 1
            return ps[b], ("ps", b)

    PG = PsRot([0, 1, 2, 3])
    PGA = PsRot([0, 1])
    PW = PsRot([4, 5, 6, 7])
    PGM = PsRot([2, 3, 4])
    PGB = PsRot([2, 3, 4, 5])
    cur = {"PG": PG}
    A_RATIO = 1.15
    PRE_A_STEPS = 1
    PS_ = PsRot([2, 3, 4, 5])
    ATT_SKEW = 3
    PO = PsRot([6, 7])

    def bfv(p):
        return p[:].bitcast(BF16)

    RES_A = dict(pg=PGA, f1=Rf1, b1=Rb1, qk=Rqk, ssqk=ssqk, rqk=rqk, rtmp=rtmpA, n="")
    RES_B = dict(pg=PsRot([2, 3]),
                 f1=RotK([(xwb[0], ("xw", 0)), (xwb[1], ("xw", 1))]),
                 b1=RotK([(b1024[2], ("b1024", 2)), (junk, "junk")]),
                 qk=RotK([(f512[0], ("f512", 0)), (f512[1], ("f512", 1))]),
                 ssqk=ssqk2, rqk=rqk2, rtmp=rtmpA2, n="B")
    RES_A0 = dict(RES_A, b1=RotK([(b1024[0], ("b1024", 0)), (b1024[1], ("b1024", 1))]))

    sched = []
    for l in range(L):
        ll = l + first_layer
        sched.append(("memK", w_mkv_d[ll][:, 0:512], 8, 512))
        sched.append(("memV", w_mkv_d[ll][:, 512:1024], 8, 512))
        for i in range(NT):
            def wi(a, b_):
                return w_in_d[ll][:, a:b_]
            for n in (1, 0, 2, 3):
                if n == 0:
                    sched.append(("gate0", wi(3488, 4000), 8, 512))
                if n == 1:
                    sched.append(("conv_c", wi(928, 1440), 8, 512))
                    sched.append(("conv_x", wi(1440, 1952), 8, 512))
                    sched.append(("conv_b", wi(416, 928), 8, 512))
                    sched.append(("gate1", wi(4000, 4512), 8, 512))
                if n == 2:
                    sched.append(("sg_v", wi(2464, 2976), 8, 512))
                    sched.append(("gate2", wi(4512, 5024), 8, 512))
                    sched.append(("sg_u", wi(1952, 2464), 8, 512))
                if n == 3:
                    sched.append(("memq", wi(2976, 3488), 8, 512))
                    sched.append(("gate3", wi(5024, 5536), 8, 512))
                for hf in range(2):
                    c0_ = 5536 + n * 1024 + hf * 512
                    sched.append(("m%d_%d" % (n, hf), wi(c0_, c0_ + 512), 8, 512))
                    sched.append(("wb%d_%d" % (n, hf), w_br_d[ll, n][:, hf * 512:(hf + 1) * 512], 4, 512))
            sched.append(("wo0", w_out_d[ll][:, 0:512], 8, 512))
            sched.append(("wo1", w_out_d[ll][:, 512:1024], 8, 512))

    class WStream:
        def __init__(self):
            self.issued = 0
            self.cur = 0

        def _issue(self):
            c = self.issued
            if c >= len(sched):
                return
            name, src, nk, ncol = sched[c]
            s = c % NSLOT
            view = wslot[s][:, 0:nk * ncol].rearrange("p (k n) -> p k n", k=nk)
            srcv = src.rearrange("(k p) n -> p k n", p=128)
            P.op("pool", lambda e, view=view, srcv=srcv: e.dma_start(out=view, in_=srcv),
                 writes=[("w", s)], dsem=("w", s))
            self.issued += 1

        def next(self, name):
            while self.issued < min(len(sched), self.cur + NSLOT - 1):
                self._issue()
            n2, src, nk, ncol = sched[self.cur]
            assert n2 == name, (n2, name)
            s = self.cur % NSLOT
            self.cur += 1
            view = wslot[s][:, 0:nk * ncol].rearrange("p (k n) -> p k n", k=nk)
            return view, ("w", s)

    W = WStream()

    def load_layer_consts(ll):
        P.op("sp", lambda e, ll=ll: e.dma_start(out=vfm[:], in_=vfm_d[ll]), writes=["vfm"], dsem="vfm")
        P.op("sp", lambda e, ll=ll: e.dma_start(out=vrow[:], in_=vrow_d[ll, :].partition_broadcast(128)),
             writes=["vrow"], dsem="vrow")
        P.op("pool", lambda e, ll=ll: e.dma_start(out=wuq[:], in_=w_uq_d[ll].rearrange("(k p) n -> p k n", p=128)),
             writes=["wuq"], dsem="wuq")
        P.op("pool", lambda e, ll=ll: e.dma_start(out=wukv[:], in_=w_ukv_d[ll]), writes=["wukv"], dsem="wukv")
        P.op("pool", lambda e, ll=ll: e.dma_start(
            out=wAv, in_=w_in_d[ll][:, 0:416].rearrange("(k p) n -> p k n", p=128)), writes=["wA"], dsem="wA")

    for _ in range(NSLOT - 1):
        W._issue()
    load_layer_consts(first_layer)

    def rsqrt(out_ap, in_ap, scale, n, rkeys, wkeys, tmp=None, tmpk="rtmp"):
        tmp = rtmp if tmp is None else tmp
        P.op("pool", lambda e: e.tensor_scalar(out=tmp[:, 0:n], in0=in_ap, scalar1=scale, scalar2=EPS,
                                               op0=ALU.mult, op1=ALU.add), reads=rkeys, writes=[tmpk])
        P.op("pool", lambda e: e.tensor_tensor(out=out_ap, in0=tmp[:, 0:n], in1=mhalf[:, 0:n], op=ALU.pow),
             reads=[tmpk, "mhalf"], writes=wkeys)

    def rsqrtA(out_ap, in_ap, scale, n, rkeys, wkeys):
        rsqrt(out_ap, in_ap, scale, n, rkeys, wkeys, tmp=rtmpA, tmpk="rtmpA")

    def mm_group(out_ap, pairs, reads, pskey):
        def fn(e):
            n = len(pairs)
            for j, (a, b_) in enumerate(pairs):
                ins = e.matmul(out_ap, a, b_, start=(j == 0), stop=(j == n - 1))
            return ins
        P.op("pe", fn, reads=reads, writes=[pskey])

    def transposes(pairs, reads, pskey, idt):
        def fn(e):
            for (o_, i_) in pairs:
                ins = e.transpose(out=o_, in_=i_, identity=idt)
            return ins
        P.op("pe", fn, reads=reads + ["ident"], writes=[pskey])

    final_keys = []
    dbg_n = [0]
    dbg_off = [0]

    def checkpoint(name, dumps):
        if stop != name:
            return
        yv = y_d.rearrange("(p a) d -> p (a d)", p=128)
        for (ap, n, keys) in dumps:
            for c0 in range(0, n, 1024):
                w = min(1024, n - c0)
                stg, stgk = Rf1.next()
                off = dbg_off[0]
                idx = dbg_n[0]
                P.op("dve", lambda e, stg=stg, ap=ap, c0=c0, w=w: e.tensor_copy(out=stg[:, 0:w], in_=ap[:, c0:c0 + w]),
                     reads=keys, writes=[stgk])
                P.op("sp", lambda e, stg=stg, off=off, w=w: e.dma_start(out=yv[:, off:off + w], in_=stg[:, 0:w]),
                     reads=[stgk], writes=[("ydbg", idx)], dsem=("dbg", idx))
                final_keys.append(("ydbg", idx))
                dbg_off[0] += w
                dbg_n[0] += 1
        raise _Stop()

    P.op("pool", lambda e: e.memset(ones_bf[:], 1.0), writes=["ones_bf"])
    P.op("pool", lambda e: e.memset(ones_f[:], 1.0), writes=[("f512", 3)])
    P.op("pool", lambda e: e.memset(mhalf[:], -0.5), writes=["mhalf"])
    P.op("pool", lambda e: e.affine_select(out=ident[:], in_=ones_bf[:], pattern=[[1, 128]],
                                           compare_op=ALU.is_equal, fill=0.0, base=0, channel_multiplier=-1),
         reads=["ones_bf"], writes=["ident"])
    P.op("pool", lambda e: e.affine_select(out=identf[:], in_=ones_f[:], pattern=[[1, 128]],
                                           compare_op=ALU.is_equal, fill=0.0, base=0, channel_multiplier=-1),
         reads=[("f512", 3)], writes=["identf"])
    P.op("pool", lambda e: e.affine_select(out=tri[:], in_=ones_bf[:], pattern=[[1, 128]],
                                           compare_op=ALU.is_ge, fill=0.0, base=0, channel_multiplier=-1),
         reads=["ones_bf"], writes=["tri"])
    inv = np.power(np.float32(10000.0), -np.arange(16, dtype=np.float32) / np.float32(16)).astype(np.float32)
    for j in range(16):
        P.op("pool", lambda e, j=j: e.memset(invf[:, j:j + 1], float(inv[j])), writes=["invf"])
    P.op("pool", lambda e: e.memset(Vaug[:].rearrange("p a b c -> p (a b c)"), 1.0), writes=[("V", b_) for b_ in range(16)])
    P.op("sp", lambda e: e.dma_start(out=posi[:], in_=pos_d), writes=[("f512", 1)], dsem=("f512", 1))
    P.op("dve", lambda e: e.tensor_copy(out=posf[:], in_=posi[:]), reads=[("f512", 1)], writes=[("f512", 2)])
    P.op("pe", lambda e: e.transpose(out=ps[0][:, 0:16], in_=posf[:], identity=identf[0:16, 0:16]),
         reads=[("f512", 2), "identf"], writes=[("ps", 0)])
    P.op("dve", lambda e: e.tensor_copy(out=post[:], in_=ps[0][:, 0:16]), reads=[("ps", 0)], writes=["post"])
    for s_ in range(16):
        P.op("dve", lambda e, s_=s_: e.tensor_scalar(out=angt[:, s_ * 16:(s_ + 1) * 16], in0=invf[:],
                                                      scalar1=post[:, s_:s_ + 1], scalar2=None, op0=ALU.mult),
             reads=["post", "invf"], writes=[("f512", 0)])
    C1 = 6.28125
    C2 = 2 * math.pi - 6.28125
    P.op("dve", lambda e: e.tensor_scalar(out=kkt[:], in0=angt[:], scalar1=1.0 / (2 * math.pi), scalar2=None,
                                          op0=ALU.mult), reads=[("f512", 0)], writes=[("f512", 1)])
    P.op("dve", lambda e: e.tensor_copy(out=kit[:], in_=kkt[:]), reads=[("f512", 1)], writes=[("f512", 2)])
    P.op("dve", lambda e: e.tensor_copy(out=kkt[:], in_=kit[:]), reads=[("f512", 2)], writes=[("f512", 1)])
    P.op("dve", lambda e: e.scalar_tensor_tensor(out=redt[:], in0=kkt[:], scalar=-C1, in1=angt[:],
                                                 op0=ALU.mult, op1=ALU.add), reads=[("f512", 1), ("f512", 0)], writes=[("f512", 3)])
    P.op("dve", lambda e: e.scalar_tensor_tensor(out=redt[:], in0=kkt[:], scalar=-C2, in1=redt[:],
                                                 op0=ALU.mult, op1=ALU.add), reads=[("f512", 1), ("f512", 3)], writes=[("f512", 3)])

    def wrap():
        P.op("dve", lambda e: e.tensor_scalar(out=kkt[:], in0=redt[:], scalar1=math.pi, scalar2=-2 * math.pi,
                                              op0=ALU.is_gt, op1=ALU.mult), reads=[("f512", 3)], writes=[("f512", 1)])
        P.op("dve", lambda e: e.tensor_tensor(out=redt[:], in0=redt[:], in1=kkt[:], op=ALU.add),
             reads=[("f512", 3), ("f512", 1)], writes=[("f512", 3)])
        P.op("dve", lambda e: e.tensor_scalar(out=kkt[:], in0=redt[:], scalar1=-math.pi, scalar2=2 * math.pi,
                                              op0=ALU.is_lt, op1=ALU.mult), reads=[("f512", 3)], writes=[("f512", 1)])
        P.op("dve", lambda e: e.tensor_tensor(out=redt[:], in0=redt[:], in1=kkt[:], op=ALU.add),
             reads=[("f512", 3), ("f512", 1)], writes=[("f512", 3)])
        P.op("dve", lambda e: e.tensor_scalar(out=redt[:], in0=redt[:], scalar1=math.pi, scalar2=-math.pi,
                                              op0=ALU.min, op1=ALU.max), reads=[("f512", 3)], writes=[("f512", 3)])

    wrap()
    P.op("act", lambda e: e.activation(out=sinT[:].rearrange("p a b -> p (a b)"), in_=redt[:], func=AF.Sin),
         reads=[("f512", 3)], writes=["sinT"])
    P.op("dve", lambda e: e.tensor_scalar(out=redt[:], in0=redt[:], scalar1=math.pi / 2, scalar2=None,
                                          op0=ALU.add), reads=[("f512", 3), "sinT"], writes=[("f512", 3)])
    wrap()
    P.op("act", lambda e: e.activation(out=cosT[:].rearrange("p a b -> p (a b)"), in_=redt[:], func=AF.Sin),
         reads=[("f512", 3)], writes=["cosT"])

    done_ = False
    try:
        checkpoint("setup", [(cosT[:].rearrange("p a b -> p (a b)"), 256, ["cosT"]),
                             (sinT[:].rearrange("p a b -> p (a b)"), 256, ["sinT"]),
                             (post[:], 16, ["post"]), (tri[:], 128, ["tri"]), (ident[:], 128, ["ident"])])
    except _Stop:
        done_ = True
    def rope(src, dst, sg, skey, dkey):
        cosb = cosT[:, sg, :].unsqueeze(1).broadcast_to([128, 8, 16])
        sinb = sinT[:, sg, :].unsqueeze(1).broadcast_to([128, 8, 16])
        t1, k1 = Rf128.next()
        t2, k2 = Rf128.next()
        t1v = t1[:].rearrange("p (h d) -> p h d", h=8)
        t2v = t2[:].rearrange("p (h d) -> p h d", h=8)
        P.op("dve", lambda e: e.tensor_tensor(out=t1v, in0=src[:, :, 0:16], in1=cosb, op=ALU.mult),
             reads=[skey, "cosT"], writes=[k1])
        P.op("dve", lambda e: e.tensor_tensor(out=t2v, in0=src[:, :, 16:32], in1=sinb, op=ALU.mult),
             reads=[skey, "sinT"], writes=[k2])
        P.op("dve", lambda e: e.tensor_tensor(out=dst[:, :, 0:16], in0=t1v, in1=t2v, op=ALU.subtract),
             reads=[k1, k2], writes=[dkey])
        t3, k3 = Rf128.next()
        t4, k4 = Rf128.next()
        t3v = t3[:].rearrange("p (h d) -> p h d", h=8)
        t4v = t4[:].rearrange("p (h d) -> p h d", h=8)
        P.op("dve", lambda e: e.tensor_tensor(out=t3v, in0=src[:, :, 0:16], in1=sinb, op=ALU.mult),
             reads=[skey, "sinT"], writes=[k3])
        P.op("dve", lambda e: e.tensor_tensor(out=t4v, in0=src[:, :, 16:32], in1=cosb, op=ALU.mult),
             reads=[skey, "cosT"], writes=[k4])
        P.op("dve", lambda e: e.tensor_tensor(out=dst[:, :, 16:32], in0=t3v, in1=t4v, op=ALU.add),
             reads=[k3, k4], writes=[dkey])

    YALLK = [("yall", c) for c in range(4)]

    def gate_chunk(wg, wgk, c, ysrc_ap, ysrc_keys):
        pg, pgk = cur["PG"].next()
        mm_group(pg[:], [(wg[:, k, c * 128:(c + 1) * 128], cur["hT"][:, k, :]) for k in range(8)],
                 [wgk] + cur["hTk"], pgk)
        tg, tgk = Rb5.next()
        P.op("act", lambda e: e.activation(out=tg[:], in_=pg[:], func=AF.Tanh, scale=0.5),
             reads=[pgk], writes=[tgk])
        u, uk = Rb5.next()
        P.op("dve", lambda e: e.scalar_tensor_tensor(out=u[:], in0=tg[:], scalar=1.0, in1=pg[:],
                                                     op0=ALU.add, op1=ALU.mult), reads=[tgk, pgk], writes=[uk])
        P.op("dve", lambda e: e.tensor_tensor(out=ys[:, c, :], in0=u[:], in1=ysrc_ap, op=ALU.mult),
             reads=[uk] + ysrc_keys, writes=[("ys", c)])

    def gate_pre(wg, wgk, c):
        pg, pgk = cur["PG"].next()
        mm_group(pg[:], [(wg[:, k, c * 128:(c + 1) * 128], cur["hT"][:, k, :]) for k in range(8)],
                 [wgk] + cur["hTk"], pgk)
        tg, tgk = Rb5.next()
        P.op("act", lambda e: e.activation(out=tg[:], in_=pg[:], func=AF.Tanh, scale=0.5),
             reads=[pgk], writes=[tgk])
        P.op("dve", lambda e: e.scalar_tensor_tensor(out=ys[:, c, :], in0=tg[:], scalar=1.0, in1=pg[:],
                                                     op0=ALU.add, op1=ALU.mult), reads=[tgk, pgk],
             writes=[("ys", c)])

    def gate_post(c, ysrc_ap, ysrc_keys):
        P.op("dve", lambda e: e.tensor_tensor(out=ys[:, c, :], in0=ys[:, c, :], in1=ysrc_ap, op=ALU.mult),
             reads=[("ys", c)] + ysrc_keys, writes=[("ys", c)])

    def tm_post(ytm, ykey):
        for c in range(4):
            pt, ptk = cur["PG"].next()
            ptb = bfv(pt)
            transposes([(ptb[:, qs * 128:(qs + 1) * 128], ytm[:, qs, c * 128:(c + 1) * 128]) for qs in range(4)],
                       [ykey], ptk, ident[:])
            gate_post(c, ptb[:, 0:512], [ptk])
            yield

    def merge_branch(n, first):
        for hf in range(2):
            wm, wmk = W.next("m%d_%d" % (n, hf))
            wb, wbk = W.next("wb%d_%d" % (n, hf))

            def logits(c4):
                dc = hf * 4 + c4
                pl, plk = cur["PG"].next()
                mm_group(pl[:], [(wm[:, k, c4 * 128:(c4 + 1) * 128], cur["hT"][:, k, :]) for k in range(8)],
                         [wmk] + cur["hTk"], plk)
                tm, tmk = Rb5.next()
                P.op("act", lambda e, pl=pl, tm=tm, dc=dc: e.activation(out=tm[:], in_=pl[:], func=AF.Tanh,
                                                                       bias=hbm[:, n * 8 + dc:n * 8 + dc + 1],
                                                                       scale=0.5),
                     reads=[plk, "hbm"], writes=[tmk])
                return tm, tmk

            nxt = logits(0)
            yield
            for c4 in range(4):
                dc = hf * 4 + c4
                tm, tmk = nxt
                if c4 + 1 < 4:
                    nxt = logits(c4 + 1)
                    yield
                pz, pzk = cur["PG"].next()
                mm_group(pz[:], [(wb[:, kc, c4 * 128:(c4 + 1) * 128], ys[:, kc, :]) for kc in range(4)],
                         [wbk] + [("ys", c) for c in range(4)], pzk)
                if first:
                    P.op("dve", lambda e, tm=tm, pz=pz, dc=dc: e.scalar_tensor_tensor(
                        out=acc[:, dc, :], in0=tm[:], scalar=1.0, in1=pz[:], op0=ALU.add, op1=ALU.mult),
                        reads=[tmk, pzk], writes=[("acc", dc)])
                else:
                    tp, tpk = Rf5.next()
                    P.op("dve", lambda e, tm=tm, pz=pz, tp=tp: e.scalar_tensor_tensor(
                        out=tp[:], in0=tm[:], scalar=1.0, in1=pz[:], op0=ALU.add, op1=ALU.mult),
                        reads=[tmk, pzk], writes=[tpk])
                    P.op("dve", lambda e, tp=tp, dc=dc: e.tensor_tensor(out=acc[:, dc, :], in0=acc[:, dc, :],
                                                                       in1=tp[:], op=ALU.add),
                         reads=[tpk, ("acc", dc)], writes=[("acc", dc)])
                yield

    def tm_to_ys(ytm, ykey, wg, wgk):
        for c in range(4):
            pt, ptk = cur["PG"].next()
            ptb = bfv(pt)
            transposes([(ptb[:, qs * 128:(qs + 1) * 128], ytm[:, qs, c * 128:(c + 1) * 128]) for qs in range(4)],
                       [ykey], ptk, ident[:])
            gate_chunk(wg, wgk, c, ptb[:, 0:512], [ptk])
            yield

    def _layers():
        for l in range(L):
            ll = l + first_layer
            last = (l == L - 1)
            if l > 0:
                load_layer_consts(ll)
            P.op("dve", lambda e: e.tensor_scalar(out=hbm[:], in0=vfm[:, 39:71], scalar1=0.5, scalar2=None,
                                                  op0=ALU.mult), reads=["vfm"], writes=["hbm"])
            P.op("sp", lambda e, ll=ll: e.dma_start(out=wspf[:], in_=w_sp_d[ll].rearrange("g t s -> t g s")),
                 writes=[("f512", 0)], dsem=("f512", 0))
            for g in range(4):
                pt, ptk = PG.next()
                P.op("pe", lambda e, pt=pt, g=g: e.transpose(out=pt[:, 0:128], in_=wspf[:, g, :], identity=identf[:]),
                     reads=[("f512", 0), "identf"], writes=[ptk])
                tf, tfk = f512[1 + g % 3][:, 0:128], ("f512", 1 + g % 3)
                P.op("act", lambda e, pt=pt, tf=tf: e.activation(out=tf[:], in_=pt[:, 0:128], func=AF.Copy),
                     reads=[ptk], writes=[tfk])
                P.op("pool", lambda e, tf=tf, g=g: e.affine_select(out=wTs[:, g, :], in_=tf[:], pattern=[[1, 128]],
                                                                  compare_op=ALU.is_ge, fill=0.0, base=0,
                                                                  channel_multiplier=-1),
                     reads=[tfk], writes=["wTs"])
                pr, prk = PG.next()
                P.op("pe", lambda e, pr=pr, g=g: e.matmul(pr[:, 0:128], ones_bf[:, 0:128], wTs[:, g, :], start=True, stop=True),
                     reads=["ones_bf", "wTs"], writes=[prk])
                P.op("dve", lambda e, pr=pr, g=g: e.scalar_tensor_tensor(
                    out=Cg[:, g, :], in0=pr[:, 0:128], scalar=vfm[:, 75 + g:76 + g], in1=vrow[:, 64 + g * 128:64 + (g + 1) * 128],
                    op0=ALU.mult, op1=ALU.add), reads=[prk, "vfm", "vrow"], writes=["Cg"])
            P.op("dve", lambda e: e.memset(zc[:, :, 0:2], 0.0), writes=[("z", c) for c in range(4)])
            memt = [f1024[0], f1024[1]]
            mkeys = [("f1024", 0), ("f1024", 1)]
            Rf1.i = 2
            for mb in range(2):
                P.op("sp", lambda e, mb=mb: e.dma_start(out=memt[mb][:], in_=mem_d[mb * 128:(mb + 1) * 128, :]),
                     writes=[mkeys[mb]], dsem=("memt", mb))
            for mb in range(2):
                P.op("act", lambda e, mb=mb: e.activation(out=junk[:], in_=memt[mb][:], func=AF.Square,
                                                          accum_out=ssm[:, mb:mb + 1]),
                     reads=[mkeys[mb]], writes=[("ssm", mb), "junk"])
            rsqrt(rm[:], ssm[:], 1.0 / D, 2, [("ssm", 0), ("ssm", 1)], ["rm"])
            for mb in range(2):
                mnb, mnbk = Rb1.next()
                P.op("dve", lambda e, mb=mb, mnb=mnb: e.tensor_scalar(out=mnb[:], in0=memt[mb][:], scalar1=rm[:, mb:mb + 1],
                                                                      scalar2=None, op0=ALU.mult),
                     reads=[mkeys[mb], "rm"], writes=[mnbk])
                pt, ptk = PG.next()
                ptb = bfv(pt)
                transposes([(ptb[:, k * 128:(k + 1) * 128], mnb[:, k * 128:(k + 1) * 128]) for k in range(8)],
                           [mnbk], ptk, ident[:])
                P.op("dve", lambda e, ptb=ptb, mb=mb: e.tensor_tensor(
                    out=memnT[:, :, mb * 128:(mb + 1) * 128], in0=ptb.rearrange("p (k t) -> p k t", k=8),
                    in1=vfm[:, 8:16].unsqueeze(2).broadcast_to([128, 8, 128]), op=ALU.mult),
                    reads=[ptk, "vfm"], writes=[*YALLK])
            wk_, wkk = W.next("memK")
            for mb in range(2):
                pk, pkk = PG.next()
                mm_group(pk[:], [(memnT[:, k, mb * 128:(mb + 1) * 128], wk_[:, k, :]) for k in range(8)],
                         [wkk, *YALLK], pkk)
                kf, kfk = Rf5.next()
                P.op("act", lambda e, pk=pk, kf=kf: e.activation(out=kf[:], in_=pk[:], func=AF.Copy),
                     reads=[pkk], writes=[kfk])
                sq, sqk = Rf5.next()
                P.op("act", lambda e, kf=kf, sq=sq: e.activation(out=sq[:], in_=kf[:], func=AF.Square),
                     reads=[kfk], writes=[sqk])
                P.op("dve", lambda e, sq=sq: e.tensor_reduce(out=ss4[:], in_=sq[:].rearrange("p (h d) -> p h d", h=4),
                                                            axis=AX.X, op=ALU.add), reads=[sqk], writes=["ss4"])
                rsqrt(r4[:], ss4[:], 1.0 / 128, 4, ["ss4"], ["r4"])
                knb, knbk = Rb5.next()
                P.op("dve", lambda e, kf=kf, knb=knb: e.tensor_tensor(
                    out=knb[:].rearrange("p (h d) -> p h d", h=4), in0=kf[:].rearrange("p (h d) -> p h d", h=4),
                    in1=r4[:].unsqueeze(2).broadcast_to([128, 4, 128]), op=ALU.mult),
                    reads=[kfk, "r4"], writes=[knbk])
                pt, ptk = PG.next()
                ptb = bfv(pt)
                transposes([(ptb[:, h * 128:(h + 1) * 128], knb[:, h * 128:(h + 1) * 128]) for h in range(4)],
                           [knbk], ptk, ident[:])
                P.op("dve", lambda e, ptb=ptb, mb=mb: e.tensor_scalar(
                    out=KmT[:, :, mb * 128:(mb + 1) * 128], in0=ptb[:, 0:512].rearrange("p (h t) -> p h t", h=4),
                    scalar1=vfm[:, 38:39], scalar2=None, op0=ALU.mult),
                    reads=[ptk, "vfm"], writes=[("KmT", mb)])
            wv_, wvk = W.next("memV")
            for mb in range(2):
                pv, pvk = PG.next()
                mm_group(pv[:], [(memnT[:, k, mb * 128:(mb + 1) * 128], wv_[:, k, :]) for k in range(8)],
                         [wvk, *YALLK], pvk)
                P.op("act", lambda e, pv=pv, mb=mb: e.activation(out=Vm[:, mb, :, :].rearrange("p h d -> p (h d)"),
                                                                in_=pv[:], func=AF.Copy),
                     reads=[pvk], writes=[("Vm", mb)])

            if l == 0:
                checkpoint("lsetup", [(wTs[:].rearrange("p a b -> p (a b)"), 512, ["wTs"]),
                                      (KmT[:].rearrange("p a b -> p (a b)"), 1024, [("KmT", 0), ("KmT", 1)]),
                                      (Vm[:].rearrange("p a b c -> p (a b c)"), 1024, [("Vm", 0), ("Vm", 1)]),
                                      (hbm[:], 32, ["hbm"]), (vrow[:, 0:64], 64, ["vrow"])])
            def thA_head(i):
                hT_ = hTb[i % 2]
                t0 = i * T
                for st in range(4):
                    xs, xsk = Rf1.next()
                    P.op('sp', lambda e, xs=xs, st=st, xsrc=xsrc, t0=t0: e.dma_start(out=xs[:], in_=xsrc[t0 + st * 128:t0 + (st + 1) * 128, :]), reads=[(skey, i, st)], writes=[xsk], dsem=xsk)
                    yield
                    P.op('act', lambda e, xs=xs, st=st: e.activation(out=junk[:], in_=xs[:], func=AF.Square, accum_out=ssx[:, st:st + 1]), reads=[xsk], writes=[('ssx', st), 'junk'])
                    yield
                    rsqrtA(rx[:, st:st + 1], ssx[:, st:st + 1], 1.0 / D, 1, [('ssx', st)], [('rx', st)])
                    yield
                    hb, hbk = Rb1.next()
                    P.op('dve', lambda e, xs=xs, st=st, hb=hb: e.tensor_scalar(out=hb[:], in0=xs[:], scalar1=rx[:, st:st + 1], scalar2=None, op0=ALU.mult), reads=[xsk, ('rx', st)], writes=[hbk])
                    yield
                    pt, ptk = PGA.next()
                    ptb = bfv(pt)
                    transposes([(ptb[:, k * 128:(k + 1) * 128], hb[:, k * 128:(k + 1) * 128]) for k in range(8)], [hbk], ptk, ident[:])
                    yield
                    P.op('dve', lambda e, ptb=ptb, st=st: e.tensor_tensor(out=hT_[:, :, st * 128:(st + 1) * 128], in0=ptb.rearrange('p (k t) -> p k t', k=8), in1=vfm[:, 0:8].unsqueeze(2).broadcast_to([128, 8, 128]), op=ALU.mult), reads=[ptk, 'vfm'], writes=[('hT', i % 2, st)])
                    yield
                wA, wAk = (wAv, 'wA')
                for st in range(4):
                    pg, pgk = PGA.next()
                    mm_group(pg[:, 0:416], [(hT_[:, k, st * 128:(st + 1) * 128], wA[:, k, 0:416]) for k in range(8)], [wAk, ('hT', i % 2, st)], pgk)
                    yield
                    for j, (a, b_) in enumerate([(0, 256), (256, 384), (384, 416)]):
                        P.op('act', lambda e, pg=pg, a=a, b_=b_, st=st, j=j: e.activation(out=junk[:, a:b_], in_=pg[:, a:b_], func=AF.Square, accum_out=ss3[:, st * 3 + j:st * 3 + j + 1]), reads=[pgk], writes=[('ss3', st, j)])
                        yield
                    P.op('dve', lambda e, pg=pg, st=st: e.tensor_copy(out=kr[:, st, :], in_=pg[:, 384:416]), reads=[pgk], writes=[('kr', st)])
                    yield
                    P.op('pool', lambda e, st=st: e.tensor_scalar(out=rtmpA[:, 0:1], in0=ss3[:, st * 3:st * 3 + 1], scalar1=1.0 / 256, scalar2=EPS, op0=ALU.mult, op1=ALU.add), reads=[('ss3', st, 0)], writes=['rtmpA'])
                    yield
                    P.op('pool', lambda e, st=st: e.tensor_scalar(out=rtmpA[:, 1:2], in0=ss3[:, st * 3 + 1:st * 3 + 2], scalar1=1.0 / 128, scalar2=EPS, op0=ALU.mult, op1=ALU.add), reads=[('ss3', st, 1)], writes=['rtmpA'])
                    yield
                    P.op('pool', lambda e: e.tensor_tensor(out=r2[:], in0=rtmpA[:, 0:2], in1=mhalf[:, 0:2], op=ALU.pow), reads=['rtmpA', 'mhalf'], writes=['r2'])
                    yield
                    cqn, cqnk = RbA.next()
                    P.op('dve', lambda e, pg=pg, cqn=cqn: e.tensor_scalar(out=cqn[:, 0:256], in0=pg[:, 0:256], scalar1=r2[:, 0:1], scalar2=None, op0=ALU.mult), reads=[pgk, 'r2'], writes=[cqnk])
                    yield
                    P.op('dve', lambda e, pg=pg, cqn=cqn: e.tensor_scalar(out=cqn[:, 256:384], in0=pg[:, 256:384], scalar1=r2[:, 1:2], scalar2=None, op0=ALU.mult), reads=[pgk, 'r2'], writes=[cqnk])
                    yield
                    pt, ptk = PGA.next()
                    ptb = bfv(pt)
                    transposes([(ptb[:, k * 128:(k + 1) * 128], cqn[:, k * 128:(k + 1) * 128]) for k in range(3)], [cqnk], ptk, ident[:])
                    yield
                    P.op('dve', lambda e, ptb=ptb, st=st: e.tensor_tensor(out=cqT[:, :, st * 128:(st + 1) * 128], in0=ptb[:, 0:384].rearrange('p (k t) -> p k t', k=3), in1=vfm[:, 16:19].unsqueeze(2).broadcast_to([128, 3, 128]), op=ALU.mult), reads=[ptk, 'vfm'], writes=[('cqT', st)])
                    yield
            def thA_head_wide(i):
                hT_ = hTb[i % 2]
                t0 = i * T
                xbufs = [(f1024[0], ('f1024', 0)), (f1024[1], ('f1024', 1)), (xwb[0], ('xw', 0)), (xwb[1], ('xw', 1))]
                for st in range(4):
                    xs, xsk = xbufs[st]
                    P.op('sp', lambda e, xs=xs, st=st, xsrc=xsrc, t0=t0: e.dma_start(out=xs[:], in_=xsrc[t0 + st * 128:t0 + (st + 1) * 128, :]), reads=[(skey, i, st)], writes=[xsk], dsem=xsk)
                for st in range(4):
                    xs, xsk = xbufs[st]
                    P.op('act', lambda e, xs=xs, st=st: e.activation(out=junk[:], in_=xs[:], func=AF.Square, accum_out=ssx[:, st:st + 1]), reads=[xsk], writes=[('ssx', st), 'junk'])
                rsqrtA(rx[:, 0:4], ssx[:, 0:4], 1.0 / D, 4, [('ssx', st) for st in range(4)], [('rx', st) for st in range(4)])
                yield
                for st in range(4):
                    xs, xsk = xbufs[st]
                    hb, hbk = Rb1.next()
                    P.op('dve', lambda e, xs=xs, st=st, hb=hb: e.tensor_scalar(out=hb[:], in0=xs[:], scalar1=rx[:, st:st + 1], scalar2=None, op0=ALU.mult), reads=[xsk, ('rx', st)], writes=[hbk])
                    pt, ptk = PG.next()
                    ptb = bfv(pt)
                    transposes([(ptb[:, k * 128:(k + 1) * 128], hb[:, k * 128:(k + 1) * 128]) for k in range(8)], [hbk], ptk, ident[:])
                    P.op('dve', lambda e, ptb=ptb, st=st: e.tensor_tensor(out=hT_[:, :, st * 128:(st + 1) * 128], in0=ptb.rearrange('p (k t) -> p k t', k=8), in1=vfm[:, 0:8].unsqueeze(2).broadcast_to([128, 8, 128]), op=ALU.mult), reads=[ptk, 'vfm'], writes=[('hT', i % 2, st)])
                    yield
                wA, wAk = (wAv, 'wA')
                pgs = []
                for st in range(4):
                    pg, pgk = PW.next()
                    mm_group(pg[:, 0:416], [(hT_[:, k, st * 128:(st + 1) * 128], wA[:, k, 0:416]) for k in range(8)], [wAk, ('hT', i % 2, st)], pgk)
                    pgs.append((pg, pgk))
                for st in range(4):
                    pg, pgk = pgs[st]
                    for j, (a, b_) in enumerate([(0, 256), (256, 384), (384, 416)]):
                        P.op('act', lambda e, pg=pg, a=a, b_=b_, st=st, j=j: e.activation(out=junk[:, a:b_], in_=pg[:, a:b_], func=AF.Square, accum_out=ss3[:, st * 3 + j:st * 3 + j + 1]), reads=[pgk], writes=[('ss3', st, j)])
                    P.op('dve', lambda e, pg=pg, st=st: e.tensor_copy(out=kr[:, st, :], in_=pg[:, 384:416]), reads=[pgk], writes=[('kr', st)])
                ss3v = ss3[:].rearrange('p (s j) -> p s j', j=3)
                rtv = rtmpA[:, 0:8].rearrange('p (s j) -> p s j', j=2)
                P.op('pool', lambda e: e.tensor_scalar(out=rtv[:, :, 0], in0=ss3v[:, :, 0], scalar1=1.0 / 256, scalar2=EPS, op0=ALU.mult, op1=ALU.add), reads=[('ss3', st, 0) for st in range(4)], writes=['rtmpA'])
                P.op('pool', lambda e: e.tensor_scalar(out=rtv[:, :, 1], in0=ss3v[:, :, 1], scalar1=1.0 / 128, scalar2=EPS, op0=ALU.mult, op1=ALU.add), reads=[('ss3', st, 1) for st in range(4)], writes=['rtmpA'])
                P.op('pool', lambda e: e.tensor_tensor(out=r2w[:], in0=rtmpA[:, 0:8], in1=mhalf[:, 0:8], op=ALU.pow), reads=['rtmpA', 'mhalf'], writes=['r2w'])
                yield
                for st in range(4):
                    pg, pgk = pgs[st]
                    cqn, cqnk = Rb5.next()
                    P.op('dve', lambda e, pg=pg, cqn=cqn, st=st: e.tensor_scalar(out=cqn[:, 0:256], in0=pg[:, 0:256], scalar1=r2w[:, 2 * st:2 * st + 1], scalar2=None, op0=ALU.mult), reads=[pgk, 'r2w'], writes=[cqnk])
                    P.op('dve', lambda e, pg=pg, cqn=cqn, st=st: e.tensor_scalar(out=cqn[:, 256:384], in0=pg[:, 256:384], scalar1=r2w[:, 2 * st + 1:2 * st + 2], scalar2=None, op0=ALU.mult), reads=[pgk, 'r2w'], writes=[cqnk])
                    pt, ptk = PG.next()
                    ptb = bfv(pt)
                    transposes([(ptb[:, k * 128:(k + 1) * 128], cqn[:, k * 128:(k + 1) * 128]) for k in range(3)], [cqnk], ptk, ident[:])
                    P.op('dve', lambda e, ptb=ptb, st=st: e.tensor_tensor(out=cqT[:, :, st * 128:(st + 1) * 128], in0=ptb[:, 0:384].rearrange('p (k t) -> p k t', k=3), in1=vfm[:, 16:19].unsqueeze(2).broadcast_to([128, 3, 128]), op=ALU.mult), reads=[ptk, 'vfm'], writes=[('cqT', st)])
                    yield
            def thA_m2(i, sts, R):
                for st in sts:
                    blk = 4 * i + st
                    pa, pak = R['pg'].next()
                    pb, pbk = R['pg'].next()
                    mm_group(pa[:, 0:384], [(cqT[:, kc, st * 128:(st + 1) * 128], wuq[:, kc, 0:384]) for kc in range(2)], ['wuq', ('cqT', st)], pak)
                    yield
                    mm_group(pb[:, 0:384], [(cqT[:, kc, st * 128:(st + 1) * 128], wuq[:, kc, 384:768]) for kc in range(2)], ['wuq', ('cqT', st)], pbk)
                    yield
                    qf, qfk = R['f1'].next()
                    P.op('act', lambda e, pa=pa, qf=qf: e.activation(out=qf[:, 0:384], in_=pa[:, 0:384], func=AF.Copy), reads=[pak], writes=[qfk])
                    P.op('act', lambda e, pb=pb, qf=qf: e.activation(out=qf[:, 384:768], in_=pb[:, 0:384], func=AF.Copy), reads=[pbk], writes=[qfk])
                    yield
                    pa2, pa2k = R['pg'].next()
                    pb2, pb2k = R['pg'].next()
                    mm_group(pa2[:], [(cqT[:, 2, st * 128:(st + 1) * 128], wukv[:, 0:512])], ['wukv', ('cqT', st)], pa2k)
                    yield
                    mm_group(pb2[:], [(cqT[:, 2, st * 128:(st + 1) * 128], wukv[:, 512:1024])], ['wukv', ('cqT', st)], pb2k)
                    yield
                    kvf, kvfk = R['f1'].next()
                    P.op('act', lambda e, pa2=pa2, kvf=kvf: e.activation(out=kvf[:, 0:512], in_=pa2[:], func=AF.Copy), reads=[pa2k], writes=[kvfk])
                    P.op('act', lambda e, pb2=pb2, kvf=kvf: e.activation(out=kvf[:, 512:1024], in_=pb2[:], func=AF.Copy), reads=[pb2k], writes=[kvfk])
                    yield
                    qf3 = qf[:, 0:768].rearrange('p (h d) -> p h d', h=8)
                    kvf3 = kvf[:].rearrange('p (h d) -> p h d', h=8)
                    sq, sqk = R['b1'].next()
                    P.op('act', lambda e, qf=qf, sq=sq: e.activation(out=sq[:, 0:768], in_=qf[:, 0:768], func=AF.Square), reads=[qfk], writes=[sqk])
                    yield
                    P.op('dve', lambda e, sq=sq: e.tensor_reduce(out=R['ssqk'][:, 0:8], in_=sq[:, 0:768].rearrange('p (h d) -> p h d', h=8), axis=AX.X, op=ALU.add), reads=[sqk], writes=[(R['n'] + 'ssqk', 0)])
                    yield
                    sq2, sq2k = R['b1'].next()
                    P.op('act', lambda e, kvf=kvf, sq2=sq2: e.activation(out=sq2[:], in_=kvf[:], func=AF.Square), reads=[kvfk], writes=[sq2k])
                    yield
                    P.op('dve', lambda e, sq2=sq2: e.tensor_reduce(out=R['ssqk'][:, 8:16], in_=sq2[:].rearrange('p (h d) -> p h d', h=8)[:, :, 0:64], axis=AX.X, op=ALU.add), reads=[sq2k], writes=[(R['n'] + 'ssqk', 1)])
                    yield
                    P.op('dve', lambda e, st=st: e.tensor_scalar(out=R['ssqk'][:, 8:16], in0=R['ssqk'][:, 8:16], scalar1=ss3[:, st * 3 + 2:st * 3 + 3], scalar2=None, op0=ALU.add), reads=[(R['n'] + 'ssqk', 1), ('ss3', st, 2)], writes=[(R['n'] + 'ssqk', 1)])
                    yield
                    rsqrt(R['rqk'][:], R['ssqk'][:], 1.0 / 96, 16, [(R['n'] + 'ssqk', 0), (R['n'] + 'ssqk', 1)], [(R['n'] + 'rqk')], tmp=R['rtmp'], tmpk=R['n'] + 'rtmpA')
                    yield
                    P.op('act', lambda e, kvf3=kvf3, blk=blk: e.activation(out=Vaug[:, blk, :, 0:64], in_=kvf3[:, :, 64:128], func=AF.Copy), reads=[kvfk], writes=[('V', blk)])
                    yield
                    qb, qbk = R['b1'].next()
                    qb3 = qb[:, 0:768].rearrange('p (h d) -> p h d', h=8)
                    P.op('dve', lambda e, qb3=qb3, qf3=qf3: e.tensor_tensor(out=qb3[:, :, 0:64], in0=qf3[:, :, 0:64], in1=R['rqk'][:, 0:8].unsqueeze(2).broadcast_to([128, 8, 64]), op=ALU.mult), reads=[qfk, (R['n'] + 'rqk')], writes=[qbk])
                    yield
                    kb, kbk = R['b1'].next()
                    kb3 = kb[:, 0:768].rearrange('p (h d) -> p h d', h=8)
                    P.op('dve', lambda e, kb3=kb3, kvf3=kvf3: e.tensor_tensor(out=kb3[:, :, 0:64], in0=kvf3[:, :, 0:64], in1=R['rqk'][:, 8:16].unsqueeze(2).broadcast_to([128, 8, 64]), op=ALU.mult), reads=[kvfk, (R['n'] + 'rqk')], writes=[kbk])
                    yield
                    xr, xrk = R['qk'].next()
                    tr_, trk = R['qk'].next()
                    xr3 = xr[:].rearrange('p (h d) -> p h d', h=16)
                    tr3 = tr_[:].rearrange('p (h d) -> p h d', h=16)
                    P.op('dve', lambda e, xr3=xr3, qf3=qf3: e.tensor_tensor(out=xr3[:, 0:8, :], in0=qf3[:, :, 64:96], in1=R['rqk'][:, 0:8].unsqueeze(2).broadcast_to([128, 8, 32]), op=ALU.mult), reads=[qfk, (R['n'] + 'rqk')], writes=[xrk])
                    yield
                    P.op('dve', lambda e, xr3=xr3, st=st: e.tensor_tensor(out=xr3[:, 8:16, :], in0=kr[:, st, :].unsqueeze(1).broadcast_to([128, 8, 32]), in1=R['rqk'][:, 8:16].unsqueeze(2).broadcast_to([128, 8, 32]), op=ALU.mult), reads=[('kr', st), (R['n'] + 'rqk')], writes=[xrk])
                    yield
                    P.op('dve', lambda e, xr=xr: e.tensor_tensor(out=xr[:].rearrange('p (a h d) -> p a h d', a=2, h=8), in0=xr[:].rearrange('p (a h d) -> p a h d', a=2, h=8), in1=vrow[:, 0:64].rearrange('p (a d) -> p a d', a=2).unsqueeze(2).broadcast_to([128, 2, 8, 32]), op=ALU.mult), reads=[xrk, 'vrow'], writes=[xrk])
                    yield
                    cosb = cosT[:, blk, :].unsqueeze(1).broadcast_to([128, 16, 16])
                    sinb = sinT[:, blk, :].unsqueeze(1).broadcast_to([128, 16, 16])
                    P.op('dve', lambda e, xr3=xr3, tr3=tr3, sinb=sinb: e.scalar_tensor_tensor(out=tr3[:, :, 0:16], in0=xr3[:, :, 16:32], scalar=-1.0, in1=sinb, op0=ALU.mult, op1=ALU.mult), reads=[xrk, 'sinT'], writes=[trk])
                    yield
                    P.op('dve', lambda e, xr3=xr3, tr3=tr3, sinb=sinb: e.tensor_tensor(out=tr3[:, :, 16:32], in0=xr3[:, :, 0:16], in1=sinb, op=ALU.mult), reads=[xrk, 'sinT'], writes=[trk])
                    yield
                    P.op('dve', lambda e, xr3=xr3, cosb=cosb: e.tensor_tensor(out=xr3[:, :, 0:16], in0=xr3[:, :, 0:16], in1=cosb, op=ALU.mult), reads=[xrk, trk, 'cosT'], writes=[xrk])
                    P.op('dve', lambda e, xr3=xr3, cosb=cosb: e.tensor_tensor(out=xr3[:, :, 16:32], in0=xr3[:, :, 16:32], in1=cosb, op=ALU.mult), reads=[xrk, trk, 'cosT'], writes=[xrk])
                    yield
                    P.op('dve', lambda e, xr3=xr3, tr3=tr3, qb3=qb3: e.tensor_tensor(out=qb3[:, :, 64:96], in0=xr3[:, 0:8, :], in1=tr3[:, 0:8, :], op=ALU.add), reads=[xrk, trk], writes=[qbk])
                    yield
                    P.op('dve', lambda e, xr3=xr3, tr3=tr3, kb3=kb3: e.tensor_tensor(out=kb3[:, :, 64:96], in0=xr3[:, 8:16, :], in1=tr3[:, 8:16, :], op=ALU.add), reads=[xrk, trk], writes=[kbk])
                    yield
                    pt, ptk = R['pg'].next()
                    ptb = bfv(pt)
                    transposes([(ptb[0:96, h * 128:(h + 1) * 128], qb3[:, h, :]) for h in range(8)], [qbk], ptk, ident[:])
                    yield
                    pt2, pt2k = R['pg'].next()
                    pt2b = bfv(pt2)
                    transposes([(pt2b[0:96, h * 128:(h + 1) * 128], kb3[:, h, :]) for h in range(8)], [kbk], pt2k, ident[:])
                    yield
                    P.op('dve', lambda e, ptb=ptb, st=st: e.tensor_scalar(out=QT[0:96, :, st * 128:(st + 1) * 128], in0=ptb[0:96, :].rearrange('p (h t) -> p h t', h=8), scalar1=vfm[0:96, 19:20], scalar2=None, op0=ALU.mult), reads=[ptk, 'vfm'], writes=[('QT', st)])
                    yield
                    P.op('dve', lambda e, pt2b=pt2b, blk=blk: e.tensor_scalar(out=KT[0:96, :, blk * 128:(blk + 1) * 128], in0=pt2b[0:96, :].rearrange('p (h t) -> p h t', h=8), scalar1=vfm[0:96, 20:21], scalar2=None, op0=ALU.mult), reads=[pt2k, 'vfm'], writes=[('KT', blk)])
                    yield
                return
                yield
            def thA(i, wide=False):
                if wide:
                    yield from thA_head_wide(i)
                    g0 = thA_m2(i, [0, 2], RES_A0)
                    g1 = thA_m2(i, [1, 3], RES_B)
                    al = [True, True]
                    for _ in range(14):
                        next(g0)
                        yield
                    while al[0] or al[1]:
                        for j, g in enumerate((g0, g1)):
                            if al[j]:
                                try:
                                    next(g)
                                except StopIteration:
                                    al[j] = False
                        yield
                else:
                    yield from thA_head(i)
                    yield from thA_m2(i, range(4), RES_A)
            def thM_pre(i):
                cur['hT'] = hTb[i % 2]
                cur['hTk'] = [('hT', i % 2, st) for st in range(4)]
                wcg, wcgk = W.next('conv_c')
                wxi, wxik = W.next('conv_x')
                for c in range(4):
                    pc, pck = cur['PG'].next()
                    mm_group(pc[:], [(wcg[:, k, c * 128:(c + 1) * 128], cur['hT'][:, k, :]) for k in range(8)], [wcgk] + cur['hTk'], pck)
                    yield
                    px, pxk = cur['PG'].next()
                    mm_group(px[:], [(wxi[:, k, c * 128:(c + 1) * 128], cur['hT'][:, k, :]) for k in range(8)], [wxik] + cur['hTk'], pxk)
                    yield
                    xs, xsk = Rf5.next()
                    P.op('act', lambda e, px=px, xs=xs: e.activation(out=xs[:], in_=px[:], func=AF.Copy), reads=[pxk], writes=[xsk])
                    P.op('dve', lambda e, pc=pc, xs=xs, c=c: e.tensor_tensor(out=zc[:, c, 2:514], in0=pc[:], in1=xs[:], op=ALU.mult), reads=[pck, xsk], writes=[('z', c)])
                    y0, y0k = Rf5.next()
                    P.op('dve', lambda e, y0=y0, c=c: e.tensor_scalar(out=y0[:], in0=zc[:, c, 2:514], scalar1=vfm[:, 21 + 8 + c:21 + 8 + c + 1], scalar2=vfm[:, 33 + c:34 + c], op0=ALU.mult, op1=ALU.add), reads=[('z', c), 'vfm'], writes=[y0k])
                    y1, y1k = Rf5.next()
                    P.op('dve', lambda e, y0=y0, y1=y1, c=c: e.scalar_tensor_tensor(out=y1[:], in0=zc[:, c, 1:513], scalar=vfm[:, 21 + 4 + c:21 + 4 + c + 1], in1=y0[:], op0=ALU.mult, op1=ALU.add), reads=[('z', c), 'vfm', y0k], writes=[y1k])
                    P.op('dve', lambda e, y1=y1, c=c: e.scalar_tensor_tensor(out=yall[:, c, :], in0=zc[:, c, 0:512], scalar=vfm[:, 21 + c:21 + c + 1], in1=y1[:], op0=ALU.mult, op1=ALU.add), reads=[('z', c), 'vfm', y1k], writes=[('yall', c)])
                    P.op('dve', lambda e, c=c: e.tensor_copy(out=zc[:, c, 0:2], in_=zc[:, c, 512:514]), reads=[('z', c)], writes=[('z', c)])
                    yield
                wbg, wbgk = W.next('conv_b')
                for c in range(4):
                    pbg, pbgk = cur['PG'].next()
                    mm_group(pbg[:], [(wbg[:, k, c * 128:(c + 1) * 128], cur['hT'][:, k, :]) for k in range(8)], [wbgk] + cur['hTk'], pbgk)
                    yield
                    P.op('dve', lambda e, pbg=pbg, c=c: e.tensor_tensor(out=yall[:, c, :], in0=yall[:, c, :], in1=pbg[:], op=ALU.mult), reads=[('yall', c), pbgk], writes=[('yall', c)])
                    yield
                wg, wgk = W.next('gate1')
                for c in range(4):
                    gate_chunk(wg, wgk, c, yall[:, c, :], [('yall', c)])
                    yield
                yield from merge_branch(1, True)
                return
                yield
            def thM_post(i):
                t0 = i * T
                wg, wgk = W.next('gate0')
                yield from tm_to_ys(ya, 'ya', wg, wgk)
                yield from merge_branch(0, False)
                wv2, wv2k = W.next('sg_v')
                vcs = []
                for st in range(4):
                    pv_, pvk_ = cur['PG'].next()
                    mm_group(pv_[:], [(cur['hT'][:, k, st * 128:(st + 1) * 128], wv2[:, k, :]) for k in range(8)], [wv2k, ('hT', i % 2, st)], pvk_)
                    vn, vnk = Rf5.next()
                    P.op('act', lambda e, pv_=pv_, vn=vn: e.activation(out=vn[:], in_=pv_[:], func=AF.Copy), reads=[pvk_], writes=[vnk])
                    vcs.append((vn, vnk))
                    yield
                wg, wgk = W.next('gate2')
                for st in range(4):
                    vn, vnk = vcs[st]
                    P.op('dve', lambda e, vn=vn, st=st: e.bn_stats(out=st6s[:, st, :], in_=vn[:]), reads=[vnk], writes=[('st6', st)])
                    P.op('dve', lambda e, st=st: e.bn_aggr(out=mvs[:, st, :], in_=st6s[:, st, :]), reads=[('st6', st)], writes=[('mv', st)])
                P.op('pool', lambda e: e.tensor_scalar(out=rtmpB[:, 0:4], in0=mvs[:, :, 1], scalar1=EPS, scalar2=None, op0=ALU.add), reads=[('mv', st) for st in range(4)], writes=['rtmpB'])
                P.op('pool', lambda e: e.tensor_tensor(out=rs4[:], in0=rtmpB[:, 0:4], in1=mhalf[:, 0:4], op=ALU.pow), reads=['rtmpB', 'mhalf'], writes=['rs4'])
                yield
                for g in range(4):
                    gate_pre(wg, wgk, g)
                    yield
                P.op('dve', lambda e: e.scalar_tensor_tensor(out=nb4[:], in0=mvs[:, :, 0], scalar=-1.0, in1=rs4[:], op0=ALU.mult, op1=ALU.mult), reads=[('mv', st) for st in range(4)] + ['rs4'], writes=['nb4'])
                for st in range(4):
                    vn, vnk = vcs[st]
                    P.op('act', lambda e, vn=vn, st=st: e.activation(out=vln[:, st, :], in_=vn[:], func=AF.Identity, bias=nb4[:, st:st + 1], scale=rs4[:, st:st + 1]), reads=[vnk, 'nb4', 'rs4'], writes=['ya'])
                    yield
                wu, wuk = W.next('sg_u')
                for g in range(4):
                    pm, pmk = cur['PG'].next()

                    def mix(e, pm=pm, g=g):
                        for st in range(4):
                            ins = e.matmul(pm[:, st * 128:(st + 1) * 128], vln[:, st, g * 128:(g + 1) * 128], wTs[:, g, :], start=True, stop=True)
                        return ins
                    P.op('pe', mix, reads=['ya', 'wTs'], writes=[pmk])
                    yield
                    pu, puk = cur['PG'].next()
                    mm_group(pu[:], [(wu[:, k, g * 128:(g + 1) * 128], cur['hT'][:, k, :]) for k in range(8)], [wuk] + cur['hTk'], puk)
                    yield
                    mt, mtk = Rb5.next()
                    P.op('dve', lambda e, pm=pm, mt=mt, g=g: e.scalar_tensor_tensor(out=mt[:].rearrange('p (s t) -> p s t', s=4), in0=pm[:].rearrange('p (s t) -> p s t', s=4), scalar=vfm[:, 71 + g:72 + g], in1=Cg[:, g, :].unsqueeze(1).broadcast_to([128, 4, 128]), op0=ALU.mult, op1=ALU.add), reads=[pmk, 'vfm', 'Cg'], writes=[mtk])
                    P.op('dve', lambda e, pu=pu, mt=mt, g=g: e.tensor_tensor(out=mt[:], in0=pu[:], in1=mt[:], op=ALU.mult), reads=[puk, mtk], writes=[mtk])
                    gate_post(g, mt[:], [mtk])
                    yield
                yield from merge_branch(2, False)
                wq_, wqk = W.next('memq')
                wg, wgk = W.next('gate3')
                pqs = []
                for st in range(4):
                    pq, pqk = cur['PG'].next()
                    mm_group(pq[:], [(cur['hT'][:, k, st * 128:(st + 1) * 128], wq_[:, k, :]) for k in range(8)], [wqk, ('hT', i % 2, st)], pqk)
                    pqs.append((pq, pqk))
                    yield
                mqfs = []
                for st in range(4):
                    pq, pqk = pqs[st]
                    mqf, mqfk = Rf5.next()
                    P.op('act', lambda e, pq=pq, mqf=mqf: e.activation(out=mqf[:], in_=pq[:], func=AF.Copy), reads=[pqk], writes=[mqfk])
                    mqfs.append((mqf, mqfk))
                for st in range(4):
                    mqf, mqfk = mqfs[st]
                    sq, sqk = Rb5.next()
                    P.op('act', lambda e, mqf=mqf, sq=sq: e.activation(out=sq[:], in_=mqf[:], func=AF.Square), reads=[mqfk], writes=[sqk])
                    P.op('dve', lambda e, sq=sq, st=st: e.tensor_reduce(out=ss16[:, st * 4:(st + 1) * 4], in_=sq[:].rearrange('p (h d) -> p h d', h=4), axis=AX.X, op=ALU.add), reads=[sqk], writes=[('ss16', st)])
                rsqrt(r16[:], ss16[:], 1.0 / 128, 16, [('ss16', st) for st in range(4)], ['r16'], tmp=rtmp16, tmpk='rtmp16')
                yield
                for c in range(4):
                    gate_pre(wg, wgk, c)
                    yield
                for st in range(4):
                    mqf, mqfk = mqfs[st]
                    mqn, mqnk = Rb5.next()
                    P.op('dve', lambda e, mqf=mqf, mqn=mqn, st=st: e.tensor_tensor(out=mqn[:].rearrange('p (h d) -> p h d', h=4), in0=mqf[:].rearrange('p (h d) -> p h d', h=4), in1=r16[:, st * 4:(st + 1) * 4].unsqueeze(2).broadcast_to([128, 4, 128]), op=ALU.mult), reads=[mqfk, 'r16'], writes=[mqnk])
                    pt, ptk = cur['PG'].next()
                    ptb = bfv(pt)
                    transposes([(ptb[:, h * 128:(h + 1) * 128], mqn[:, h * 128:(h + 1) * 128]) for h in range(4)], [mqnk], ptk, ident[:])
                    yield
                    P.op('dve', lambda e, ptb=ptb, st=st: e.tensor_scalar(out=QmT[:, :, st * 128:(st + 1) * 128], in0=ptb[:, 0:512].rearrange('p (h t) -> p h t', h=4), scalar1=vfm[:, 37:38], scalar2=None, op0=ALU.mult), reads=[ptk, 'vfm'], writes=[('QmT', st)])
                QmT_keys = [('QmT', st) for st in range(4)]
                pR, pRk = (ps[7], ('ps', 7))
                mpairs = [(h, mb) for h in range(4) for mb in range(2)]
                mpend = []
                for step in range(len(mpairs) + 1):
                    if step < len(mpairs):
                        h, mb = mpairs[step]
                        pS, pSk = PGM.next()
                        P.op('pe', lambda e, pS=pS, h=h, mb=mb: e.matmul(pS[:], KmT[:, h, mb * 128:(mb + 1) * 128], QmT[:, h, :], start=True, stop=True), reads=[('KmT', mb)] + QmT_keys, writes=[pSk])
                        PT, PTk = Rb5.next()
                        P.op('act', lambda e, pS=pS, PT=PT: e.activation(out=PT[:], in_=pS[:], func=AF.Exp, scale=128 ** (-0.5)), reads=[pSk], writes=[PTk])
                        mpend.append((h, mb, PT, PTk))
                        yield
                    if step >= 1:
                        h, mb, PT, PTk = mpend.pop(0)
                        pO, pOk = (ps[5 + h % 2], ('ps', 5 + h % 2))

                        def pvm(e, PT=PT, h=h, mb=mb, pO=pO, pR=pR):
                            for qs in range(4):
                                e.matmul(pO[:, qs * 128:(qs + 1) * 128], PT[:, qs * 128:(qs + 1) * 128], Vm[:, mb, h, :], start=mb == 0 and qs == 0, stop=mb == 1, skip_group_check=True)
                            for qs in range(4):
                                ins = e.matmul(pR[:, h * 8 + qs * 2:h * 8 + qs * 2 + 2], PT[:, qs * 128:(qs + 1) * 128], ones_bf[:, 0:2], start=h == 0 and mb == 0 and qs == 0, stop=mb == 1, skip_group_check=True)
                            return ins
                        P.op('pe', pvm, reads=[PTk, ('Vm', mb), 'ones_bf'], writes=[pOk, pRk])
                        yield
                        if mb == 1:
                            P.op('dve', lambda e, pR=pR, h=h: e.reciprocal(out=rec[:], in_=pR[:, h * 8:h * 8 + 8].rearrange('p (q t) -> p q t', t=2)[:, :, 0]), reads=[pRk], writes=['rec'])
                            P.op('dve', lambda e, pO=pO, h=h: e.tensor_tensor(out=ya[:, :, h * 128:(h + 1) * 128], in0=pO[:].rearrange('p (q d) -> p q d', q=4), in1=rec[:].unsqueeze(2).broadcast_to([128, 4, 128]), op=ALU.mult), reads=[pOk, 'rec'], writes=['ya'])
                yield from tm_post(ya, 'ya')
                yield from merge_branch(3, False)
                acc_keys = [('acc', dc) for dc in range(8)]
                wo0, wo0k = W.next('wo0')
                wo1, wo1k = W.next('wo1')
                for st in range(4):
                    xw, xwk = Rxw.next()
                    P.op('sp', lambda e, xw=xw, st=st, xsrc=xsrc, t0=t0: e.dma_start(out=xw[:], in_=xsrc[t0 + st * 128:t0 + (st + 1) * 128, :]), reads=[(skey, i, st)], writes=[xwk], dsem=xwk)
                    for half, (wo, wok) in enumerate([(wo0, wo0k), (wo1, wo1k)]):
                        po, pok = cur['PG'].next()
                        mm_group(po[:], [(acc[:, kc, st * 128:(st + 1) * 128], wo[:, kc, :]) for kc in range(8)], [wok] + acc_keys, pok)
                        yield
                        P.op('dve', lambda e, po=po, xw=xw, half=half: e.scalar_tensor_tensor(out=xw[:, half * 512:(half + 1) * 512], in0=po[:], scalar=0.25, in1=xw[:, half * 512:(half + 1) * 512], op0=ALU.mult, op1=ALU.add), reads=[pok, xwk], writes=[xwk])
                    P.op('sp', lambda e, dst=dst, t0=t0, st=st, xw=xw: e.dma_start(out=dst[t0 + st * 128:t0 + (st + 1) * 128, :], in_=xw[:]), reads=[xwk], writes=[(dkey, i, st)], dsem=('xst', xwk[1]))
                    yield
                return
                yield
            def attention(i):
                nkb = 4 * (i + 1)
                pairs = [(h, kb_) for h in range(8) for kb_ in range(nkb)]
                pend = []
                psO_cur = {}
                QT_keys = [("QT", st) for st in range(4)]
                for step in range(len(pairs) + ATT_SKEW):
                    new = None
                    if step < len(pairs):
                        h, kb_ = pairs[step]
                        j = kb_ - 4 * i
                        c0 = 128 * j if j > 0 else 0
                        pS, pSk = PS_.next()
                        P.op("pe", lambda e, pS=pS, h=h, kb_=kb_, c0=c0: e.matmul(
                            pS[:, c0:512], KT[0:96, h, kb_ * 128:(kb_ + 1) * 128], QT[0:96, h, c0:512],
                            start=True, stop=True),
                            reads=[("KT", kb_)] + QT_keys, writes=[pSk])
                        PT, PTk = Rb5.next()
                        P.op("act", lambda e, pS=pS, PT=PT, c0=c0: e.activation(
                            out=PT[:, c0:512], in_=pS[:, c0:512], func=AF.Exp, scale=96 ** -0.5),
                            reads=[pSk], writes=[PTk])
                        if j >= 0:
                            P.op("dve", lambda e, PT=PT, c0=c0: e.tensor_tensor(
                                out=PT[:, c0:c0 + 128], in0=PT[:, c0:c0 + 128], in1=tri[:], op=ALU.mult),
                                reads=[PTk, "tri"], writes=[PTk])
                        new = (h, kb_, j, PT, PTk)
                    if new is not None:
                        pend.append(new)
                    if step >= ATT_SKEW:
                        h, kb_, j, PT, PTk = pend.pop(0)
                        if kb_ == 0:
                            psO_cur[h] = PO.next()
                        pO, pOk = psO_cur[h]

                        def pv(e, pO=pO, PT=PT, h=h, kb_=kb_, j=j):
                            for qs in range(max(j, 0), 4):
                                ins = e.matmul(pO[:, qs * 128:qs * 128 + 65], PT[:, qs * 128:(qs + 1) * 128],
                                               Vaug[:, kb_, h, :], start=(kb_ == 0 and qs == 0),
                                               stop=(kb_ == 4 * i + qs), skip_group_check=True)
                            return ins
                        P.op("pe", pv, reads=[PTk, ("V", kb_)], writes=[pOk])
                        if kb_ == nkb - 1:
                            pO3 = pO[:].rearrange("p (q d) -> p q d", q=4)
                            P.op("dve", lambda e, pO3=pO3: e.reciprocal(out=rec[:], in_=pO3[:, :, 64]),
                                 reads=[pOk], writes=["rec"])
                            P.op("dve", lambda e, pO3=pO3, h=h: e.tensor_tensor(
                                out=ya[:, :, h * 64:(h + 1) * 64], in0=pO3[:, :, 0:64],
                                in1=rec[:].unsqueeze(2).broadcast_to([128, 4, 64]), op=ALU.mult),
                                reads=[pOk, "rec"], writes=["ya"])
            xsrc = x_d if l == 0 else x1_d
            dst = y_d if last else x1_d
            dkey = "y" if last else "x1"
            skey = "x" if xsrc is x_d else "x1"
            cur["PG"] = PG
            ga = thA(0, True)
            for i in range(NT):
                cur["hT"] = hTb[i % 2]
                cur["hTk"] = [("hT", i % 2, st) for st in range(4)]
                cur["PG"] = PGB if i > 0 else PG
                gm = thM_pre(i)
                a_alive = m_alive = True
                if i == 0:
                    for _ in ga:
                        pass
                    a_alive = False
                    cur["PG"] = PGB
                while a_alive or m_alive:
                    if a_alive:
                        for _ in range(PRE_A_STEPS if m_alive else 1000):
                            try:
                                next(ga)
                            except StopIteration:
                                a_alive = False
                                break
                    if m_alive:
                        try:
                            next(gm)
                        except StopIteration:
                            m_alive = False
                cur["PG"] = PG
                attention(i)
                cur["PG"] = PGB
                ga = thA(i + 1) if i + 1 < NT else iter(())
                gm = thM_post(i)
                a_alive = m_alive = True
                credit = 0.0
                while m_alive:
                    credit += A_RATIO
                    while credit >= 1.0 and a_alive:
                        credit -= 1.0
                        try:
                            next(ga)
                        except StopIteration:
                            a_alive = False
                    try:
                        next(gm)
                    except StopIteration:
                        m_alive = False
            for _ in ga:
                pass
            cur["PG"] = PG
    try:
        if not done_:
            _layers()
            assert W.cur == len(sched), (W.cur, len(sched))
    except _Stop:
        pass
    P.op("sp", None, reads=[("y", i, st) for i in range(NT) for st in range(4)] + final_keys)
    P.emit()
    es.close()
    return nc


def _pack(inputs):
    f = lambda k: np.asarray(inputs[k], dtype=np.float32)
    vfm = np.zeros((2, 128, NVF), np.float32)
    vrow = np.zeros((2, NVR), np.float32)
    for l in range(2):
        vfm[l, :, 0:8] = f("norm_g")[l].reshape(8, 128).T
        vfm[l, :, 8:16] = f("mem_norm_g")[l].reshape(8, 128).T
        vfm[l, :, 16:18] = f("cq_norm_g")[l].reshape(2, 128).T
        vfm[l, :, 18] = f("ckv_norm_g")[l]
        vfm[l, :, 19] = 1.0
        vfm[l, 0:64, 19] = f("mla_q_norm_g")[l][0:64]
        vfm[l, :, 20] = 1.0
        vfm[l, 0:64, 20] = f("mla_k_norm_g")[l][0:64]
        cw = f("conv_w")[l]
        for j in range(3):
            vfm[l, :, 21 + j * 4:21 + (j + 1) * 4] = cw[j].reshape(4, 128).T
        vfm[l, :, 33:37] = f("conv_b")[l].reshape(4, 128).T
        vfm[l, :, 37] = f("mem_q_norm_g")[l]
        vfm[l, :, 38] = f("mem_k_norm_g")[l]
        vfm[l, :, 39:71] = f("b_merge")[l].reshape(32, 128).T
        vrow[l, 0:32] = f("mla_q_norm_g")[l][64:96]
        vrow[l, 32:64] = f("mla_k_norm_g")[l][64:96]
        vfm[l, :, 71:75] = f("sg_ln_g")[l].reshape(4, 128).T
        vfm[l, :, 75:79] = f("sg_ln_b")[l].reshape(4, 128).T
        vrow[l, 64:576] = f("b_spatial")[l].reshape(512)
    return vfm, vrow


_NC_CACHE = {}


def _in_maps(inputs, cores, x_override=None):
    vfm, vrow = _pack(inputs)
    f = lambda k: np.ascontiguousarray(np.asarray(inputs[k], dtype=np.float32))
    shared = dict(vfm=vfm, vrow=vrow, w_in=f("w_in"), w_uq=f("w_uq"), w_ukv=f("w_ukv"),
                  w_spatial=f("w_spatial"), w_mem_kv=f("w_mem_kv"), w_branch=f("w_branch"), w_out=f("w_out"))
    x = f("x") if x_override is None else x_override
    mem = f("mem")
    pos = np.ascontiguousarray(np.asarray(inputs["positions"], dtype=np.int32))
    maps = []
    for b in cores:
        m = dict(shared)
        m["x"] = np.ascontiguousarray(x[b])
        m["mem"] = np.ascontiguousarray(mem[b])
        m["pos"] = np.ascontiguousarray(pos[b].reshape(16, 128))
        maps.append(m)
    return maps


def kernel(**inputs):
    if "nc" not in _NC_CACHE:
        _NC_CACHE["nc"] = build(2, 0)
    nc = _NC_CACHE["nc"]
    maps = _in_maps(inputs, list(range(8)))
    res = run_bass_kernel_spmd(nc, maps, core_ids=list(range(8)))
    out = np.stack([np.asarray(r["y"], dtype=np.float32) for r in res.results], axis=0)
    return out
```

```python
import contextlib
import math
import numpy as np
import concourse.bass as bass
import concourse.mybir as mybir
from concourse.bass_utils import run_bass_kernel_spmd

F32 = mybir.dt.float32
BF16 = mybir.dt.bfloat16
I32 = mybir.dt.int32
AF = mybir.ActivationFunctionType
ALU = mybir.AluOpType
AX = mybir.AxisListType

S = 2048
D = 1024
T = 512
NT = S // T
EPS = 1e-6
IN_W = 9632
NVF = 79
NVR = 576
NSLOT = 4


class Prog:
    CH = 8000

    def __init__(self, nc):
        self.nc = nc
        self.ops = []
        self.last_w = {}
        self.readers = {}
        self.dma_counts = {}

    def op(self, eng, fn, reads=(), writes=(), dsem=None):
        idx = len(self.ops)
        deps = set()
        for k in reads:
            if k in self.last_w:
                deps.add((self.last_w[k], "raw"))
            if isinstance(k, tuple) and k[0] == "ps":
                for r in self.readers.get(k, ()):
                    deps.add((r, "war"))
        for k in writes:
            if k in self.last_w:
                deps.add((self.last_w[k], "waw"))
            for r in self.readers.get(k, ()):
                deps.add((r, "war"))
        o = dict(idx=idx, eng=eng, fn=fn, deps=deps, dsem=dsem)
        if dsem is not None:
            self.dma_counts[dsem] = self.dma_counts.get(dsem, 0) + 1
            o["dcount"] = self.dma_counts[dsem]
        self.ops.append(o)
        for k in reads:
            self.readers.setdefault(k, []).append(idx)
        for k in writes:
            self.last_w[k] = idx
            self.readers[k] = []
        return idx

    def emit(self):
        nc = self.nc
        ops = self.ops
        need = set()
        for o in ops:
            w = []
            for (d, typ) in o["deps"]:
                p = ops[d]
                if p["dsem"] is not None:
                    w.append(("dma", p["dsem"], p["dcount"]))
                    continue
                if p["eng"] == o["eng"] and o["dsem"] is None:
                    if o["eng"] == "pe" or typ == "war":
                        continue
                need.add(d)
                w.append(("eng", p["eng"], d))
            o["waits"] = w
        ordc = {}
        for o in ops:
            if o["idx"] in need:
                e = o["eng"]
                ordc[e] = ordc.get(e, 0) + 1
                o["ord"] = ordc[e]
        with contextlib.ExitStack() as es:
            esem = {}
            for e, n in ordc.items():
                esem[e] = [es.enter_context(nc.semaphore("s_%s_%d" % (e, c)))
                           for c in range((n + self.CH - 1) // self.CH + 1)]
            dsem = {}
            for j, k in enumerate(self.dma_counts):
                dsem[k] = es.enter_context(nc.semaphore("d_%d" % j))
            block = es.enter_context(nc.Block())
            per = {}
            for o in ops:
                per.setdefault(o["eng"], []).append(o)

            def run(ename, e):
                waited_e = {}
                waited_d = {}
                for o in per.get(ename, []):
                    we = {}
                    wd = {}
                    for w in o["waits"]:
                        if w[0] == "eng":
                            oo = ops[w[2]]["ord"]
                            we[w[1]] = max(we.get(w[1], 0), oo)
                        else:
                            wd[w[1]] = max(wd.get(w[1], 0), w[2])
                    for pe_, oo in we.items():
                        if waited_e.get(pe_, 0) >= oo:
                            continue
                        waited_e[pe_] = oo
                        e.wait_ge(esem[pe_][(oo - 1) // self.CH], (oo - 1) % self.CH + 1)
                    for k, cnt in wd.items():
                        if waited_d.get(k, 0) >= cnt:
                            continue
                        waited_d[k] = cnt
                        e.wait_ge(dsem[k], 16 * cnt)
                    if o["fn"] is None:
                        continue
                    ins = o["fn"](e)
                    if o["dsem"] is not None:
                        ins.then_inc(dsem[o["dsem"]], 16)
                    elif "ord" in o:
                        oo = o["ord"]
                        ins.then_inc(esem[ename][(oo - 1) // self.CH], 1)

            @block.tensor
            def _(e):
                run("pe", e)

            @block.scalar
            def _(e):
                run("act", e)

            @block.vector
            def _(e):
                run("dve", e)

            @block.gpsimd
            def _(e):
                run("pool", e)

            @block.sync
            def _(e):
                run("sp", e)


class _Stop(Exception):
    pass


def build(n_layers=2, first_layer=0, stop=None):
    nc = bass.Bass("TRN2", target_bir_lowering=False)
    L = n_layers

    def din(name, shape, dt=F32):
        return nc.dram_tensor(name, shape, dt, kind="ExternalInput").ap()

    x_d = din("x", [S, D])
    mem_d = din("mem", [256, D])
    pos_d = din("pos", [16, 128], I32)
    vfm_d = din("vfm", [2, 128, NVF])
    vrow_d = din("vrow", [2, NVR])
    w_in_d = din("w_in", [2, D, IN_W])
    w_uq_d = din("w_uq", [2, 256, 768])
    w_ukv_d = din("w_ukv", [2, 128, 1024])
    w_sp_d = din("w_spatial", [2, 4, 128, 128])
    w_mkv_d = din("w_mem_kv", [2, D, 1024])
    w_br_d = din("w_branch", [2, 4, 512, D])
    w_out_d = din("w_out", [2, D, D])
    y_d = nc.dram_tensor("y", [S, D], F32, kind="ExternalOutput").ap()
    x1_d = nc.dram_tensor("x1s", [S, D], F32, kind="Internal").ap()

    P = Prog(nc)
    es = contextlib.ExitStack()

    def sb(name, shape, dt):
        return es.enter_context(nc.sbuf_tensor(name, shape, dt))

    xwb = [sb("xw%d" % j, [128, D], F32) for j in range(2)]
    KT = sb("KT", [128, 8, S], BF16)
    Vaug = sb("Vaug", [128, 16, 8, 65], BF16)
    hTb = [sb("hT%d" % j, [128, 8, T], BF16) for j in range(2)]
    QT = sb("QT", [128, 8, T], BF16)
    ys = sb("ys", [128, 4, T], BF16)
    acc = sb("acc", [128, 8, T], BF16)
    wslot = [sb("wslot%d" % j, [128, 4096], BF16) for j in range(NSLOT)]
    vfm = sb("vfm_sb", [128, NVF], F32)
    vrow = sb("vrow_sb", [128, NVR], F32)
    hbm = sb("hbm", [128, 32], F32)
    wuq = sb("wuq", [128, 2, 768], BF16)
    wukv = sb("wukv", [128, 1024], BF16)
    wTs = sb("wTs", [128, 4, 128], BF16)
    Cg = sb("Cg", [128, 4, 128], F32)
    nb4 = sb("nb4", [128, 4], F32)
    ones_bf = sb("ones_bf", [128, 128], BF16)
    ident = sb("ident", [128, 128], BF16)
    identf = sb("identf", [128, 128], F32)
    tri = sb("tri", [128, 128], BF16)
    mhalf = sb("mhalf", [128, 32], F32)
    invf = sb("invf", [128, 16], F32)
    post = sb("post", [128, 16], F32)
    cosT = sb("cosT", [128, 16, 16], F32)
    sinT = sb("sinT", [128, 16, 16], F32)
    KmT = sb("KmT", [128, 4, 256], BF16)
    Vm = sb("Vm", [128, 2, 4, 128], BF16)
    QmT = sb("QmT", [128, 4, T], BF16)
    cqT = sb("cqT", [128, 3, T], BF16)
    ya = sb("ya", [128, 4, 512], BF16)
    yall = sb("yall", [128, 4, 512], BF16)
    zc = sb("zc", [128, 4, 516], BF16)
    junk = sb("junk", [128, 1024], BF16)
    kr = sb("kr", [128, 4, 32], F32)
    ssx = sb("ssx", [128, 4], F32)
    rx = sb("rx", [128, 4], F32)
    ss3 = sb("ss3", [128, 12], F32)
    r2 = sb("r2", [128, 2], F32)
    r2w = sb("r2w", [128, 8], F32)
    ssq = sb("ssq", [128, 8], F32)
    rq = sb("rq", [128, 8], F32)
    ssk = sb("ssk", [128, 8], F32)
    rk = sb("rk", [128, 8], F32)
    rtmp = sb("rtmp", [128, 8], F32)
    rtmpA = sb("rtmpA", [128, 16], F32)
    ssqk = sb("ssqk", [128, 16], F32)
    rqk = sb("rqk", [128, 16], F32)
    rtmpB = sb("rtmpB", [128, 8], F32)
    wAs = sb("wAs", [128, 8 * 416], BF16)
    bA = [sb("bA_%d" % j, [128, 512], BF16) for j in range(2)]
    st6 = sb("st6", [128, 6], F32)
    st6s = sb("st6s", [128, 4, 6], F32)
    mvs = sb("mvs", [128, 4, 2], F32)
    rs4 = sb("rs4", [128, 4], F32)
    ss16 = sb("ss16", [128, 16], F32)
    r16 = sb("r16", [128, 16], F32)
    rtmp16 = sb("rtmp16", [128, 16], F32)
    mv = sb("mv", [128, 2], F32)
    rs1 = sb("rs1", [128, 1], F32)
    ss4 = sb("ss4", [128, 4], F32)
    r4 = sb("r4", [128, 4], F32)
    rec = sb("rec", [128, 4], F32)
    ssm = sb("ssm", [128, 2], F32)
    rm = sb("rm", [128, 2], F32)
    NF1 = 2
    f1024 = [sb("f1024_%d" % j, [128, 1024], F32) for j in range(NF1)]
    NF5 = 4
    f512 = [sb("f512_%d" % j, [128, 512], F32) for j in range(NF5)]
    NB5 = 6
    b512 = [sb("b512_%d" % j, [128, 512], BF16) for j in range(NB5)]
    NB1 = 3
    b1024 = [sb("b1024_%d" % j, [128, 1024], BF16) for j in range(NB1)]
    NR = 0
    f256 = [sb("f256_%d" % j, [128, 256], F32) for j in range(NR)]
    f128 = []
    fqk = [sb("fqk_%d" % j, [128, 512], F32) for j in range(2)]
    ps = [es.enter_context(nc.psum_tensor("ps%d" % j, [128, 512], F32)) for j in range(8)]

    class Rot:
        def __init__(self, items, key):
            self.items = items
            self.key = key
            self.i = 0

        def next(self):
            j = self.i % len(self.items)
            self.i += 1
            return self.items[j], (self.key, j)

    wspf = f512[0][:].rearrange("p (g s) -> p g s", g=4)
    posi = f512[1][0:16, 0:128].bitcast(I32)
    posf = f512[2][0:16, 0:128]
    ones_f = f512[3][:, 0:128]
    angt = f512[0][:, 0:256]
    kkt = f512[1][:, 0:256]
    kit = f512[2][:, 0:256].bitcast(I32)
    redt = f512[3][:, 0:256]
    memnT = yall[:].rearrange("p c t -> p (c t)").rearrange("p (k m) -> p k m", k=8)
    vln = ya
    RbA = Rot(bA, "bA")
    wAv = wAs[:].rearrange("p (k n) -> p k n", k=8)
    Rxw = Rot(xwb, "xw")
    Rqk = Rot(fqk, "fqk")
    Rf1 = Rot(f1024, "f1024")
    ssqk2 = sb("ssqk2", [128, 16], F32)
    rqk2 = sb("rqk2", [128, 16], F32)
    rtmpA2 = sb("rtmpA2", [128, 16], F32)

    class RotK:
        def __init__(self, items):
            self.items = items
            self.i = 0

        def next(self):
            j = self.i % len(self.items)
            self.i += 1
            return self.items[j]

    Rf5 = Rot(f512, "f512")
    Rb5 = Rot(b512, "b512")
    Rb1 = Rot(b1024, "b1024")
    Rf2 = Rot(f256, "f256")
    Rf128 = Rot(f128, "f128")

    class PsRot:
        def __init__(self, banks):
            self.banks = banks
            self.i = 0

        def next(self):
            b = self.banks[self.i % len(self.banks)]
            self.i += 1
            return ps[b], ("ps", b)

    PG = PsRot([0, 1, 2, 3])
    PGA = PsRot([0, 1])
    PW = PsRot([4, 5, 6, 7])
    PGM = PsRot([2, 3, 4])
    PGB = PsRot([2, 3, 4, 5])
    cur = {"PG": PG}
    A_RATIO = 1.15
    PRE_A_STEPS = 1
    PS_ = PsRot([2, 3, 4, 5])
    ATT_SKEW = 4
    PO = PsRot([6, 7])

    def bfv(p):
        return p[:].bitcast(BF16)

    RES_A = dict(pg=PGA, f1=Rf1, b1=Rb1, qk=Rqk, ssqk=ssqk, rqk=rqk, rtmp=rtmpA, n="")
    RES_B = dict(pg=PsRot([2, 3]),
                 f1=RotK([(xwb[0], ("xw", 0)), (xwb[1], ("xw", 1))]),
                 b1=RotK([(b1024[2], ("b1024", 2)), (junk, "junk")]),
                 qk=RotK([(f512[0], ("f512", 0)), (f512[1], ("f512", 1))]),
                 ssqk=ssqk2, rqk=rqk2, rtmp=rtmpA2, n="B")
    RES_A0 = dict(RES_A, b1=RotK([(b1024[0], ("b1024", 0)), (b1024[1], ("b1024", 1))]))

    sched = []
    for l in range(L):
        ll = l + first_layer
        sched.append(("memK", w_mkv_d[ll][:, 0:512], 8, 512))
        sched.append(("memV", w_mkv_d[ll][:, 512:1024], 8, 512))
        for i in range(NT):
            def wi(a, b_):
                return w_in_d[ll][:, a:b_]
            for n in (1, 0, 2, 3):
                if n == 0:
                    sched.append(("gate0", wi(3488, 4000), 8, 512))
                if n == 1:
                    sched.append(("conv_c", wi(928, 1440), 8, 512))
                    sched.append(("conv_x", wi(1440, 1952), 8, 512))
                    sched.append(("conv_b", wi(416, 928), 8, 512))
                    sched.append(("gate1", wi(4000, 4512), 8, 512))
                if n == 2:
                    sched.append(("sg_v", wi(2464, 2976), 8, 512))
                    sched.append(("gate2", wi(4512, 5024), 8, 512))
                    sched.append(("sg_u", wi(1952, 2464), 8, 512))
                if n == 3:
                    sched.append(("memq", wi(2976, 3488), 8, 512))
                    sched.append(("gate3", wi(5024, 5536), 8, 512))
                for hf in range(2):
                    c0_ = 5536 + n * 1024 + hf * 512
                    sched.append(("m%d_%d" % (n, hf), wi(c0_, c0_ + 512), 8, 512))
                    sched.append(("wb%d_%d" % (n, hf), w_br_d[ll, n][:, hf * 512:(hf + 1) * 512], 4, 512))
            sched.append(("wo0", w_out_d[ll][:, 0:512], 8, 512))
            sched.append(("wo1", w_out_d[ll][:, 512:1024], 8, 512))

    class WStream:
        def __init__(self):
            self.issued = 0
            self.cur = 0

        def _issue(self):
            c = self.issued
            if c >= len(sched):
                return
            name, src, nk, ncol = sched[c]
            s = c % NSLOT
            view = wslot[s][:, 0:nk * ncol].rearrange("p (k n) -> p k n", k=nk)
            srcv = src.rearrange("(k p) n -> p k n", p=128)
            P.op("pool", lambda e, view=view, srcv=srcv: e.dma_start(out=view, in_=srcv),
                 writes=[("w", s)], dsem=("w", s))
            self.issued += 1

        def next(self, name):
            while self.issued < min(len(sched), self.cur + NSLOT - 1):
                self._issue()
            n2, src, nk, ncol = sched[self.cur]
            assert n2 == name, (n2, name)
            s = self.cur % NSLOT
            self.cur += 1
            view = wslot[s][:, 0:nk * ncol].rearrange("p (k n) -> p k n", k=nk)
            return view, ("w", s)

    W = WStream()

    def load_layer_consts(ll):
        P.op("sp", lambda e, ll=ll: e.dma_start(out=vfm[:], in_=vfm_d[ll]), writes=["vfm"], dsem="vfm")
        P.op("sp", lambda e, ll=ll: e.dma_start(out=vrow[:], in_=vrow_d[ll, :].partition_broadcast(128)),
             writes=["vrow"], dsem="vrow")
        P.op("pool", lambda e, ll=ll: e.dma_start(out=wuq[:], in_=w_uq_d[ll].rearrange("(k p) n -> p k n", p=128)),
             writes=["wuq"], dsem="wuq")
        P.op("pool", lambda e, ll=ll: e.dma_start(out=wukv[:], in_=w_ukv_d[ll]), writes=["wukv"], dsem="wukv")
        P.op("pool", lambda e, ll=ll: e.dma_start(
            out=wAv, in_=w_in_d[ll][:, 0:416].rearrange("(k p) n -> p k n", p=128)), writes=["wA"], dsem="wA")

    for _ in range(NSLOT - 1):
        W._issue()
    load_layer_consts(first_layer)

    def rsqrt(out_ap, in_ap, scale, n, rkeys, wkeys, tmp=None, tmpk="rtmp"):
        tmp = rtmp if tmp is None else tmp
        P.op("pool", lambda e: e.tensor_scalar(out=tmp[:, 0:n], in0=in_ap, scalar1=scale, scalar2=EPS,
                                               op0=ALU.mult, op1=ALU.add), reads=rkeys, writes=[tmpk])
        P.op("pool", lambda e: e.tensor_tensor(out=out_ap, in0=tmp[:, 0:n], in1=mhalf[:, 0:n], op=ALU.pow),
             reads=[tmpk, "mhalf"], writes=wkeys)

    def rsqrtA(out_ap, in_ap, scale, n, rkeys, wkeys):
        rsqrt(out_ap, in_ap, scale, n, rkeys, wkeys, tmp=rtmpA, tmpk="rtmpA")

    def mm_group(out_ap, pairs, reads, pskey):
        def fn(e):
            n = len(pairs)
            for j, (a, b_) in enumerate(pairs):
                ins = e.matmul(out_ap, a, b_, start=(j == 0), stop=(j == n - 1))
            return ins
        P.op("pe", fn, reads=reads, writes=[pskey])

    def transposes(pairs, reads, pskey, idt):
        def fn(e):
            for (o_, i_) in pairs:
                ins = e.transpose(out=o_, in_=i_, identity=idt)
            return ins
        P.op("pe", fn, reads=reads + ["ident"], writes=[pskey])

    final_keys = []
    dbg_n = [0]
    dbg_off = [0]

    def checkpoint(name, dumps):
        if stop != name:
            return
        yv = y_d.rearrange("(p a) d -> p (a d)", p=128)
        for (ap, n, keys) in dumps:
            for c0 in range(0, n, 1024):
                w = min(1024, n - c0)
                stg, stgk = Rf1.next()
                off = dbg_off[0]
                idx = dbg_n[0]
                P.op("dve", lambda e, stg=stg, ap=ap, c0=c0, w=w: e.tensor_copy(out=stg[:, 0:w], in_=ap[:, c0:c0 + w]),
                     reads=keys, writes=[stgk])
                P.op("sp", lambda e, stg=stg, off=off, w=w: e.dma_start(out=yv[:, off:off + w], in_=stg[:, 0:w]),
                     reads=[stgk], writes=[("ydbg", idx)], dsem=("dbg", idx))
                final_keys.append(("ydbg", idx))
                dbg_off[0] += w
                dbg_n[0] += 1
        raise _Stop()

    P.op("pool", lambda e: e.memset(ones_bf[:], 1.0), writes=["ones_bf"])
    P.op("pool", lambda e: e.memset(ones_f[:], 1.0), writes=[("f512", 3)])
    P.op("pool", lambda e: e.memset(mhalf[:], -0.5), writes=["mhalf"])
    P.op("pool", lambda e: e.affine_select(out=ident[:], in_=ones_bf[:], pattern=[[1, 128]],
                                           compare_op=ALU.is_equal, fill=0.0, base=0, channel_multiplier=-1),
         reads=["ones_bf"], writes=["ident"])
    P.op("pool", lambda e: e.affine_select(out=identf[:], in_=ones_f[:], pattern=[[1, 128]],
                                           compare_op=ALU.is_equal, fill=0.0, base=0, channel_multiplier=-1),
         reads=[("f512", 3)], writes=["identf"])
    P.op("pool", lambda e: e.affine_select(out=tri[:], in_=ones_bf[:], pattern=[[1, 128]],
                                           compare_op=ALU.is_ge, fill=0.0, base=0, channel_multiplier=-1),
         reads=["ones_bf"], writes=["tri"])
    inv = np.power(np.float32(10000.0), -np.arange(16, dtype=np.float32) / np.float32(16)).astype(np.float32)
    for j in range(16):
        P.op("pool", lambda e, j=j: e.memset(invf[:, j:j + 1], float(inv[j])), writes=["invf"])
    P.op("pool", lambda e: e.memset(Vaug[:].rearrange("p a b c -> p (a b c)"), 1.0), writes=[("V", b_) for b_ in range(16)])
    P.op("sp", lambda e: e.dma_start(out=posi[:], in_=pos_d), writes=[("f512", 1)], dsem=("f512", 1))
    P.op("dve", lambda e: e.tensor_copy(out=posf[:], in_=posi[:]), reads=[("f512", 1)], writes=[("f512", 2)])
    P.op("pe", lambda e: e.transpose(out=ps[0][:, 0:16], in_=posf[:], identity=identf[0:16, 0:16]),
         reads=[("f512", 2), "identf"], writes=[("ps", 0)])
    P.op("dve", lambda e: e.tensor_copy(out=post[:], in_=ps[0][:, 0:16]), reads=[("ps", 0)], writes=["post"])
    for s_ in range(16):
        P.op("dve", lambda e, s_=s_: e.tensor_scalar(out=angt[:, s_ * 16:(s_ + 1) * 16], in0=invf[:],
                                                      scalar1=post[:, s_:s_ + 1], scalar2=None, op0=ALU.mult),
             reads=["post", "invf"], writes=[("f512", 0)])
    C1 = 6.28125
    C2 = 2 * math.pi - 6.28125
    P.op("dve", lambda e: e.tensor_scalar(out=kkt[:], in0=angt[:], scalar1=1.0 / (2 * math.pi), scalar2=None,
                                          op0=ALU.mult), reads=[("f512", 0)], writes=[("f512", 1)])
    P.op("dve", lambda e: e.tensor_copy(out=kit[:], in_=kkt[:]), reads=[("f512", 1)], writes=[("f512", 2)])
    P.op("dve", lambda e: e.tensor_copy(out=kkt[:], in_=kit[:]), reads=[("f512", 2)], writes=[("f512", 1)])
    P.op("dve", lambda e: e.scalar_tensor_tensor(out=redt[:], in0=kkt[:], scalar=-C1, in1=angt[:],
                                                 op0=ALU.mult, op1=ALU.add), reads=[("f512", 1), ("f512", 0)], writes=[("f512", 3)])
    P.op("dve", lambda e: e.scalar_tensor_tensor(out=redt[:], in0=kkt[:], scalar=-C2, in1=redt[:],
                                                 op0=ALU.mult, op1=ALU.add), reads=[("f512", 1), ("f512", 3)], writes=[("f512", 3)])

    def wrap():
        P.op("dve", lambda e: e.tensor_scalar(out=kkt[:], in0=redt[:], scalar1=math.pi, scalar2=-2 * math.pi,
                                              op0=ALU.is_gt, op1=ALU.mult), reads=[("f512", 3)], writes=[("f512", 1)])
        P.op("dve", lambda e: e.tensor_tensor(out=redt[:], in0=redt[:], in1=kkt[:], op=ALU.add),
             reads=[("f512", 3), ("f512", 1)], writes=[("f512", 3)])
        P.op("dve", lambda e: e.tensor_scalar(out=kkt[:], in0=redt[:], scalar1=-math.pi, scalar2=2 * math.pi,
                                              op0=ALU.is_lt, op1=ALU.mult), reads=[("f512", 3)], writes=[("f512", 1)])
        P.op("dve", lambda e: e.tensor_tensor(out=redt[:], in0=redt[:], in1=kkt[:], op=ALU.add),
             reads=[("f512", 3), ("f512", 1)], writes=[("f512", 3)])
        P.op("dve", lambda e: e.tensor_scalar(out=redt[:], in0=redt[:], scalar1=math.pi, scalar2=-math.pi,
                                              op0=ALU.min, op1=ALU.max), reads=[("f512", 3)], writes=[("f512", 3)])

    wrap()
    P.op("act", lambda e: e.activation(out=sinT[:].rearrange("p a b -> p (a b)"), in_=redt[:], func=AF.Sin),
         reads=[("f512", 3)], writes=["sinT"])
    P.op("dve", lambda e: e.tensor_scalar(out=redt[:], in0=redt[:], scalar1=math.pi / 2, scalar2=None,
                                          op0=ALU.add), reads=[("f512", 3), "sinT"], writes=[("f512", 3)])
    wrap()
    P.op("act", lambda e: e.activation(out=cosT[:].rearrange("p a b -> p (a b)"), in_=redt[:], func=AF.Sin),
         reads=[("f512", 3)], writes=["cosT"])

    done_ = False
    try:
        checkpoint("setup", [(cosT[:].rearrange("p a b -> p (a b)"), 256, ["cosT"]),
                             (sinT[:].rearrange("p a b -> p (a b)"), 256, ["sinT"]),
                             (post[:], 16, ["post"]), (tri[:], 128, ["tri"]), (ident[:], 128, ["ident"])])
    except _Stop:
        done_ = True
    def rope(src, dst, sg, skey, dkey):
        cosb = cosT[:, sg, :].unsqueeze(1).broadcast_to([128, 8, 16])
        sinb = sinT[:, sg, :].unsqueeze(1).broadcast_to([128, 8, 16])
        t1, k1 = Rf128.next()
        t2, k2 = Rf128.next()
        t1v = t1[:].rearrange("p (h d) -> p h d", h=8)
        t2v = t2[:].rearrange("p (h d) -> p h d", h=8)
        P.op("dve", lambda e: e.tensor_tensor(out=t1v, in0=src[:, :, 0:16], in1=cosb, op=ALU.mult),
             reads=[skey, "cosT"], writes=[k1])
        P.op("dve", lambda e: e.tensor_tensor(out=t2v, in0=src[:, :, 16:32], in1=sinb, op=ALU.mult),
             reads=[skey, "sinT"], writes=[k2])
        P.op("dve", lambda e: e.tensor_tensor(out=dst[:, :, 0:16], in0=t1v, in1=t2v, op=ALU.subtract),
             reads=[k1, k2], writes=[dkey])
        t3, k3 = Rf128.next()
        t4, k4 = Rf128.next()
        t3v = t3[:].rearrange("p (h d) -> p h d", h=8)
        t4v = t4[:].rearrange("p (h d) -> p h d", h=8)
        P.op("dve", lambda e: e.tensor_tensor(out=t3v, in0=src[:, :, 0:16], in1=sinb, op=ALU.mult),
             reads=[skey, "sinT"], writes=[k3])
        P.op("dve", lambda e: e.tensor_tensor(out=t4v, in0=src[:, :, 16:32], in1=cosb, op=ALU.mult),
             reads=[skey, "cosT"], writes=[k4])
        P.op("dve", lambda e: e.tensor_tensor(out=dst[:, :, 16:32], in0=t3v, in1=t4v, op=ALU.add),
             reads=[k3, k4], writes=[dkey])

    YALLK = [("yall", c) for c in range(4)]

    def gate_chunk(wg, wgk, c, ysrc_ap, ysrc_keys):
        pg, pgk = cur["PG"].next()
        mm_group(pg[:], [(wg[:, k, c * 128:(c + 1) * 128], cur["hT"][:, k, :]) for k in range(8)],
                 [wgk] + cur["hTk"], pgk)
        tg, tgk = Rb5.next()
        P.op("act", lambda e: e.activation(out=tg[:], in_=pg[:], func=AF.Tanh, scale=0.5),
             reads=[pgk], writes=[tgk])
        u, uk = Rb5.next()
        P.op("dve", lambda e: e.scalar_tensor_tensor(out=u[:], in0=tg[:], scalar=1.0, in1=pg[:],
                                                     op0=ALU.add, op1=ALU.mult), reads=[tgk, pgk], writes=[uk])
        P.op("dve", lambda e: e.tensor_tensor(out=ys[:, c, :], in0=u[:], in1=ysrc_ap, op=ALU.mult),
             reads=[uk] + ysrc_keys, writes=[("ys", c)])

    def gate_pre(wg, wgk, c):
        pg, pgk = cur["PG"].next()
        mm_group(pg[:], [(wg[:, k, c * 128:(c + 1) * 128], cur["hT"][:, k, :]) for k in range(8)],
                 [wgk] + cur["hTk"], pgk)
        tg, tgk = Rb5.next()
        P.op("act", lambda e: e.activation(out=tg[:], in_=pg[:], func=AF.Tanh, scale=0.5),
             reads=[pgk], writes=[tgk])
        P.op("dve", lambda e: e.scalar_tensor_tensor(out=ys[:, c, :], in0=tg[:], scalar=1.0, in1=pg[:],
                                                     op0=ALU.add, op1=ALU.mult), reads=[tgk, pgk],
             writes=[("ys", c)])

    def gate_post(c, ysrc_ap, ysrc_keys):
        P.op("dve", lambda e: e.tensor_tensor(out=ys[:, c, :], in0=ys[:, c, :], in1=ysrc_ap, op=ALU.mult),
             reads=[("ys", c)] + ysrc_keys, writes=[("ys", c)])

    def tm_post(ytm, ykey):
        for c in range(4):
            pt, ptk = cur["PG"].next()
            ptb = bfv(pt)
            transposes([(ptb[:, qs * 128:(qs + 1) * 128], ytm[:, qs, c * 128:(c + 1) * 128]) for qs in range(4)],
                       [ykey], ptk, ident[:])
            gate_post(c, ptb[:, 0:512], [ptk])
            yield

    def merge_branch(n, first):
        for hf in range(2):
            wm, wmk = W.next("m%d_%d" % (n, hf))
            wb, wbk = W.next("wb%d_%d" % (n, hf))

            def logits(c4):
                dc = hf * 4 + c4
                pl, plk = cur["PG"].next()
                mm_group(pl[:], [(wm[:, k, c4 * 128:(c4 + 1) * 128], cur["hT"][:, k, :]) for k in range(8)],
                         [wmk] + cur["hTk"], plk)
                tm, tmk = Rb5.next()
                P.op("act", lambda e, pl=pl, tm=tm, dc=dc: e.activation(out=tm[:], in_=pl[:], func=AF.Tanh,
                                                                       bias=hbm[:, n * 8 + dc:n * 8 + dc + 1],
                                                                       scale=0.5),
                     reads=[plk, "hbm"], writes=[tmk])
                return tm, tmk

            nxt = logits(0)
            yield
            for c4 in range(4):
                dc = hf * 4 + c4
                tm, tmk = nxt
                if c4 + 1 < 4:
                    nxt = logits(c4 + 1)
                    yield
                pz, pzk = cur["PG"].next()
                mm_group(pz[:], [(wb[:, kc, c4 * 128:(c4 + 1) * 128], ys[:, kc, :]) for kc in range(4)],
                         [wbk] + [("ys", c) for c in range(4)], pzk)
                if first:
                    P.op("dve", lambda e, tm=tm, pz=pz, dc=dc: e.scalar_tensor_tensor(
                        out=acc[:, dc, :], in0=tm[:], scalar=1.0, in1=pz[:], op0=ALU.add, op1=ALU.mult),
                        reads=[tmk, pzk], writes=[("acc", dc)])
                else:
                    tp, tpk = Rf5.next()
                    P.op("dve", lambda e, tm=tm, pz=pz, tp=tp: e.scalar_tensor_tensor(
                        out=tp[:], in0=tm[:], scalar=1.0, in1=pz[:], op0=ALU.add, op1=ALU.mult),
                        reads=[tmk, pzk], writes=[tpk])
                    P.op("dve", lambda e, tp=tp, dc=dc: e.tensor_tensor(out=acc[:, dc, :], in0=acc[:, dc, :],
                                                                       in1=tp[:], op=ALU.add),
                         reads=[tpk, ("acc", dc)], writes=[("acc", dc)])
                yield

    def tm_to_ys(ytm, ykey, wg, wgk):
        for c in range(4):
            pt, ptk = cur["PG"].next()
            ptb = bfv(pt)
            transposes([(ptb[:, qs * 128:(qs + 1) * 128], ytm[:, qs, c * 128:(c + 1) * 128]) for qs in range(4)],
                       [ykey], ptk, ident[:])
            gate_chunk(wg, wgk, c, ptb[:, 0:512], [ptk])
            yield

    def _layers():
        for l in range(L):
            ll = l + first_layer
            last = (l == L - 1)
            if l > 0:
                load_layer_consts(ll)
            P.op("dve", lambda e: e.tensor_scalar(out=hbm[:], in0=vfm[:, 39:71], scalar1=0.5, scalar2=None,
                                                  op0=ALU.mult), reads=["vfm"], writes=["hbm"])
            P.op("sp", lambda e, ll=ll: e.dma_start(out=wspf[:], in_=w_sp_d[ll].rearrange("g t s -> t g s")),
                 writes=[("f512", 0)], dsem=("f512", 0))
            for g in range(4):
                pt, ptk = PG.next()
                P.op("pe", lambda e, pt=pt, g=g: e.transpose(out=pt[:, 0:128], in_=wspf[:, g, :], identity=identf[:]),
                     reads=[("f512", 0), "identf"], writes=[ptk])
                tf, tfk = f512[1 + g % 3][:, 0:128], ("f512", 1 + g % 3)
                P.op("act", lambda e, pt=pt, tf=tf: e.activation(out=tf[:], in_=pt[:, 0:128], func=AF.Copy),
                     reads=[ptk], writes=[tfk])
                P.op("pool", lambda e, tf=tf, g=g: e.affine_select(out=wTs[:, g, :], in_=tf[:], pattern=[[1, 128]],
                                                                  compare_op=ALU.is_ge, fill=0.0, base=0,
                                                                  channel_multiplier=-1),
                     reads=[tfk], writes=["wTs"])
                pr, prk = PG.next()
                P.op("pe", lambda e, pr=pr, g=g: e.matmul(pr[:, 0:128], ones_bf[:, 0:128], wTs[:, g, :], start=True, stop=True),
                     reads=["ones_bf", "wTs"], writes=[prk])
                P.op("dve", lambda e, pr=pr, g=g: e.scalar_tensor_tensor(
                    out=Cg[:, g, :], in0=pr[:, 0:128], scalar=vfm[:, 75 + g:76 + g], in1=vrow[:, 64 + g * 128:64 + (g + 1) * 128],
                    op0=ALU.mult, op1=ALU.add), reads=[prk, "vfm", "vrow"], writes=["Cg"])
            P.op("dve", lambda e: e.memset(zc[:, :, 0:2], 0.0), writes=[("z", c) for c in range(4)])
            memt = [f1024[0], f1024[1]]
            mkeys = [("f1024", 0), ("f1024", 1)]
            Rf1.i = 2
            for mb in range(2):
                P.op("sp", lambda e, mb=mb: e.dma_start(out=memt[mb][:], in_=mem_d[mb * 128:(mb + 1) * 128, :]),
                     writes=[mkeys[mb]], dsem=("memt", mb))
            for mb in range(2):
                P.op("act", lambda e, mb=mb: e.activation(out=junk[:], in_=memt[mb][:], func=AF.Square,
                                                          accum_out=ssm[:, mb:mb + 1]),
                     reads=[mkeys[mb]], writes=[("ssm", mb), "junk"])
            rsqrt(rm[:], ssm[:], 1.0 / D, 2, [("ssm", 0), ("ssm", 1)], ["rm"])
            for mb in range(2):
                mnb, mnbk = Rb1.next()
                P.op("dve", lambda e, mb=mb, mnb=mnb: e.tensor_scalar(out=mnb[:], in0=memt[mb][:], scalar1=rm[:, mb:mb + 1],
                                                                      scalar2=None, op0=ALU.mult),
                     reads=[mkeys[mb], "rm"], writes=[mnbk])
                pt, ptk = PG.next()
                ptb = bfv(pt)
                transposes([(ptb[:, k * 128:(k + 1) * 128], mnb[:, k * 128:(k + 1) * 128]) for k in range(8)],
                           [mnbk], ptk, ident[:])
                P.op("dve", lambda e, ptb=ptb, mb=mb: e.tensor_tensor(
                    out=memnT[:, :, mb * 128:(mb + 1) * 128], in0=ptb.rearrange("p (k t) -> p k t", k=8),
                    in1=vfm[:, 8:16].unsqueeze(2).broadcast_to([128, 8, 128]), op=ALU.mult),
                    reads=[ptk, "vfm"], writes=[*YALLK])
            wk_, wkk = W.next("memK")
            for mb in range(2):
                pk, pkk = PG.next()
                mm_group(pk[:], [(memnT[:, k, mb * 128:(mb + 1) * 128], wk_[:, k, :]) for k in range(8)],
                         [wkk, *YALLK], pkk)
                kf, kfk = Rf5.next()
                P.op("act", lambda e, pk=pk, kf=kf: e.activation(out=kf[:], in_=pk[:], func=AF.Copy),
                     reads=[pkk], writes=[kfk])
                sq, sqk = Rf5.next()
                P.op("act", lambda e, kf=kf, sq=sq: e.activation(out=sq[:], in_=kf[:], func=AF.Square),
                     reads=[kfk], writes=[sqk])
                P.op("dve", lambda e, sq=sq: e.tensor_reduce(out=ss4[:], in_=sq[:].rearrange("p (h d) -> p h d", h=4),
                                                            axis=AX.X, op=ALU.add), reads=[sqk], writes=["ss4"])
                rsqrt(r4[:], ss4[:], 1.0 / 128, 4, ["ss4"], ["r4"])
                knb, knbk = Rb5.next()
                P.op("dve", lambda e, kf=kf, knb=knb: e.tensor_tensor(
                    out=knb[:].rearrange("p (h d) -> p h d", h=4), in0=kf[:].rearrange("p (h d) -> p h d", h=4),
                    in1=r4[:].unsqueeze(2).broadcast_to([128, 4, 128]), op=ALU.mult),
                    reads=[kfk, "r4"], writes=[knbk])
                pt, ptk = PG.next()
                ptb = bfv(pt)
                transposes([(ptb[:, h * 128:(h + 1) * 128], knb[:, h * 128:(h + 1) * 128]) for h in range(4)],
                           [knbk], ptk, ident[:])
                P.op("dve", lambda e, ptb=ptb, mb=mb: e.tensor_scalar(
                    out=KmT[:, :, mb * 128:(mb + 1) * 128], in0=ptb[:, 0:512].rearrange("p (h t) -> p h t", h=4),
                    scalar1=vfm[:, 38:39], scalar2=None, op0=ALU.mult),
                    reads=[ptk, "vfm"], writes=[("KmT", mb)])
            wv_, wvk = W.next("memV")
            for mb in range(2):
                pv, pvk = PG.next()
                mm_group(pv[:], [(memnT[:, k, mb * 128:(mb + 1) * 128], wv_[:, k, :]) for k in range(8)],
                         [wvk, *YALLK], pvk)
                P.op("act", lambda e, pv=pv, mb=mb: e.activation(out=Vm[:, mb, :, :].rearrange("p h d -> p (h d)"),
                                                                in_=pv[:], func=AF.Copy),
                     reads=[pvk], writes=[("Vm", mb)])

            if l == 0:
                checkpoint("lsetup", [(wTs[:].rearrange("p a b -> p (a b)"), 512, ["wTs"]),
                                      (KmT[:].rearrange("p a b -> p (a b)"), 1024, [("KmT", 0), ("KmT", 1)]),
                                      (Vm[:].rearrange("p a b c -> p (a b c)"), 1024, [("Vm", 0), ("Vm", 1)]),
                                      (hbm[:], 32, ["hbm"]), (vrow[:, 0:64], 64, ["vrow"])])
            def thA_head(i):
                hT_ = hTb[i % 2]
                t0 = i * T
                for st in range(4):
                    xs, xsk = Rf1.next()
                    P.op('sp', lambda e, xs=xs, st=st, xsrc=xsrc, t0=t0: e.dma_start(out=xs[:], in_=xsrc[t0 + st * 128:t0 + (st + 1) * 128, :]), reads=[(skey, i, st)], writes=[xsk], dsem=xsk)
                    yield
                    P.op('act', lambda e, xs=xs, st=st: e.activation(out=junk[:], in_=xs[:], func=AF.Square, accum_out=ssx[:, st:st + 1]), reads=[xsk], writes=[('ssx', st), 'junk'])
                    yield
                    rsqrtA(rx[:, st:st + 1], ssx[:, st:st + 1], 1.0 / D, 1, [('ssx', st)], [('rx', st)])
                    yield
                    hb, hbk = Rb1.next()
                    P.op('dve', lambda e, xs=xs, st=st, hb=hb: e.tensor_scalar(out=hb[:], in0=xs[:], scalar1=rx[:, st:st + 1], scalar2=None, op0=ALU.mult), reads=[xsk, ('rx', st)], writes=[hbk])
                    yield
                    pt, ptk = PGA.next()
                    ptb = bfv(pt)
                    transposes([(ptb[:, k * 128:(k + 1) * 128], hb[:, k * 128:(k + 1) * 128]) for k in range(8)], [hbk], ptk, ident[:])
                    yield
                    P.op('dve', lambda e, ptb=ptb, st=st: e.tensor_tensor(out=hT_[:, :, st * 128:(st + 1) * 128], in0=ptb.rearrange('p (k t) -> p k t', k=8), in1=vfm[:, 0:8].unsqueeze(2).broadcast_to([128, 8, 128]), op=ALU.mult), reads=[ptk, 'vfm'], writes=[('hT', i % 2, st)])
                    yield
                wA, wAk = (wAv, 'wA')
                for st in range(4):
                    pg, pgk = PGA.next()
                    mm_group(pg[:, 0:416], [(hT_[:, k, st * 128:(st + 1) * 128], wA[:, k, 0:416]) for k in range(8)], [wAk, ('hT', i % 2, st)], pgk)
                    yield
                    for j, (a, b_) in enumerate([(0, 256), (256, 384), (384, 416)]):
                        P.op('act', lambda e, pg=pg, a=a, b_=b_, st=st, j=j: e.activation(out=junk[:, a:b_], in_=pg[:, a:b_], func=AF.Square, accum_out=ss3[:, st * 3 + j:st * 3 + j + 1]), reads=[pgk], writes=[('ss3', st, j)])
                        yield
                    P.op('dve', lambda e, pg=pg, st=st: e.tensor_copy(out=kr[:, st, :], in_=pg[:, 384:416]), reads=[pgk], writes=[('kr', st)])
                    yield
                    P.op('pool', lambda e, st=st: e.tensor_scalar(out=rtmpA[:, 0:1], in0=ss3[:, st * 3:st * 3 + 1], scalar1=1.0 / 256, scalar2=EPS, op0=ALU.mult, op1=ALU.add), reads=[('ss3', st, 0)], writes=['rtmpA'])
                    yield
                    P.op('pool', lambda e, st=st: e.tensor_scalar(out=rtmpA[:, 1:2], in0=ss3[:, st * 3 + 1:st * 3 + 2], scalar1=1.0 / 128, scalar2=EPS, op0=ALU.mult, op1=ALU.add), reads=[('ss3', st, 1)], writes=['rtmpA'])
                    yield
                    P.op('pool', lambda e: e.tensor_tensor(out=r2[:], in0=rtmpA[:, 0:2], in1=mhalf[:, 0:2], op=ALU.pow), reads=['rtmpA', 'mhalf'], writes=['r2'])
                    yield
                    cqn, cqnk = RbA.next()
                    P.op('dve', lambda e, pg=pg, cqn=cqn: e.tensor_scalar(out=cqn[:, 0:256], in0=pg[:, 0:256], scalar1=r2[:, 0:1], scalar2=None, op0=ALU.mult), reads=[pgk, 'r2'], writes=[cqnk])
                    yield
                    P.op('dve', lambda e, pg=pg, cqn=cqn: e.tensor_scalar(out=cqn[:, 256:384], in0=pg[:, 256:384], scalar1=r2[:, 1:2], scalar2=None, op0=ALU.mult), reads=[pgk, 'r2'], writes=[cqnk])
                    yield
                    pt, ptk = PGA.next()
                    ptb = bfv(pt)
                    transposes([(ptb[:, k * 128:(k + 1) * 128], cqn[:, k * 128:(k + 1) * 128]) for k in range(3)], [cqnk], ptk, ident[:])
                    yield
                    P.op('dve', lambda e, ptb=ptb, st=st: e.tensor_tensor(out=cqT[:, :, st * 128:(st + 1) * 128], in0=ptb[:, 0:384].rearrange('p (k t) -> p k t', k=3), in1=vfm[:, 16:19].unsqueeze(2).broadcast_to([128, 3, 128]), op=ALU.mult), reads=[ptk, 'vfm'], writes=[('cqT', st)])
                    yield
            def thA_head_wide(i):
                hT_ = hTb[i % 2]
                t0 = i * T
                xbufs = [(f1024[0], ('f1024', 0)), (f1024[1], ('f1024', 1)), (xwb[0], ('xw', 0)), (xwb[1], ('xw', 1))]
                for st in range(4):
                    xs, xsk = xbufs[st]
                    P.op('sp', lambda e, xs=xs, st=st, xsrc=xsrc, t0=t0: e.dma_start(out=xs[:], in_=xsrc[t0 + st * 128:t0 + (st + 1) * 128, :]), reads=[(skey, i, st)], writes=[xsk], dsem=xsk)
                for st in range(4):
                    xs, xsk = xbufs[st]
                    P.op('act', lambda e, xs=xs, st=st: e.activation(out=junk[:], in_=xs[:], func=AF.Square, accum_out=ssx[:, st:st + 1]), reads=[xsk], writes=[('ssx', st), 'junk'])
                rsqrtA(rx[:, 0:4], ssx[:, 0:4], 1.0 / D, 4, [('ssx', st) for st in range(4)], [('rx', st) for st in range(4)])
                yield
                for st in range(4):
                    xs, xsk = xbufs[st]
                    hb, hbk = Rb1.next()
                    P.op('dve', lambda e, xs=xs, st=st, hb=hb: e.tensor_scalar(out=hb[:], in0=xs[:], scalar1=rx[:, st:st + 1], scalar2=None, op0=ALU.mult), reads=[xsk, ('rx', st)], writes=[hbk])
                    pt, ptk = PG.next()
                    ptb = bfv(pt)
                    transposes([(ptb[:, k * 128:(k + 1) * 128], hb[:, k * 128:(k + 1) * 128]) for k in range(8)], [hbk], ptk, ident[:])
                    P.op('dve', lambda e, ptb=ptb, st=st: e.tensor_tensor(out=hT_[:, :, st * 128:(st + 1) * 128], in0=ptb.rearrange('p (k t) -> p k t', k=8), in1=vfm[:, 0:8].unsqueeze(2).broadcast_to([128, 8, 128]), op=ALU.mult), reads=[ptk, 'vfm'], writes=[('hT', i % 2, st)])
                    yield
                wA, wAk = (wAv, 'wA')
                pgs = []
                for st in range(4):
                    pg, pgk = PW.next()
                    mm_group(pg[:, 0:416], [(hT_[:, k, st * 128:(st + 1) * 128], wA[:, k, 0:416]) for k in range(8)], [wAk, ('hT', i % 2, st)], pgk)
                    pgs.append((pg, pgk))
                for st in range(4):
                    pg, pgk = pgs[st]
                    for j, (a, b_) in enumerate([(0, 256), (256, 384), (384, 416)]):
                        P.op('act', lambda e, pg=pg, a=a, b_=b_, st=st, j=j: e.activation(out=junk[:, a:b_], in_=pg[:, a:b_], func=AF.Square, accum_out=ss3[:, st * 3 + j:st * 3 + j + 1]), reads=[pgk], writes=[('ss3', st, j)])
                    P.op('dve', lambda e, pg=pg, st=st: e.tensor_copy(out=kr[:, st, :], in_=pg[:, 384:416]), reads=[pgk], writes=[('kr', st)])
                ss3v = ss3[:].rearrange('p (s j) -> p s j', j=3)
                rtv = rtmpA[:, 0:8].rearrange('p (s j) -> p s j', j=2)
                P.op('pool', lambda e: e.tensor_scalar(out=rtv[:, :, 0], in0=ss3v[:, :, 0], scalar1=1.0 / 256, scalar2=EPS, op0=ALU.mult, op1=ALU.add), reads=[('ss3', st, 0) for st in range(4)], writes=['rtmpA'])
                P.op('pool', lambda e: e.tensor_scalar(out=rtv[:, :, 1], in0=ss3v[:, :, 1], scalar1=1.0 / 128, scalar2=EPS, op0=ALU.mult, op1=ALU.add), reads=[('ss3', st, 1) for st in range(4)], writes=['rtmpA'])
                P.op('pool', lambda e: e.tensor_tensor(out=r2w[:], in0=rtmpA[:, 0:8], in1=mhalf[:, 0:8], op=ALU.pow), reads=['rtmpA', 'mhalf'], writes=['r2w'])
                yield
                for st in range(4):
                    pg, pgk = pgs[st]
                    cqn, cqnk = Rb5.next()
                    P.op('dve', lambda e, pg=pg, cqn=cqn, st=st: e.tensor_scalar(out=cqn[:, 0:256], in0=pg[:, 0:256], scalar1=r2w[:, 2 * st:2 * st + 1], scalar2=None, op0=ALU.mult), reads=[pgk, 'r2w'], writes=[cqnk])
                    P.op('dve', lambda e, pg=pg, cqn=cqn, st=st: e.tensor_scalar(out=cqn[:, 256:384], in0=pg[:, 256:384], scalar1=r2w[:, 2 * st + 1:2 * st + 2], scalar2=None, op0=ALU.mult), reads=[pgk, 'r2w'], writes=[cqnk])
                    pt, ptk = PG.next()
                    ptb = bfv(pt)
                    transposes([(ptb[:, k * 128:(k + 1) * 128], cqn[:, k * 128:(k + 1) * 128]) for k in range(3)], [cqnk], ptk, ident[:])
                    P.op('dve', lambda e, ptb=ptb, st=st: e.tensor_tensor(out=cqT[:, :, st * 128:(st + 1) * 128], in0=ptb[:, 0:384].rearrange('p (k t) -> p k t', k=3), in1=vfm[:, 16:19].unsqueeze(2).broadcast_to([128, 3, 128]), op=ALU.mult), reads=[ptk, 'vfm'], writes=[('cqT', st)])
                    yield
            def thA_m2(i, sts, R):
                for st in sts:
                    blk = 4 * i + st
                    pa, pak = R['pg'].next()
                    pb, pbk = R['pg'].next()
                    mm_group(pa[:, 0:384], [(cqT[:, kc, st * 128:(st + 1) * 128], wuq[:, kc, 0:384]) for kc in range(2)], ['wuq', ('cqT', st)], pak)
                    yield
                    mm_group(pb[:, 0:384], [(cqT[:, kc, st * 128:(st + 1) * 128], wuq[:, kc, 384:768]) for kc in range(2)], ['wuq', ('cqT', st)], pbk)
                    yield
                    qf, qfk = R['f1'].next()
                    P.op('act', lambda e, pa=pa, qf=qf: e.activation(out=qf[:, 0:384], in_=pa[:, 0:384], func=AF.Copy), reads=[pak], writes=[qfk])
                    P.op('act', lambda e, pb=pb, qf=qf: e.activation(out=qf[:, 384:768], in_=pb[:, 0:384], func=AF.Copy), reads=[pbk], writes=[qfk])
                    yield
                    pa2, pa2k = R['pg'].next()
                    pb2, pb2k = R['pg'].next()
                    mm_group(pa2[:], [(cqT[:, 2, st * 128:(st + 1) * 128], wukv[:, 0:512])], ['wukv', ('cqT', st)], pa2k)
                    yield
                    mm_group(pb2[:], [(cqT[:, 2, st * 128:(st + 1) * 128], wukv[:, 512:1024])], ['wukv', ('cqT', st)], pb2k)
                    yield
                    kvf, kvfk = R['f1'].next()
                    P.op('act', lambda e, pa2=pa2, kvf=kvf: e.activation(out=kvf[:, 0:512], in_=pa2[:], func=AF.Copy), reads=[pa2k], writes=[kvfk])
                    P.op('act', lambda e, pb2=pb2, kvf=kvf: e.activation(out=kvf[:, 512:1024], in_=pb2[:], func=AF.Copy), reads=[pb2k], writes=[kvfk])
                    yield
                    qf3 = qf[:, 0:768].rearrange('p (h d) -> p h d', h=8)
                    kvf3 = kvf[:].rearrange('p (h d) -> p h d', h=8)
                    sq, sqk = R['b1'].next()
                    P.op('act', lambda e, qf=qf, sq=sq: e.activation(out=sq[:, 0:768], in_=qf[:, 0:768], func=AF.Square), reads=[qfk], writes=[sqk])
                    yield
                    P.op('dve', lambda e, sq=sq: e.tensor_reduce(out=R['ssqk'][:, 0:8], in_=sq[:, 0:768].rearrange('p (h d) -> p h d', h=8), axis=AX.X, op=ALU.add), reads=[sqk], writes=[(R['n'] + 'ssqk', 0)])
                    yield
                    sq2, sq2k = R['b1'].next()
                    P.op('act', lambda e, kvf=kvf, sq2=sq2: e.activation(out=sq2[:], in_=kvf[:], func=AF.Square), reads=[kvfk], writes=[sq2k])
                    yield
                    P.op('dve', lambda e, sq2=sq2: e.tensor_reduce(out=R['ssqk'][:, 8:16], in_=sq2[:].rearrange('p (h d) -> p h d', h=8)[:, :, 0:64], axis=AX.X, op=ALU.add), reads=[sq2k], writes=[(R['n'] + 'ssqk', 1)])
                    yield
                    P.op('dve', lambda e, st=st: e.tensor_scalar(out=R['ssqk'][:, 8:16], in0=R['ssqk'][:, 8:16], scalar1=ss3[:, st * 3 + 2:st * 3 + 3], scalar2=None, op0=ALU.add), reads=[(R['n'] + 'ssqk', 1), ('ss3', st, 2)], writes=[(R['n'] + 'ssqk', 1)])
                    yield
                    rsqrt(R['rqk'][:], R['ssqk'][:], 1.0 / 96, 16, [(R['n'] + 'ssqk', 0), (R['n'] + 'ssqk', 1)], [(R['n'] + 'rqk')], tmp=R['rtmp'], tmpk=R['n'] + 'rtmpA')
                    yield
                    P.op('act', lambda e, kvf3=kvf3, blk=blk: e.activation(out=Vaug[:, blk, :, 0:64], in_=kvf3[:, :, 64:128], func=AF.Copy), reads=[kvfk], writes=[('V', blk)])
                    yield
                    qb, qbk = R['b1'].next()
                    qb3 = qb[:, 0:768].rearrange('p (h d) -> p h d', h=8)
                    P.op('dve', lambda e, qb3=qb3, qf3=qf3: e.tensor_tensor(out=qb3[:, :, 0:64], in0=qf3[:, :, 0:64], in1=R['rqk'][:, 0:8].unsqueeze(2).broadcast_to([128, 8, 64]), op=ALU.mult), reads=[qfk, (R['n'] + 'rqk')], writes=[qbk])
                    yield
                    kb, kbk = R['b1'].next()
                    kb3 = kb[:, 0:768].rearrange('p (h d) -> p h d', h=8)
                    P.op('dve', lambda e, kb3=kb3, kvf3=kvf3: e.tensor_tensor(out=kb3[:, :, 0:64], in0=kvf3[:, :, 0:64], in1=R['rqk'][:, 8:16].unsqueeze(2).broadcast_to([128, 8, 64]), op=ALU.mult), reads=[kvfk, (R['n'] + 'rqk')], writes=[kbk])
                    yield
                    xr, xrk = R['qk'].next()
                    tr_, trk = R['qk'].next()
                    xr3 = xr[:].rearrange('p (h d) -> p h d', h=16)
                    tr3 = tr_[:].rearrange('p (h d) -> p h d', h=16)
                    P.op('dve', lambda e, xr3=xr3, qf3=qf3: e.tensor_tensor(out=xr3[:, 0:8, :], in0=qf3[:, :, 64:96], in1=R['rqk'][:, 0:8].unsqueeze(2).broadcast_to([128, 8, 32]), op=ALU.mult), reads=[qfk, (R['n'] + 'rqk')], writes=[xrk])
                    yield
                    P.op('dve', lambda e, xr3=xr3, st=st: e.tensor_tensor(out=xr3[:, 8:16, :], in0=kr[:, st, :].unsqueeze(1).broadcast_to([128, 8, 32]), in1=R['rqk'][:, 8:16].unsqueeze(2).broadcast_to([128, 8, 32]), op=ALU.mult), reads=[('kr', st), (R['n'] + 'rqk')], writes=[xrk])
                    yield
                    P.op('dve', lambda e, xr=xr: e.tensor_tensor(out=xr[:].rearrange('p (a h d) -> p a h d', a=2, h=8), in0=xr[:].rearrange('p (a h d) -> p a h d', a=2, h=8), in1=vrow[:, 0:64].rearrange('p (a d) -> p a d', a=2).unsqueeze(2).broadcast_to([128, 2, 8, 32]), op=ALU.mult), reads=[xrk, 'vrow'], writes=[xrk])
                    yield
                    cosb = cosT[:, blk, :].unsqueeze(1).broadcast_to([128, 16, 16])
                    sinb = sinT[:, blk, :].unsqueeze(1).broadcast_to([128, 16, 16])
                    P.op('dve', lambda e, xr3=xr3, tr3=tr3, sinb=sinb: e.scalar_tensor_tensor(out=tr3[:, :, 0:16], in0=xr3[:, :, 16:32], scalar=-1.0, in1=sinb, op0=ALU.mult, op1=ALU.mult), reads=[xrk, 'sinT'], writes=[trk])
                    yield
                    P.op('dve', lambda e, xr3=xr3, tr3=tr3, sinb=sinb: e.tensor_tensor(out=tr3[:, :, 16:32], in0=xr3[:, :, 0:16], in1=sinb, op=ALU.mult), reads=[xrk, 'sinT'], writes=[trk])
                    yield
                    P.op('dve', lambda e, xr3=xr3, cosb=cosb: e.tensor_tensor(out=xr3[:, :, 0:16], in0=xr3[:, :, 0:16], in1=cosb, op=ALU.mult), reads=[xrk, trk, 'cosT'], writes=[xrk])
                    P.op('dve', lambda e, xr3=xr3, cosb=cosb: e.tensor_tensor(out=xr3[:, :, 16:32], in0=xr3[:, :, 16:32], in1=cosb, op=ALU.mult), reads=[xrk, trk, 'cosT'], writes=[xrk])
                    yield
                    P.op('dve', lambda e, xr3=xr3, tr3=tr3, qb3=qb3: e.tensor_tensor(out=qb3[:, :, 64:96], in0=xr3[:, 0:8, :], in1=tr3[:, 0:8, :], op=ALU.add), reads=[xrk, trk], writes=[qbk])
                    yield
                    P.op('dve', lambda e, xr3=xr3, tr3=tr3, kb3=kb3: e.tensor_tensor(out=kb3[:, :, 64:96], in0=xr3[:, 8:16, :], in1=tr3[:, 8:16, :], op=ALU.add), reads=[xrk, trk], writes=[kbk])
                    yield
                    pt, ptk = R['pg'].next()
                    ptb = bfv(pt)
                    transposes([(ptb[0:96, h * 128:(h + 1) * 128], qb3[:, h, :]) for h in range(8)], [qbk], ptk, ident[:])
                    yield
                    pt2, pt2k = R['pg'].next()
                    pt2b = bfv(pt2)
                    transposes([(pt2b[0:96, h * 128:(h + 1) * 128], kb3[:, h, :]) for h in range(8)], [kbk], pt2k, ident[:])
                    yield
                    P.op('dve', lambda e, ptb=ptb, st=st: e.tensor_scalar(out=QT[0:96, :, st * 128:(st + 1) * 128], in0=ptb[0:96, :].rearrange('p (h t) -> p h t', h=8), scalar1=vfm[0:96, 19:20], scalar2=None, op0=ALU.mult), reads=[ptk, 'vfm'], writes=[('QT', st)])
                    yield
                    P.op('dve', lambda e, pt2b=pt2b, blk=blk: e.tensor_scalar(out=KT[0:96, :, blk * 128:(blk + 1) * 128], in0=pt2b[0:96, :].rearrange('p (h t) -> p h t', h=8), scalar1=vfm[0:96, 20:21], scalar2=None, op0=ALU.mult), reads=[pt2k, 'vfm'], writes=[('KT', blk)])
                    yield
                return
                yield
            def thA(i, wide=False):
                if wide:
                    yield from thA_head_wide(i)
                    g0 = thA_m2(i, [0, 2], RES_A0)
                    g1 = thA_m2(i, [1, 3], RES_B)
                    al = [True, True]
                    for _ in range(14):
                        next(g0)
                        yield
                    while al[0] or al[1]:
                        for j, g in enumerate((g0, g1)):
                            if al[j]:
                                try:
                                    next(g)
                                except StopIteration:
                                    al[j] = False
                        yield
                else:
                    yield from thA_head(i)
                    yield from thA_m2(i, range(4), RES_A)
            def thM_pre(i):
                cur['hT'] = hTb[i % 2]
                cur['hTk'] = [('hT', i % 2, st) for st in range(4)]
                wcg, wcgk = W.next('conv_c')
                wxi, wxik = W.next('conv_x')
                for c in range(4):
                    pc, pck = cur['PG'].next()
                    mm_group(pc[:], [(wcg[:, k, c * 128:(c + 1) * 128], cur['hT'][:, k, :]) for k in range(8)], [wcgk] + cur['hTk'], pck)
                    yield
                    px, pxk = cur['PG'].next()
                    mm_group(px[:], [(wxi[:, k, c * 128:(c + 1) * 128], cur['hT'][:, k, :]) for k in range(8)], [wxik] + cur['hTk'], pxk)
                    yield
                    xs, xsk = Rf5.next()
                    P.op('act', lambda e, px=px, xs=xs: e.activation(out=xs[:], in_=px[:], func=AF.Copy), reads=[pxk], writes=[xsk])
                    P.op('dve', lambda e, pc=pc, xs=xs, c=c: e.tensor_tensor(out=zc[:, c, 2:514], in0=pc[:], in1=xs[:], op=ALU.mult), reads=[pck, xsk], writes=[('z', c)])
                    y0, y0k = Rf5.next()
                    P.op('dve', lambda e, y0=y0, c=c: e.tensor_scalar(out=y0[:], in0=zc[:, c, 2:514], scalar1=vfm[:, 21 + 8 + c:21 + 8 + c + 1], scalar2=vfm[:, 33 + c:34 + c], op0=ALU.mult, op1=ALU.add), reads=[('z', c), 'vfm'], writes=[y0k])
                    y1, y1k = Rf5.next()
                    P.op('dve', lambda e, y0=y0, y1=y1, c=c: e.scalar_tensor_tensor(out=y1[:], in0=zc[:, c, 1:513], scalar=vfm[:, 21 + 4 + c:21 + 4 + c + 1], in1=y0[:], op0=ALU.mult, op1=ALU.add), reads=[('z', c), 'vfm', y0k], writes=[y1k])
                    P.op('dve', lambda e, y1=y1, c=c: e.scalar_tensor_tensor(out=yall[:, c, :], in0=zc[:, c, 0:512], scalar=vfm[:, 21 + c:21 + c + 1], in1=y1[:], op0=ALU.mult, op1=ALU.add), reads=[('z', c), 'vfm', y1k], writes=[('yall', c)])
                    P.op('dve', lambda e, c=c: e.tensor_copy(out=zc[:, c, 0:2], in_=zc[:, c, 512:514]), reads=[('z', c)], writes=[('z', c)])
                    yield
                wbg, wbgk = W.next('conv_b')
                for c in range(4):
                    pbg, pbgk = cur['PG'].next()
                    mm_group(pbg[:], [(wbg[:, k, c * 128:(c + 1) * 128], cur['hT'][:, k, :]) for k in range(8)], [wbgk] + cur['hTk'], pbgk)
                    yield
                    P.op('dve', lambda e, pbg=pbg, c=c: e.tensor_tensor(out=yall[:, c, :], in0=yall[:, c, :], in1=pbg[:], op=ALU.mult), reads=[('yall', c), pbgk], writes=[('yall', c)])
                    yield
                wg, wgk = W.next('gate1')
                for c in range(4):
                    gate_chunk(wg, wgk, c, yall[:, c, :], [('yall', c)])
                    yield
                yield from merge_branch(1, True)
                return
                yield
            def thM_post(i):
                t0 = i * T
                wg, wgk = W.next('gate0')
                yield from tm_to_ys(ya, 'ya', wg, wgk)
                yield from merge_branch(0, False)
                wv2, wv2k = W.next('sg_v')
                vcs = []
                for st in range(4):
                    pv_, pvk_ = cur['PG'].next()
                    mm_group(pv_[:], [(cur['hT'][:, k, st * 128:(st + 1) * 128], wv2[:, k, :]) for k in range(8)], [wv2k, ('hT', i % 2, st)], pvk_)
                    vn, vnk = Rf5.next()
                    P.op('act', lambda e, pv_=pv_, vn=vn: e.activation(out=vn[:], in_=pv_[:], func=AF.Copy), reads=[pvk_], writes=[vnk])
                    vcs.append((vn, vnk))
                    yield
                wg, wgk = W.next('gate2')
                for st in range(4):
                    vn, vnk = vcs[st]
                    P.op('dve', lambda e, vn=vn, st=st: e.bn_stats(out=st6s[:, st, :], in_=vn[:]), reads=[vnk], writes=[('st6', st)])
                    P.op('dve', lambda e, st=st: e.bn_aggr(out=mvs[:, st, :], in_=st6s[:, st, :]), reads=[('st6', st)], writes=[('mv', st)])
                P.op('pool', lambda e: e.tensor_scalar(out=rtmpB[:, 0:4], in0=mvs[:, :, 1], scalar1=EPS, scalar2=None, op0=ALU.add), reads=[('mv', st) for st in range(4)], writes=['rtmpB'])
                P.op('pool', lambda e: e.tensor_tensor(out=rs4[:], in0=rtmpB[:, 0:4], in1=mhalf[:, 0:4], op=ALU.pow), reads=['rtmpB', 'mhalf'], writes=['rs4'])
                yield
                for g in range(4):
                    gate_pre(wg, wgk, g)
                    yield
                P.op('dve', lambda e: e.scalar_tensor_tensor(out=nb4[:], in0=mvs[:, :, 0], scalar=-1.0, in1=rs4[:], op0=ALU.mult, op1=ALU.mult), reads=[('mv', st) for st in range(4)] + ['rs4'], writes=['nb4'])
                for st in range(4):
                    vn, vnk = vcs[st]
                    P.op('act', lambda e, vn=vn, st=st: e.activation(out=vln[:, st, :], in_=vn[:], func=AF.Identity, bias=nb4[:, st:st + 1], scale=rs4[:, st:st + 1]), reads=[vnk, 'nb4', 'rs4'], writes=['ya'])
                    yield
                wu, wuk = W.next('sg_u')
                for g in range(4):
                    pm, pmk = cur['PG'].next()

                    def mix(e, pm=pm, g=g):
                        for st in range(4):
                            ins = e.matmul(pm[:, st * 128:(st + 1) * 128], vln[:, st, g * 128:(g + 1) * 128], wTs[:, g, :], start=True, stop=True)
                        return ins
                    P.op('pe', mix, reads=['ya', 'wTs'], writes=[pmk])
                    yield
                    pu, puk = cur['PG'].next()
                    mm_group(pu[:], [(wu[:, k, g * 128:(g + 1) * 128], cur['hT'][:, k, :]) for k in range(8)], [wuk] + cur['hTk'], puk)
                    yield
                    mt, mtk = Rb5.next()
                    P.op('dve', lambda e, pm=pm, mt=mt, g=g: e.scalar_tensor_tensor(out=mt[:].rearrange('p (s t) -> p s t', s=4), in0=pm[:].rearrange('p (s t) -> p s t', s=4), scalar=vfm[:, 71 + g:72 + g], in1=Cg[:, g, :].unsqueeze(1).broadcast_to([128, 4, 128]), op0=ALU.mult, op1=ALU.add), reads=[pmk, 'vfm', 'Cg'], writes=[mtk])
                    P.op('dve', lambda e, pu=pu, mt=mt, g=g: e.tensor_tensor(out=mt[:], in0=pu[:], in1=mt[:], op=ALU.mult), reads=[puk, mtk], writes=[mtk])
                    gate_post(g, mt[:], [mtk])
                    yield
                yield from merge_branch(2, False)
                wq_, wqk = W.next('memq')
                wg, wgk = W.next('gate3')
                pqs = []
                for st in range(4):
                    pq, pqk = cur['PG'].next()
                    mm_group(pq[:], [(cur['hT'][:, k, st * 128:(st + 1) * 128], wq_[:, k, :]) for k in range(8)], [wqk, ('hT', i % 2, st)], pqk)
                    pqs.append((pq, pqk))
                    yield
                mqfs = []
                for st in range(4):
                    pq, pqk = pqs[st]
                    mqf, mqfk = Rf5.next()
                    P.op('act', lambda e, pq=pq, mqf=mqf: e.activation(out=mqf[:], in_=pq[:], func=AF.Copy), reads=[pqk], writes=[mqfk])
                    mqfs.append((mqf, mqfk))
                for st in range(4):
                    mqf, mqfk = mqfs[st]
                    sq, sqk = Rb5.next()
                    P.op('act', lambda e, mqf=mqf, sq=sq: e.activation(out=sq[:], in_=mqf[:], func=AF.Square), reads=[mqfk], writes=[sqk])
                    P.op('dve', lambda e, sq=sq, st=st: e.tensor_reduce(out=ss16[:, st * 4:(st + 1) * 4], in_=sq[:].rearrange('p (h d) -> p h d', h=4), axis=AX.X, op=ALU.add), reads=[sqk], writes=[('ss16', st)])
                rsqrt(r16[:], ss16[:], 1.0 / 128, 16, [('ss16', st) for st in range(4)], ['r16'], tmp=rtmp16, tmpk='rtmp16')
                yield
                for c in range(4):
                    gate_pre(wg, wgk, c)
                    yield
                for st in range(4):
                    mqf, mqfk = mqfs[st]
                    mqn, mqnk = Rb5.next()
                    P.op('dve', lambda e, mqf=mqf, mqn=mqn, st=st: e.tensor_tensor(out=mqn[:].rearrange('p (h d) -> p h d', h=4), in0=mqf[:].rearrange('p (h d) -> p h d', h=4), in1=r16[:, st * 4:(st + 1) * 4].unsqueeze(2).broadcast_to([128, 4, 128]), op=ALU.mult), reads=[mqfk, 'r16'], writes=[mqnk])
                    pt, ptk = cur['PG'].next()
                    ptb = bfv(pt)
                    transposes([(ptb[:, h * 128:(h + 1) * 128], mqn[:, h * 128:(h + 1) * 128]) for h in range(4)], [mqnk], ptk, ident[:])
                    yield
                    P.op('dve', lambda e, ptb=ptb, st=st: e.tensor_scalar(out=QmT[:, :, st * 128:(st + 1) * 128], in0=ptb[:, 0:512].rearrange('p (h t) -> p h t', h=4), scalar1=vfm[:, 37:38], scalar2=None, op0=ALU.mult), reads=[ptk, 'vfm'], writes=[('QmT', st)])
                QmT_keys = [('QmT', st) for st in range(4)]
                pR, pRk = (ps[7], ('ps', 7))
                mpairs = [(h, mb) for h in range(4) for mb in range(2)]
                mpend = []
                for step in range(len(mpairs) + 1):
                    if step < len(mpairs):
                        h, mb = mpairs[step]
                        pS, pSk = PGM.next()
                        P.op('pe', lambda e, pS=pS, h=h, mb=mb: e.matmul(pS[:], KmT[:, h, mb * 128:(mb + 1) * 128], QmT[:, h, :], start=True, stop=True), reads=[('KmT', mb)] + QmT_keys, writes=[pSk])
                        PT, PTk = Rb5.next()
                        P.op('act', lambda e, pS=pS, PT=PT: e.activation(out=PT[:], in_=pS[:], func=AF.Exp, scale=128 ** (-0.5)), reads=[pSk], writes=[PTk])
                        mpend.append((h, mb, PT, PTk))
                        yield
                    if step >= 1:
                        h, mb, PT, PTk = mpend.pop(0)
                        pO, pOk = (ps[5 + h % 2], ('ps', 5 + h % 2))

                        def pvm(e, PT=PT, h=h, mb=mb, pO=pO, pR=pR):
                            for qs in range(4):
                                e.matmul(pO[:, qs * 128:(qs + 1) * 128], PT[:, qs * 128:(qs + 1) * 128], Vm[:, mb, h, :], start=mb == 0 and qs == 0, stop=mb == 1, skip_group_check=True)
                            for qs in range(4):
                                ins = e.matmul(pR[:, h * 8 + qs * 2:h * 8 + qs * 2 + 2], PT[:, qs * 128:(qs + 1) * 128], ones_bf[:, 0:2], start=h == 0 and mb == 0 and qs == 0, stop=mb == 1, skip_group_check=True)
                            return ins
                        P.op('pe', pvm, reads=[PTk, ('Vm', mb), 'ones_bf'], writes=[pOk, pRk])
                        yield
                        if mb == 1:
                            P.op('dve', lambda e, pR=pR, h=h: e.reciprocal(out=rec[:], in_=pR[:, h * 8:h * 8 + 8].rearrange('p (q t) -> p q t', t=2)[:, :, 0]), reads=[pRk], writes=['rec'])
                            P.op('dve', lambda e, pO=pO, h=h: e.tensor_tensor(out=ya[:, :, h * 128:(h + 1) * 128], in0=pO[:].rearrange('p (q d) -> p q d', q=4), in1=rec[:].unsqueeze(2).broadcast_to([128, 4, 128]), op=ALU.mult), reads=[pOk, 'rec'], writes=['ya'])
                yield from tm_post(ya, 'ya')
                yield from merge_branch(3, False)
                acc_keys = [('acc', dc) for dc in range(8)]
                wo0, wo0k = W.next('wo0')
                wo1, wo1k = W.next('wo1')
                for st in range(4):
                    xw, xwk = Rxw.next()
                    P.op('sp', lambda e, xw=xw, st=st, xsrc=xsrc, t0=t0: e.dma_start(out=xw[:], in_=xsrc[t0 + st * 128:t0 + (st + 1) * 128, :]), reads=[(skey, i, st)], writes=[xwk], dsem=xwk)
                    for half, (wo, wok) in enumerate([(wo0, wo0k), (wo1, wo1k)]):
                        po, pok = cur['PG'].next()
                        mm_group(po[:], [(acc[:, kc, st * 128:(st + 1) * 128], wo[:, kc, :]) for kc in range(8)], [wok] + acc_keys, pok)
                        yield
                        P.op('dve', lambda e, po=po, xw=xw, half=half: e.scalar_tensor_tensor(out=xw[:, half * 512:(half + 1) * 512], in0=po[:], scalar=0.25, in1=xw[:, half * 512:(half + 1) * 512], op0=ALU.mult, op1=ALU.add), reads=[pok, xwk], writes=[xwk])
                    P.op('sp', lambda e, dst=dst, t0=t0, st=st, xw=xw: e.dma_start(out=dst[t0 + st * 128:t0 + (st + 1) * 128, :], in_=xw[:]), reads=[xwk], writes=[(dkey, i, st)], dsem=('xst', xwk[1]))
                    yield
                return
                yield
            def attention(i):
                nkb = 4 * (i + 1)
                pairs = [(h, kb_) for h in range(8) for kb_ in range(nkb)]
                pend = []
                psO_cur = {}
                QT_keys = [("QT", st) for st in range(4)]
                for step in range(len(pairs) + ATT_SKEW):
                    new = None
                    if step < len(pairs):
                        h, kb_ = pairs[step]
                        j = kb_ - 4 * i
                        c0 = 128 * j if j > 0 else 0
                        pS, pSk = PS_.next()
                        P.op("pe", lambda e, pS=pS, h=h, kb_=kb_, c0=c0: e.matmul(
                            pS[:, c0:512], KT[0:96, h, kb_ * 128:(kb_ + 1) * 128], QT[0:96, h, c0:512],
                            start=True, stop=True),
                            reads=[("KT", kb_)] + QT_keys, writes=[pSk])
                        PT, PTk = Rb5.next()
                        P.op("act", lambda e, pS=pS, PT=PT, c0=c0: e.activation(
                            out=PT[:, c0:512], in_=pS[:, c0:512], func=AF.Exp, scale=96 ** -0.5),
                            reads=[pSk], writes=[PTk])
                        if j >= 0:
                            P.op("dve", lambda e, PT=PT, c0=c0: e.tensor_tensor(
                                out=PT[:, c0:c0 + 128], in0=PT[:, c0:c0 + 128], in1=tri[:], op=ALU.mult),
                                reads=[PTk, "tri"], writes=[PTk])
                        new = (h, kb_, j, PT, PTk)
                    if new is not None:
                        pend.append(new)
                    if step >= ATT_SKEW:
                        h, kb_, j, PT, PTk = pend.pop(0)
                        if kb_ == 0:
                            psO_cur[h] = PO.next()
                        pO, pOk = psO_cur[h]

                        def pv(e, pO=pO, PT=PT, h=h, kb_=kb_, j=j):
                            for qs in range(max(j, 0), 4):
                                ins = e.matmul(pO[:, qs * 128:qs * 128 + 65], PT[:, qs * 128:(qs + 1) * 128],
                                               Vaug[:, kb_, h, :], start=(kb_ == 0 and qs == 0),
                                               stop=(kb_ == 4 * i + qs), skip_group_check=True)
                            return ins
                        P.op("pe", pv, reads=[PTk, ("V", kb_)], writes=[pOk])
                        if kb_ == nkb - 1:
                            pO3 = pO[:].rearrange("p (q d) -> p q d", q=4)
                            P.op("dve", lambda e, pO3=pO3: e.reciprocal(out=rec[:], in_=pO3[:, :, 64]),
                                 reads=[pOk], writes=["rec"])
                            P.op("dve", lambda e, pO3=pO3, h=h: e.tensor_tensor(
                                out=ya[:, :, h * 64:(h + 1) * 64], in0=pO3[:, :, 0:64],
                                in1=rec[:].unsqueeze(2).broadcast_to([128, 4, 64]), op=ALU.mult),
                                reads=[pOk, "rec"], writes=["ya"])
            xsrc = x_d if l == 0 else x1_d
            dst = y_d if last else x1_d
            dkey = "y" if last else "x1"
            skey = "x" if xsrc is x_d else "x1"
            cur["PG"] = PG
            ga = thA(0, True)
            for i in range(NT):
                cur["hT"] = hTb[i % 2]
                cur["hTk"] = [("hT", i % 2, st) for st in range(4)]
                cur["PG"] = PGB if i > 0 else PG
                gm = thM_pre(i)
                a_alive = m_alive = True
                if i == 0:
                    for _ in ga:
                        pass
                    a_alive = False
                    cur["PG"] = PGB
                while a_alive or m_alive:
                    if a_alive:
                        for _ in range(PRE_A_STEPS if m_alive else 1000):
                            try:
                                next(ga)
                            except StopIteration:
                                a_alive = False
                                break
                    if m_alive:
                        try:
                            next(gm)
                        except StopIteration:
                            m_alive = False
                cur["PG"] = PG
                attention(i)
                cur["PG"] = PGB
                ga = thA(i + 1) if i + 1 < NT else iter(())
                gm = thM_post(i)
                a_alive = m_alive = True
                credit = 0.0
                while m_alive:
                    credit += A_RATIO
                    while credit >= 1.0 and a_alive:
                        credit -= 1.0
                        try:
                            next(ga)
                        except StopIteration:
                            a_alive = False
                    try:
                        next(gm)
                    except StopIteration:
                        m_alive = False
            for _ in ga:
                pass
            cur["PG"] = PG
    try:
        if not done_:
            _layers()
            assert W.cur == len(sched), (W.cur, len(sched))
    except _Stop:
        pass
    P.op("sp", None, reads=[("y", i, st) for i in range(NT) for st in range(4)] + final_keys)
    P.emit()
    es.close()
    return nc


def _pack(inputs):
    f = lambda k: np.asarray(inputs[k], dtype=np.float32)
    vfm = np.zeros((2, 128, NVF), np.float32)
    vrow = np.zeros((2, NVR), np.float32)
    for l in range(2):
        vfm[l, :, 0:8] = f("norm_g")[l].reshape(8, 128).T
        vfm[l, :, 8:16] = f("mem_norm_g")[l].reshape(8, 128).T
        vfm[l, :, 16:18] = f("cq_norm_g")[l].reshape(2, 128).T
        vfm[l, :, 18] = f("ckv_norm_g")[l]
        vfm[l, :, 19] = 1.0
        vfm[l, 0:64, 19] = f("mla_q_norm_g")[l][0:64]
        vfm[l, :, 20] = 1.0
        vfm[l, 0:64, 20] = f("mla_k_norm_g")[l][0:64]
        cw = f("conv_w")[l]
        for j in range(3):
            vfm[l, :, 21 + j * 4:21 + (j + 1) * 4] = cw[j].reshape(4, 128).T
        vfm[l, :, 33:37] = f("conv_b")[l].reshape(4, 128).T
        vfm[l, :, 37] = f("mem_q_norm_g")[l]
        vfm[l, :, 38] = f("mem_k_norm_g")[l]
        vfm[l, :, 39:71] = f("b_merge")[l].reshape(32, 128).T
        vrow[l, 0:32] = f("mla_q_norm_g")[l][64:96]
        vrow[l, 32:64] = f("mla_k_norm_g")[l][64:96]
        vfm[l, :, 71:75] = f("sg_ln_g")[l].reshape(4, 128).T
        vfm[l, :, 75:79] = f("sg_ln_b")[l].reshape(4, 128).T
        vrow[l, 64:576] = f("b_spatial")[l].reshape(512)
    return vfm, vrow


_NC_CACHE = {}


def _in_maps(inputs, cores, x_override=None):
    vfm, vrow = _pack(inputs)
    f = lambda k: np.ascontiguousarray(np.asarray(inputs[k], dtype=np.float32))
    shared = dict(vfm=vfm, vrow=vrow, w_in=f("w_in"), w_uq=f("w_uq"), w_ukv=f("w_ukv"),
                  w_spatial=f("w_spatial"), w_mem_kv=f("w_mem_kv"), w_branch=f("w_branch"), w_out=f("w_out"))
    x = f("x") if x_override is None else x_override
    mem = f("mem")
    pos = np.ascontiguousarray(np.asarray(inputs["positions"], dtype=np.int32))
    maps = []
    for b in cores:
        m = dict(shared)
        m["x"] = np.ascontiguousarray(x[b])
        m["mem"] = np.ascontiguousarray(mem[b])
        m["pos"] = np.ascontiguousarray(pos[b].reshape(16, 128))
        maps.append(m)
    return maps


def kernel(**inputs):
    if "nc" not in _NC_CACHE:
        _NC_CACHE["nc"] = build(2, 0)
    nc = _NC_CACHE["nc"]
    maps = _in_maps(inputs, list(range(8)))
    res = run_bass_kernel_spmd(nc, maps, core_ids=list(range(8)))
    out = np.stack([np.asarray(r["y"], dtype=np.float32) for r in res.results], axis=0)
    return out
```
